# Optimizing a Trainium2 kernel written in Bass

```python
import jax, jax.numpy as jnp
from jax import lax
import numpy as np

D_MODEL = 1024
BATCH = 2
SEQ = 8192
DEPTH = 2

HEAD_DIM = 64
SB_HEADS = 8
MOBA_HEADS = 8
SB_WIDTH = SB_HEADS * HEAD_DIM
MOBA_WIDTH = MOBA_HEADS * HEAD_DIM
SB_Q_BLOCK = 128
MOBA_BLOCK = 256
MOBA_TOPK = 3
MOBA_Q_CHUNK = 128
ROPE_THETA = 500000.0
ROPE_DIM = HEAD_DIM // 4
D_FF = -(-8 * D_MODEL // (3 * 256)) * 256
DEEPNORM_ALPHA = (2 * DEPTH) ** 0.25
DEEPNORM_BETA = (8 * DEPTH) ** -0.25
LN_EPS = 1e-5
IN_COLS = 3 * SB_WIDTH + 3 * MOBA_WIDTH + 2 * D_MODEL

kernel_name = "hybrid_stickbreaking_moba_swiglu_deepnorm"


def layer_norm(x, g, b):
    xf = x.astype(jnp.float32)
    mu = jnp.mean(xf, axis=-1, keepdims=True)
    var = jnp.mean(jnp.square(xf - mu), axis=-1, keepdims=True)
    y = (xf - mu) * lax.rsqrt(var + LN_EPS) * g.astype(jnp.float32) + b.astype(jnp.float32)
    return y.astype(x.dtype)


def partial_rope(x, pos):
    half = ROPE_DIM // 2
    inv_freq = ROPE_THETA ** (-jnp.arange(0, ROPE_DIM, 2, dtype=jnp.float32) / ROPE_DIM)
    ang = pos.astype(jnp.float32)[:, None] * inv_freq[None, :]
    cos, sin = jnp.cos(ang), jnp.sin(ang)
    xf = x.astype(jnp.float32)
    x1, x2, rest = xf[..., :half], xf[..., half:ROPE_DIM], xf[..., ROPE_DIM:]
    out = jnp.concatenate([x1 * cos - x2 * sin, x2 * cos + x1 * sin, rest], axis=-1)
    return out.astype(x.dtype)


def stick_breaking_attention(q, k, v):
    B, H, S, dh = q.shape
    nblk = S // SB_Q_BLOCK
    scale = dh ** -0.5
    kpos = jnp.arange(S)
    qb = q.reshape(B, H, nblk, SB_Q_BLOCK, dh).transpose(2, 0, 1, 3, 4)

    def block(args):
        qi, i = args
        qpos = i * SB_Q_BLOCK + jnp.arange(SB_Q_BLOCK)
        z = jnp.einsum('bhqd,bhkd->bhqk', qi, k).astype(jnp.float32) * scale
        past = kpos[None, :] < qpos[:, None]
        log_keep = jnp.where(past, jax.nn.log_sigmoid(-z), 0.0)
        suffix = lax.cumsum(log_keep, axis=3, reverse=True) - log_keep
        w = jnp.where(past, jnp.exp(jax.nn.log_sigmoid(z) + suffix), 0.0)
        return jnp.einsum('bhqk,bhkd->bhqd', w.astype(v.dtype), v)

    out = lax.map(block, (qb, jnp.arange(nblk)))
    return out.transpose(1, 2, 0, 3, 4).reshape(B, H, S, dh)


def moba_attention(q, k, v):
    B, H, S, dh = q.shape
    nb = -(-S // MOBA_BLOCK)
    pad = nb * MOBA_BLOCK - S
    kp = jnp.pad(k, ((0, 0), (0, 0), (0, pad), (0, 0)))
    vp = jnp.pad(v, ((0, 0), (0, 0), (0, pad), (0, 0)))
    kb = kp.reshape(B, H, nb, MOBA_BLOCK, dh)
    vb = vp.reshape(B, H, nb, MOBA_BLOCK, dh)
    kmean = jnp.mean(kb.astype(jnp.float32), axis=3).astype(k.dtype)
    topk = min(MOBA_TOPK, nb)
    nchunk = S // MOBA_Q_CHUNK
    qc = q.reshape(B, H, nchunk, MOBA_Q_CHUNK, dh).transpose(2, 0, 1, 3, 4)
    bi = jnp.arange(B)[:, None, None, None]
    hi = jnp.arange(H)[None, :, None, None]
    blk_ids = jnp.arange(nb)
    scale = dh ** -0.5

    def chunk(args):
        qi, c = args
        qpos = c * MOBA_Q_CHUNK + jnp.arange(MOBA_Q_CHUNK)
        own = (c * MOBA_Q_CHUNK) // MOBA_BLOCK
        gate = jnp.einsum('bhqd,bhnd->bhqn', qi, kmean).astype(jnp.float32)
        gate = jnp.where(blk_ids < own, gate, -jnp.inf)
        _, idx = lax.top_k(gate, topk)
        valid = idx < own
        k_sel = kb[bi, hi, idx]
        v_sel = vb[bi, hi, idx]
        s_sel = jnp.einsum('bhqd,bhqnkd->bhqnk', qi, k_sel).astype(jnp.float32) * scale
        s_sel = jnp.where(valid[..., None], s_sel, -jnp.inf).reshape(B, H, MOBA_Q_CHUNK, topk * MOBA_BLOCK)
        k_own = lax.dynamic_slice_in_dim(kb, own, 1, axis=2)[:, :, 0]
        v_own = lax.dynamic_slice_in_dim(vb, own, 1, axis=2)[:, :, 0]
        s_own = jnp.einsum('bhqd,bhkd->bhqk', qi, k_own).astype(jnp.float32) * scale
        own_pos = own * MOBA_BLOCK + jnp.arange(MOBA_BLOCK)
        s_own = jnp.where(own_pos[None, :] <= qpos[:, None], s_own, -jnp.inf)
        p = jax.nn.softmax(jnp.concatenate([s_sel, s_own], axis=-1), axis=-1).astype(v.dtype)
        p_sel = p[..., :topk * MOBA_BLOCK].reshape(B, H, MOBA_Q_CHUNK, topk, MOBA_BLOCK)
        p_own = p[..., topk * MOBA_BLOCK:]
        return (jnp.einsum('bhqnk,bhqnkd->bhqd', p_sel, v_sel)
                + jnp.einsum('bhqk,bhkd->bhqd', p_own, v_own))

    out = lax.map(chunk, (qc, jnp.arange(nchunk)))
    return out.transpose(1, 2, 0, 3, 4).reshape(B, H, S, dh)


def heads(t, n_heads):
    B, S, _ = t.shape
    return t.reshape(B, S, n_heads, HEAD_DIM).transpose(0, 2, 1, 3)


def merge_heads(t):
    B, H, S, dh = t.shape
    return t.transpose(0, 2, 1, 3).reshape(B, S, H * dh)


def hybrid_mixer(x, w_in, w_branch_sb, w_branch_moba, w_out):
    S = x.shape[1]
    proj = x @ w_in
    cuts = np.cumsum([SB_WIDTH, SB_WIDTH, SB_WIDTH, MOBA_WIDTH, MOBA_WIDTH, MOBA_WIDTH, D_MODEL]).tolist()
    q_sb, k_sb, v_sb, q_mb, k_mb, v_mb, g_sb, g_mb = jnp.split(proj, cuts, axis=-1)
    pos = jnp.arange(S)
    o_sb = stick_breaking_attention(heads(q_sb, SB_HEADS), heads(k_sb, SB_HEADS), heads(v_sb, SB_HEADS))
    o_mb = moba_attention(partial_rope(heads(q_mb, MOBA_HEADS), pos),
                          partial_rope(heads(k_mb, MOBA_HEADS), pos),
                          heads(v_mb, MOBA_HEADS))
    branch_sb = merge_heads(o_sb) @ w_branch_sb
    branch_mb = merge_heads(o_mb) @ w_branch_moba
    merged = jax.nn.sigmoid(g_sb) * branch_sb + jax.nn.sigmoid(g_mb) * branch_mb
    return merged @ w_out


def swiglu(x, w_gate, w_up, w_down):
    return (jax.nn.silu(x @ w_gate) * (x @ w_up)) @ w_down


def setup_inputs(seed: int = 0) -> dict:
    key = jax.random.key(seed)
    ks = jax.random.split(key, 12)
    nrm = lambda k, shape, s: jax.random.normal(k, shape, jnp.float32) * s
    L = DEPTH
    return {
        "x": jax.random.normal(ks[0], (BATCH, SEQ, D_MODEL), jnp.float32),
        "w_in": nrm(ks[1], (L, D_MODEL, IN_COLS), D_MODEL ** -0.5),
        "w_branch_sb": nrm(ks[2], (L, SB_WIDTH, D_MODEL), SB_WIDTH ** -0.5),
        "w_branch_moba": nrm(ks[3], (L, MOBA_WIDTH, D_MODEL), MOBA_WIDTH ** -0.5),
        "w_out": nrm(ks[4], (L, D_MODEL, D_MODEL), D_MODEL ** -0.5 * DEEPNORM_BETA),
        "ln_mix_g": 1.0 + nrm(ks[5], (L, D_MODEL), 0.02),
        "ln_mix_b": nrm(ks[6], (L, D_MODEL), 0.02),
        "w_ffn_gate": nrm(ks[7], (L, D_MODEL, D_FF), D_MODEL ** -0.5),
        "w_ffn_up": nrm(ks[8], (L, D_MODEL, D_FF), D_MODEL ** -0.5),
        "w_ffn_down": nrm(ks[9], (L, D_FF, D_MODEL), D_FF ** -0.5 * DEEPNORM_BETA),
        "ln_ffn_g": 1.0 + nrm(ks[10], (L, D_MODEL), 0.02),
        "ln_ffn_b": nrm(ks[11], (L, D_MODEL), 0.02),
    }


def reference(x, w_in, w_branch_sb, w_branch_moba, w_out, ln_mix_g, ln_mix_b,
              w_ffn_gate, w_ffn_up, w_ffn_down, ln_ffn_g, ln_ffn_b):
    for l in range(DEPTH):
        mix = hybrid_mixer(x, w_in[l], w_branch_sb[l], w_branch_moba[l], w_out[l])
        x = layer_norm(DEEPNORM_ALPHA * x + mix, ln_mix_g[l], ln_mix_b[l])
        ffn = swiglu(x, w_ffn_gate[l], w_ffn_up[l], w_ffn_down[l])
        x = layer_norm(DEEPNORM_ALPHA * x + ffn, ln_ffn_g[l], ln_ffn_b[l])
    return x
```

```python
from contextlib import ExitStack
import os
DBG_GATE = int(os.environ.get('DBG_GATE', '1'))
DBG_NORM = int(os.environ.get('DBG_NORM', '1'))
DBG_MB = int(os.environ.get('DBG_MB', '0'))
import numpy as np
import ml_dtypes
import concourse.bass as bass
import concourse.mybir as mybir
from concourse.bass_utils import run_bass_kernel_spmd

F32 = mybir.dt.float32
BF16 = mybir.dt.bfloat16
AF = mybir.ActivationFunctionType
ALU = mybir.AluOpType
AX = mybir.AxisListType

D = 1024
S = 8192
B = 2
DEPTH = 2
DFF = 2816
NFC = DFF // 128
NG = S // 512
ALPHA = (2 * DEPTH) ** 0.25
EPS = 1e-5
BIG = 30000.0
NEG = -1.0e30
TOK = 2048
NTG = TOK // 512


class Prog:
    SAME_ENGINE_SYNC = True

    def __init__(self, nc):
        self.nc = nc
        self.eng = {"pe": nc.tensor, "act": nc.scalar, "dve": nc.vector,
                    "pool": nc.gpsimd, "sp": nc.sync}
        self.sem = {}
        self.cnt = {}
        self.res = {}
        self.known = {e: {} for e in self.eng}
        self.nwait = 0
        self.nops = 0

    def _sem(self, lane):
        if lane not in self.sem:
            self.sem[lane] = self.nc.alloc_semaphore(name="s_" + lane.replace(":", "_"))
            self.cnt[lane] = 0
        return self.sem[lane]

    def _need(self, e, lane, val):
        if lane == e and (e == "pe" or not self.SAME_ENGINE_SYNC):
            return
        if self.known[e].get(lane, 0) >= val:
            return
        self.eng[e].wait_ge(self._sem(lane), val)
        self.known[e][lane] = val
        self.nwait += 1

    def _deps(self, e, reads, writes):
        for r in reads:
            ent = self.res.get(r)
            if ent and ent[0]:
                self._need(e, *ent[0])
        for w in writes:
            ent = self.res.get(w)
            if ent:
                if ent[0]:
                    self._need(e, *ent[0])
                for lane, val in ent[1].items():
                    self._need(e, lane, val)

    def _record(self, lane, val, reads, writes):
        for r in reads:
            ent = self.res.setdefault(r, [None, {}])
            ent[1][lane] = max(ent[1].get(lane, 0), val)
        for w in writes:
            self.res[w] = [(lane, val), {}]

    def op(self, e, fn, reads=(), writes=(), signal=True):
        self._sem(e)
        self._deps(e, reads, writes)
        ins = fn(self.eng[e])
        val = self.cnt[e] + 1
        if signal:
            ins.then_inc(self.sem[e], 1)
            self.cnt[e] = val
        self._record(e, val, reads, writes)
        self.nops += 1
        return ins

    def dma(self, e, lane, out, in_, reads=(), writes=(), **kw):
        lane = "d:" + lane
        self._sem(lane)
        self._deps(e, reads, writes)
        ins = self.eng[e].dma_start(out=out, in_=in_, **kw)
        val = self.cnt[lane] + 16
        ins.then_inc(self.sem[lane], 16)
        self.cnt[lane] = val
        self._record(lane, val, reads, writes)
        self.nops += 1
        return ins

    def collective(self, lane, kind, rg, in_ap, out_ap, reads=(), writes=()):
        lane = "c:" + lane
        self._sem(lane)
        self._deps("pool", reads, writes)
        ins = self.nc.gpsimd.collective_compute(kind, ALU.bypass, replica_groups=rg,
                                                ins=[in_ap], outs=[out_ap])
        val = self.cnt[lane] + 1
        ins.then_inc(self.sem[lane], 1)
        self.cnt[lane] = val
        self._record(lane, val, reads, writes)
        self.nops += 1
        return ins

    def barrier(self):
        for e in self.eng:
            for lane, val in self.cnt.items():
                if val > 0 and lane != e:
                    if self.known[e].get(lane, 0) < val:
                        self.eng[e].wait_ge(self.sem[lane], val)
                        self.known[e][lane] = val
        self.res = {}

    def finish(self, e="sp"):
        for lane, val in self.cnt.items():
            if val > 0 and lane != e:
                self.eng[e].wait_ge(self.sem[lane], val)


def host_consts():
    c = {}
    k = np.arange(128)[:, None]
    s = np.arange(128)[None, :]
    c["negtri"] = np.where(k >= s, -1.0, 0.0).astype(np.float32)
    c["negones"] = np.full((128, 128), -1.0, np.float32)
    c["ident"] = np.eye(128, dtype=np.float32)
    sk = np.arange(128)[:, None, None] + 128 * np.arange(4)[None, :, None]
    t = np.arange(512)[None, None, :]
    c["msb"] = (sk < t).astype(np.float32)
    c["mmb"] = (sk <= t).astype(np.float32)
    half = 8
    inv = (500000.0 ** (-np.arange(0, 16, 2, dtype=np.float32) / 16)).astype(np.float32)
    ang = (np.arange(S, dtype=np.float32)[:, None] * inv[None, :]).astype(np.float32)
    cos = np.cos(ang).astype(np.float32).T
    sin = np.sin(ang).astype(np.float32).T
    cosT = np.ones((64, S), np.float32)
    sinT = np.zeros((64, S), np.float32)
    cosT[0:8] = cos
    cosT[8:16] = cos
    sinT[0:8] = sin
    sinT[8:16] = sin
    c["cosT"] = np.concatenate([cosT, cosT], 0)
    c["sinT"] = np.concatenate([sinT, sinT], 0)
    n = np.arange(32)[None, :]
    own = (np.arange(64) // 2)[:, None]
    cb = np.where(n < own, 0.0, NEG).astype(np.float32)
    vb = np.where(n < own, BIG, 0.0).astype(np.float32)
    oo = np.where(n == own, 0.0, -BIG).astype(np.float32)
    c["cb"] = np.broadcast_to(cb.reshape(1, 64 * 32), (128, 64 * 32)).copy()
    c["vb"] = np.broadcast_to(vb.reshape(1, 64 * 32), (128, 64 * 32)).copy()
    c["oo"] = np.broadcast_to(oo.reshape(1, 64 * 32), (128, 64 * 32)).copy()
    c["onehot"] = (np.arange(32)[:, None] == (np.arange(S)[None, :] // 256)).astype(np.float32)
    return c


def build_A(nc, P, ctx, sfx, xsrc, xeng, xreads, wA, oT, C, parts=3):
    def A(name, shape, dt):
        return ctx.enter_context(nc.sbuf_tensor(name + sfx, shape, dt))
    wv = wA.rearrange("(c p) n -> p c n", p=128)

    negtri = A("negtri", [128, 128], BF16)
    negones = A("negones", [128, 128], BF16)
    identb = A("identb", [128, 128], BF16)
    onesf = A("onesf", [128, 64], F32)
    msb = A("msb", [128, 4, 512], BF16)
    mmb = A("mmb", [128, 4, 512], BF16)
    w = A("wA_sb", [128, 8, 1024], BF16)
    P.dma("pool", "c0", negtri[:], C["negtri"], writes=["negtri"])
    P.dma("pool", "c1", negones[:], C["negones"], writes=["negones"])
    P.dma("pool", "c2", identb[:], C["ident"], writes=["identb"])
    P.dma("pool", "c3", msb[:], C["msb"], writes=["msb"])
    P.dma("pool", "c4", mmb[:], C["mmb"], writes=["mmb"])
    P.dma("pool", "w", w[:, :, 0:768], wv, writes=["w"])
    P.op("dve", lambda e: e.memset(onesf[:], 1.0), writes=["onesf"])
    P.op("dve", lambda e: e.memset(w[:, :, 768:1024], 0.0), reads=["w"], writes=["w"])
    for qk in range(2):
        for hd in range(2):
            src = 384 + qk * 128 + hd * 64
            dst = 768 + qk * 128 + hd * 64
            P.op("act", lambda e, s_=src, d_=dst: e.mul(w[:, :, d_:d_ + 8], w[:, :, s_ + 8:s_ + 16], -1.0),
                 reads=["w"], writes=["w"])
            P.op("act", lambda e, s_=src, d_=dst: e.copy(w[:, :, d_ + 8:d_ + 16], w[:, :, s_:s_ + 8]),
                 reads=["w"], writes=["w"])

    ps = [ctx.enter_context(nc.psum_tensor("ps%d" % i + sfx, [128, 512], F32)) for i in range(7)]
    pst = ctx.enter_context(nc.psum_tensor("pst" + sfx, [128, 128], BF16))
    psn = ["ps%d" % i for i in range(7)]

    xg = [A("xg%d" % i, [128, 8, 512], BF16) for i in range(2)]
    ost = [A("ost%d" % i, [128, 512], BF16) for i in range(2)]

    def load_x(tg):
        b = tg % 2
        for pi, (c0, c1, sap) in enumerate(xsrc(tg)):
            P.dma(xeng, "xg%d_%d" % (b, pi), xg[b][:, c0:c1, :], sap, reads=xreads, writes=["xg%d" % b])

    def proj_fm(bank, col0, tg):
        b = tg % 2
        for c in range(8):
            P.op("pe", lambda e, c=c: e.matmul(ps[bank][:], w[:, c, col0:col0 + 128], xg[b][:, c, :],
                                               start=(c == 0), stop=(c == 7)),
                 reads=["w", "xg%d" % b], writes=[psn[bank]], signal=(c == 7))

    def proj_tm(bank, col0, tg, ncols=128):
        b = tg % 2
        for ts in range(4):
            for c in range(8):
                P.op("pe", lambda e, c=c, ts=ts: e.matmul(ps[bank][:, ts * 128:(ts + 1) * 128],
                                                          xg[b][:, c, ts * 128:(ts + 1) * 128],
                                                          w[:, c, col0:col0 + 128],
                                                          start=(c == 0), stop=(c == 7)),
                     reads=["w", "xg%d" % b], writes=[psn[bank]], signal=(c == 7 and ts == 3))

    R = [A("R%d" % i, [128, S], BF16) for i in range(4)]
    Vm = A("Vm", [128, 64, 2, 128], BF16)
    QT, KT = R[0], R[1]
    V = A("V", [128, 64, 128], BF16)
    if parts & 1:
        load_x(0)
    for tg in range(NG if parts & 1 else 0):
        if tg + 1 < NG:
            load_x(tg + 1)
        sl = slice(tg * 512, (tg + 1) * 512)
        proj_fm(0, 0, tg)
        P.op("act", lambda e, sl=sl: e.mul(QT[:, sl], ps[0][:], 0.125), reads=["ps0"], writes=["QT"])
        proj_fm(1, 128, tg)
        P.op("dve", lambda e, sl=sl: e.tensor_copy(KT[:, sl], ps[1][:]), reads=["ps1"], writes=["KT"])
        proj_tm(2, 256, tg)
        P.op("act", lambda e, tg=tg: e.copy(V[:, tg * 4:(tg + 1) * 4, :],
                                           ps[2][:].rearrange("p (a b) -> p a b", a=4)),
             reads=["ps2"], writes=["V"])

    Eb = [A("Eb%d" % i, [128, 512], F32) for i in range(2)]
    Lb = [A("Lb%d" % i, [128, 512], BF16) for i in range(3)]
    Ab = [A("Ab%d" % i, [128, 512], BF16) for i in range(3)]
    Lrun = [A("Lrun%d" % i, [128, 512], F32) for i in range(2)]
    Lrb = [[A("Lrb%d_%d" % (h, i), [128, 512], BF16) for i in range(2)] for h in range(2)]

    items = []
    for g in range(NG):
        njt = 4 * g + 4
        for k in range(njt):
            for h in range(2):
                items.append((h, g, k, njt - 1 - k, njt))
    NI = len(items) if parts & 1 else 0

    def sb_s1(i):
        h, g, k, j, njt = items[i]
        zb = i % 4
        hp = slice(h * 64, (h + 1) * 64)
        P.op("pe", lambda e: e.matmul(ps[zb][:], KT[hp, j * 128:(j + 1) * 128], QT[hp, g * 512:(g + 1) * 512],
                                      start=True, stop=True),
             reads=["QT", "KT"], writes=[psn[zb]])

    def sb_s2(i):
        h, g, k, j, njt = items[i]
        zb = i % 4
        eb = "Eb%d" % (i % 2)
        lb = "Lb%d" % (i % 3)
        E = Eb[i % 2]
        L = Lb[i % 3]
        P.op("act", lambda e: e.activation(E[:], ps[zb][:], AF.Exp), reads=[psn[zb]], writes=[eb])
        P.op("act", lambda e: e.activation(L[:], E[:], AF.Ln, bias=1.0), reads=[eb], writes=[lb])
        if k < 4:
            jj = 3 - k
            P.op("dve", lambda e: e.tensor_tensor(L[:], L[:], msb[:, jj, :], ALU.mult),
                 reads=[lb, "msb"], writes=[lb])
        if k < njt - 1:
            lr = "Lrun%d" % h
            nb = "Lrb%d_%d" % (h, (k + 1) % 2)
            dst = Lrb[h][(k + 1) % 2]
            if k == 0:
                P.op("dve", lambda e: e.tensor_copy(Lrun[h][:], L[:]), reads=[lb], writes=[lr])
                P.op("dve", lambda e: e.tensor_copy(dst[:], L[:]), reads=[lb], writes=[nb])
            else:
                P.op("dve", lambda e: e.tensor_tensor(Lrun[h][:], Lrun[h][:], L[:], ALU.add),
                     reads=[lb, lr], writes=[lr])
                P.op("dve", lambda e: e.tensor_copy(dst[:], Lrun[h][:]), reads=[lr], writes=[nb])

    def sb_s3(i):
        h, g, k, j, njt = items[i]
        zb = i % 4
        lb = "Lb%d" % (i % 3)
        L = Lb[i % 3]
        P.op("pe", lambda e: e.matmul(ps[zb][:], negtri[:], L[:], start=False, stop=(k == 0),
                                      skip_group_check=True),
             reads=["negtri", lb], writes=[psn[zb]], signal=(k == 0))
        if k > 0:
            cur = Lrb[h][k % 2]
            P.op("pe", lambda e: e.matmul(ps[zb][:], negones[:], cur[:], start=False, stop=True,
                                          skip_group_check=True),
                 reads=["negones", "Lrb%d_%d" % (h, k % 2)], writes=[psn[zb]])

    def sb_s4(i):
        h, g, k, j, njt = items[i]
        zb = i % 4
        ab = "Ab%d" % (i % 3)
        Aa = Ab[i % 3]
        P.op("act", lambda e: e.activation(Aa[:], ps[zb][:], AF.Exp), reads=[psn[zb]], writes=[ab])
        if k < 4:
            jj = 3 - k
            P.op("dve", lambda e: e.tensor_tensor(Aa[:], Aa[:], msb[:, jj, :], ALU.mult),
                 reads=[ab, "msb"], writes=[ab])

    def sb_s5(i):
        h, g, k, j, njt = items[i]
        ab = "Ab%d" % (i % 3)
        Aa = Ab[i % 3]
        ob = 4 + (2 * g + h) % 3
        P.op("pe", lambda e: e.matmul(ps[ob][0:64, :], V[:, j, h * 64:(h + 1) * 64], Aa[:],
                                      start=(k == 0), stop=(k == njt - 1), skip_group_check=True),
             reads=["V", ab], writes=[psn[ob]], signal=True)
        if k == njt - 1:
            so = "ost%d" % (g % 2)
            P.op("dve", lambda e: e.tensor_copy(ost[g % 2][h * 64:(h + 1) * 64, :], ps[ob][0:64, :]),
                 reads=[psn[ob]], writes=[so + "_%d" % h])
            if h == 1:
                P.dma("sp", so, oT(0, g), ost[g % 2][:],
                      reads=[so + "_0", so + "_1"], writes=["oT0_%d" % g])

    stages = [sb_s1, sb_s2, sb_s3, sb_s4, sb_s5]
    for step in range(NI + len(stages) - 1):
        for s_, fn in enumerate(stages):
            i = step - s_
            if 0 <= i < NI:
                fn(i)

    P.barrier()
    if not parts & 2:
        return
    QTa = [R[0], R[1]]
    KTa = [R[2], R[3]]
    cosv, sinv = C["cosT"], C["sinT"]
    csb = [[A("cs%d_%d" % (a, i), [128, 512], F32) for i in range(2)] for a in range(2)]
    t1 = A("t1", [128, 512], F32)
    t2 = A("t2", [128, 512], F32)
    qf = A("qf", [128, 512], F32)
    kf = A("kf", [128, 512], F32)
    kmT = A("kmT", [128, 2, 32], F32)
    gm = [A("gm%d" % i, [128, 32], F32) for i in range(2)]
    top8 = [A("top8_%d" % i, [128, 8], F32) for i in range(2)]
    mb1 = [A("mb1_%d" % i, [128, 32], F32) for i in range(2)]
    mbpad = [A("mbpad%d" % i, [128, 128], BF16) for i in range(2)]
    cbrow = A("cbrow", [128, 64], F32)
    vbrow = A("vbrow", [128, 64], F32)
    oorow = A("oorow", [128, 64], F32)
    Pb = [A("Pb%d" % i, [128, 512], BF16) for i in range(3)]
    rec = A("rec", [128, 512], F32)
    bcs = A("bcs", [64, 512], F32)

    P.op("dve", lambda e: e.memset(cbrow[:, 0:32], 0.0), writes=["cbrow"])
    P.op("dve", lambda e: e.memset(cbrow[:, 32:64], NEG), writes=["cbrow"])
    P.op("dve", lambda e: e.memset(vbrow[:, 0:32], BIG), writes=["vbrow"])
    P.op("dve", lambda e: e.memset(vbrow[:, 32:64], 0.0), writes=["vbrow"])
    P.op("dve", lambda e: e.memset(oorow[:], -BIG), writes=["oorow"])
    P.op("dve", lambda e: e.memset(oorow[:, 31:32], 0.0), writes=["oorow"])
    P.op("dve", lambda e: e.memset(kmT[:], 0.0), writes=["kmT"])
    P.op("dve", lambda e: e.memset(Vm[:], 1.0), writes=["Vm"])
    for i in range(2):
        P.op("dve", lambda e, i=i: e.memset(mbpad[i][:], 0.0), writes=["mbpad%d" % i])
    for h in range(2):
        P.op("dve", lambda e, h=h: e.memset(QTa[h][64:128, :], 0.0), writes=["QTa%d" % h])
        P.op("dve", lambda e, h=h: e.memset(KTa[h][64:128, :], 0.0), writes=["KTa%d" % h])
        P.dma("pool", "oh%d" % h, KTa[h][64:96, :], C["onehot"], writes=["KTa%d" % h])

    def load_cs(tg):
        b = tg % 2
        P.dma("sp", "cs0_%d" % b, csb[0][b][:], cosv[:, tg * 512:(tg + 1) * 512], writes=["cs0_%d" % b])
        P.dma("sp", "cs1_%d" % b, csb[1][b][:], sinv[:, tg * 512:(tg + 1) * 512], writes=["cs1_%d" % b])

    def rope(dst, b, bank_a, bank_b):
        cn, sn = "cs0_%d" % b, "cs1_%d" % b
        P.op("dve", lambda e: e.tensor_tensor(t1[:], ps[bank_a][:], csb[0][b][:], ALU.mult),
             reads=[psn[bank_a], cn], writes=["t1"])
        P.op("dve", lambda e: e.tensor_tensor(t2[:], ps[bank_b][:], csb[1][b][:], ALU.mult),
             reads=[psn[bank_b], sn], writes=["t2"])
        nm = "qf" if dst is qf else "kf"
        P.op("dve", lambda e: e.tensor_tensor(dst[:], t1[:], t2[:], ALU.add),
             reads=["t1", "t2"], writes=[nm])

    load_x(0)
    load_cs(0)
    for tg in range(NG):
        if tg + 1 < NG:
            load_x(tg + 1)
            load_cs(tg + 1)
        b = tg % 2
        sl = slice(tg * 512, (tg + 1) * 512)
        proj_fm(0, 512, tg)
        proj_fm(1, 896, tg)
        rope(kf, b, 0, 1)
        for h in range(2):
            P.op("act", lambda e, h=h: e.copy(KTa[h][0:64, sl], kf[h * 64:(h + 1) * 64, :]),
                 reads=["kf"], writes=["KTa%d" % h])
        for h in range(2):
            hp2 = slice(h * 64, (h + 1) * 64)
            P.op("dve", lambda e, h=h, hp2=hp2: e.tensor_reduce(
                kmT[hp2, h, 2 * tg:2 * tg + 2], kf[hp2, :].rearrange("p (a b) -> p a b", a=2), AX.X, ALU.add),
                reads=["kf"], writes=["kmT"])
        proj_fm(2, 384, tg)
        proj_fm(3, 768, tg)
        rope(qf, b, 2, 3)
        for h in range(2):
            P.op("act", lambda e, h=h: e.mul(QTa[h][0:64, sl], qf[h * 64:(h + 1) * 64, :], 0.125),
                 reads=["qf"], writes=["QTa%d" % h])
        proj_tm(4, 640, tg)
        P.op("act", lambda e: e.copy(Vm[:, tg * 4:(tg + 1) * 4, :, 0:64],
                                    ps[4][:].rearrange("p (a h d) -> p a h d", a=4, h=2)),
             reads=["ps4"], writes=["Vm"])
        for idx in range(8 if not (DBG_MB & 2) else 0):
            cq, h = idx // 2, idx % 2
            P.op("pe", lambda e, cq=cq, h=h, idx=idx: e.matmul(
                ps[5][:, idx * 32:(idx + 1) * 32], qf[:, cq * 128:(cq + 1) * 128],
                kmT[:, h, :], start=True, stop=True),
                reads=["qf", "kmT"], writes=["ps5"], signal=False)
        P.op("pe", lambda e: e.matmul(ps[5][:, 256:384], identb[:], identb[:], start=True, stop=True),
             reads=["identb"], writes=["ps5"])
        for rnd in range(1):
            for idx in range(8 if not (DBG_MB & 2) else 0):
                cq, h = idx // 2, idx % 2
                cch = tg * 4 + cq
                own = cch // 2
                u = idx % 2
                P.op("dve", lambda e, idx=idx, own=own, u=u: e.tensor_tensor(
                    gm[u][:], ps[5][:, idx * 32:(idx + 1) * 32], cbrow[:, 32 - own:64 - own], ALU.add),
                    reads=["ps5", "cbrow"], writes=["gm%d" % u])
                P.op("dve", lambda e, u=u: e.max(top8[u][:], gm[u][:]), reads=["gm%d" % u], writes=["top8_%d" % u])
                P.op("dve", lambda e, own=own, u=u: e.scalar_tensor_tensor(
                    mb1[u][:], gm[u][:], top8[u][:, 2:3], vbrow[:, 32 - own:64 - own], ALU.is_ge, ALU.mult),
                    reads=["gm%d" % u, "top8_%d" % u, "vbrow"], writes=["mb1_%d" % u])
                P.op("dve", lambda e, own=own, u=u: e.tensor_tensor(
                    mbpad[u][:, 64:96], mb1[u][:], oorow[:, 31 - own:63 - own], ALU.add),
                    reads=["mb1_%d" % u, "oorow"], writes=["mbpad%d" % u])
                if DBG_MB & 8:
                    continue
                P.op("pe", lambda e, u=u: e.matmul(ps[6][:, u * 128:(u + 1) * 128], mbpad[u][:], identb[:],
                                                    start=True, stop=True),
                     reads=["mbpad%d" % u, "identb"], writes=["ps6_%d" % u], signal=False)
                P.op("pe", lambda e, u=u: e.matmul(ps[6][:, 256 + u * 128:256 + (u + 1) * 128], identb[:], identb[:],
                                                    start=True, stop=True),
                     reads=["identb"], writes=["ps6_%d" % u])
                P.op("act", lambda e, h=h, cch=cch, u=u: e.copy(QTa[h][64:96, cch * 128:(cch + 1) * 128],
                                                              ps[6][64:96, u * 128:(u + 1) * 128]),
                     reads=["ps6_%d" % u], writes=["QTa%d" % h])

    mitems = []
    for g in range(NG):
        njt = 4 * g + 4
        for j in range(njt):
            for h in range(2):
                mitems.append((h, g, j, njt))
    NM = len(mitems) if not (DBG_MB & 1) else 0

    def mb_s1(i):
        h, g, j, njt = mitems[i]
        zb = i % 3
        P.op("pe", lambda e: e.matmul(ps[zb][:], KTa[h][:, j * 128:(j + 1) * 128], QTa[h][:, g * 512:(g + 1) * 512],
                                      start=True, stop=True),
             reads=["QTa%d" % h, "KTa%d" % h], writes=[psn[zb]])

    def mb_s2(i):
        h, g, j, njt = mitems[i]
        zb = i % 3
        pb = "Pb%d" % (i % 3)
        Pt = Pb[i % 3]
        P.op("act", lambda e: e.activation(Pt[:], ps[zb][:], AF.Exp), reads=[psn[zb]], writes=[pb])
        if j >= 4 * g:
            jj = j - 4 * g
            P.op("dve", lambda e: e.tensor_tensor(Pt[:], Pt[:], mmb[:, jj, :], ALU.mult),
                 reads=[pb, "mmb"], writes=[pb])

    def mb_s3(i):
        h, g, j, njt = mitems[i]
        pb = "Pb%d" % (i % 3)
        Pt = Pb[i % 3]
        ob = 4 + (2 * g + h) % 3
        P.op("pe", lambda e: e.matmul(ps[ob][:, :], Vm[:, j, h, :], Pt[:],
                                      start=(j == 0), stop=(j == njt - 1), skip_group_check=True),
             reads=["Vm", pb], writes=[psn[ob]], signal=True)
        if j == njt - 1:
            so = "ost%d" % (g % 2)
            if DBG_NORM:
                P.op("dve", lambda e: e.reciprocal(rec[64:128, :], ps[ob][64:128, :]), reads=[psn[ob]], writes=["rec"])
                P.op("dve", lambda e: e.tensor_tensor(ost[g % 2][h * 64:(h + 1) * 64, :], ps[ob][0:64, :],
                                                      rec[64:128, :], ALU.mult),
                     reads=[psn[ob], "rec"], writes=[so + "_%d" % h])
            else:
                P.op("dve", lambda e: e.reciprocal(rec[64:65, :], ps[ob][64:65, :]), reads=[psn[ob]], writes=["rec"])
                P.op("pe", lambda e: e.matmul(ps[3][0:64, :], onesf[64:65, 0:64], rec[64:65, :], start=True, stop=True),
                     reads=["onesf", "rec"], writes=["ps3"])
                P.op("act", lambda e: e.copy(bcs[:], ps[3][0:64, :]), reads=["ps3"], writes=["bcs"])
                P.op("dve", lambda e: e.tensor_tensor(ost[g % 2][h * 64:(h + 1) * 64, :], ps[ob][0:64, :], bcs[:],
                                                      ALU.mult),
                     reads=[psn[ob], "bcs"], writes=[so + "_%d" % h])
            if h == 1:
                P.dma("sp", so, oT(1, g), ost[g % 2][:],
                      reads=[so + "_0", so + "_1"], writes=["oT1_%d" % g])

    mstages = [mb_s1, mb_s2, mb_s3]
    for step in range(NM + len(mstages) - 1):
        for s_, fn in enumerate(mstages):
            i = step - s_
            if 0 <= i < NM:
                fn(i)
    P.barrier()


def build_B(nc, P, ctx, sfx, xT_d, oTs_v, oTm_v, oreads, W, lnp_d, out_d, outb_d=None):
    def A(name, shape, dt):
        return ctx.enter_context(nc.sbuf_tensor(name + sfx, shape, dt))
    xT = A("xT_sb", [128, 8, TOK], F32)
    xb = A("xb_sb", [128, 8, TOK], BF16)
    arena = A("arena", [128, 32768], BF16)
    oTs = arena[:, 0:8192].rearrange("p (a b) -> p a b", a=4)
    oTm = arena[:, 8192:16384].rearrange("p (a b) -> p a b", a=4)
    mg = arena[:, 16384:32768].rearrange("p (a b) -> p a b", a=8)
    hT = arena[:, 0:NFC * 1024].rearrange("p (a b) -> p a b", a=NFC)
    lnp = A("lnp_sb", [128, 32], F32)
    onesD = A("onesD", [128, 128], F32)
    warena = A("warena", [128, 9728], BF16)
    wgb = [warena[:, i * 2048:(i + 1) * 2048].rearrange("p (a b) -> p a b", a=8) for i in range(2)]
    wbb = [warena[:, 4096 + i * 1024:4096 + (i + 1) * 1024].rearrange("p (a b) -> p a b", a=4) for i in range(2)]
    wob = [warena[:, 6144 + i * 1024:6144 + (i + 1) * 1024].rearrange("p (a b) -> p a b", a=8) for i in range(2)]
    wfgu = [warena[:, i * 2048:(i + 1) * 2048].rearrange("p (a b) -> p a b", a=8) for i in range(2)]
    wfdb = [warena[:, 4096 + i * 2816:4096 + (i + 1) * 2816].rearrange("p (a b) -> p a b", a=NFC) for i in range(2)]
    sg = [A("sg%d" % i, [128, 512], F32) for i in range(4)]
    m1 = [A("m1_%d" % i, [128, 512], F32) for i in range(2)]
    m2 = [A("m2_%d" % i, [128, 512], F32) for i in range(2)]
    ysq = [A("ysq%d" % i, [128, 512], F32) for i in range(2)]
    mean_sb = A("mean_sb", [128, 512], F32)
    rstd_sb = A("rstd_sb", [128, 512], F32)
    tn = [A("tn%d" % i, [128, 512], F32) for i in range(2)]
    ps = [ctx.enter_context(nc.psum_tensor("pb%d" % i + sfx, [128, 512], F32)) for i in range(8)]
    psn = ["pb%d" % i for i in range(8)]
    bank = [0]
    XN = [["xT%d_%d" % (c, g) for g in range(NTG)] for c in range(8)]
    XB = [["xb%d_%d" % (c, g) for g in range(NTG)] for c in range(8)]
    MG = [["mg%d_%d" % (c, g) for g in range(NTG)] for c in range(8)]
    HT = [["hT%d_%d" % (k, g) for g in range(2)] for k in range(NFC)]
    allx = [n for r_ in XN for n in r_]
    allxb = [n for r_ in XB for n in r_]

    def nb():
        bank[0] = (bank[0] + 1) % 8
        return bank[0]

    xdv = xT_d.rearrange("(c p) t -> p c t", p=128)
    P.dma("sp", "xT", xT[:], xdv, writes=allx)
    P.dma("pool", "xb", xb[:], xdv, writes=allxb)
    P.dma("sp", "oTs", oTs, oTs_v, reads=oreads, writes=["oTs"])
    P.dma("sp", "oTm", oTm, oTm_v, reads=oreads, writes=["oTm"])
    P.dma("sp", "lnp", lnp[:], lnp_d, writes=["lnp"])
    P.op("dve", lambda e: e.memset(onesD[:], 1.0 / D), writes=["onesD"])

    wGv = W["wG"].rearrange("(k p) n -> p k n", p=128)
    wbsv = W["wbs"].rearrange("(k p) n -> p k n", p=128)
    wbmv = W["wbm"].rearrange("(k p) n -> p k n", p=128)
    wov = W["wout"].rearrange("(k p) n -> p k n", p=128)
    wfgv = W["wfg"].rearrange("(k p) n -> p k n", p=128)
    wfuv = W["wfu"].rearrange("(k p) n -> p k n", p=128)
    wfdv = W["wfd"].rearrange("(k p) n -> p k n", p=128)

    def tsl(tg):
        return slice(tg * 512, (tg + 1) * 512)

    def load_b1(c):
        b = c % 2
        cs = slice(c * 128, (c + 1) * 128)
        P.dma("pool", "wgb%da" % b, wgb[b][:, :, 0:128], wGv[:, :, cs], writes=["wgb%da" % b])
        P.dma("pool", "wgb%db" % b, wgb[b][:, :, 128:256], wGv[:, :, D + c * 128:D + (c + 1) * 128],
              writes=["wgb%db" % b])
        P.dma("pool", "wbb%da" % b, wbb[b][:, :, 0:128], wbsv[:, :, cs], writes=["wbb%da" % b])
        P.dma("pool", "wbb%db" % b, wbb[b][:, :, 128:256], wbmv[:, :, cs], writes=["wbb%db" % b])

    load_b1(0)
    it = 0
    for c in range(8):
        if c + 1 < 8:
            load_b1(c + 1)
        b = c % 2
        for tg in range(NTG):
            t = tsl(tg)
            bk = [nb() for _ in range(4)]
            for half, (bkk, wn) in enumerate(zip(bk[0:2], ["wgb%da" % b, "wgb%db" % b])):
                for k in range(8):
                    P.op("pe", lambda e, k=k, half=half, bkk=bkk: e.matmul(
                        ps[bkk][:], wgb[b][:, k, half * 128:(half + 1) * 128], xb[:, k, t],
                        start=(k == 0), stop=(k == 7)),
                        reads=[wn, XB[k][tg]], writes=[psn[bkk]], signal=(k == 7))
            for half, (bkk, wn, src, sn) in enumerate(zip(bk[2:4], ["wbb%da" % b, "wbb%db" % b],
                                                          [oTs, oTm], ["oTs", "oTm"])):
                for k in range(4):
                    P.op("pe", lambda e, k=k, half=half, bkk=bkk, src=src: e.matmul(
                        ps[bkk][:], wbb[b][:, k, half * 128:(half + 1) * 128], src[:, k, t],
                        start=(k == 0), stop=(k == 3)),
                        reads=[wn, sn], writes=[psn[bkk]], signal=(k == 3))
            u = it % 2
            s0, s1 = sg[2 * u], sg[2 * u + 1]
            P.op("act", lambda e, s0=s0, bkk=bk[0]: e.activation(s0[:], ps[bkk][:], AF.Sigmoid),
                 reads=[psn[bk[0]]], writes=["sg%d" % (2 * u)])
            P.op("act", lambda e, s1=s1, bkk=bk[1]: e.activation(s1[:], ps[bkk][:], AF.Sigmoid),
                 reads=[psn[bk[1]]], writes=["sg%d" % (2 * u + 1)])
            P.op("dve", lambda e, s0=s0, u=u, bkk=bk[2]: e.tensor_tensor(m1[u][:], s0[:], ps[bkk][:], ALU.mult),
                 reads=[psn[bk[2]], "sg%d" % (2 * u)], writes=["m1_%d" % u])
            P.op("dve", lambda e, s1=s1, u=u, bkk=bk[3]: e.tensor_tensor(m2[u][:], s1[:], ps[bkk][:], ALU.mult),
                 reads=[psn[bk[3]], "sg%d" % (2 * u + 1)], writes=["m2_%d" % u])
            P.op("pool", lambda e, u=u, c=c, t=t: e.tensor_tensor(mg[:, c, t], m1[u][:], m2[u][:], ALU.add),
                 reads=["m1_%d" % u, "m2_%d" % u], writes=[MG[c][tg]])
            it += 1

    def layer_norm(tg, gcol, bcol):
        t = tsl(tg)
        ba, bb_ = nb(), nb()
        for c in range(8):
            q = ysq[c % 2]
            P.op("act", lambda e, q=q, c=c: e.activation(q[:], xT[:, c, t], AF.Square),
                 reads=[XN[c][tg]], writes=["ysq%d" % (c % 2)])
            P.op("pe", lambda e, c=c: e.matmul(ps[ba][:], onesD[:], xT[:, c, t], start=(c == 0), stop=(c == 7)),
                 reads=["onesD", XN[c][tg]], writes=[psn[ba]], signal=(c == 7))
            P.op("pe", lambda e, q=q, c=c: e.matmul(ps[bb_][:], onesD[:], q[:], start=(c == 0), stop=(c == 7),
                                                   skip_group_check=True),
                 reads=["onesD", "ysq%d" % (c % 2)], writes=[psn[bb_]], signal=True)
        P.op("dve", lambda e: e.tensor_copy(mean_sb[:], ps[ba][:]), reads=[psn[ba]], writes=["mean_sb"])
        P.op("dve", lambda e: e.tensor_tensor(rstd_sb[:], mean_sb[:], mean_sb[:], ALU.mult),
             reads=["mean_sb"], writes=["rstd_sb"])
        P.op("dve", lambda e: e.tensor_tensor(rstd_sb[:], ps[bb_][:], rstd_sb[:], ALU.subtract),
             reads=[psn[bb_], "rstd_sb"], writes=["rstd_sb"])
        P.op("act", lambda e: e.activation(rstd_sb[:], rstd_sb[:], AF.Ln, bias=EPS),
             reads=["rstd_sb"], writes=["rstd_sb"])
        P.op("act", lambda e: e.activation(rstd_sb[:], rstd_sb[:], AF.Exp, scale=-0.5),
             reads=["rstd_sb"], writes=["rstd_sb"])
        for c in range(8):
            tt = tn[c % 2]
            tnn = "tn%d" % (c % 2)
            P.op("dve", lambda e, tt=tt, c=c: e.tensor_tensor(tt[:], xT[:, c, t], mean_sb[:], ALU.subtract),
                 reads=[XN[c][tg], "mean_sb"], writes=[tnn])
            P.op("dve", lambda e, tt=tt: e.tensor_tensor(tt[:], tt[:], rstd_sb[:], ALU.mult),
                 reads=[tnn, "rstd_sb"], writes=[tnn])
            P.op("act", lambda e, tt=tt, c=c: e.activation(xT[:, c, t], tt[:], AF.Identity,
                                                          bias=lnp[:, bcol + c:bcol + c + 1],
                                                          scale=lnp[:, gcol + c:gcol + c + 1]),
                 reads=[tnn, "lnp"], writes=[XN[c][tg]])
            P.op("pool", lambda e, c=c: e.tensor_copy(xb[:, c, t], xT[:, c, t]), reads=[XN[c][tg]], writes=[XB[c][tg]])

    def load_wo(c):
        b = c % 2
        P.dma("pool", "wob%d" % b, wob[b][:], wov[:, :, c * 128:(c + 1) * 128], writes=["wob%d" % b])

    load_wo(0)
    for c in range(8):
        if c + 1 < 8:
            load_wo(c + 1)
        b = c % 2
        for tg in range(NTG):
            t = tsl(tg)
            bkk = nb()
            for k in range(8):
                P.op("pe", lambda e, k=k, bkk=bkk: e.matmul(ps[bkk][:], wob[b][:, k, :], mg[:, k, t],
                                                            start=(k == 0), stop=(k == 7)),
                     reads=["wob%d" % b, MG[k][tg]], writes=[psn[bkk]], signal=(k == 7))
            P.op("dve", lambda e, bkk=bkk, c=c, t=t: e.scalar_tensor_tensor(
                xT[:, c, t], xT[:, c, t], ALPHA, ps[bkk][:], ALU.mult, ALU.add),
                reads=[psn[bkk], XN[c][tg]], writes=[XN[c][tg]])
    for tg in range(NTG):
        layer_norm(tg, 0, 8)

    def load_gu(i):
        k = i % NFC
        b = i % 2
        P.dma("pool", "wfgu%da" % b, wfgu[b][:, :, 0:128], wfgv[:, :, k * 128:(k + 1) * 128],
              writes=["wfgu%da" % b])
        P.dma("pool", "wfgu%db" % b, wfgu[b][:, :, 128:256], wfuv[:, :, k * 128:(k + 1) * 128],
              writes=["wfgu%db" % b])

    def load_dn(i):
        c = i % 8
        b = i % 2
        P.dma("pool", "wfdb%d" % b, wfdb[b][:], wfdv[:, :, c * 128:(c + 1) * 128], writes=["wfdb%d" % b])

    gi = 0
    di = 0
    P.barrier()
    for hf in range(2):
        load_gu(gi)
        for k in range(NFC):
            if k + 1 < NFC:
                load_gu(gi + 1)
            b = gi % 2
            for t2_ in range(2):
                tg = hf * 2 + t2_
                t = tsl(tg)
                b0, b1 = nb(), nb()
                for half, bkk in enumerate([b0, b1]):
                    wn = "wfgu%d%s" % (b, "ab"[half])
                    for kk in range(8):
                        P.op("pe", lambda e, kk=kk, half=half, bkk=bkk: e.matmul(
                            ps[bkk][:], wfgu[b][:, kk, half * 128:(half + 1) * 128], xb[:, kk, t],
                            start=(kk == 0), stop=(kk == 7)),
                            reads=[wn, XB[kk][tg]], writes=[psn[bkk]], signal=(kk == 7))
                u = it % 2
                s0 = sg[2 * u]
                P.op("act", lambda e, s0=s0, b0=b0: e.activation(s0[:], ps[b0][:], AF.Silu),
                     reads=[psn[b0]], writes=["sg%d" % (2 * u)])
                P.op("dve", lambda e, s0=s0, b1=b1, k=k, t2_=t2_: e.tensor_tensor(
                    hT[:, k, t2_ * 512:(t2_ + 1) * 512], s0[:], ps[b1][:], ALU.mult),
                    reads=[psn[b1], "sg%d" % (2 * u)], writes=[HT[k][t2_]])
                it += 1
            gi += 1
        load_dn(di)
        for c in range(8):
            if c + 1 < 8:
                load_dn(di + 1)
            b = di % 2
            for t2_ in range(2):
                tg = hf * 2 + t2_
                t = tsl(tg)
                bkk = nb()
                for k in range(NFC):
                    P.op("pe", lambda e, k=k, bkk=bkk, t2_=t2_: e.matmul(
                        ps[bkk][:], wfdb[b][:, k, :], hT[:, k, t2_ * 512:(t2_ + 1) * 512],
                        start=(k == 0), stop=(k == NFC - 1)),
                        reads=["wfdb%d" % b, HT[k][t2_]], writes=[psn[bkk]], signal=(k == NFC - 1))
                P.op("dve", lambda e, bkk=bkk, c=c, t=t: e.scalar_tensor_tensor(
                    xT[:, c, t], xT[:, c, t], ALPHA, ps[bkk][:], ALU.mult, ALU.add),
                    reads=[psn[bkk], XN[c][tg]], writes=[XN[c][tg]])
            di += 1
        for t2_ in range(2):
            layer_norm(hf * 2 + t2_, 16, 24)
    P.dma("sp", "out", out_d.rearrange("(c p) t -> p c t", p=128), xT[:], reads=allx, writes=["out"])
    if outb_d is not None:
        P.dma("sp", "outb", outb_d.rearrange("(c p) t -> p c t", p=128), xb[:], reads=allxb, writes=["outb"])


W_KEYS = ["wG", "wbs", "wbm", "wout", "wfg", "wfu", "wfd"]
W_SHAPES = {"wG": [D, 2048], "wbs": [512, D], "wbm": [512, D], "wout": [D, D],
            "wfg": [D, DFF], "wfu": [D, DFF], "wfd": [DFF, D]}
RG = [[0, 1, 2, 3], [4, 5, 6, 7]]


def _pp(v):
    return np.ascontiguousarray(v.reshape(8, 128).T)


def build_fused(nlayers=DEPTH, skipA=False, skipB=False, samex=False, parts=3):
    nc = bass.Bass("TRN2", target_bir_lowering=False)
    P = Prog(nc)
    hc = host_consts()
    C = {k: nc.dram_tensor("c_" + k, list(v.shape), F32, kind="ExternalInput").ap() for k, v in hc.items()}
    xTall0 = nc.dram_tensor("xTall0", [D, S], F32, kind="ExternalInput").ap()
    xT0 = nc.dram_tensor("xT0", [D, TOK], F32, kind="ExternalInput").ap()
    wA = nc.dram_tensor("wA", [DEPTH, D, 768], F32, kind="ExternalInput").ap()
    Wd = {k: nc.dram_tensor(k, [DEPTH] + W_SHAPES[k], F32, kind="ExternalInput").ap() for k in W_KEYS}
    lnp = nc.dram_tensor("lnp", [DEPTH, 128, 32], F32, kind="ExternalInput").ap()
    out = nc.dram_tensor("out", [D, TOK], F32, kind="ExternalOutput").ap()
    oT_loc = nc.dram_tensor("oT_loc", [1024, TOK], BF16, kind="Internal").ap()
    oT_all = nc.dram_tensor("oT_all", [4096, TOK], BF16, kind="Internal").ap()
    xb_loc = nc.dram_tensor("xb_loc", [D, TOK], BF16, kind="Internal").ap()
    xb_all = nc.dram_tensor("xb_all", [4 * D, TOK], BF16, kind="Internal").ap()
    xres = nc.dram_tensor("xres", [D, TOK], F32, kind="Internal").ap()
    pid = nc.sync.partition_id()
    roff = (pid % 4) * 1024
    olv = oT_loc.rearrange("(q b p) t -> b p q t", q=4, b=2)

    def odst(br, g):
        return olv[br, :, g // 4, (g % 4) * 512:(g % 4 + 1) * 512]

    for l in range(nlayers):
        last = l == nlayers - 1
        with ExitStack() as ctx:
            if l == 0 or samex:
                xv0 = xTall0.rearrange("(c p) t -> p c t", p=128)
                xsrc = lambda tg: [(0, 8, xv0[:, :, tg * 512:(tg + 1) * 512])]
                xeng, xreads = "pool", []
            else:
                xv1 = xb_all.rearrange("(j r c p) t -> p r j c t", j=4, r=4, c=2, p=128)
                xsrc = lambda tg: [(2 * j, 2 * j + 2, xv1[:, tg // 4, j, :, (tg % 4) * 512:(tg % 4 + 1) * 512])
                                   for j in range(4)]
                xeng, xreads = "sp", ["xb_all%d" % j for j in range(4)]
            if not skipA:
                build_A(nc, P, ctx, "_a%d" % l, xsrc, xeng, xreads, wA[l], odst, C, parts)
        P.barrier()
        for q in range(4):
            P.collective("oT%d" % q, "AllGather", RG, oT_loc[q * 256:(q + 1) * 256, :],
                         oT_all[q * 1024:(q + 1) * 1024, :], writes=["oT_all%d" % q])
        with ExitStack() as ctx:
            ov = oT_all[bass.ds(roff, 1024), :].rearrange("(k b p) t -> p k b t", k=4, b=2, p=128)
            oTs_v = ov[:, :, 0, :]
            oTm_v = ov[:, :, 1, :]
            Wl = {k: Wd[k][l] for k in W_KEYS}
            if not skipB:
              build_B(nc, P, ctx, "_b%d" % l, xT0 if l == 0 else xres, oTs_v, oTm_v,
                    ["oT_all%d" % q for q in range(4)], Wl, lnp[l],
                    out if last else xres, None if last else xb_loc)
        P.barrier()
        if not last:
            for j in range(4):
                P.collective("xb%d" % j, "AllGather", RG, xb_loc[j * 256:(j + 1) * 256, :],
                             xb_all[j * 1024:(j + 1) * 1024, :], writes=["xb_all%d" % j])
    P.finish("sp")
    return nc, hc


def _wA_slices(w_in_l, r):
    def cols(base):
        return w_in_l[:, base + 128 * r: base + 128 * r + 128]
    return np.concatenate([cols(0), cols(512), cols(1024), cols(1536), cols(2048), cols(2560)], axis=1)


def kernel(x, w_in, w_branch_sb, w_branch_moba, w_out, ln_mix_g, ln_mix_b,
           w_ffn_gate, w_ffn_up, w_ffn_down, ln_ffn_g, ln_ffn_b):
    f = lambda a: np.asarray(a, dtype=np.float32)
    x, w_in, w_branch_sb, w_branch_moba, w_out = f(x), f(w_in), f(w_branch_sb), f(w_branch_moba), f(w_out)
    ln_mix_g, ln_mix_b, ln_ffn_g, ln_ffn_b = f(ln_mix_g), f(ln_mix_b), f(ln_ffn_g), f(ln_ffn_b)
    w_ffn_gate, w_ffn_up, w_ffn_down = f(w_ffn_gate), f(w_ffn_up), f(w_ffn_down)
    nc, hc = build_fused()
    cores = list(range(8))
    xTb = [np.ascontiguousarray(x[b].T) for b in range(B)]
    Wfull = {"wG": np.ascontiguousarray(w_in[:, :, 3072:5120]), "wbs": w_branch_sb, "wbm": w_branch_moba,
             "wout": w_out, "wfg": w_ffn_gate, "wfu": w_ffn_up, "wfd": w_ffn_down}
    lnp = np.ascontiguousarray(np.stack([np.concatenate(
        [_pp(ln_mix_g[l]), _pp(ln_mix_b[l]), _pp(ln_ffn_g[l]), _pp(ln_ffn_b[l])], axis=1) for l in range(DEPTH)]))
    maps = []
    for c in cores:
        b, r = c // 4, c % 4
        m = {"xTall0": xTb[b], "xT0": np.ascontiguousarray(xTb[b][:, r * TOK:(r + 1) * TOK]),
             "wA": np.ascontiguousarray(np.stack([_wA_slices(w_in[l], r) for l in range(DEPTH)])),
             "lnp": lnp}
        m.update({k: np.ascontiguousarray(v) for k, v in Wfull.items()})
        m.update({"c_" + k: v for k, v in hc.items()})
        maps.append(m)
    res = run_bass_kernel_spmd(nc, maps, core_ids=cores)
    xn = np.empty_like(x)
    for c in cores:
        b, r = c // 4, c % 4
        xn[b, r * TOK:(r + 1) * TOK, :] = np.asarray(res.results[c]["out"]).T
    return xn
```

```python
from contextlib import ExitStack
import os
DBG_GATE = int(os.environ.get('DBG_GATE', '1'))
DBG_NORM = int(os.environ.get('DBG_NORM', '1'))
DBG_MB = int(os.environ.get('DBG_MB', '0'))
import numpy as np
import ml_dtypes
import concourse.bass as bass
import concourse.mybir as mybir
from concourse.bass_utils import run_bass_kernel_spmd

F32 = mybir.dt.float32
BF16 = mybir.dt.bfloat16
AF = mybir.ActivationFunctionType
ALU = mybir.AluOpType
AX = mybir.AxisListType

D = 1024
S = 8192
B = 2
DEPTH = 2
DFF = 2816
NFC = DFF // 128
NG = S // 512
ALPHA = (2 * DEPTH) ** 0.25
EPS = 1e-5
BIG = 30000.0
NEG = -1.0e30
TOK = 2048
NTG = TOK // 512


class Prog:
    SAME_ENGINE_SYNC = True

    def __init__(self, nc):
        self.nc = nc
        self.eng = {"pe": nc.tensor, "act": nc.scalar, "dve": nc.vector,
                    "pool": nc.gpsimd, "sp": nc.sync}
        self.sem = {}
        self.cnt = {}
        self.res = {}
        self.known = {e: {} for e in self.eng}
        self.nwait = 0
        self.nops = 0

    def _sem(self, lane):
        if lane not in self.sem:
            self.sem[lane] = self.nc.alloc_semaphore(name="s_" + lane.replace(":", "_"))
            self.cnt[lane] = 0
        return self.sem[lane]

    def _need(self, e, lane, val):
        if lane == e and (e == "pe" or not self.SAME_ENGINE_SYNC):
            return
        if self.known[e].get(lane, 0) >= val:
            return
        self.eng[e].wait_ge(self._sem(lane), val)
        self.known[e][lane] = val
        self.nwait += 1

    def _deps(self, e, reads, writes):
        for r in reads:
            ent = self.res.get(r)
            if ent and ent[0]:
                self._need(e, *ent[0])
        for w in writes:
            ent = self.res.get(w)
            if ent:
                if ent[0]:
                    self._need(e, *ent[0])
                for lane, val in ent[1].items():
                    self._need(e, lane, val)

    def _record(self, lane, val, reads, writes):
        for r in reads:
            ent = self.res.setdefault(r, [None, {}])
            ent[1][lane] = max(ent[1].get(lane, 0), val)
        for w in writes:
            self.res[w] = [(lane, val), {}]

    def op(self, e, fn, reads=(), writes=(), signal=True):
        self._sem(e)
        self._deps(e, reads, writes)
        ins = fn(self.eng[e])
        val = self.cnt[e] + 1
        if signal:
            ins.then_inc(self.sem[e], 1)
            self.cnt[e] = val
        self._record(e, val, reads, writes)
        self.nops += 1
        return ins

    def dma(self, e, lane, out, in_, reads=(), writes=(), **kw):
        lane = "d:" + lane
        self._sem(lane)
        self._deps(e, reads, writes)
        ins = self.eng[e].dma_start(out=out, in_=in_, **kw)
        val = self.cnt[lane] + 16
        ins.then_inc(self.sem[lane], 16)
        self.cnt[lane] = val
        self._record(lane, val, reads, writes)
        self.nops += 1
        return ins

    def collective(self, lane, kind, rg, in_ap, out_ap, reads=(), writes=()):
        lane = "c:" + lane
        self._sem(lane)
        self._deps("pool", reads, writes)
        ins = self.nc.gpsimd.collective_compute(kind, ALU.bypass, replica_groups=rg,
                                                ins=[in_ap], outs=[out_ap])
        val = self.cnt[lane] + 1
        ins.then_inc(self.sem[lane], 1)
        self.cnt[lane] = val
        self._record(lane, val, reads, writes)
        self.nops += 1
        return ins

    def barrier(self):
        for e in self.eng:
            for lane, val in self.cnt.items():
                if val > 0 and lane != e:
                    if self.known[e].get(lane, 0) < val:
                        self.eng[e].wait_ge(self.sem[lane], val)
                        self.known[e][lane] = val
        self.res = {}

    def finish(self, e="sp"):
        for lane, val in self.cnt.items():
            if val > 0 and lane != e:
                self.eng[e].wait_ge(self.sem[lane], val)


def host_consts():
    c = {}
    k = np.arange(128)[:, None]
    s = np.arange(128)[None, :]
    c["negtri"] = np.where(k >= s, -1.0, 0.0).astype(np.float32)
    c["negones"] = np.full((128, 128), -1.0, np.float32)
    c["ident"] = np.eye(128, dtype=np.float32)
    sk = np.arange(128)[:, None, None] + 128 * np.arange(4)[None, :, None]
    t = np.arange(512)[None, None, :]
    c["msb"] = (sk < t).astype(np.float32)
    c["mmb"] = (sk <= t).astype(np.float32)
    half = 8
    inv = (500000.0 ** (-np.arange(0, 16, 2, dtype=np.float32) / 16)).astype(np.float32)
    ang = (np.arange(S, dtype=np.float32)[:, None] * inv[None, :]).astype(np.float32)
    cos = np.cos(ang).astype(np.float32).T
    sin = np.sin(ang).astype(np.float32).T
    cosT = np.ones((64, S), np.float32)
    sinT = np.zeros((64, S), np.float32)
    cosT[0:8] = cos
    cosT[8:16] = cos
    sinT[0:8] = sin
    sinT[8:16] = sin
    c["cosT"] = np.concatenate([cosT, cosT], 0)
    c["sinT"] = np.concatenate([sinT, sinT], 0)
    n = np.arange(32)[None, :]
    own = (np.arange(64) // 2)[:, None]
    cb = np.where(n < own, 0.0, NEG).astype(np.float32)
    vb = np.where(n < own, BIG, 0.0).astype(np.float32)
    oo = np.where(n == own, 0.0, -BIG).astype(np.float32)
    c["cb"] = np.broadcast_to(cb.reshape(1, 64 * 32), (128, 64 * 32)).copy()
    c["vb"] = np.broadcast_to(vb.reshape(1, 64 * 32), (128, 64 * 32)).copy()
    c["oo"] = np.broadcast_to(oo.reshape(1, 64 * 32), (128, 64 * 32)).copy()
    c["onehot"] = (np.arange(32)[:, None] == (np.arange(S)[None, :] // 256)).astype(np.float32)
    return c


def build_A(nc, P, ctx, sfx, xsrc, xeng, xreads, wA, oT, C, parts=3):
    def A(name, shape, dt):
        return ctx.enter_context(nc.sbuf_tensor(name + sfx, shape, dt))
    wv = wA.rearrange("(c p) n -> p c n", p=128)

    negtri = A("negtri", [128, 128], BF16)
    negones = A("negones", [128, 128], BF16)
    identb = A("identb", [128, 128], BF16)
    onesf = A("onesf", [128, 64], F32)
    msb = A("msb", [128, 4, 512], BF16)
    mmb = A("mmb", [128, 4, 512], BF16)
    w = A("wA_sb", [128, 8, 1024], BF16)
    P.dma("pool", "c0", negtri[:], C["negtri"], writes=["negtri"])
    P.dma("pool", "c1", negones[:], C["negones"], writes=["negones"])
    P.dma("pool", "c2", identb[:], C["ident"], writes=["identb"])
    P.dma("pool", "c3", msb[:], C["msb"], writes=["msb"])
    P.dma("pool", "c4", mmb[:], C["mmb"], writes=["mmb"])
    P.dma("pool", "w", w[:, :, 0:768], wv, writes=["w"])
    P.op("dve", lambda e: e.memset(onesf[:], 1.0), writes=["onesf"])
    P.op("dve", lambda e: e.memset(w[:, :, 768:1024], 0.0), reads=["w"], writes=["w"])
    for qk in range(2):
        for hd in range(2):
            src = 384 + qk * 128 + hd * 64
            dst = 768 + qk * 128 + hd * 64
            P.op("act", lambda e, s_=src, d_=dst: e.mul(w[:, :, d_:d_ + 8], w[:, :, s_ + 8:s_ + 16], -1.0),
                 reads=["w"], writes=["w"])
            P.op("act", lambda e, s_=src, d_=dst: e.copy(w[:, :, d_ + 8:d_ + 16], w[:, :, s_:s_ + 8]),
                 reads=["w"], writes=["w"])

    ps = [ctx.enter_context(nc.psum_tensor("ps%d" % i + sfx, [128, 512], F32)) for i in range(7)]
    pst = ctx.enter_context(nc.psum_tensor("pst" + sfx, [128, 128], BF16))
    psn = ["ps%d" % i for i in range(7)]

    xg = [A("xg%d" % i, [128, 8, 512], BF16) for i in range(2)]
    ost = [A("ost%d" % i, [128, 512], BF16) for i in range(2)]

    def load_x(tg):
        b = tg % 2
        for pi, (c0, c1, sap) in enumerate(xsrc(tg)):
            P.dma(xeng, "xg%d_%d" % (b, pi), xg[b][:, c0:c1, :], sap, reads=xreads, writes=["xg%d" % b])

    def proj_fm(bank, col0, tg):
        b = tg % 2
        for c in range(8):
            P.op("pe", lambda e, c=c: e.matmul(ps[bank][:], w[:, c, col0:col0 + 128], xg[b][:, c, :],
                                               start=(c == 0), stop=(c == 7)),
                 reads=["w", "xg%d" % b], writes=[psn[bank]], signal=(c == 7))

    def proj_tm(bank, col0, tg, ncols=128):
        b = tg % 2
        for ts in range(4):
            for c in range(8):
                P.op("pe", lambda e, c=c, ts=ts: e.matmul(ps[bank][:, ts * 128:(ts + 1) * 128],
                                                          xg[b][:, c, ts * 128:(ts + 1) * 128],
                                                          w[:, c, col0:col0 + 128],
                                                          start=(c == 0), stop=(c == 7)),
                     reads=["w", "xg%d" % b], writes=[psn[bank]], signal=(c == 7 and ts == 3))

    R = [A("R%d" % i, [128, S], BF16) for i in range(4)]
    Vm = A("Vm", [128, 64, 2, 128], BF16)
    QT, KT = R[0], R[1]
    V = A("V", [128, 64, 128], BF16)
    if parts & 1:
        load_x(0)
    for tg in range(NG if parts & 1 else 0):
        if tg + 1 < NG:
            load_x(tg + 1)
        sl = slice(tg * 512, (tg + 1) * 512)
        proj_fm(0, 0, tg)
        P.op("act", lambda e, sl=sl: e.mul(QT[:, sl], ps[0][:], 0.125), reads=["ps0"], writes=["QT"])
        proj_fm(1, 128, tg)
        P.op("dve", lambda e, sl=sl: e.tensor_copy(KT[:, sl], ps[1][:]), reads=["ps1"], writes=["KT"])
        proj_tm(2, 256, tg)
        P.op("act", lambda e, tg=tg: e.copy(V[:, tg * 4:(tg + 1) * 4, :],
                                           ps[2][:].rearrange("p (a b) -> p a b", a=4)),
             reads=["ps2"], writes=["V"])

    Eb = [A("Eb%d" % i, [128, 512], F32) for i in range(2)]
    Lb = [A("Lb%d" % i, [128, 512], BF16) for i in range(3)]
    Ab = [A("Ab%d" % i, [128, 512], BF16) for i in range(3)]
    Lrun = [A("Lrun%d" % i, [128, 512], F32) for i in range(2)]
    Lrb = [[A("Lrb%d_%d" % (h, i), [128, 512], BF16) for i in range(2)] for h in range(2)]

    items = []
    for g in range(NG):
        njt = 4 * g + 4
        for k in range(njt):
            for h in range(2):
                items.append((h, g, k, njt - 1 - k, njt))
    NI = len(items) if parts & 1 else 0

    def sb_s1(i):
        h, g, k, j, njt = items[i]
        zb = i % 4
        hp = slice(h * 64, (h + 1) * 64)
        P.op("pe", lambda e: e.matmul(ps[zb][:], KT[hp, j * 128:(j + 1) * 128], QT[hp, g * 512:(g + 1) * 512],
                                      start=True, stop=True),
             reads=["QT", "KT"], writes=[psn[zb]])

    def sb_s2(i):
        h, g, k, j, njt = items[i]
        zb = i % 4
        eb = "Eb%d" % (i % 2)
        lb = "Lb%d" % (i % 3)
        E = Eb[i % 2]
        L = Lb[i % 3]
        P.op("act", lambda e: e.activation(E[:], ps[zb][:], AF.Exp), reads=[psn[zb]], writes=[eb])

    def sb_s2b(i):
        h, g, k, j, njt = items[i]
        zb = i % 4
        eb = "Eb%d" % (i % 2)
        lb = "Lb%d" % (i % 3)
        E = Eb[i % 2]
        L = Lb[i % 3]
        P.op("act", lambda e: e.activation(L[:], E[:], AF.Ln, bias=1.0), reads=[eb], writes=[lb])
        if k < 4:
            jj = 3 - k
            P.op("dve", lambda e: e.tensor_tensor(L[:], L[:], msb[:, jj, :], ALU.mult),
                 reads=[lb, "msb"], writes=[lb])
        if k < njt - 1:
            lr = "Lrun%d" % h
            nb = "Lrb%d_%d" % (h, (k + 1) % 2)
            dst = Lrb[h][(k + 1) % 2]
            if k == 0:
                P.op("dve", lambda e: e.tensor_copy(Lrun[h][:], L[:]), reads=[lb], writes=[lr])
                P.op("dve", lambda e: e.tensor_copy(dst[:], L[:]), reads=[lb], writes=[nb])
            else:
                P.op("dve", lambda e: e.tensor_tensor(Lrun[h][:], Lrun[h][:], L[:], ALU.add),
                     reads=[lb, lr], writes=[lr])
                P.op("dve", lambda e: e.tensor_copy(dst[:], Lrun[h][:]), reads=[lr], writes=[nb])

    def sb_s3(i):
        h, g, k, j, njt = items[i]
        zb = i % 4
        lb = "Lb%d" % (i % 3)
        L = Lb[i % 3]
        P.op("pe", lambda e: e.matmul(ps[zb][:], negtri[:], L[:], start=False, stop=(k == 0),
                                      skip_group_check=True),
             reads=["negtri", lb], writes=[psn[zb]], signal=(k == 0))
        if k > 0:
            cur = Lrb[h][k % 2]
            P.op("pe", lambda e: e.matmul(ps[zb][:], negones[:], cur[:], start=False, stop=True,
                                          skip_group_check=True),
                 reads=["negones", "Lrb%d_%d" % (h, k % 2)], writes=[psn[zb]])

    def sb_s4(i):
        h, g, k, j, njt = items[i]
        zb = i % 4
        ab = "Ab%d" % (i % 3)
        Aa = Ab[i % 3]
        P.op("act", lambda e: e.activation(Aa[:], ps[zb][:], AF.Exp), reads=[psn[zb]], writes=[ab])
        if k < 4:
            jj = 3 - k
            P.op("dve", lambda e: e.tensor_tensor(Aa[:], Aa[:], msb[:, jj, :], ALU.mult),
                 reads=[ab, "msb"], writes=[ab])

    def sb_s5(i):
        h, g, k, j, njt = items[i]
        ab = "Ab%d" % (i % 3)
        Aa = Ab[i % 3]
        ob = 4 + (2 * g + h) % 3
        P.op("pe", lambda e: e.matmul(ps[ob][0:64, :], V[:, j, h * 64:(h + 1) * 64], Aa[:],
                                      start=(k == 0), stop=(k == njt - 1), skip_group_check=True),
             reads=["V", ab], writes=[psn[ob]], signal=True)
        if k == njt - 1:
            so = "ost%d" % (g % 2)
            P.op("dve", lambda e: e.tensor_copy(ost[g % 2][h * 64:(h + 1) * 64, :], ps[ob][0:64, :]),
                 reads=[psn[ob]], writes=[so + "_%d" % h])
            if h == 1:
                P.dma("sp", so, oT(0, g), ost[g % 2][:],
                      reads=[so + "_0", so + "_1"], writes=["oT0_%d" % g])

    order = [(sb_s1, 0), (sb_s2, 1), (sb_s4, 3), (sb_s2b, 1), (sb_s3, 2), (sb_s5, 4)]
    for step in range(NI + 4):
        for fn, lag in order:
            i = step - lag
            if 0 <= i < NI:
                fn(i)

    P.barrier()
    if not parts & 2:
        return
    QTa = [R[0], R[1]]
    KTa = [R[2], R[3]]
    cosv, sinv = C["cosT"], C["sinT"]
    csb = [[A("cs%d_%d" % (a, i), [128, 512], F32) for i in range(2)] for a in range(2)]
    t1 = A("t1", [128, 512], F32)
    t2 = A("t2", [128, 512], F32)
    qf = A("qf", [128, 512], F32)
    kf = A("kf", [128, 512], F32)
    kmT = A("kmT", [128, 2, 32], F32)
    gm = [A("gm%d" % i, [128, 32], F32) for i in range(2)]
    top8 = [A("top8_%d" % i, [128, 8], F32) for i in range(2)]
    mb1 = [A("mb1_%d" % i, [128, 32], F32) for i in range(2)]
    mbpad = [A("mbpad%d" % i, [128, 128], BF16) for i in range(2)]
    cbrow = A("cbrow", [128, 64], F32)
    vbrow = A("vbrow", [128, 64], F32)
    oorow = A("oorow", [128, 64], F32)
    Pb = [A("Pb%d" % i, [128, 512], BF16) for i in range(3)]
    rec = A("rec", [128, 512], F32)
    bcs = A("bcs", [64, 512], F32)

    P.op("dve", lambda e: e.memset(cbrow[:, 0:32], 0.0), writes=["cbrow"])
    P.op("dve", lambda e: e.memset(cbrow[:, 32:64], NEG), writes=["cbrow"])
    P.op("dve", lambda e: e.memset(vbrow[:, 0:32], BIG), writes=["vbrow"])
    P.op("dve", lambda e: e.memset(vbrow[:, 32:64], 0.0), writes=["vbrow"])
    P.op("dve", lambda e: e.memset(oorow[:], -BIG), writes=["oorow"])
    P.op("dve", lambda e: e.memset(oorow[:, 31:32], 0.0), writes=["oorow"])
    P.op("dve", lambda e: e.memset(kmT[:], 0.0), writes=["kmT"])
    P.op("dve", lambda e: e.memset(Vm[:], 1.0), writes=["Vm"])
    for i in range(2):
        P.op("dve", lambda e, i=i: e.memset(mbpad[i][:], 0.0), writes=["mbpad%d" % i])
    for h in range(2):
        P.op("dve", lambda e, h=h: e.memset(QTa[h][64:128, :], 0.0), writes=["QTa%d" % h])
        P.op("dve", lambda e, h=h: e.memset(KTa[h][64:128, :], 0.0), writes=["KTa%d" % h])
        P.dma("pool", "oh%d" % h, KTa[h][64:96, :], C["onehot"], writes=["KTa%d" % h])

    def load_cs(tg):
        b = tg % 2
        P.dma("sp", "cs0_%d" % b, csb[0][b][:], cosv[:, tg * 512:(tg + 1) * 512], writes=["cs0_%d" % b])
        P.dma("sp", "cs1_%d" % b, csb[1][b][:], sinv[:, tg * 512:(tg + 1) * 512], writes=["cs1_%d" % b])

    def rope(dst, b, bank_a, bank_b):
        cn, sn = "cs0_%d" % b, "cs1_%d" % b
        P.op("dve", lambda e: e.tensor_tensor(t1[:], ps[bank_a][:], csb[0][b][:], ALU.mult),
             reads=[psn[bank_a], cn], writes=["t1"])
        P.op("dve", lambda e: e.tensor_tensor(t2[:], ps[bank_b][:], csb[1][b][:], ALU.mult),
             reads=[psn[bank_b], sn], writes=["t2"])
        nm = "qf" if dst is qf else "kf"
        P.op("dve", lambda e: e.tensor_tensor(dst[:], t1[:], t2[:], ALU.add),
             reads=["t1", "t2"], writes=[nm])

    load_x(0)
    load_cs(0)
    for tg in range(NG):
        if tg + 1 < NG:
            load_x(tg + 1)
            load_cs(tg + 1)
        b = tg % 2
        sl = slice(tg * 512, (tg + 1) * 512)
        proj_fm(0, 512, tg)
        proj_fm(1, 896, tg)
        rope(kf, b, 0, 1)
        for h in range(2):
            P.op("act", lambda e, h=h: e.copy(KTa[h][0:64, sl], kf[h * 64:(h + 1) * 64, :]),
                 reads=["kf"], writes=["KTa%d" % h])
        for h in range(2):
            hp2 = slice(h * 64, (h + 1) * 64)
            P.op("dve", lambda e, h=h, hp2=hp2: e.tensor_reduce(
                kmT[hp2, h, 2 * tg:2 * tg + 2], kf[hp2, :].rearrange("p (a b) -> p a b", a=2), AX.X, ALU.add),
                reads=["kf"], writes=["kmT"])
        proj_fm(2, 384, tg)
        proj_fm(3, 768, tg)
        rope(qf, b, 2, 3)
        for h in range(2):
            P.op("act", lambda e, h=h: e.mul(QTa[h][0:64, sl], qf[h * 64:(h + 1) * 64, :], 0.125),
                 reads=["qf"], writes=["QTa%d" % h])
        proj_tm(4, 640, tg)
        P.op("act", lambda e: e.copy(Vm[:, tg * 4:(tg + 1) * 4, :, 0:64],
                                    ps[4][:].rearrange("p (a h d) -> p a h d", a=4, h=2)),
             reads=["ps4"], writes=["Vm"])
        for idx in range(8 if not (DBG_MB & 2) else 0):
            cq, h = idx // 2, idx % 2
            P.op("pe", lambda e, cq=cq, h=h, idx=idx: e.matmul(
                ps[5][:, idx * 32:(idx + 1) * 32], qf[:, cq * 128:(cq + 1) * 128],
                kmT[:, h, :], start=True, stop=True),
                reads=["qf", "kmT"], writes=["ps5"], signal=False)
        P.op("pe", lambda e: e.matmul(ps[5][:, 256:384], identb[:], identb[:], start=True, stop=True),
             reads=["identb"], writes=["ps5"])
        for rnd in range(1):
            for idx in range(8 if not (DBG_MB & 2) else 0):
                cq, h = idx // 2, idx % 2
                cch = tg * 4 + cq
                own = cch // 2
                u = idx % 2
                P.op("dve", lambda e, idx=idx, own=own, u=u: e.tensor_tensor(
                    gm[u][:], ps[5][:, idx * 32:(idx + 1) * 32], cbrow[:, 32 - own:64 - own], ALU.add),
                    reads=["ps5", "cbrow"], writes=["gm%d" % u])
                P.op("dve", lambda e, u=u: e.max(top8[u][:], gm[u][:]), reads=["gm%d" % u], writes=["top8_%d" % u])
                P.op("dve", lambda e, own=own, u=u: e.scalar_tensor_tensor(
                    mb1[u][:], gm[u][:], top8[u][:, 2:3], vbrow[:, 32 - own:64 - own], ALU.is_ge, ALU.mult),
                    reads=["gm%d" % u, "top8_%d" % u, "vbrow"], writes=["mb1_%d" % u])
                P.op("dve", lambda e, own=own, u=u: e.tensor_tensor(
                    mbpad[u][:, 64:96], mb1[u][:], oorow[:, 31 - own:63 - own], ALU.add),
                    reads=["mb1_%d" % u, "oorow"], writes=["mbpad%d" % u])
                if DBG_MB & 8:
                    continue
                P.op("pe", lambda e, u=u: e.matmul(ps[6][:, u * 128:(u + 1) * 128], mbpad[u][:], identb[:],
                                                    start=True, stop=True),
                     reads=["mbpad%d" % u, "identb"], writes=["ps6_%d" % u], signal=False)
                P.op("pe", lambda e, u=u: e.matmul(ps[6][:, 256 + u * 128:256 + (u + 1) * 128], identb[:], identb[:],
                                                    start=True, stop=True),
                     reads=["identb"], writes=["ps6_%d" % u])
                P.op("act", lambda e, h=h, cch=cch, u=u: e.copy(QTa[h][64:96, cch * 128:(cch + 1) * 128],
                                                              ps[6][64:96, u * 128:(u + 1) * 128]),
                     reads=["ps6_%d" % u], writes=["QTa%d" % h])

    P.barrier()
    mitems = []
    for g in range(NG):
        njt = 4 * g + 4
        for j in range(njt):
            for h in range(2):
                mitems.append((h, g, j, njt))
    NM = len(mitems) if not (DBG_MB & 1) else 0

    def mb_s1(i):
        h, g, j, njt = mitems[i]
        zb = i % 3
        P.op("pe", lambda e: e.matmul(ps[zb][:], KTa[h][:, j * 128:(j + 1) * 128], QTa[h][:, g * 512:(g + 1) * 512],
                                      start=True, stop=True),
             reads=["QTa%d" % h, "KTa%d" % h], writes=[psn[zb]])

    def mb_s2(i):
        h, g, j, njt = mitems[i]
        zb = i % 3
        pb = "Pb%d" % (i % 3)
        Pt = Pb[i % 3]
        P.op("act", lambda e: e.activation(Pt[:], ps[zb][:], AF.Exp), reads=[psn[zb]], writes=[pb])
        if j >= 4 * g:
            jj = j - 4 * g
            P.op("dve", lambda e: e.tensor_tensor(Pt[:], Pt[:], mmb[:, jj, :], ALU.mult),
                 reads=[pb, "mmb"], writes=[pb])

    def mb_s3(i):
        h, g, j, njt = mitems[i]
        pb = "Pb%d" % (i % 3)
        Pt = Pb[i % 3]
        ob = 4 + (2 * g + h) % 3
        P.op("pe", lambda e: e.matmul(ps[ob][:, :], Vm[:, j, h, :], Pt[:],
                                      start=(j == 0), stop=(j == njt - 1), skip_group_check=True),
             reads=["Vm", pb], writes=[psn[ob]], signal=True)
        if j == njt - 1:
            so = "ost%d" % (g % 2)
            if DBG_NORM:
                P.op("dve", lambda e: e.reciprocal(rec[64:128, :], ps[ob][64:128, :]), reads=[psn[ob]], writes=["rec"])
                P.op("dve", lambda e: e.tensor_tensor(ost[g % 2][h * 64:(h + 1) * 64, :], ps[ob][0:64, :],
                                                      rec[64:128, :], ALU.mult),
                     reads=[psn[ob], "rec"], writes=[so + "_%d" % h])
            else:
                P.op("dve", lambda e: e.reciprocal(rec[64:65, :], ps[ob][64:65, :]), reads=[psn[ob]], writes=["rec"])
                P.op("pe", lambda e: e.matmul(ps[3][0:64, :], onesf[64:65, 0:64], rec[64:65, :], start=True, stop=True),
                     reads=["onesf", "rec"], writes=["ps3"])
                P.op("act", lambda e: e.copy(bcs[:], ps[3][0:64, :]), reads=["ps3"], writes=["bcs"])
                P.op("dve", lambda e: e.tensor_tensor(ost[g % 2][h * 64:(h + 1) * 64, :], ps[ob][0:64, :], bcs[:],
                                                      ALU.mult),
                     reads=[psn[ob], "bcs"], writes=[so + "_%d" % h])
            if h == 1:
                P.dma("sp", so, oT(1, g), ost[g % 2][:],
                      reads=[so + "_0", so + "_1"], writes=["oT1_%d" % g])

    mstages = [mb_s1, mb_s2, mb_s3]
    for step in range(NM + len(mstages) - 1):
        for s_, fn in enumerate(mstages):
            i = step - s_
            if 0 <= i < NM:
                fn(i)
    P.barrier()


def build_B(nc, P, ctx, sfx, xT_d, oTs_v, oTm_v, oreads, W, lnp_d, out_d, outb_d=None):
    def A(name, shape, dt):
        return ctx.enter_context(nc.sbuf_tensor(name + sfx, shape, dt))
    xT = A("xT_sb", [128, 8, TOK], F32)
    xb = A("xb_sb", [128, 8, TOK], BF16)
    arena = A("arena", [128, 32768], BF16)
    oTs = arena[:, 0:8192].rearrange("p (a b) -> p a b", a=4)
    oTm = arena[:, 8192:16384].rearrange("p (a b) -> p a b", a=4)
    mg = arena[:, 16384:32768].rearrange("p (a b) -> p a b", a=8)
    hT = arena[:, 0:NFC * 1024].rearrange("p (a b) -> p a b", a=NFC)
    lnp = A("lnp_sb", [128, 32], F32)
    onesD = A("onesD", [128, 128], F32)
    warena = A("warena", [128, 9728], BF16)
    wgb = [warena[:, i * 2048:(i + 1) * 2048].rearrange("p (a b) -> p a b", a=8) for i in range(2)]
    wbb = [warena[:, 4096 + i * 1024:4096 + (i + 1) * 1024].rearrange("p (a b) -> p a b", a=4) for i in range(2)]
    wob = [warena[:, 6144 + i * 1024:6144 + (i + 1) * 1024].rearrange("p (a b) -> p a b", a=8) for i in range(2)]
    wfgu = [warena[:, i * 2048:(i + 1) * 2048].rearrange("p (a b) -> p a b", a=8) for i in range(2)]
    wfdb = [warena[:, 4096 + i * 2816:4096 + (i + 1) * 2816].rearrange("p (a b) -> p a b", a=NFC) for i in range(2)]
    sg = [A("sg%d" % i, [128, 512], F32) for i in range(4)]
    m1 = [A("m1_%d" % i, [128, 512], F32) for i in range(2)]
    m2 = [A("m2_%d" % i, [128, 512], F32) for i in range(2)]
    ysq = [A("ysq%d" % i, [128, 512], F32) for i in range(2)]
    mean_sb = A("mean_sb", [128, 512], F32)
    rstd_sb = A("rstd_sb", [128, 512], F32)
    tn = [A("tn%d" % i, [128, 512], F32) for i in range(2)]
    ps = [ctx.enter_context(nc.psum_tensor("pb%d" % i + sfx, [128, 512], F32)) for i in range(8)]
    psn = ["pb%d" % i for i in range(8)]
    bank = [0]
    XN = [["xT%d_%d" % (c, g) for g in range(NTG)] for c in range(8)]
    XB = [["xb%d_%d" % (c, g) for g in range(NTG)] for c in range(8)]
    MG = [["mg%d_%d" % (c, g) for g in range(NTG)] for c in range(8)]
    HT = [["hT%d_%d" % (k, g) for g in range(2)] for k in range(NFC)]
    allx = [n for r_ in XN for n in r_]
    allxb = [n for r_ in XB for n in r_]

    def nb():
        bank[0] = (bank[0] + 1) % 8
        return bank[0]

    xdv = xT_d.rearrange("(c p) t -> p c t", p=128)
    P.dma("sp", "xT", xT[:], xdv, writes=allx)
    P.dma("pool", "xb", xb[:], xdv, writes=allxb)
    P.dma("sp", "oTs", oTs, oTs_v, reads=oreads, writes=["oTs"])
    P.dma("sp", "oTm", oTm, oTm_v, reads=oreads, writes=["oTm"])
    P.dma("sp", "lnp", lnp[:], lnp_d, writes=["lnp"])
    P.op("dve", lambda e: e.memset(onesD[:], 1.0 / D), writes=["onesD"])

    wGv = W["wG"].rearrange("(k p) n -> p k n", p=128)
    wbsv = W["wbs"].rearrange("(k p) n -> p k n", p=128)
    wbmv = W["wbm"].rearrange("(k p) n -> p k n", p=128)
    wov = W["wout"].rearrange("(k p) n -> p k n", p=128)
    wfgv = W["wfg"].rearrange("(k p) n -> p k n", p=128)
    wfuv = W["wfu"].rearrange("(k p) n -> p k n", p=128)
    wfdv = W["wfd"].rearrange("(k p) n -> p k n", p=128)

    def tsl(tg):
        return slice(tg * 512, (tg + 1) * 512)

    def load_b1(c):
        b = c % 2
        cs = slice(c * 128, (c + 1) * 128)
        P.dma("pool", "wgb%da" % b, wgb[b][:, :, 0:128], wGv[:, :, cs], writes=["wgb%da" % b])
        P.dma("pool", "wgb%db" % b, wgb[b][:, :, 128:256], wGv[:, :, D + c * 128:D + (c + 1) * 128],
              writes=["wgb%db" % b])
        P.dma("pool", "wbb%da" % b, wbb[b][:, :, 0:128], wbsv[:, :, cs], writes=["wbb%da" % b])
        P.dma("pool", "wbb%db" % b, wbb[b][:, :, 128:256], wbmv[:, :, cs], writes=["wbb%db" % b])

    load_b1(0)
    it = 0
    for c in range(8):
        if c + 1 < 8:
            load_b1(c + 1)
        b = c % 2
        for tg in range(NTG):
            t = tsl(tg)
            bk = [nb() for _ in range(4)]
            for half, (bkk, wn) in enumerate(zip(bk[0:2], ["wgb%da" % b, "wgb%db" % b])):
                for k in range(8):
                    P.op("pe", lambda e, k=k, half=half, bkk=bkk: e.matmul(
                        ps[bkk][:], wgb[b][:, k, half * 128:(half + 1) * 128], xb[:, k, t],
                        start=(k == 0), stop=(k == 7)),
                        reads=[wn, XB[k][tg]], writes=[psn[bkk]], signal=(k == 7))
            for half, (bkk, wn, src, sn) in enumerate(zip(bk[2:4], ["wbb%da" % b, "wbb%db" % b],
                                                          [oTs, oTm], ["oTs", "oTm"])):
                for k in range(4):
                    P.op("pe", lambda e, k=k, half=half, bkk=bkk, src=src: e.matmul(
                        ps[bkk][:], wbb[b][:, k, half * 128:(half + 1) * 128], src[:, k, t],
                        start=(k == 0), stop=(k == 3)),
                        reads=[wn, sn], writes=[psn[bkk]], signal=(k == 3))
            u = it % 2
            s0, s1 = sg[2 * u], sg[2 * u + 1]
            P.op("act", lambda e, s0=s0, bkk=bk[0]: e.activation(s0[:], ps[bkk][:], AF.Sigmoid),
                 reads=[psn[bk[0]]], writes=["sg%d" % (2 * u)])
            P.op("act", lambda e, s1=s1, bkk=bk[1]: e.activation(s1[:], ps[bkk][:], AF.Sigmoid),
                 reads=[psn[bk[1]]], writes=["sg%d" % (2 * u + 1)])
            P.op("dve", lambda e, s0=s0, u=u, bkk=bk[2]: e.tensor_tensor(m1[u][:], s0[:], ps[bkk][:], ALU.mult),
                 reads=[psn[bk[2]], "sg%d" % (2 * u)], writes=["m1_%d" % u])
            P.op("dve", lambda e, s1=s1, u=u, bkk=bk[3]: e.tensor_tensor(m2[u][:], s1[:], ps[bkk][:], ALU.mult),
                 reads=[psn[bk[3]], "sg%d" % (2 * u + 1)], writes=["m2_%d" % u])
            P.op("pool", lambda e, u=u, c=c, t=t: e.tensor_tensor(mg[:, c, t], m1[u][:], m2[u][:], ALU.add),
                 reads=["m1_%d" % u, "m2_%d" % u], writes=[MG[c][tg]])
            it += 1

    def layer_norm(tg, gcol, bcol):
        t = tsl(tg)
        ba, bb_ = nb(), nb()
        for c in range(8):
            q = ysq[c % 2]
            P.op("act", lambda e, q=q, c=c: e.activation(q[:], xT[:, c, t], AF.Square),
                 reads=[XN[c][tg]], writes=["ysq%d" % (c % 2)])
            P.op("pe", lambda e, c=c: e.matmul(ps[ba][:], onesD[:], xT[:, c, t], start=(c == 0), stop=(c == 7)),
                 reads=["onesD", XN[c][tg]], writes=[psn[ba]], signal=(c == 7))
            P.op("pe", lambda e, q=q, c=c: e.matmul(ps[bb_][:], onesD[:], q[:], start=(c == 0), stop=(c == 7),
                                                   skip_group_check=True),
                 reads=["onesD", "ysq%d" % (c % 2)], writes=[psn[bb_]], signal=True)
        P.op("dve", lambda e: e.tensor_copy(mean_sb[:], ps[ba][:]), reads=[psn[ba]], writes=["mean_sb"])
        P.op("dve", lambda e: e.tensor_tensor(rstd_sb[:], mean_sb[:], mean_sb[:], ALU.mult),
             reads=["mean_sb"], writes=["rstd_sb"])
        P.op("dve", lambda e: e.tensor_tensor(rstd_sb[:], ps[bb_][:], rstd_sb[:], ALU.subtract),
             reads=[psn[bb_], "rstd_sb"], writes=["rstd_sb"])
        P.op("act", lambda e: e.activation(rstd_sb[:], rstd_sb[:], AF.Ln, bias=EPS),
             reads=["rstd_sb"], writes=["rstd_sb"])
        P.op("act", lambda e: e.activation(rstd_sb[:], rstd_sb[:], AF.Exp, scale=-0.5),
             reads=["rstd_sb"], writes=["rstd_sb"])
        for c in range(8):
            tt = tn[c % 2]
            tnn = "tn%d" % (c % 2)
            P.op("dve", lambda e, tt=tt, c=c: e.tensor_tensor(tt[:], xT[:, c, t], mean_sb[:], ALU.subtract),
                 reads=[XN[c][tg], "mean_sb"], writes=[tnn])
            P.op("dve", lambda e, tt=tt: e.tensor_tensor(tt[:], tt[:], rstd_sb[:], ALU.mult),
                 reads=[tnn, "rstd_sb"], writes=[tnn])
            P.op("act", lambda e, tt=tt, c=c: e.activation(xT[:, c, t], tt[:], AF.Identity,
                                                          bias=lnp[:, bcol + c:bcol + c + 1],
                                                          scale=lnp[:, gcol + c:gcol + c + 1]),
                 reads=[tnn, "lnp"], writes=[XN[c][tg]])
            P.op("pool", lambda e, c=c: e.tensor_copy(xb[:, c, t], xT[:, c, t]), reads=[XN[c][tg]], writes=[XB[c][tg]])

    def load_wo(c):
        b = c % 2
        P.dma("pool", "wob%d" % b, wob[b][:], wov[:, :, c * 128:(c + 1) * 128], writes=["wob%d" % b])

    load_wo(0)
    for c in range(8):
        if c + 1 < 8:
            load_wo(c + 1)
        b = c % 2
        for tg in range(NTG):
            t = tsl(tg)
            bkk = nb()
            for k in range(8):
                P.op("pe", lambda e, k=k, bkk=bkk: e.matmul(ps[bkk][:], wob[b][:, k, :], mg[:, k, t],
                                                            start=(k == 0), stop=(k == 7)),
                     reads=["wob%d" % b, MG[k][tg]], writes=[psn[bkk]], signal=(k == 7))
            P.op("dve", lambda e, bkk=bkk, c=c, t=t: e.scalar_tensor_tensor(
                xT[:, c, t], xT[:, c, t], ALPHA, ps[bkk][:], ALU.mult, ALU.add),
                reads=[psn[bkk], XN[c][tg]], writes=[XN[c][tg]])
    for tg in range(NTG):
        layer_norm(tg, 0, 8)

    def load_gu(i):
        k = i % NFC
        b = i % 2
        P.dma("pool", "wfgu%da" % b, wfgu[b][:, :, 0:128], wfgv[:, :, k * 128:(k + 1) * 128],
              writes=["wfgu%da" % b])
        P.dma("pool", "wfgu%db" % b, wfgu[b][:, :, 128:256], wfuv[:, :, k * 128:(k + 1) * 128],
              writes=["wfgu%db" % b])

    def load_dn(i):
        c = i % 8
        b = i % 2
        P.dma("pool", "wfdb%d" % b, wfdb[b][:], wfdv[:, :, c * 128:(c + 1) * 128], writes=["wfdb%d" % b])

    gi = 0
    di = 0
    P.barrier()
    for hf in range(2):
        load_gu(gi)
        for k in range(NFC):
            if k + 1 < NFC:
                load_gu(gi + 1)
            b = gi % 2
            for t2_ in range(2):
                tg = hf * 2 + t2_
                t = tsl(tg)
                b0, b1 = nb(), nb()
                for half, bkk in enumerate([b0, b1]):
                    wn = "wfgu%d%s" % (b, "ab"[half])
                    for kk in range(8):
                        P.op("pe", lambda e, kk=kk, half=half, bkk=bkk: e.matmul(
                            ps[bkk][:], wfgu[b][:, kk, half * 128:(half + 1) * 128], xb[:, kk, t],
                            start=(kk == 0), stop=(kk == 7)),
                            reads=[wn, XB[kk][tg]], writes=[psn[bkk]], signal=(kk == 7))
                u = it % 2
                s0 = sg[2 * u]
                P.op("act", lambda e, s0=s0, b0=b0: e.activation(s0[:], ps[b0][:], AF.Silu),
                     reads=[psn[b0]], writes=["sg%d" % (2 * u)])
                P.op("dve", lambda e, s0=s0, b1=b1, k=k, t2_=t2_: e.tensor_tensor(
                    hT[:, k, t2_ * 512:(t2_ + 1) * 512], s0[:], ps[b1][:], ALU.mult),
                    reads=[psn[b1], "sg%d" % (2 * u)], writes=[HT[k][t2_]])
                it += 1
            gi += 1
        load_dn(di)
        for c in range(8):
            if c + 1 < 8:
                load_dn(di + 1)
            b = di % 2
            for t2_ in range(2):
                tg = hf * 2 + t2_
                t = tsl(tg)
                bkk = nb()
                for k in range(NFC):
                    P.op("pe", lambda e, k=k, bkk=bkk, t2_=t2_: e.matmul(
                        ps[bkk][:], wfdb[b][:, k, :], hT[:, k, t2_ * 512:(t2_ + 1) * 512],
                        start=(k == 0), stop=(k == NFC - 1)),
                        reads=["wfdb%d" % b, HT[k][t2_]], writes=[psn[bkk]], signal=(k == NFC - 1))
                P.op("dve", lambda e, bkk=bkk, c=c, t=t: e.scalar_tensor_tensor(
                    xT[:, c, t], xT[:, c, t], ALPHA, ps[bkk][:], ALU.mult, ALU.add),
                    reads=[psn[bkk], XN[c][tg]], writes=[XN[c][tg]])
            di += 1
        for t2_ in range(2):
            layer_norm(hf * 2 + t2_, 16, 24)
    P.dma("sp", "out", out_d.rearrange("(c p) t -> p c t", p=128), xT[:], reads=allx, writes=["out"])
    if outb_d is not None:
        P.dma("sp", "outb", outb_d.rearrange("(c p) t -> p c t", p=128), xb[:], reads=allxb, writes=["outb"])


W_KEYS = ["wG", "wbs", "wbm", "wout", "wfg", "wfu", "wfd"]
W_SHAPES = {"wG": [D, 2048], "wbs": [512, D], "wbm": [512, D], "wout": [D, D],
            "wfg": [D, DFF], "wfu": [D, DFF], "wfd": [DFF, D]}
RG = [[0, 1, 2, 3], [4, 5, 6, 7]]


def _pp(v):
    return np.ascontiguousarray(v.reshape(8, 128).T)


def build_fused(nlayers=DEPTH, skipA=False, skipB=False, samex=False, parts=3):
    nc = bass.Bass("TRN2", target_bir_lowering=False)
    P = Prog(nc)
    hc = host_consts()
    C = {k: nc.dram_tensor("c_" + k, list(v.shape), F32, kind="ExternalInput").ap() for k, v in hc.items()}
    xTall0 = nc.dram_tensor("xTall0", [D, S], F32, kind="ExternalInput").ap()
    xT0 = nc.dram_tensor("xT0", [D, TOK], F32, kind="ExternalInput").ap()
    wA = nc.dram_tensor("wA", [DEPTH, D, 768], F32, kind="ExternalInput").ap()
    Wd = {k: nc.dram_tensor(k, [DEPTH] + W_SHAPES[k], F32, kind="ExternalInput").ap() for k in W_KEYS}
    lnp = nc.dram_tensor("lnp", [DEPTH, 128, 32], F32, kind="ExternalInput").ap()
    out = nc.dram_tensor("out", [D, TOK], F32, kind="ExternalOutput").ap()
    oT_loc = nc.dram_tensor("oT_loc", [1024, TOK], BF16, kind="Internal").ap()
    oT_all = nc.dram_tensor("oT_all", [4096, TOK], BF16, kind="Internal").ap()
    xb_loc = nc.dram_tensor("xb_loc", [D, TOK], BF16, kind="Internal").ap()
    xb_all = nc.dram_tensor("xb_all", [4 * D, TOK], BF16, kind="Internal").ap()
    xres = nc.dram_tensor("xres", [D, TOK], F32, kind="Internal").ap()
    pid = nc.sync.partition_id()
    roff = (pid % 4) * 1024
    olv = oT_loc.rearrange("(q b p) t -> b p q t", q=4, b=2)

    def odst(br, g):
        return olv[br, :, g // 4, (g % 4) * 512:(g % 4 + 1) * 512]

    for l in range(nlayers):
        last = l == nlayers - 1
        with ExitStack() as ctx:
            if l == 0 or samex:
                xv0 = xTall0.rearrange("(c p) t -> p c t", p=128)
                xsrc = lambda tg: [(0, 8, xv0[:, :, tg * 512:(tg + 1) * 512])]
                xeng, xreads = "pool", []
            else:
                xv1 = xb_all.rearrange("(j r c p) t -> p r j c t", j=4, r=4, c=2, p=128)
                xsrc = lambda tg: [(2 * j, 2 * j + 2, xv1[:, tg // 4, j, :, (tg % 4) * 512:(tg % 4 + 1) * 512])
                                   for j in range(4)]
                xeng, xreads = "sp", ["xb_all%d" % j for j in range(4)]
            if not skipA:
                build_A(nc, P, ctx, "_a%d" % l, xsrc, xeng, xreads, wA[l], odst, C, parts)
        P.barrier()
        for q in range(4):
            P.collective("oT%d" % q, "AllGather", RG, oT_loc[q * 256:(q + 1) * 256, :],
                         oT_all[q * 1024:(q + 1) * 1024, :], writes=["oT_all%d" % q])
        with ExitStack() as ctx:
            ov = oT_all[bass.ds(roff, 1024), :].rearrange("(k b p) t -> p k b t", k=4, b=2, p=128)
            oTs_v = ov[:, :, 0, :]
            oTm_v = ov[:, :, 1, :]
            Wl = {k: Wd[k][l] for k in W_KEYS}
            if not skipB:
              build_B(nc, P, ctx, "_b%d" % l, xT0 if l == 0 else xres, oTs_v, oTm_v,
                    ["oT_all%d" % q for q in range(4)], Wl, lnp[l],
                    out if last else xres, None if last else xb_loc)
        P.barrier()
        if not last:
            for j in range(4):
                P.collective("xb%d" % j, "AllGather", RG, xb_loc[j * 256:(j + 1) * 256, :],
                             xb_all[j * 1024:(j + 1) * 1024, :], writes=["xb_all%d" % j])
    P.finish("sp")
    return nc, hc


def _wA_slices(w_in_l, r):
    def cols(base):
        return w_in_l[:, base + 128 * r: base + 128 * r + 128]
    return np.concatenate([cols(0), cols(512), cols(1024), cols(1536), cols(2048), cols(2560)], axis=1)


def kernel(x, w_in, w_branch_sb, w_branch_moba, w_out, ln_mix_g, ln_mix_b,
           w_ffn_gate, w_ffn_up, w_ffn_down, ln_ffn_g, ln_ffn_b):
    f = lambda a: np.asarray(a, dtype=np.float32)
    x, w_in, w_branch_sb, w_branch_moba, w_out = f(x), f(w_in), f(w_branch_sb), f(w_branch_moba), f(w_out)
    ln_mix_g, ln_mix_b, ln_ffn_g, ln_ffn_b = f(ln_mix_g), f(ln_mix_b), f(ln_ffn_g), f(ln_ffn_b)
    w_ffn_gate, w_ffn_up, w_ffn_down = f(w_ffn_gate), f(w_ffn_up), f(w_ffn_down)
    nc, hc = build_fused()
    cores = list(range(8))
    xTb = [np.ascontiguousarray(x[b].T) for b in range(B)]
    Wfull = {"wG": np.ascontiguousarray(w_in[:, :, 3072:5120]), "wbs": w_branch_sb, "wbm": w_branch_moba,
             "wout": w_out, "wfg": w_ffn_gate, "wfu": w_ffn_up, "wfd": w_ffn_down}
    lnp = np.ascontiguousarray(np.stack([np.concatenate(
        [_pp(ln_mix_g[l]), _pp(ln_mix_b[l]), _pp(ln_ffn_g[l]), _pp(ln_ffn_b[l])], axis=1) for l in range(DEPTH)]))
    maps = []
    for c in cores:
        b, r = c // 4, c % 4
        m = {"xTall0": xTb[b], "xT0": np.ascontiguousarray(xTb[b][:, r * TOK:(r + 1) * TOK]),
             "wA": np.ascontiguousarray(np.stack([_wA_slices(w_in[l], r) for l in range(DEPTH)])),
             "lnp": lnp}
        m.update({k: np.ascontiguousarray(v) for k, v in Wfull.items()})
        m.update({"c_" + k: v for k, v in hc.items()})
        maps.append(m)
    res = run_bass_kernel_spmd(nc, maps, core_ids=cores)
    xn = np.empty_like(x)
    for c in cores:
        b, r = c // 4, c % 4
        xn[b, r * TOK:(r + 1) * TOK, :] = np.asarray(res.results[c]["out"]).T
    return xn
```

```python
from contextlib import ExitStack
import os
DBG_GATE = int(os.environ.get('DBG_GATE', '1'))
DBG_NORM = int(os.environ.get('DBG_NORM', '1'))
DBG_MB = int(os.environ.get('DBG_MB', '0'))
import numpy as np
import ml_dtypes
import concourse.bass as bass
import concourse.mybir as mybir
from concourse.bass_utils import run_bass_kernel_spmd

F32 = mybir.dt.float32
BF16 = mybir.dt.bfloat16
AF = mybir.ActivationFunctionType
ALU = mybir.AluOpType
AX = mybir.AxisListType

D = 1024
S = 8192
B = 2
DEPTH = 2
DFF = 2816
NFC = DFF // 128
NG = S // 512
ALPHA = (2 * DEPTH) ** 0.25
EPS = 1e-5
BIG = 30000.0
NEG = -1.0e30
TOK = 2048
NTG = TOK // 512


class Prog:
    SAME_ENGINE_SYNC = True

    def __init__(self, nc):
        self.nc = nc
        self.eng = {"pe": nc.tensor, "act": nc.scalar, "dve": nc.vector,
                    "pool": nc.gpsimd, "sp": nc.sync}
        self.sem = {}
        self.cnt = {}
        self.res = {}
        self.known = {e: {} for e in self.eng}
        self.nwait = 0
        self.nops = 0

    def _sem(self, lane):
        if lane not in self.sem:
            self.sem[lane] = self.nc.alloc_semaphore(name="s_" + lane.replace(":", "_"))
            self.cnt[lane] = 0
        return self.sem[lane]

    def _need(self, e, lane, val):
        if lane == e and (e == "pe" or not self.SAME_ENGINE_SYNC):
            return
        if self.known[e].get(lane, 0) >= val:
            return
        self.eng[e].wait_ge(self._sem(lane), val)
        self.known[e][lane] = val
        self.nwait += 1

    def _deps(self, e, reads, writes):
        for r in reads:
            ent = self.res.get(r)
            if ent and ent[0]:
                self._need(e, *ent[0])
        for w in writes:
            ent = self.res.get(w)
            if ent:
                if ent[0]:
                    self._need(e, *ent[0])
                for lane, val in ent[1].items():
                    self._need(e, lane, val)

    def _record(self, lane, val, reads, writes):
        for r in reads:
            ent = self.res.setdefault(r, [None, {}])
            ent[1][lane] = max(ent[1].get(lane, 0), val)
        for w in writes:
            self.res[w] = [(lane, val), {}]

    def op(self, e, fn, reads=(), writes=(), signal=True):
        self._sem(e)
        self._deps(e, reads, writes)
        ins = fn(self.eng[e])
        val = self.cnt[e] + 1
        if signal:
            ins.then_inc(self.sem[e], 1)
            self.cnt[e] = val
        self._record(e, val, reads, writes)
        self.nops += 1
        return ins

    def dma(self, e, lane, out, in_, reads=(), writes=(), **kw):
        lane = "d:" + lane
        self._sem(lane)
        self._deps(e, reads, writes)
        ins = self.eng[e].dma_start(out=out, in_=in_, **kw)
        val = self.cnt[lane] + 16
        ins.then_inc(self.sem[lane], 16)
        self.cnt[lane] = val
        self._record(lane, val, reads, writes)
        self.nops += 1
        return ins

    def collective(self, lane, kind, rg, in_ap, out_ap, reads=(), writes=()):
        lane = "c:" + lane
        self._sem(lane)
        self._deps("pool", reads, writes)
        ins = self.nc.gpsimd.collective_compute(kind, ALU.bypass, replica_groups=rg,
                                                ins=[in_ap], outs=[out_ap])
        val = self.cnt[lane] + 1
        ins.then_inc(self.sem[lane], 1)
        self.cnt[lane] = val
        self._record(lane, val, reads, writes)
        self.nops += 1
        return ins

    def barrier(self):
        for e in self.eng:
            for lane, val in self.cnt.items():
                if val > 0 and lane != e:
                    if self.known[e].get(lane, 0) < val:
                        self.eng[e].wait_ge(self.sem[lane], val)
                        self.known[e][lane] = val
        self.res = {}

    def finish(self, e="sp"):
        for lane, val in self.cnt.items():
            if val > 0 and lane != e:
                self.eng[e].wait_ge(self.sem[lane], val)


def host_consts():
    c = {}
    k = np.arange(128)[:, None]
    s = np.arange(128)[None, :]
    c["negtri"] = np.where(k >= s, -1.0, 0.0).astype(np.float32)
    c["negones"] = np.full((128, 128), -1.0, np.float32)
    c["ident"] = np.eye(128, dtype=np.float32)
    sk = np.arange(128)[:, None, None] + 128 * np.arange(4)[None, :, None]
    t = np.arange(512)[None, None, :]
    c["msb"] = (sk < t).astype(np.float32)
    c["mmb"] = (sk <= t).astype(np.float32)
    half = 8
    inv = (500000.0 ** (-np.arange(0, 16, 2, dtype=np.float32) / 16)).astype(np.float32)
    ang = (np.arange(S, dtype=np.float32)[:, None] * inv[None, :]).astype(np.float32)
    cos = np.cos(ang).astype(np.float32).T
    sin = np.sin(ang).astype(np.float32).T
    cosT = np.ones((64, S), np.float32)
    sinT = np.zeros((64, S), np.float32)
    cosT[0:8] = cos
    cosT[8:16] = cos
    sinT[0:8] = sin
    sinT[8:16] = sin
    c["cosT"] = np.concatenate([cosT, cosT], 0)
    c["sinT"] = np.concatenate([sinT, sinT], 0)
    n = np.arange(32)[None, :]
    own = (np.arange(64) // 2)[:, None]
    cb = np.where(n < own, 0.0, NEG).astype(np.float32)
    vb = np.where(n < own, BIG, 0.0).astype(np.float32)
    oo = np.where(n == own, 0.0, -BIG).astype(np.float32)
    c["cb"] = np.broadcast_to(cb.reshape(1, 64 * 32), (128, 64 * 32)).copy()
    c["vb"] = np.broadcast_to(vb.reshape(1, 64 * 32), (128, 64 * 32)).copy()
    c["oo"] = np.broadcast_to(oo.reshape(1, 64 * 32), (128, 64 * 32)).copy()
    c["onehot"] = (np.arange(32)[:, None] == (np.arange(S)[None, :] // 256)).astype(np.float32)
    return c


def build_A(nc, P, ctx, sfx, xsrc, xeng, xreads, wA, oT, C, parts=3):
    def A(name, shape, dt):
        return ctx.enter_context(nc.sbuf_tensor(name + sfx, shape, dt))
    wv = wA.rearrange("(c p) n -> p c n", p=128)

    negtri = A("negtri", [128, 128], BF16)
    negones = A("negones", [128, 128], BF16)
    identb = A("identb", [128, 128], BF16)
    onesf = A("onesf", [128, 64], F32)
    msb = A("msb", [128, 4, 512], BF16)
    mmb = A("mmb", [128, 4, 512], BF16)
    w = A("wA_sb", [128, 8, 1024], BF16)
    P.dma("pool", "c0", negtri[:], C["negtri"], writes=["negtri"])
    P.dma("pool", "c1", negones[:], C["negones"], writes=["negones"])
    P.dma("pool", "c2", identb[:], C["ident"], writes=["identb"])
    P.dma("pool", "c3", msb[:], C["msb"], writes=["msb"])
    P.dma("pool", "c4", mmb[:], C["mmb"], writes=["mmb"])
    P.dma("pool", "w", w[:, :, 0:768], wv, writes=["w"])
    P.op("dve", lambda e: e.memset(onesf[:], 1.0), writes=["onesf"])
    P.op("dve", lambda e: e.memset(w[:, :, 768:1024], 0.0), reads=["w"], writes=["w"])
    for qk in range(2):
        for hd in range(2):
            src = 384 + qk * 128 + hd * 64
            dst = 768 + qk * 128 + hd * 64
            P.op("act", lambda e, s_=src, d_=dst: e.mul(w[:, :, d_:d_ + 8], w[:, :, s_ + 8:s_ + 16], -1.0),
                 reads=["w"], writes=["w"])
            P.op("act", lambda e, s_=src, d_=dst: e.copy(w[:, :, d_ + 8:d_ + 16], w[:, :, s_:s_ + 8]),
                 reads=["w"], writes=["w"])

    psbig = ctx.enter_context(nc.psum_tensor("psbig" + sfx, [128, 4096], F32))
    ps = [psbig[:, i * 512:(i + 1) * 512] for i in range(8)]
    psn = ["ps%d" % i for i in range(7)]

    xg = [A("xg%d" % i, [128, 8, 512], BF16) for i in range(2)]
    ost = [A("ost%d" % i, [128, 512], BF16) for i in range(2)]

    def load_x(tg):
        b = tg % 2
        for pi, (c0, c1, sap) in enumerate(xsrc(tg)):
            P.dma(xeng, "xg%d_%d" % (b, pi), xg[b][:, c0:c1, :], sap, reads=xreads, writes=["xg%d" % b])

    def proj_fm(bank, col0, tg):
        b = tg % 2
        for c in range(8):
            P.op("pe", lambda e, c=c: e.matmul(ps[bank][:], w[:, c, col0:col0 + 128], xg[b][:, c, :],
                                               start=(c == 0), stop=(c == 7)),
                 reads=["w", "xg%d" % b], writes=[psn[bank]], signal=(c == 7))

    def proj_tm(bank, col0, tg, ncols=128):
        b = tg % 2
        for ts in range(4):
            for c in range(8):
                P.op("pe", lambda e, c=c, ts=ts: e.matmul(ps[bank][:, ts * 128:(ts + 1) * 128],
                                                          xg[b][:, c, ts * 128:(ts + 1) * 128],
                                                          w[:, c, col0:col0 + 128],
                                                          start=(c == 0), stop=(c == 7)),
                     reads=["w", "xg%d" % b], writes=[psn[bank]], signal=(c == 7 and ts == 3))

    R = [A("R%d" % i, [128, S], BF16) for i in range(4)]
    Vm = A("Vm", [128, 64, 2, 128], BF16)
    QT, KT = R[0], R[1]
    V = A("V", [128, 64, 128], BF16)
    if parts & 1:
        load_x(0)
    for tg in range(NG if parts & 1 else 0):
        if tg + 1 < NG:
            load_x(tg + 1)
        sl = slice(tg * 512, (tg + 1) * 512)
        proj_fm(0, 0, tg)
        P.op("act", lambda e, sl=sl: e.mul(QT[:, sl], ps[0][:], 0.125), reads=["ps0"], writes=["QT"])
        proj_fm(1, 128, tg)
        P.op("dve", lambda e, sl=sl: e.tensor_copy(KT[:, sl], ps[1][:]), reads=["ps1"], writes=["KT"])
        proj_tm(2, 256, tg)
        P.op("act", lambda e, tg=tg: e.copy(V[:, tg * 4:(tg + 1) * 4, :],
                                           ps[2][:].rearrange("p (a b) -> p a b", a=4)),
             reads=["ps2"], writes=["V"])

    Eb = [A("Eb%d" % i, [128, 512], F32) for i in range(2)]
    Lb = [A("Lb%d" % i, [128, 512], BF16) for i in range(3)]
    Ab = [A("Ab%d" % i, [128, 512], BF16) for i in range(3)]
    Lrun = [A("Lrun%d" % i, [128, 512], F32) for i in range(2)]
    Lrb = [[A("Lrb%d_%d" % (h, i), [128, 512], BF16) for i in range(2)] for h in range(2)]

    items = []
    for g in range(NG):
        njt = 4 * g + 4
        for k in range(njt):
            for h in range(2):
                items.append((h, g, k, njt - 1 - k, njt))
    NI = len(items) if parts & 1 else 0

    def sb_s1(i):
        h, g, k, j, njt = items[i]
        zb = i % 4
        hp = slice(h * 64, (h + 1) * 64)
        P.op("pe", lambda e: e.matmul(ps[zb][:], KT[hp, j * 128:(j + 1) * 128], QT[hp, g * 512:(g + 1) * 512],
                                      start=True, stop=True),
             reads=["QT", "KT"], writes=[psn[zb]])

    def sb_s2(i):
        h, g, k, j, njt = items[i]
        zb = i % 4
        eb = "Eb%d" % (i % 2)
        lb = "Lb%d" % (i % 3)
        E = Eb[i % 2]
        L = Lb[i % 3]
        P.op("act", lambda e: e.activation(E[:], ps[zb][:], AF.Exp), reads=[psn[zb]], writes=[eb])

    def sb_s2b(i):
        h, g, k, j, njt = items[i]
        zb = i % 4
        eb = "Eb%d" % (i % 2)
        lb = "Lb%d" % (i % 3)
        E = Eb[i % 2]
        L = Lb[i % 3]
        P.op("act", lambda e: e.activation(L[:], E[:], AF.Ln, bias=1.0), reads=[eb], writes=[lb])
        if k < 4:
            jj = 3 - k
            P.op("dve", lambda e: e.tensor_tensor(L[:], L[:], msb[:, jj, :], ALU.mult),
                 reads=[lb, "msb"], writes=[lb])
        if k < njt - 1:
            lr = "Lrun%d" % h
            nb = "Lrb%d_%d" % (h, (k + 1) % 2)
            dst = Lrb[h][(k + 1) % 2]
            if k == 0:
                P.op("dve", lambda e: e.tensor_copy(Lrun[h][:], L[:]), reads=[lb], writes=[lr])
                P.op("dve", lambda e: e.tensor_copy(dst[:], L[:]), reads=[lb], writes=[nb])
            else:
                P.op("dve", lambda e: e.tensor_tensor(Lrun[h][:], Lrun[h][:], L[:], ALU.add),
                     reads=[lb, lr], writes=[lr])
                P.op("dve", lambda e: e.tensor_copy(dst[:], Lrun[h][:]), reads=[lr], writes=[nb])

    def sb_s3(i):
        h, g, k, j, njt = items[i]
        zb = i % 4
        lb = "Lb%d" % (i % 3)
        L = Lb[i % 3]
        P.op("pe", lambda e: e.matmul(ps[zb][:], negtri[:], L[:], start=False, stop=(k == 0),
                                      skip_group_check=True),
             reads=["negtri", lb], writes=[psn[zb]], signal=(k == 0))
        if k > 0:
            cur = Lrb[h][k % 2]
            P.op("pe", lambda e: e.matmul(ps[zb][:], negones[:], cur[:], start=False, stop=True,
                                          skip_group_check=True),
                 reads=["negones", "Lrb%d_%d" % (h, k % 2)], writes=[psn[zb]])

    def sb_s4(i):
        h, g, k, j, njt = items[i]
        zb = i % 4
        ab = "Ab%d" % (i % 3)
        Aa = Ab[i % 3]
        P.op("act", lambda e: e.activation(Aa[:], ps[zb][:], AF.Exp), reads=[psn[zb]], writes=[ab])
        if k < 4:
            jj = 3 - k
            P.op("dve", lambda e: e.tensor_tensor(Aa[:], Aa[:], msb[:, jj, :], ALU.mult),
                 reads=[ab, "msb"], writes=[ab])

    def sb_s5(i):
        h, g, k, j, njt = items[i]
        ab = "Ab%d" % (i % 3)
        Aa = Ab[i % 3]
        ob = 4 + (2 * g + h) % 3
        P.op("pe", lambda e: e.matmul(ps[ob][0:64, :], V[:, j, h * 64:(h + 1) * 64], Aa[:],
                                      start=(k == 0), stop=(k == njt - 1), skip_group_check=True),
             reads=["V", ab], writes=[psn[ob]], signal=True)
        if k == njt - 1:
            so = "ost%d" % (g % 2)
            P.op("dve", lambda e: e.tensor_copy(ost[g % 2][h * 64:(h + 1) * 64, :], ps[ob][0:64, :]),
                 reads=[psn[ob]], writes=[so + "_%d" % h])
            if h == 1:
                P.dma("sp", so, oT(0, g), ost[g % 2][:],
                      reads=[so + "_0", so + "_1"], writes=["oT0_%d" % g])

    order = [(sb_s1, 0), (sb_s2, 1), (sb_s4, 3), (sb_s2b, 1), (sb_s3, 2), (sb_s5, 4)]
    for step in range(NI + 4):
        for fn, lag in order:
            i = step - lag
            if 0 <= i < NI:
                fn(i)

    P.barrier()
    if not parts & 2:
        return
    QTa = [R[0], R[1]]
    KTa = [R[2], R[3]]
    cosv, sinv = C["cosT"], C["sinT"]
    csb = [[A("cs%d_%d" % (a, i), [128, 512], F32) for i in range(2)] for a in range(2)]
    t1 = A("t1", [128, 512], F32)
    t2 = A("t2", [128, 512], F32)
    qf = A("qf", [128, 512], F32)
    kf = A("kf", [128, 512], F32)
    kmT = A("kmT", [128, 2, 32], F32)
    gm = [A("gm%d" % i, [128, 32], F32) for i in range(2)]
    top8 = [A("top8_%d" % i, [128, 8], F32) for i in range(2)]
    mb1 = [A("mb1_%d" % i, [128, 32], F32) for i in range(2)]
    mbpad = [A("mbpad%d" % i, [128, 128], BF16) for i in range(2)]
    cbrow = A("cbrow", [128, 64], F32)
    vbrow = A("vbrow", [128, 64], F32)
    oorow = A("oorow", [128, 64], F32)
    Pb = [A("Pb%d" % i, [128, 512], BF16) for i in range(3)]
    rec = A("rec", [128, 512], F32)
    bcs = A("bcs", [64, 512], F32)

    P.op("dve", lambda e: e.memset(cbrow[:, 0:32], 0.0), writes=["cbrow"])
    P.op("dve", lambda e: e.memset(cbrow[:, 32:64], NEG), writes=["cbrow"])
    P.op("dve", lambda e: e.memset(vbrow[:, 0:32], BIG), writes=["vbrow"])
    P.op("dve", lambda e: e.memset(vbrow[:, 32:64], 0.0), writes=["vbrow"])
    P.op("dve", lambda e: e.memset(oorow[:], -BIG), writes=["oorow"])
    P.op("dve", lambda e: e.memset(oorow[:, 31:32], 0.0), writes=["oorow"])
    P.op("dve", lambda e: e.memset(kmT[:], 0.0), writes=["kmT"])
    P.op("dve", lambda e: e.memset(Vm[:], 1.0), writes=["Vm"])
    for i in range(2):
        P.op("dve", lambda e, i=i: e.memset(mbpad[i][:], 0.0), writes=["mbpad%d" % i])
    for h in range(2):
        P.op("dve", lambda e, h=h: e.memset(QTa[h][64:128, :], 0.0), writes=["QTa%d" % h])
        P.op("dve", lambda e, h=h: e.memset(KTa[h][64:128, :], 0.0), writes=["KTa%d" % h])
        P.dma("pool", "oh%d" % h, KTa[h][64:96, :], C["onehot"], writes=["KTa%d" % h])

    def load_cs(tg):
        b = tg % 2
        P.dma("sp", "cs0_%d" % b, csb[0][b][:], cosv[:, tg * 512:(tg + 1) * 512], writes=["cs0_%d" % b])
        P.dma("sp", "cs1_%d" % b, csb[1][b][:], sinv[:, tg * 512:(tg + 1) * 512], writes=["cs1_%d" % b])

    def rope(dst, b, bank_a, bank_b):
        cn, sn = "cs0_%d" % b, "cs1_%d" % b
        P.op("dve", lambda e: e.tensor_tensor(t1[:], ps[bank_a][:], csb[0][b][:], ALU.mult),
             reads=[psn[bank_a], cn], writes=["t1"])
        P.op("dve", lambda e: e.tensor_tensor(t2[:], ps[bank_b][:], csb[1][b][:], ALU.mult),
             reads=[psn[bank_b], sn], writes=["t2"])
        nm = "qf" if dst is qf else "kf"
        P.op("dve", lambda e: e.tensor_tensor(dst[:], t1[:], t2[:], ALU.add),
             reads=["t1", "t2"], writes=[nm])

    load_x(0)
    load_cs(0)
    for tg in range(NG):
        if tg + 1 < NG:
            load_x(tg + 1)
            load_cs(tg + 1)
        b = tg % 2
        sl = slice(tg * 512, (tg + 1) * 512)
        proj_fm(0, 512, tg)
        proj_fm(1, 896, tg)
        rope(kf, b, 0, 1)
        for h in range(2):
            P.op("act", lambda e, h=h: e.copy(KTa[h][0:64, sl], kf[h * 64:(h + 1) * 64, :]),
                 reads=["kf"], writes=["KTa%d" % h])
        for h in range(2):
            hp2 = slice(h * 64, (h + 1) * 64)
            P.op("dve", lambda e, h=h, hp2=hp2: e.tensor_reduce(
                kmT[hp2, h, 2 * tg:2 * tg + 2], kf[hp2, :].rearrange("p (a b) -> p a b", a=2), AX.X, ALU.add),
                reads=["kf"], writes=["kmT"])
        proj_fm(2, 384, tg)
        proj_fm(3, 768, tg)
        rope(qf, b, 2, 3)
        for h in range(2):
            P.op("act", lambda e, h=h: e.mul(QTa[h][0:64, sl], qf[h * 64:(h + 1) * 64, :], 0.125),
                 reads=["qf"], writes=["QTa%d" % h])
        proj_tm(4, 640, tg)
        P.op("act", lambda e: e.copy(Vm[:, tg * 4:(tg + 1) * 4, :, 0:64],
                                    ps[4][:].rearrange("p (a h d) -> p a h d", a=4, h=2)),
             reads=["ps4"], writes=["Vm"])
        for idx in range(8 if not (DBG_MB & 2) else 0):
            cq, h = idx // 2, idx % 2
            P.op("pe", lambda e, cq=cq, h=h, idx=idx: e.matmul(
                ps[5][:, idx * 32:(idx + 1) * 32], qf[:, cq * 128:(cq + 1) * 128],
                kmT[:, h, :], start=True, stop=True),
                reads=["qf", "kmT"], writes=["ps5"], signal=False)
        P.op("pe", lambda e: e.matmul(ps[5][:, 256:384], identb[:], identb[:], start=True, stop=True),
             reads=["identb"], writes=["ps5"])
        for rnd in range(1):
            for idx in range(8 if not (DBG_MB & 2) else 0):
                cq, h = idx // 2, idx % 2
                cch = tg * 4 + cq
                own = cch // 2
                u = idx % 2
                P.op("dve", lambda e, idx=idx, own=own, u=u: e.tensor_tensor(
                    gm[u][:], ps[5][:, idx * 32:(idx + 1) * 32], cbrow[:, 32 - own:64 - own], ALU.add),
                    reads=["ps5", "cbrow"], writes=["gm%d" % u])
                P.op("dve", lambda e, u=u: e.max(top8[u][:], gm[u][:]), reads=["gm%d" % u], writes=["top8_%d" % u])
                P.op("dve", lambda e, own=own, u=u: e.scalar_tensor_tensor(
                    mb1[u][:], gm[u][:], top8[u][:, 2:3], vbrow[:, 32 - own:64 - own], ALU.is_ge, ALU.mult),
                    reads=["gm%d" % u, "top8_%d" % u, "vbrow"], writes=["mb1_%d" % u])
                P.op("dve", lambda e, own=own, u=u: e.tensor_tensor(
                    mbpad[u][:, 64:96], mb1[u][:], oorow[:, 31 - own:63 - own], ALU.add),
                    reads=["mb1_%d" % u, "oorow"], writes=["mbpad%d" % u])
                if DBG_MB & 8:
                    continue
                P.op("pe", lambda e, u=u: e.matmul(ps[6][:, u * 128:(u + 1) * 128], mbpad[u][:], identb[:],
                                                    start=True, stop=True),
                     reads=["mbpad%d" % u, "identb"], writes=["ps6_%d" % u], signal=False)
                P.op("pe", lambda e, u=u: e.matmul(ps[6][:, 256 + u * 128:256 + (u + 1) * 128], identb[:], identb[:],
                                                    start=True, stop=True),
                     reads=["identb"], writes=["ps6_%d" % u])
                P.op("act", lambda e, h=h, cch=cch, u=u: e.copy(QTa[h][64:96, cch * 128:(cch + 1) * 128],
                                                              ps[6][64:96, u * 128:(u + 1) * 128]),
                     reads=["ps6_%d" % u], writes=["QTa%d" % h])

    P.barrier()
    Pb2 = [A("Pb2_%d" % i, [128, 1024], BF16) for i in range(3)]
    mitems = []
    for g in range(NG):
        njt = 4 * g + 4
        for j in range(njt):
            mitems.append((g, j, njt))
    NM = len(mitems) if not (DBG_MB & 1) else 0

    def mb_s1(i):
        g, j, njt = mitems[i]
        zp = i % 2
        for h in range(2):
            P.op("pe", lambda e, h=h: e.matmul(psbig[:, zp * 1024 + h * 512:zp * 1024 + (h + 1) * 512],
                                               KTa[h][:, j * 128:(j + 1) * 128], QTa[h][:, g * 512:(g + 1) * 512],
                                               start=True, stop=True),
                 reads=["QTa%d" % h, "KTa%d" % h], writes=["zp%d" % zp], signal=(h == 1))

    def mb_s2(i):
        g, j, njt = mitems[i]
        zp = i % 2
        pb = "Pb2_%d" % (i % 3)
        Pt = Pb2[i % 3]
        P.op("act", lambda e: e.activation(Pt[:], psbig[:, zp * 1024:(zp + 1) * 1024], AF.Exp),
             reads=["zp%d" % zp], writes=[pb])
        if j >= 4 * g:
            jj = j - 4 * g
            for h in range(2):
                P.op("dve", lambda e, h=h: e.tensor_tensor(Pt[:, h * 512:(h + 1) * 512], Pt[:, h * 512:(h + 1) * 512],
                                                           mmb[:, jj, :], ALU.mult),
                     reads=[pb, "mmb"], writes=[pb])

    def mb_s3(i):
        g, j, njt = mitems[i]
        pb = "Pb2_%d" % (i % 3)
        Pt = Pb2[i % 3]
        for h in range(2):
            ob = 4 + (2 * g + h) % 3
            P.op("pe", lambda e, h=h, ob=ob: e.matmul(ps[ob][:, :], Vm[:, j, h, :], Pt[:, h * 512:(h + 1) * 512],
                                                      start=(j == 0), stop=(j == njt - 1), skip_group_check=True),
                 reads=["Vm", pb], writes=[psn[ob]], signal=True)
            if j == njt - 1:
                so = "ost%d" % (g % 2)
                P.op("dve", lambda e, ob=ob: e.reciprocal(rec[64:128, :], ps[ob][64:128, :]),
                     reads=[psn[ob]], writes=["rec"])
                P.op("dve", lambda e, h=h, ob=ob: e.tensor_tensor(ost[g % 2][h * 64:(h + 1) * 64, :], ps[ob][0:64, :],
                                                                  rec[64:128, :], ALU.mult),
                     reads=[psn[ob], "rec"], writes=[so + "_%d" % h])
                if h == 1:
                    P.dma("sp", so, oT(1, g), ost[g % 2][:],
                          reads=[so + "_0", so + "_1"], writes=["oT1_%d" % g])

    mstages = [mb_s1, mb_s2, mb_s3]
    for step in range(NM + len(mstages) - 1):
        for s_, fn in enumerate(mstages):
            i = step - s_
            if 0 <= i < NM:
                fn(i)
    P.barrier()


def build_B(nc, P, ctx, sfx, xT_d, oTs_v, oTm_v, oreads, W, lnp_d, out_d, outb_d=None):
    def A(name, shape, dt):
        return ctx.enter_context(nc.sbuf_tensor(name + sfx, shape, dt))
    xT = A("xT_sb", [128, 8, TOK], F32)
    xb = A("xb_sb", [128, 8, TOK], BF16)
    arena = A("arena", [128, 32768], BF16)
    oTs = arena[:, 0:8192].rearrange("p (a b) -> p a b", a=4)
    oTm = arena[:, 8192:16384].rearrange("p (a b) -> p a b", a=4)
    mg = arena[:, 16384:32768].rearrange("p (a b) -> p a b", a=8)
    hT = arena[:, 0:NFC * 1024].rearrange("p (a b) -> p a b", a=NFC)
    lnp = A("lnp_sb", [128, 32], F32)
    onesD = A("onesD", [128, 128], F32)
    warena = A("warena", [128, 9728], BF16)
    wgb = [warena[:, i * 2048:(i + 1) * 2048].rearrange("p (a b) -> p a b", a=8) for i in range(2)]
    wbb = [warena[:, 4096 + i * 1024:4096 + (i + 1) * 1024].rearrange("p (a b) -> p a b", a=4) for i in range(2)]
    wob = [warena[:, 6144 + i * 1024:6144 + (i + 1) * 1024].rearrange("p (a b) -> p a b", a=8) for i in range(2)]
    wfgu = [warena[:, i * 2048:(i + 1) * 2048].rearrange("p (a b) -> p a b", a=8) for i in range(2)]
    wfdb = [warena[:, 4096 + i * 2816:4096 + (i + 1) * 2816].rearrange("p (a b) -> p a b", a=NFC) for i in range(2)]
    sg = [A("sg%d" % i, [128, 512], F32) for i in range(4)]
    m1 = [A("m1_%d" % i, [128, 512], F32) for i in range(2)]
    m2 = [A("m2_%d" % i, [128, 512], F32) for i in range(2)]
    ysq = [A("ysq%d" % i, [128, 512], F32) for i in range(2)]
    mean_sb = A("mean_sb", [128, 512], F32)
    rstd_sb = A("rstd_sb", [128, 512], F32)
    tn = [A("tn%d" % i, [128, 512], F32) for i in range(2)]
    ps = [ctx.enter_context(nc.psum_tensor("pb%d" % i + sfx, [128, 512], F32)) for i in range(8)]
    psn = ["pb%d" % i for i in range(8)]
    bank = [0]
    XN = [["xT%d_%d" % (c, g) for g in range(NTG)] for c in range(8)]
    XB = [["xb%d_%d" % (c, g) for g in range(NTG)] for c in range(8)]
    MG = [["mg%d_%d" % (c, g) for g in range(NTG)] for c in range(8)]
    HT = [["hT%d_%d" % (k, g) for g in range(2)] for k in range(NFC)]
    allx = [n for r_ in XN for n in r_]
    allxb = [n for r_ in XB for n in r_]

    def nb():
        bank[0] = (bank[0] + 1) % 8
        return bank[0]

    xdv = xT_d.rearrange("(c p) t -> p c t", p=128)
    P.dma("sp", "xT", xT[:], xdv, writes=allx)
    P.dma("pool", "xb", xb[:], xdv, writes=allxb)
    P.dma("sp", "oTs", oTs, oTs_v, reads=oreads, writes=["oTs"])
    P.dma("sp", "oTm", oTm, oTm_v, reads=oreads, writes=["oTm"])
    P.dma("sp", "lnp", lnp[:], lnp_d, writes=["lnp"])
    P.op("dve", lambda e: e.memset(onesD[:], 1.0 / D), writes=["onesD"])

    wGv = W["wG"].rearrange("(k p) n -> p k n", p=128)
    wbsv = W["wbs"].rearrange("(k p) n -> p k n", p=128)
    wbmv = W["wbm"].rearrange("(k p) n -> p k n", p=128)
    wov = W["wout"].rearrange("(k p) n -> p k n", p=128)
    wfgv = W["wfg"].rearrange("(k p) n -> p k n", p=128)
    wfuv = W["wfu"].rearrange("(k p) n -> p k n", p=128)
    wfdv = W["wfd"].rearrange("(k p) n -> p k n", p=128)

    def tsl(tg):
        return slice(tg * 512, (tg + 1) * 512)

    def load_b1(c):
        b = c % 2
        cs = slice(c * 128, (c + 1) * 128)
        P.dma("pool", "wgb%da" % b, wgb[b][:, :, 0:128], wGv[:, :, cs], writes=["wgb%da" % b])
        P.dma("pool", "wgb%db" % b, wgb[b][:, :, 128:256], wGv[:, :, D + c * 128:D + (c + 1) * 128],
              writes=["wgb%db" % b])
        P.dma("pool", "wbb%da" % b, wbb[b][:, :, 0:128], wbsv[:, :, cs], writes=["wbb%da" % b])
        P.dma("pool", "wbb%db" % b, wbb[b][:, :, 128:256], wbmv[:, :, cs], writes=["wbb%db" % b])

    load_b1(0)
    it = 0
    for c in range(8):
        if c + 1 < 8:
            load_b1(c + 1)
        b = c % 2
        for tg in range(NTG):
            t = tsl(tg)
            bk = [nb() for _ in range(4)]
            for half, (bkk, wn) in enumerate(zip(bk[0:2], ["wgb%da" % b, "wgb%db" % b])):
                for k in range(8):
                    P.op("pe", lambda e, k=k, half=half, bkk=bkk: e.matmul(
                        ps[bkk][:], wgb[b][:, k, half * 128:(half + 1) * 128], xb[:, k, t],
                        start=(k == 0), stop=(k == 7)),
                        reads=[wn, XB[k][tg]], writes=[psn[bkk]], signal=(k == 7))
            for half, (bkk, wn, src, sn) in enumerate(zip(bk[2:4], ["wbb%da" % b, "wbb%db" % b],
                                                          [oTs, oTm], ["oTs", "oTm"])):
                for k in range(4):
                    P.op("pe", lambda e, k=k, half=half, bkk=bkk, src=src: e.matmul(
                        ps[bkk][:], wbb[b][:, k, half * 128:(half + 1) * 128], src[:, k, t],
                        start=(k == 0), stop=(k == 3)),
                        reads=[wn, sn], writes=[psn[bkk]], signal=(k == 3))
            u = it % 2
            s0, s1 = sg[2 * u], sg[2 * u + 1]
            P.op("act", lambda e, s0=s0, bkk=bk[0]: e.activation(s0[:], ps[bkk][:], AF.Sigmoid),
                 reads=[psn[bk[0]]], writes=["sg%d" % (2 * u)])
            P.op("act", lambda e, s1=s1, bkk=bk[1]: e.activation(s1[:], ps[bkk][:], AF.Sigmoid),
                 reads=[psn[bk[1]]], writes=["sg%d" % (2 * u + 1)])
            P.op("dve", lambda e, s0=s0, u=u, bkk=bk[2]: e.tensor_tensor(m1[u][:], s0[:], ps[bkk][:], ALU.mult),
                 reads=[psn[bk[2]], "sg%d" % (2 * u)], writes=["m1_%d" % u])
            P.op("dve", lambda e, s1=s1, u=u, bkk=bk[3]: e.tensor_tensor(m2[u][:], s1[:], ps[bkk][:], ALU.mult),
                 reads=[psn[bk[3]], "sg%d" % (2 * u + 1)], writes=["m2_%d" % u])
            P.op("pool", lambda e, u=u, c=c, t=t: e.tensor_tensor(mg[:, c, t], m1[u][:], m2[u][:], ALU.add),
                 reads=["m1_%d" % u, "m2_%d" % u], writes=[MG[c][tg]])
            it += 1

    def layer_norm(tg, gcol, bcol):
        t = tsl(tg)
        ba, bb_ = nb(), nb()
        for c in range(8):
            q = ysq[c % 2]
            P.op("act", lambda e, q=q, c=c: e.activation(q[:], xT[:, c, t], AF.Square),
                 reads=[XN[c][tg]], writes=["ysq%d" % (c % 2)])
            P.op("pe", lambda e, c=c: e.matmul(ps[ba][:], onesD[:], xT[:, c, t], start=(c == 0), stop=(c == 7)),
                 reads=["onesD", XN[c][tg]], writes=[psn[ba]], signal=(c == 7))
            P.op("pe", lambda e, q=q, c=c: e.matmul(ps[bb_][:], onesD[:], q[:], start=(c == 0), stop=(c == 7),
                                                   skip_group_check=True),
                 reads=["onesD", "ysq%d" % (c % 2)], writes=[psn[bb_]], signal=True)
        P.op("dve", lambda e: e.tensor_copy(mean_sb[:], ps[ba][:]), reads=[psn[ba]], writes=["mean_sb"])
        P.op("dve", lambda e: e.tensor_tensor(rstd_sb[:], mean_sb[:], mean_sb[:], ALU.mult),
             reads=["mean_sb"], writes=["rstd_sb"])
        P.op("dve", lambda e: e.tensor_tensor(rstd_sb[:], ps[bb_][:], rstd_sb[:], ALU.subtract),
             reads=[psn[bb_], "rstd_sb"], writes=["rstd_sb"])
        P.op("act", lambda e: e.activation(rstd_sb[:], rstd_sb[:], AF.Ln, bias=EPS),
             reads=["rstd_sb"], writes=["rstd_sb"])
        P.op("act", lambda e: e.activation(rstd_sb[:], rstd_sb[:], AF.Exp, scale=-0.5),
             reads=["rstd_sb"], writes=["rstd_sb"])
        for c in range(8):
            tt = tn[c % 2]
            tnn = "tn%d" % (c % 2)
            P.op("dve", lambda e, tt=tt, c=c: e.tensor_tensor(tt[:], xT[:, c, t], mean_sb[:], ALU.subtract),
                 reads=[XN[c][tg], "mean_sb"], writes=[tnn])
            P.op("dve", lambda e, tt=tt: e.tensor_tensor(tt[:], tt[:], rstd_sb[:], ALU.mult),
                 reads=[tnn, "rstd_sb"], writes=[tnn])
            P.op("act", lambda e, tt=tt, c=c: e.activation(xT[:, c, t], tt[:], AF.Identity,
                                                          bias=lnp[:, bcol + c:bcol + c + 1],
                                                          scale=lnp[:, gcol + c:gcol + c + 1]),
                 reads=[tnn, "lnp"], writes=[XN[c][tg]])
            P.op("pool", lambda e, c=c: e.tensor_copy(xb[:, c, t], xT[:, c, t]), reads=[XN[c][tg]], writes=[XB[c][tg]])

    def load_wo(c):
        b = c % 2
        P.dma("pool", "wob%d" % b, wob[b][:], wov[:, :, c * 128:(c + 1) * 128], writes=["wob%d" % b])

    load_wo(0)
    for c in range(8):
        if c + 1 < 8:
            load_wo(c + 1)
        b = c % 2
        for tg in range(NTG):
            t = tsl(tg)
            bkk = nb()
            for k in range(8):
                P.op("pe", lambda e, k=k, bkk=bkk: e.matmul(ps[bkk][:], wob[b][:, k, :], mg[:, k, t],
                                                            start=(k == 0), stop=(k == 7)),
                     reads=["wob%d" % b, MG[k][tg]], writes=[psn[bkk]], signal=(k == 7))
            P.op("dve", lambda e, bkk=bkk, c=c, t=t: e.scalar_tensor_tensor(
                xT[:, c, t], xT[:, c, t], ALPHA, ps[bkk][:], ALU.mult, ALU.add),
                reads=[psn[bkk], XN[c][tg]], writes=[XN[c][tg]])
    for tg in range(NTG):
        layer_norm(tg, 0, 8)

    def load_gu(i):
        k = i % NFC
        b = i % 2
        P.dma("pool", "wfgu%da" % b, wfgu[b][:, :, 0:128], wfgv[:, :, k * 128:(k + 1) * 128],
              writes=["wfgu%da" % b])
        P.dma("pool", "wfgu%db" % b, wfgu[b][:, :, 128:256], wfuv[:, :, k * 128:(k + 1) * 128],
              writes=["wfgu%db" % b])

    def load_dn(i):
        c = i % 8
        b = i % 2
        P.dma("pool", "wfdb%d" % b, wfdb[b][:], wfdv[:, :, c * 128:(c + 1) * 128], writes=["wfdb%d" % b])

    gi = 0
    di = 0
    P.barrier()
    for hf in range(2):
        load_gu(gi)
        for k in range(NFC):
            if k + 1 < NFC:
                load_gu(gi + 1)
            b = gi % 2
            for t2_ in range(2):
                tg = hf * 2 + t2_
                t = tsl(tg)
                b0, b1 = nb(), nb()
                for half, bkk in enumerate([b0, b1]):
                    wn = "wfgu%d%s" % (b, "ab"[half])
                    for kk in range(8):
                        P.op("pe", lambda e, kk=kk, half=half, bkk=bkk: e.matmul(
                            ps[bkk][:], wfgu[b][:, kk, half * 128:(half + 1) * 128], xb[:, kk, t],
                            start=(kk == 0), stop=(kk == 7)),
                            reads=[wn, XB[kk][tg]], writes=[psn[bkk]], signal=(kk == 7))
                u = it % 2
                s0 = sg[2 * u]
                P.op("act", lambda e, s0=s0, b0=b0: e.activation(s0[:], ps[b0][:], AF.Silu),
                     reads=[psn[b0]], writes=["sg%d" % (2 * u)])
                P.op("dve", lambda e, s0=s0, b1=b1, k=k, t2_=t2_: e.tensor_tensor(
                    hT[:, k, t2_ * 512:(t2_ + 1) * 512], s0[:], ps[b1][:], ALU.mult),
                    reads=[psn[b1], "sg%d" % (2 * u)], writes=[HT[k][t2_]])
                it += 1
            gi += 1
        load_dn(di)
        for c in range(8):
            if c + 1 < 8:
                load_dn(di + 1)
            b = di % 2
            for t2_ in range(2):
                tg = hf * 2 + t2_
                t = tsl(tg)
                bkk = nb()
                for k in range(NFC):
                    P.op("pe", lambda e, k=k, bkk=bkk, t2_=t2_: e.matmul(
                        ps[bkk][:], wfdb[b][:, k, :], hT[:, k, t2_ * 512:(t2_ + 1) * 512],
                        start=(k == 0), stop=(k == NFC - 1)),
                        reads=["wfdb%d" % b, HT[k][t2_]], writes=[psn[bkk]], signal=(k == NFC - 1))
                P.op("dve", lambda e, bkk=bkk, c=c, t=t: e.scalar_tensor_tensor(
                    xT[:, c, t], xT[:, c, t], ALPHA, ps[bkk][:], ALU.mult, ALU.add),
                    reads=[psn[bkk], XN[c][tg]], writes=[XN[c][tg]])
            di += 1
        for t2_ in range(2):
            layer_norm(hf * 2 + t2_, 16, 24)
    P.dma("sp", "out", out_d.rearrange("(c p) t -> p c t", p=128), xT[:], reads=allx, writes=["out"])
    if outb_d is not None:
        P.dma("sp", "outb", outb_d.rearrange("(c p) t -> p c t", p=128), xb[:], reads=allxb, writes=["outb"])


W_KEYS = ["wG", "wbs", "wbm", "wout", "wfg", "wfu", "wfd"]
W_SHAPES = {"wG": [D, 2048], "wbs": [512, D], "wbm": [512, D], "wout": [D, D],
            "wfg": [D, DFF], "wfu": [D, DFF], "wfd": [DFF, D]}
RG = [[0, 1, 2, 3], [4, 5, 6, 7]]


def _pp(v):
    return np.ascontiguousarray(v.reshape(8, 128).T)


def build_fused(nlayers=DEPTH, skipA=False, skipB=False, samex=False, parts=3):
    nc = bass.Bass("TRN2", target_bir_lowering=False)
    P = Prog(nc)
    hc = host_consts()
    C = {k: nc.dram_tensor("c_" + k, list(v.shape), F32, kind="ExternalInput").ap() for k, v in hc.items()}
    xTall0 = nc.dram_tensor("xTall0", [D, S], F32, kind="ExternalInput").ap()
    xT0 = nc.dram_tensor("xT0", [D, TOK], F32, kind="ExternalInput").ap()
    wA = nc.dram_tensor("wA", [DEPTH, D, 768], F32, kind="ExternalInput").ap()
    Wd = {k: nc.dram_tensor(k, [DEPTH] + W_SHAPES[k], F32, kind="ExternalInput").ap() for k in W_KEYS}
    lnp = nc.dram_tensor("lnp", [DEPTH, 128, 32], F32, kind="ExternalInput").ap()
    out = nc.dram_tensor("out", [D, TOK], F32, kind="ExternalOutput").ap()
    oT_loc = nc.dram_tensor("oT_loc", [1024, TOK], BF16, kind="Internal").ap()
    oT_all = nc.dram_tensor("oT_all", [4096, TOK], BF16, kind="Internal").ap()
    xb_loc = nc.dram_tensor("xb_loc", [D, TOK], BF16, kind="Internal").ap()
    xb_all = nc.dram_tensor("xb_all", [4 * D, TOK], BF16, kind="Internal").ap()
    xres = nc.dram_tensor("xres", [D, TOK], F32, kind="Internal").ap()
    pid = nc.sync.partition_id()
    roff = (pid % 4) * 1024
    olv = oT_loc.rearrange("(q b p) t -> b p q t", q=4, b=2)

    def odst(br, g):
        return olv[br, :, g // 4, (g % 4) * 512:(g % 4 + 1) * 512]

    for l in range(nlayers):
        last = l == nlayers - 1
        with ExitStack() as ctx:
            if l == 0 or samex:
                xv0 = xTall0.rearrange("(c p) t -> p c t", p=128)
                xsrc = lambda tg: [(0, 8, xv0[:, :, tg * 512:(tg + 1) * 512])]
                xeng, xreads = "pool", []
            else:
                xv1 = xb_all.rearrange("(j r c p) t -> p r j c t", j=4, r=4, c=2, p=128)
                xsrc = lambda tg: [(2 * j, 2 * j + 2, xv1[:, tg // 4, j, :, (tg % 4) * 512:(tg % 4 + 1) * 512])
                                   for j in range(4)]
                xeng, xreads = "sp", ["xb_all%d" % j for j in range(4)]
            if not skipA:
                build_A(nc, P, ctx, "_a%d" % l, xsrc, xeng, xreads, wA[l], odst, C, parts)
        P.barrier()
        for q in range(4):
            P.collective("oT%d" % q, "AllGather", RG, oT_loc[q * 256:(q + 1) * 256, :],
                         oT_all[q * 1024:(q + 1) * 1024, :], writes=["oT_all%d" % q])
        with ExitStack() as ctx:
            ov = oT_all[bass.ds(roff, 1024), :].rearrange("(k b p) t -> p k b t", k=4, b=2, p=128)
            oTs_v = ov[:, :, 0, :]
            oTm_v = ov[:, :, 1, :]
            Wl = {k: Wd[k][l] for k in W_KEYS}
            if not skipB:
              build_B(nc, P, ctx, "_b%d" % l, xT0 if l == 0 else xres, oTs_v, oTm_v,
                    ["oT_all%d" % q for q in range(4)], Wl, lnp[l],
                    out if last else xres, None if last else xb_loc)
        P.barrier()
        if not last:
            for j in range(4):
                P.collective("xb%d" % j, "AllGather", RG, xb_loc[j * 256:(j + 1) * 256, :],
                             xb_all[j * 1024:(j + 1) * 1024, :], writes=["xb_all%d" % j])
    P.finish("sp")
    return nc, hc


def _wA_slices(w_in_l, r):
    def cols(base):
        return w_in_l[:, base + 128 * r: base + 128 * r + 128]
    return np.concatenate([cols(0), cols(512), cols(1024), cols(1536), cols(2048), cols(2560)], axis=1)


def kernel(x, w_in, w_branch_sb, w_branch_moba, w_out, ln_mix_g, ln_mix_b,
           w_ffn_gate, w_ffn_up, w_ffn_down, ln_ffn_g, ln_ffn_b):
    f = lambda a: np.asarray(a, dtype=np.float32)
    x, w_in, w_branch_sb, w_branch_moba, w_out = f(x), f(w_in), f(w_branch_sb), f(w_branch_moba), f(w_out)
    ln_mix_g, ln_mix_b, ln_ffn_g, ln_ffn_b = f(ln_mix_g), f(ln_mix_b), f(ln_ffn_g), f(ln_ffn_b)
    w_ffn_gate, w_ffn_up, w_ffn_down = f(w_ffn_gate), f(w_ffn_up), f(w_ffn_down)
    nc, hc = build_fused()
    cores = list(range(8))
    xTb = [np.ascontiguousarray(x[b].T) for b in range(B)]
    Wfull = {"wG": np.ascontiguousarray(w_in[:, :, 3072:5120]), "wbs": w_branch_sb, "wbm": w_branch_moba,
             "wout": w_out, "wfg": w_ffn_gate, "wfu": w_ffn_up, "wfd": w_ffn_down}
    lnp = np.ascontiguousarray(np.stack([np.concatenate(
        [_pp(ln_mix_g[l]), _pp(ln_mix_b[l]), _pp(ln_ffn_g[l]), _pp(ln_ffn_b[l])], axis=1) for l in range(DEPTH)]))
    maps = []
    for c in cores:
        b, r = c // 4, c % 4
        m = {"xTall0": xTb[b], "xT0": np.ascontiguousarray(xTb[b][:, r * TOK:(r + 1) * TOK]),
             "wA": np.ascontiguousarray(np.stack([_wA_slices(w_in[l], r) for l in range(DEPTH)])),
             "lnp": lnp}
        m.update({k: np.ascontiguousarray(v) for k, v in Wfull.items()})
        m.update({"c_" + k: v for k, v in hc.items()})
        maps.append(m)
    res = run_bass_kernel_spmd(nc, maps, core_ids=cores)
    xn = np.empty_like(x)
    for c in cores:
        b, r = c // 4, c % 4
        xn[b, r * TOK:(r + 1) * TOK, :] = np.asarray(res.results[c]["out"]).T
    return xn
```

```python
from contextlib import ExitStack
import os
DBG_GATE = int(os.environ.get('DBG_GATE', '1'))
DBG_NORM = int(os.environ.get('DBG_NORM', '1'))
DBG_MB = int(os.environ.get('DBG_MB', '0'))
import numpy as np
import ml_dtypes
import concourse.bass as bass
import concourse.mybir as mybir
from concourse.bass_utils import run_bass_kernel_spmd

F32 = mybir.dt.float32
BF16 = mybir.dt.bfloat16
AF = mybir.ActivationFunctionType
ALU = mybir.AluOpType
AX = mybir.AxisListType

D = 1024
S = 8192
B = 2
DEPTH = 2
DFF = 2816
NFC = DFF // 128
NG = S // 512
ALPHA = (2 * DEPTH) ** 0.25
EPS = 1e-5
BIG = 30000.0
NEG = -1.0e30
TOK = 2048
NTG = TOK // 512


class Prog:
    SAME_ENGINE_SYNC = True
    EMBED = ("act", "dve")

    def __init__(self, nc):
        self.nc = nc
        self.eng = {"pe": nc.tensor, "act": nc.scalar, "dve": nc.vector,
                    "pool": nc.gpsimd, "sp": nc.sync}
        self.sem = {}
        self.cnt = {}
        self.res = {}
        self.known = {e: {} for e in self.eng}
        self.nwait = 0
        self.nops = 0
        self._pending = None

    def _sem(self, lane):
        if lane not in self.sem:
            self.sem[lane] = self.nc.alloc_semaphore(name="s_" + lane.replace(":", "_"))
            self.cnt[lane] = 0
        return self.sem[lane]

    def _need(self, e, lane, val):
        if lane == e and (e == "pe" or not self.SAME_ENGINE_SYNC):
            return
        if self.known[e].get(lane, 0) >= val:
            return
        self.known[e][lane] = val
        self.nwait += 1
        if self._pending is not None:
            self._pending.append((lane, val))
            return
        self.eng[e].wait_ge(self._sem(lane), val)

    def _deps(self, e, reads, writes):
        for r in reads:
            ent = self.res.get(r)
            if ent and ent[0]:
                self._need(e, *ent[0])
        for w in writes:
            ent = self.res.get(w)
            if ent:
                if ent[0]:
                    self._need(e, *ent[0])
                for lane, val in ent[1].items():
                    self._need(e, lane, val)

    def _record(self, lane, val, reads, writes):
        for r in reads:
            ent = self.res.setdefault(r, [None, {}])
            ent[1][lane] = max(ent[1].get(lane, 0), val)
        for w in writes:
            self.res[w] = [(lane, val), {}]

    def op(self, e, fn, reads=(), writes=(), signal=True):
        self._sem(e)
        self._pending = [] if e in self.EMBED else None
        self._deps(e, reads, writes)
        pend = self._pending or []
        self._pending = None
        for lane_, val_ in pend[:-1]:
            self.eng[e].wait_ge(self._sem(lane_), val_)
        ins = fn(self.eng[e])
        if pend:
            ins._wait_ge(self._sem(pend[-1][0]), pend[-1][1])
        val = self.cnt[e] + 1
        if signal:
            ins.then_inc(self.sem[e], 1)
            self.cnt[e] = val
        self._record(e, val, reads, writes)
        self.nops += 1
        return ins

    def dma(self, e, lane, out, in_, reads=(), writes=(), **kw):
        lane = "d:" + lane
        self._sem(lane)
        self._deps(e, reads, writes)
        ins = self.eng[e].dma_start(out=out, in_=in_, **kw)
        val = self.cnt[lane] + 16
        ins.then_inc(self.sem[lane], 16)
        self.cnt[lane] = val
        self._record(lane, val, reads, writes)
        self.nops += 1
        return ins

    def collective(self, lane, kind, rg, in_ap, out_ap, reads=(), writes=()):
        lane = "c:" + lane
        self._sem(lane)
        self._deps("pool", reads, writes)
        ins = self.nc.gpsimd.collective_compute(kind, ALU.bypass, replica_groups=rg,
                                                ins=[in_ap], outs=[out_ap])
        val = self.cnt[lane] + 1
        ins.then_inc(self.sem[lane], 1)
        self.cnt[lane] = val
        self._record(lane, val, reads, writes)
        self.nops += 1
        return ins

    def barrier(self):
        for e in self.eng:
            for lane, val in self.cnt.items():
                if val > 0 and lane != e:
                    if self.known[e].get(lane, 0) < val:
                        self.eng[e].wait_ge(self.sem[lane], val)
                        self.known[e][lane] = val
        self.res = {}

    def finish(self, e="sp"):
        for lane, val in self.cnt.items():
            if val > 0 and lane != e:
                self.eng[e].wait_ge(self.sem[lane], val)


def host_consts():
    c = {}
    k = np.arange(128)[:, None]
    s = np.arange(128)[None, :]
    c["negtri"] = np.where(k >= s, -1.0, 0.0).astype(np.float32)
    c["negones"] = np.full((128, 128), -1.0, np.float32)
    c["ident"] = np.eye(128, dtype=np.float32)
    sk = np.arange(128)[:, None, None] + 128 * np.arange(4)[None, :, None]
    t = np.arange(512)[None, None, :]
    c["msb"] = (sk < t).astype(np.float32)
    c["mmb"] = (sk <= t).astype(np.float32)
    half = 8
    inv = (500000.0 ** (-np.arange(0, 16, 2, dtype=np.float32) / 16)).astype(np.float32)
    ang = (np.arange(S, dtype=np.float32)[:, None] * inv[None, :]).astype(np.float32)
    cos = np.cos(ang).astype(np.float32).T
    sin = np.sin(ang).astype(np.float32).T
    cosT = np.ones((64, S), np.float32)
    sinT = np.zeros((64, S), np.float32)
    cosT[0:8] = cos
    cosT[8:16] = cos
    sinT[0:8] = sin
    sinT[8:16] = sin
    c["cosT"] = np.concatenate([cosT, cosT], 0)
    c["sinT"] = np.concatenate([sinT, sinT], 0)
    n = np.arange(32)[None, :]
    own = (np.arange(64) // 2)[:, None]
    cb = np.where(n < own, 0.0, NEG).astype(np.float32)
    vb = np.where(n < own, BIG, 0.0).astype(np.float32)
    oo = np.where(n == own, 0.0, -BIG).astype(np.float32)
    c["cb"] = np.broadcast_to(cb.reshape(1, 64 * 32), (128, 64 * 32)).copy()
    c["vb"] = np.broadcast_to(vb.reshape(1, 64 * 32), (128, 64 * 32)).copy()
    c["oo"] = np.broadcast_to(oo.reshape(1, 64 * 32), (128, 64 * 32)).copy()
    c["onehot"] = (np.arange(32)[:, None] == (np.arange(S)[None, :] // 256)).astype(np.float32)
    return c


def build_A(nc, P, ctx, sfx, xsrc, xeng, xreads, wA, oT, C, parts=3):
    def A(name, shape, dt):
        return ctx.enter_context(nc.sbuf_tensor(name + sfx, shape, dt))
    wv = wA.rearrange("(c p) n -> p c n", p=128)

    negtri = A("negtri", [128, 128], BF16)
    negones = A("negones", [128, 128], BF16)
    identb = A("identb", [128, 128], BF16)
    onesf = A("onesf", [128, 64], F32)
    msb = A("msb", [128, 4, 512], BF16)
    mmb = A("mmb", [128, 4, 512], BF16)
    w = A("wA_sb", [128, 8, 1024], BF16)
    P.dma("pool", "c0", negtri[:], C["negtri"], writes=["negtri"])
    P.dma("pool", "c1", negones[:], C["negones"], writes=["negones"])
    P.dma("pool", "c2", identb[:], C["ident"], writes=["identb"])
    P.dma("pool", "c3", msb[:], C["msb"], writes=["msb"])
    P.dma("pool", "c4", mmb[:], C["mmb"], writes=["mmb"])
    P.dma("pool", "w", w[:, :, 0:768], wv, writes=["w"])
    P.op("dve", lambda e: e.memset(onesf[:], 1.0), writes=["onesf"])
    P.op("dve", lambda e: e.memset(w[:, :, 768:1024], 0.0), reads=["w"], writes=["w"])
    for qk in range(2):
        for hd in range(2):
            src = 384 + qk * 128 + hd * 64
            dst = 768 + qk * 128 + hd * 64
            P.op("act", lambda e, s_=src, d_=dst: e.mul(w[:, :, d_:d_ + 8], w[:, :, s_ + 8:s_ + 16], -1.0),
                 reads=["w"], writes=["w"])
            P.op("act", lambda e, s_=src, d_=dst: e.copy(w[:, :, d_ + 8:d_ + 16], w[:, :, s_:s_ + 8]),
                 reads=["w"], writes=["w"])

    psbig = ctx.enter_context(nc.psum_tensor("psbig" + sfx, [128, 4096], F32))
    ps = [psbig[:, i * 512:(i + 1) * 512] for i in range(8)]
    psn = ["ps%d" % i for i in range(7)]

    xg = [A("xg%d" % i, [128, 8, 512], BF16) for i in range(2)]
    ost = [A("ost%d" % i, [128, 512], BF16) for i in range(2)]

    def load_x(tg):
        b = tg % 2
        for pi, (c0, c1, sap) in enumerate(xsrc(tg)):
            P.dma(xeng, "xg%d_%d" % (b, pi), xg[b][:, c0:c1, :], sap, reads=xreads, writes=["xg%d" % b])

    def proj_fm(bank, col0, tg):
        b = tg % 2
        for c in range(8):
            P.op("pe", lambda e, c=c: e.matmul(ps[bank][:], w[:, c, col0:col0 + 128], xg[b][:, c, :],
                                               start=(c == 0), stop=(c == 7)),
                 reads=["w", "xg%d" % b], writes=[psn[bank]], signal=(c == 7))

    def proj_tm(bank, col0, tg, ncols=128):
        b = tg % 2
        for ts in range(4):
            for c in range(8):
                P.op("pe", lambda e, c=c, ts=ts: e.matmul(ps[bank][:, ts * 128:(ts + 1) * 128],
                                                          xg[b][:, c, ts * 128:(ts + 1) * 128],
                                                          w[:, c, col0:col0 + 128],
                                                          start=(c == 0), stop=(c == 7)),
                     reads=["w", "xg%d" % b], writes=[psn[bank]], signal=(c == 7 and ts == 3))

    R = [A("R%d" % i, [128, S], BF16) for i in range(4)]
    Vm = A("Vm", [128, 64, 2, 128], BF16)
    QT, KT = R[0], R[1]
    V = A("V", [128, 64, 128], BF16)
    if parts & 1:
        load_x(0)
    for tg in range(NG if parts & 1 else 0):
        if tg + 1 < NG:
            load_x(tg + 1)
        sl = slice(tg * 512, (tg + 1) * 512)
        proj_fm(0, 0, tg)
        P.op("act", lambda e, sl=sl: e.mul(QT[:, sl], ps[0][:], 0.125), reads=["ps0"], writes=["QT"])
        proj_fm(1, 128, tg)
        P.op("dve", lambda e, sl=sl: e.tensor_copy(KT[:, sl], ps[1][:]), reads=["ps1"], writes=["KT"])
        proj_tm(2, 256, tg)
        P.op("act", lambda e, tg=tg: e.copy(V[:, tg * 4:(tg + 1) * 4, :],
                                           ps[2][:].rearrange("p (a b) -> p a b", a=4)),
             reads=["ps2"], writes=["V"])

    Eb = [A("Eb%d" % i, [128, 512], F32) for i in range(2)]
    Lb = [A("Lb%d" % i, [128, 512], BF16) for i in range(3)]
    Ab = [A("Ab%d" % i, [128, 512], BF16) for i in range(3)]
    Lrun = [A("Lrun%d" % i, [128, 512], F32) for i in range(2)]
    Lrb = [[A("Lrb%d_%d" % (h, i), [128, 512], BF16) for i in range(2)] for h in range(2)]

    items = []
    for g in range(NG):
        njt = 4 * g + 4
        for k in range(njt):
            for h in range(2):
                items.append((h, g, k, njt - 1 - k, njt))
    NI = len(items) if parts & 1 else 0

    def sb_s1(i):
        h, g, k, j, njt = items[i]
        zb = i % 4
        hp = slice(h * 64, (h + 1) * 64)
        P.op("pe", lambda e: e.matmul(ps[zb][:], KT[hp, j * 128:(j + 1) * 128], QT[hp, g * 512:(g + 1) * 512],
                                      start=True, stop=True),
             reads=["QT", "KT"], writes=[psn[zb]])

    def sb_s2(i):
        h, g, k, j, njt = items[i]
        zb = i % 4
        eb = "Eb%d" % (i % 2)
        lb = "Lb%d" % (i % 3)
        E = Eb[i % 2]
        L = Lb[i % 3]
        P.op("act", lambda e: e.activation(E[:], ps[zb][:], AF.Exp), reads=[psn[zb]], writes=[eb])

    def sb_s2b(i):
        h, g, k, j, njt = items[i]
        zb = i % 4
        eb = "Eb%d" % (i % 2)
        lb = "Lb%d" % (i % 3)
        E = Eb[i % 2]
        L = Lb[i % 3]
        P.op("act", lambda e: e.activation(L[:], E[:], AF.Ln, bias=1.0), reads=[eb], writes=[lb])
        if k < 4:
            jj = 3 - k
            P.op("dve", lambda e: e.tensor_tensor(L[:], L[:], msb[:, jj, :], ALU.mult),
                 reads=[lb, "msb"], writes=[lb])
        if k < njt - 1:
            lr = "Lrun%d" % h
            nb = "Lrb%d_%d" % (h, (k + 1) % 2)
            dst = Lrb[h][(k + 1) % 2]
            if k == 0:
                P.op("dve", lambda e: e.tensor_copy(Lrun[h][:], L[:]), reads=[lb], writes=[lr])
                P.op("dve", lambda e: e.tensor_copy(dst[:], L[:]), reads=[lb], writes=[nb])
            else:
                P.op("dve", lambda e: e.tensor_tensor(Lrun[h][:], Lrun[h][:], L[:], ALU.add),
                     reads=[lb, lr], writes=[lr])
                P.op("dve", lambda e: e.tensor_copy(dst[:], Lrun[h][:]), reads=[lr], writes=[nb])

    def sb_s3(i):
        h, g, k, j, njt = items[i]
        zb = i % 4
        lb = "Lb%d" % (i % 3)
        L = Lb[i % 3]
        P.op("pe", lambda e: e.matmul(ps[zb][:], negtri[:], L[:], start=False, stop=(k == 0),
                                      skip_group_check=True),
             reads=["negtri", lb], writes=[psn[zb]], signal=(k == 0))
        if k > 0:
            cur = Lrb[h][k % 2]
            P.op("pe", lambda e: e.matmul(ps[zb][:], negones[:], cur[:], start=False, stop=True,
                                          skip_group_check=True),
                 reads=["negones", "Lrb%d_%d" % (h, k % 2)], writes=[psn[zb]])

    def sb_s4(i):
        h, g, k, j, njt = items[i]
        zb = i % 4
        ab = "Ab%d" % (i % 3)
        Aa = Ab[i % 3]
        P.op("act", lambda e: e.activation(Aa[:], ps[zb][:], AF.Exp), reads=[psn[zb]], writes=[ab])
        if k < 4:
            jj = 3 - k
            P.op("dve", lambda e: e.tensor_tensor(Aa[:], Aa[:], msb[:, jj, :], ALU.mult),
                 reads=[ab, "msb"], writes=[ab])

    def sb_s5(i):
        h, g, k, j, njt = items[i]
        ab = "Ab%d" % (i % 3)
        Aa = Ab[i % 3]
        ob = 4 + (2 * g + h) % 3
        P.op("pe", lambda e: e.matmul(ps[ob][0:64, :], V[:, j, h * 64:(h + 1) * 64], Aa[:],
                                      start=(k == 0), stop=(k == njt - 1), skip_group_check=True),
             reads=["V", ab], writes=[psn[ob]], signal=True)
        if k == njt - 1:
            so = "ost%d" % (g % 2)
            P.op("dve", lambda e: e.tensor_copy(ost[g % 2][h * 64:(h + 1) * 64, :], ps[ob][0:64, :]),
                 reads=[psn[ob]], writes=[so + "_%d" % h])
            if h == 1:
                P.dma("sp", so, oT(0, g), ost[g % 2][:],
                      reads=[so + "_0", so + "_1"], writes=["oT0_%d" % g])

    order = [(sb_s1, 0), (sb_s2, 1), (sb_s4, 3), (sb_s2b, 1), (sb_s3, 2), (sb_s5, 4)]
    for step in range(NI + 4):
        for fn, lag in order:
            i = step - lag
            if 0 <= i < NI:
                fn(i)

    P.barrier()
    if not parts & 2:
        return
    QTa = [R[0], R[1]]
    KTa = [R[2], R[3]]
    cosv, sinv = C["cosT"], C["sinT"]
    csb = [[A("cs%d_%d" % (a, i), [128, 512], F32) for i in range(2)] for a in range(2)]
    t1 = A("t1", [128, 512], F32)
    t2 = A("t2", [128, 512], F32)
    qf = A("qf", [128, 512], F32)
    kf = A("kf", [128, 512], F32)
    kmT = A("kmT", [128, 2, 32], F32)
    gm = [A("gm%d" % i, [128, 32], F32) for i in range(2)]
    top8 = [A("top8_%d" % i, [128, 8], F32) for i in range(2)]
    mb1 = [A("mb1_%d" % i, [128, 32], F32) for i in range(2)]
    mbpad = [A("mbpad%d" % i, [128, 128], BF16) for i in range(2)]
    cbrow = A("cbrow", [128, 64], F32)
    vbrow = A("vbrow", [128, 64], F32)
    oorow = A("oorow", [128, 64], F32)
    Pb = [A("Pb%d" % i, [128, 512], BF16) for i in range(3)]
    rec = A("rec", [128, 512], F32)
    bcs = A("bcs", [64, 512], F32)

    P.op("dve", lambda e: e.memset(cbrow[:, 0:32], 0.0), writes=["cbrow"])
    P.op("dve", lambda e: e.memset(cbrow[:, 32:64], NEG), writes=["cbrow"])
    P.op("dve", lambda e: e.memset(vbrow[:, 0:32], BIG), writes=["vbrow"])
    P.op("dve", lambda e: e.memset(vbrow[:, 32:64], 0.0), writes=["vbrow"])
    P.op("dve", lambda e: e.memset(oorow[:], -BIG), writes=["oorow"])
    P.op("dve", lambda e: e.memset(oorow[:, 31:32], 0.0), writes=["oorow"])
    P.op("dve", lambda e: e.memset(kmT[:], 0.0), writes=["kmT"])
    P.op("dve", lambda e: e.memset(Vm[:], 1.0), writes=["Vm"])
    for i in range(2):
        P.op("dve", lambda e, i=i: e.memset(mbpad[i][:], 0.0), writes=["mbpad%d" % i])
    for h in range(2):
        P.op("dve", lambda e, h=h: e.memset(QTa[h][64:128, :], 0.0), writes=["QTa%d" % h])
        P.op("dve", lambda e, h=h: e.memset(KTa[h][64:128, :], 0.0), writes=["KTa%d" % h])
        P.dma("pool", "oh%d" % h, KTa[h][64:96, :], C["onehot"], writes=["KTa%d" % h])

    def load_cs(tg):
        b = tg % 2
        P.dma("sp", "cs0_%d" % b, csb[0][b][:], cosv[:, tg * 512:(tg + 1) * 512], writes=["cs0_%d" % b])
        P.dma("sp", "cs1_%d" % b, csb[1][b][:], sinv[:, tg * 512:(tg + 1) * 512], writes=["cs1_%d" % b])

    def rope(dst, b, bank_a, bank_b):
        cn, sn = "cs0_%d" % b, "cs1_%d" % b
        P.op("dve", lambda e: e.tensor_tensor(t1[:], ps[bank_a][:], csb[0][b][:], ALU.mult),
             reads=[psn[bank_a], cn], writes=["t1"])
        P.op("dve", lambda e: e.tensor_tensor(t2[:], ps[bank_b][:], csb[1][b][:], ALU.mult),
             reads=[psn[bank_b], sn], writes=["t2"])
        nm = "qf" if dst is qf else "kf"
        P.op("dve", lambda e: e.tensor_tensor(dst[:], t1[:], t2[:], ALU.add),
             reads=["t1", "t2"], writes=[nm])

    load_x(0)
    load_cs(0)
    for tg in range(NG):
        if tg + 1 < NG:
            load_x(tg + 1)
            load_cs(tg + 1)
        b = tg % 2
        sl = slice(tg * 512, (tg + 1) * 512)
        proj_fm(0, 512, tg)
        proj_fm(1, 896, tg)
        rope(kf, b, 0, 1)
        for h in range(2):
            P.op("act", lambda e, h=h: e.copy(KTa[h][0:64, sl], kf[h * 64:(h + 1) * 64, :]),
                 reads=["kf"], writes=["KTa%d" % h])
        for h in range(2):
            hp2 = slice(h * 64, (h + 1) * 64)
            P.op("dve", lambda e, h=h, hp2=hp2: e.tensor_reduce(
                kmT[hp2, h, 2 * tg:2 * tg + 2], kf[hp2, :].rearrange("p (a b) -> p a b", a=2), AX.X, ALU.add),
                reads=["kf"], writes=["kmT"])
        proj_fm(2, 384, tg)
        proj_fm(3, 768, tg)
        rope(qf, b, 2, 3)
        for h in range(2):
            P.op("act", lambda e, h=h: e.mul(QTa[h][0:64, sl], qf[h * 64:(h + 1) * 64, :], 0.125),
                 reads=["qf"], writes=["QTa%d" % h])
        proj_tm(4, 640, tg)
        P.op("act", lambda e: e.copy(Vm[:, tg * 4:(tg + 1) * 4, :, 0:64],
                                    ps[4][:].rearrange("p (a h d) -> p a h d", a=4, h=2)),
             reads=["ps4"], writes=["Vm"])
        for idx in range(8 if not (DBG_MB & 2) else 0):
            cq, h = idx // 2, idx % 2
            P.op("pe", lambda e, cq=cq, h=h, idx=idx: e.matmul(
                ps[5][:, idx * 32:(idx + 1) * 32], qf[:, cq * 128:(cq + 1) * 128],
                kmT[:, h, :], start=True, stop=True),
                reads=["qf", "kmT"], writes=["ps5"], signal=False)
        P.op("pe", lambda e: e.matmul(ps[5][:, 256:384], identb[:], identb[:], start=True, stop=True),
             reads=["identb"], writes=["ps5"])
        for rnd in range(1):
            for idx in range(8 if not (DBG_MB & 2) else 0):
                cq, h = idx // 2, idx % 2
                cch = tg * 4 + cq
                own = cch // 2
                u = idx % 2
                P.op("dve", lambda e, idx=idx, own=own, u=u: e.tensor_tensor(
                    gm[u][:], ps[5][:, idx * 32:(idx + 1) * 32], cbrow[:, 32 - own:64 - own], ALU.add),
                    reads=["ps5", "cbrow"], writes=["gm%d" % u])
                P.op("dve", lambda e, u=u: e.max(top8[u][:], gm[u][:]), reads=["gm%d" % u], writes=["top8_%d" % u])
                P.op("dve", lambda e, own=own, u=u: e.scalar_tensor_tensor(
                    mb1[u][:], gm[u][:], top8[u][:, 2:3], vbrow[:, 32 - own:64 - own], ALU.is_ge, ALU.mult),
                    reads=["gm%d" % u, "top8_%d" % u, "vbrow"], writes=["mb1_%d" % u])
                P.op("dve", lambda e, own=own, u=u: e.tensor_tensor(
                    mbpad[u][:, 64:96], mb1[u][:], oorow[:, 31 - own:63 - own], ALU.add),
                    reads=["mb1_%d" % u, "oorow"], writes=["mbpad%d" % u])
                if DBG_MB & 8:
                    continue
                P.op("pe", lambda e, u=u: e.matmul(ps[6][:, u * 128:(u + 1) * 128], mbpad[u][:], identb[:],
                                                    start=True, stop=True),
                     reads=["mbpad%d" % u, "identb"], writes=["ps6_%d" % u], signal=False)
                P.op("pe", lambda e, u=u: e.matmul(ps[6][:, 256 + u * 128:256 + (u + 1) * 128], identb[:], identb[:],
                                                    start=True, stop=True),
                     reads=["identb"], writes=["ps6_%d" % u])
                P.op("act", lambda e, h=h, cch=cch, u=u: e.copy(QTa[h][64:96, cch * 128:(cch + 1) * 128],
                                                              ps[6][64:96, u * 128:(u + 1) * 128]),
                     reads=["ps6_%d" % u], writes=["QTa%d" % h])

    P.barrier()
    Pb2 = [A("Pb2_%d" % i, [128, 1024], BF16) for i in range(3)]
    mitems = []
    for g in range(NG):
        njt = 4 * g + 4
        for j in range(njt):
            mitems.append((g, j, njt))
    NM = len(mitems) if not (DBG_MB & 1) else 0

    def mb_s1(i):
        g, j, njt = mitems[i]
        zp = i % 2
        for h in range(2):
            P.op("pe", lambda e, h=h: e.matmul(psbig[:, zp * 1024 + h * 512:zp * 1024 + (h + 1) * 512],
                                               KTa[h][:, j * 128:(j + 1) * 128], QTa[h][:, g * 512:(g + 1) * 512],
                                               start=True, stop=True),
                 reads=["QTa%d" % h, "KTa%d" % h], writes=["zp%d" % zp], signal=(h == 1))

    def mb_s2(i):
        g, j, njt = mitems[i]
        zp = i % 2
        pb = "Pb2_%d" % (i % 3)
        Pt = Pb2[i % 3]
        P.op("act", lambda e: e.activation(Pt[:], psbig[:, zp * 1024:(zp + 1) * 1024], AF.Exp),
             reads=["zp%d" % zp], writes=[pb])
        if j >= 4 * g:
            jj = j - 4 * g
            for h in range(2):
                P.op("dve", lambda e, h=h: e.tensor_tensor(Pt[:, h * 512:(h + 1) * 512], Pt[:, h * 512:(h + 1) * 512],
                                                           mmb[:, jj, :], ALU.mult),
                     reads=[pb, "mmb"], writes=[pb])

    def mb_s3(i):
        g, j, njt = mitems[i]
        pb = "Pb2_%d" % (i % 3)
        Pt = Pb2[i % 3]
        for h in range(2):
            ob = 4 + (2 * g + h) % 3
            P.op("pe", lambda e, h=h, ob=ob: e.matmul(ps[ob][:, :], Vm[:, j, h, :], Pt[:, h * 512:(h + 1) * 512],
                                                      start=(j == 0), stop=(j == njt - 1), skip_group_check=True),
                 reads=["Vm", pb], writes=[psn[ob]], signal=True)
            if j == njt - 1:
                so = "ost%d" % (g % 2)
                P.op("dve", lambda e, ob=ob: e.reciprocal(rec[64:128, :], ps[ob][64:128, :]),
                     reads=[psn[ob]], writes=["rec"])
                P.op("dve", lambda e, h=h, ob=ob: e.tensor_tensor(ost[g % 2][h * 64:(h + 1) * 64, :], ps[ob][0:64, :],
                                                                  rec[64:128, :], ALU.mult),
                     reads=[psn[ob], "rec"], writes=[so + "_%d" % h])
                if h == 1:
                    P.dma("sp", so, oT(1, g), ost[g % 2][:],
                          reads=[so + "_0", so + "_1"], writes=["oT1_%d" % g])

    mstages = [mb_s1, mb_s2, mb_s3]
    for step in range(NM + len(mstages) - 1):
        for s_, fn in enumerate(mstages):
            i = step - s_
            if 0 <= i < NM:
                fn(i)
    P.barrier()


def build_B(nc, P, ctx, sfx, xT_d, oTs_v, oTm_v, oreads, W, lnp_d, out_d, outb_d=None):
    def A(name, shape, dt):
        return ctx.enter_context(nc.sbuf_tensor(name + sfx, shape, dt))
    xT = A("xT_sb", [128, 8, TOK], F32)
    xb = A("xb_sb", [128, 8, TOK], BF16)
    arena = A("arena", [128, 32768], BF16)
    oTs = arena[:, 0:8192].rearrange("p (a b) -> p a b", a=4)
    oTm = arena[:, 8192:16384].rearrange("p (a b) -> p a b", a=4)
    mg = arena[:, 16384:32768].rearrange("p (a b) -> p a b", a=8)
    hT = arena[:, 0:NFC * 1024].rearrange("p (a b) -> p a b", a=NFC)
    lnp = A("lnp_sb", [128, 32], F32)
    onesD = A("onesD", [128, 128], F32)
    warena = A("warena", [128, 9728], BF16)
    wgb = [warena[:, i * 2048:(i + 1) * 2048].rearrange("p (a b) -> p a b", a=8) for i in range(2)]
    wbb = [warena[:, 4096 + i * 1024:4096 + (i + 1) * 1024].rearrange("p (a b) -> p a b", a=4) for i in range(2)]
    wob = [warena[:, 6144 + i * 1024:6144 + (i + 1) * 1024].rearrange("p (a b) -> p a b", a=8) for i in range(2)]
    wfgu = [warena[:, i * 2048:(i + 1) * 2048].rearrange("p (a b) -> p a b", a=8) for i in range(2)]
    wfdb = [warena[:, 4096 + i * 2816:4096 + (i + 1) * 2816].rearrange("p (a b) -> p a b", a=NFC) for i in range(2)]
    sg = [A("sg%d" % i, [128, 512], F32) for i in range(4)]
    m1 = [A("m1_%d" % i, [128, 512], F32) for i in range(2)]
    m2 = [A("m2_%d" % i, [128, 512], F32) for i in range(2)]
    ysq = [A("ysq%d" % i, [128, 512], F32) for i in range(2)]
    mean_sb = A("mean_sb", [128, 512], F32)
    rstd_sb = A("rstd_sb", [128, 512], F32)
    tn = [A("tn%d" % i, [128, 512], F32) for i in range(2)]
    ps = [ctx.enter_context(nc.psum_tensor("pb%d" % i + sfx, [128, 512], F32)) for i in range(8)]
    psn = ["pb%d" % i for i in range(8)]
    bank = [0]
    XN = [["xT%d_%d" % (c, g) for g in range(NTG)] for c in range(8)]
    XB = [["xb%d_%d" % (c, g) for g in range(NTG)] for c in range(8)]
    MG = [["mg%d_%d" % (c, g) for g in range(NTG)] for c in range(8)]
    HT = [["hT%d_%d" % (k, g) for g in range(2)] for k in range(NFC)]
    allx = [n for r_ in XN for n in r_]
    allxb = [n for r_ in XB for n in r_]

    def nb():
        bank[0] = (bank[0] + 1) % 8
        return bank[0]

    xdv = xT_d.rearrange("(c p) t -> p c t", p=128)
    P.dma("sp", "xT", xT[:], xdv, writes=allx)
    P.dma("pool", "xb", xb[:], xdv, writes=allxb)
    P.dma("sp", "oTs", oTs, oTs_v, reads=oreads, writes=["oTs"])
    P.dma("sp", "oTm", oTm, oTm_v, reads=oreads, writes=["oTm"])
    P.dma("sp", "lnp", lnp[:], lnp_d, writes=["lnp"])
    P.op("dve", lambda e: e.memset(onesD[:], 1.0 / D), writes=["onesD"])

    wGv = W["wG"].rearrange("(k p) n -> p k n", p=128)
    wbsv = W["wbs"].rearrange("(k p) n -> p k n", p=128)
    wbmv = W["wbm"].rearrange("(k p) n -> p k n", p=128)
    wov = W["wout"].rearrange("(k p) n -> p k n", p=128)
    wfgv = W["wfg"].rearrange("(k p) n -> p k n", p=128)
    wfuv = W["wfu"].rearrange("(k p) n -> p k n", p=128)
    wfdv = W["wfd"].rearrange("(k p) n -> p k n", p=128)

    def tsl(tg):
        return slice(tg * 512, (tg + 1) * 512)

    def load_b1(c):
        b = c % 2
        cs = slice(c * 128, (c + 1) * 128)
        P.dma("pool", "wgb%da" % b, wgb[b][:, :, 0:128], wGv[:, :, cs], writes=["wgb%da" % b])
        P.dma("pool", "wgb%db" % b, wgb[b][:, :, 128:256], wGv[:, :, D + c * 128:D + (c + 1) * 128],
              writes=["wgb%db" % b])
        P.dma("pool", "wbb%da" % b, wbb[b][:, :, 0:128], wbsv[:, :, cs], writes=["wbb%da" % b])
        P.dma("pool", "wbb%db" % b, wbb[b][:, :, 128:256], wbmv[:, :, cs], writes=["wbb%db" % b])

    load_b1(0)
    it = 0
    for c in range(8):
        if c + 1 < 8:
            load_b1(c + 1)
        b = c % 2
        for tg in range(NTG):
            t = tsl(tg)
            bk = [nb() for _ in range(4)]
            for half, (bkk, wn) in enumerate(zip(bk[0:2], ["wgb%da" % b, "wgb%db" % b])):
                for k in range(8):
                    P.op("pe", lambda e, k=k, half=half, bkk=bkk: e.matmul(
                        ps[bkk][:], wgb[b][:, k, half * 128:(half + 1) * 128], xb[:, k, t],
                        start=(k == 0), stop=(k == 7)),
                        reads=[wn, XB[k][tg]], writes=[psn[bkk]], signal=(k == 7))
            for half, (bkk, wn, src, sn) in enumerate(zip(bk[2:4], ["wbb%da" % b, "wbb%db" % b],
                                                          [oTs, oTm], ["oTs", "oTm"])):
                for k in range(4):
                    P.op("pe", lambda e, k=k, half=half, bkk=bkk, src=src: e.matmul(
                        ps[bkk][:], wbb[b][:, k, half * 128:(half + 1) * 128], src[:, k, t],
                        start=(k == 0), stop=(k == 3)),
                        reads=[wn, sn], writes=[psn[bkk]], signal=(k == 3))
            u = it % 2
            s0, s1 = sg[2 * u], sg[2 * u + 1]
            P.op("act", lambda e, s0=s0, bkk=bk[0]: e.activation(s0[:], ps[bkk][:], AF.Sigmoid),
                 reads=[psn[bk[0]]], writes=["sg%d" % (2 * u)])
            P.op("act", lambda e, s1=s1, bkk=bk[1]: e.activation(s1[:], ps[bkk][:], AF.Sigmoid),
                 reads=[psn[bk[1]]], writes=["sg%d" % (2 * u + 1)])
            P.op("dve", lambda e, s0=s0, u=u, bkk=bk[2]: e.tensor_tensor(m1[u][:], s0[:], ps[bkk][:], ALU.mult),
                 reads=[psn[bk[2]], "sg%d" % (2 * u)], writes=["m1_%d" % u])
            P.op("dve", lambda e, s1=s1, u=u, bkk=bk[3]: e.tensor_tensor(m2[u][:], s1[:], ps[bkk][:], ALU.mult),
                 reads=[psn[bk[3]], "sg%d" % (2 * u + 1)], writes=["m2_%d" % u])
            P.op("pool", lambda e, u=u, c=c, t=t: e.tensor_tensor(mg[:, c, t], m1[u][:], m2[u][:], ALU.add),
                 reads=["m1_%d" % u, "m2_%d" % u], writes=[MG[c][tg]])
            it += 1

    def layer_norm(tg, gcol, bcol):
        t = tsl(tg)
        ba, bb_ = nb(), nb()
        for c in range(8):
            q = ysq[c % 2]
            P.op("act", lambda e, q=q, c=c: e.activation(q[:], xT[:, c, t], AF.Square),
                 reads=[XN[c][tg]], writes=["ysq%d" % (c % 2)])
            P.op("pe", lambda e, c=c: e.matmul(ps[ba][:], onesD[:], xT[:, c, t], start=(c == 0), stop=(c == 7)),
                 reads=["onesD", XN[c][tg]], writes=[psn[ba]], signal=(c == 7))
            P.op("pe", lambda e, q=q, c=c: e.matmul(ps[bb_][:], onesD[:], q[:], start=(c == 0), stop=(c == 7),
                                                   skip_group_check=True),
                 reads=["onesD", "ysq%d" % (c % 2)], writes=[psn[bb_]], signal=True)
        P.op("dve", lambda e: e.tensor_copy(mean_sb[:], ps[ba][:]), reads=[psn[ba]], writes=["mean_sb"])
        P.op("dve", lambda e: e.tensor_tensor(rstd_sb[:], mean_sb[:], mean_sb[:], ALU.mult),
             reads=["mean_sb"], writes=["rstd_sb"])
        P.op("dve", lambda e: e.tensor_tensor(rstd_sb[:], ps[bb_][:], rstd_sb[:], ALU.subtract),
             reads=[psn[bb_], "rstd_sb"], writes=["rstd_sb"])
        P.op("act", lambda e: e.activation(rstd_sb[:], rstd_sb[:], AF.Ln, bias=EPS),
             reads=["rstd_sb"], writes=["rstd_sb"])
        P.op("act", lambda e: e.activation(rstd_sb[:], rstd_sb[:], AF.Exp, scale=-0.5),
             reads=["rstd_sb"], writes=["rstd_sb"])
        for c in range(8):
            tt = tn[c % 2]
            tnn = "tn%d" % (c % 2)
            P.op("dve", lambda e, tt=tt, c=c: e.tensor_tensor(tt[:], xT[:, c, t], mean_sb[:], ALU.subtract),
                 reads=[XN[c][tg], "mean_sb"], writes=[tnn])
            P.op("dve", lambda e, tt=tt: e.tensor_tensor(tt[:], tt[:], rstd_sb[:], ALU.mult),
                 reads=[tnn, "rstd_sb"], writes=[tnn])
            P.op("act", lambda e, tt=tt, c=c: e.activation(xT[:, c, t], tt[:], AF.Identity,
                                                          bias=lnp[:, bcol + c:bcol + c + 1],
                                                          scale=lnp[:, gcol + c:gcol + c + 1]),
                 reads=[tnn, "lnp"], writes=[XN[c][tg]])
            P.op("pool", lambda e, c=c: e.tensor_copy(xb[:, c, t], xT[:, c, t]), reads=[XN[c][tg]], writes=[XB[c][tg]])

    def load_wo(c):
        b = c % 2
        P.dma("pool", "wob%d" % b, wob[b][:], wov[:, :, c * 128:(c + 1) * 128], writes=["wob%d" % b])

    load_wo(0)
    for c in range(8):
        if c + 1 < 8:
            load_wo(c + 1)
        b = c % 2
        for tg in range(NTG):
            t = tsl(tg)
            bkk = nb()
            for k in range(8):
                P.op("pe", lambda e, k=k, bkk=bkk: e.matmul(ps[bkk][:], wob[b][:, k, :], mg[:, k, t],
                                                            start=(k == 0), stop=(k == 7)),
                     reads=["wob%d" % b, MG[k][tg]], writes=[psn[bkk]], signal=(k == 7))
            P.op("dve", lambda e, bkk=bkk, c=c, t=t: e.scalar_tensor_tensor(
                xT[:, c, t], xT[:, c, t], ALPHA, ps[bkk][:], ALU.mult, ALU.add),
                reads=[psn[bkk], XN[c][tg]], writes=[XN[c][tg]])
    for tg in range(NTG):
        layer_norm(tg, 0, 8)

    def load_gu(i):
        k = i % NFC
        b = i % 2
        P.dma("pool", "wfgu%da" % b, wfgu[b][:, :, 0:128], wfgv[:, :, k * 128:(k + 1) * 128],
              writes=["wfgu%da" % b])
        P.dma("pool", "wfgu%db" % b, wfgu[b][:, :, 128:256], wfuv[:, :, k * 128:(k + 1) * 128],
              writes=["wfgu%db" % b])

    def load_dn(i):
        c = i % 8
        b = i % 2
        P.dma("pool", "wfdb%d" % b, wfdb[b][:], wfdv[:, :, c * 128:(c + 1) * 128], writes=["wfdb%d" % b])

    gi = 0
    di = 0
    P.barrier()
    for hf in range(2):
        load_gu(gi)
        for k in range(NFC):
            if k + 1 < NFC:
                load_gu(gi + 1)
            b = gi % 2
            for t2_ in range(2):
                tg = hf * 2 + t2_
                t = tsl(tg)
                b0, b1 = nb(), nb()
                for half, bkk in enumerate([b0, b1]):
                    wn = "wfgu%d%s" % (b, "ab"[half])
                    for kk in range(8):
                        P.op("pe", lambda e, kk=kk, half=half, bkk=bkk: e.matmul(
                            ps[bkk][:], wfgu[b][:, kk, half * 128:(half + 1) * 128], xb[:, kk, t],
                            start=(kk == 0), stop=(kk == 7)),
                            reads=[wn, XB[kk][tg]], writes=[psn[bkk]], signal=(kk == 7))
                u = it % 2
                s0 = sg[2 * u]
                P.op("act", lambda e, s0=s0, b0=b0: e.activation(s0[:], ps[b0][:], AF.Silu),
                     reads=[psn[b0]], writes=["sg%d" % (2 * u)])
                P.op("dve", lambda e, s0=s0, b1=b1, k=k, t2_=t2_: e.tensor_tensor(
                    hT[:, k, t2_ * 512:(t2_ + 1) * 512], s0[:], ps[b1][:], ALU.mult),
                    reads=[psn[b1], "sg%d" % (2 * u)], writes=[HT[k][t2_]])
                it += 1
            gi += 1
        load_dn(di)
        for c in range(8):
            if c + 1 < 8:
                load_dn(di + 1)
            b = di % 2
            for t2_ in range(2):
                tg = hf * 2 + t2_
                t = tsl(tg)
                bkk = nb()
                for k in range(NFC):
                    P.op("pe", lambda e, k=k, bkk=bkk, t2_=t2_: e.matmul(
                        ps[bkk][:], wfdb[b][:, k, :], hT[:, k, t2_ * 512:(t2_ + 1) * 512],
                        start=(k == 0), stop=(k == NFC - 1)),
                        reads=["wfdb%d" % b, HT[k][t2_]], writes=[psn[bkk]], signal=(k == NFC - 1))
                P.op("dve", lambda e, bkk=bkk, c=c, t=t: e.scalar_tensor_tensor(
                    xT[:, c, t], xT[:, c, t], ALPHA, ps[bkk][:], ALU.mult, ALU.add),
                    reads=[psn[bkk], XN[c][tg]], writes=[XN[c][tg]])
            di += 1
        for t2_ in range(2):
            layer_norm(hf * 2 + t2_, 16, 24)
    P.dma("sp", "out", out_d.rearrange("(c p) t -> p c t", p=128), xT[:], reads=allx, writes=["out"])
    if outb_d is not None:
        P.dma("sp", "outb", outb_d.rearrange("(c p) t -> p c t", p=128), xb[:], reads=allxb, writes=["outb"])


W_KEYS = ["wG", "wbs", "wbm", "wout", "wfg", "wfu", "wfd"]
W_SHAPES = {"wG": [D, 2048], "wbs": [512, D], "wbm": [512, D], "wout": [D, D],
            "wfg": [D, DFF], "wfu": [D, DFF], "wfd": [DFF, D]}
RG = [[0, 1, 2, 3], [4, 5, 6, 7]]


def _pp(v):
    return np.ascontiguousarray(v.reshape(8, 128).T)


def build_fused(nlayers=DEPTH, skipA=False, skipB=False, samex=False, parts=3):
    nc = bass.Bass("TRN2", target_bir_lowering=False)
    P = Prog(nc)
    hc = host_consts()
    C = {k: nc.dram_tensor("c_" + k, list(v.shape), F32, kind="ExternalInput").ap() for k, v in hc.items()}
    xTall0 = nc.dram_tensor("xTall0", [D, S], F32, kind="ExternalInput").ap()
    xT0 = nc.dram_tensor("xT0", [D, TOK], F32, kind="ExternalInput").ap()
    wA = nc.dram_tensor("wA", [DEPTH, D, 768], F32, kind="ExternalInput").ap()
    Wd = {k: nc.dram_tensor(k, [DEPTH] + W_SHAPES[k], F32, kind="ExternalInput").ap() for k in W_KEYS}
    lnp = nc.dram_tensor("lnp", [DEPTH, 128, 32], F32, kind="ExternalInput").ap()
    out = nc.dram_tensor("out", [D, TOK], F32, kind="ExternalOutput").ap()
    oT_loc = nc.dram_tensor("oT_loc", [1024, TOK], BF16, kind="Internal").ap()
    oT_all = nc.dram_tensor("oT_all", [4096, TOK], BF16, kind="Internal").ap()
    xb_loc = nc.dram_tensor("xb_loc", [D, TOK], BF16, kind="Internal").ap()
    xb_all = nc.dram_tensor("xb_all", [4 * D, TOK], BF16, kind="Internal").ap()
    xres = nc.dram_tensor("xres", [D, TOK], F32, kind="Internal").ap()
    pid = nc.sync.partition_id()
    roff = (pid % 4) * 1024
    olv = oT_loc.rearrange("(q b p) t -> b p q t", q=4, b=2)

    def odst(br, g):
        return olv[br, :, g // 4, (g % 4) * 512:(g % 4 + 1) * 512]

    for l in range(nlayers):
        last = l == nlayers - 1
        with ExitStack() as ctx:
            if l == 0 or samex:
                xv0 = xTall0.rearrange("(c p) t -> p c t", p=128)
                xsrc = lambda tg: [(0, 8, xv0[:, :, tg * 512:(tg + 1) * 512])]
                xeng, xreads = "pool", []
            else:
                xv1 = xb_all.rearrange("(j r c p) t -> p r j c t", j=4, r=4, c=2, p=128)
                xsrc = lambda tg: [(2 * j, 2 * j + 2, xv1[:, tg // 4, j, :, (tg % 4) * 512:(tg % 4 + 1) * 512])
                                   for j in range(4)]
                xeng, xreads = "sp", ["xb_all%d" % j for j in range(4)]
            if not skipA:
                build_A(nc, P, ctx, "_a%d" % l, xsrc, xeng, xreads, wA[l], odst, C, parts)
        P.barrier()
        for q in range(4):
            P.collective("oT%d" % q, "AllGather", RG, oT_loc[q * 256:(q + 1) * 256, :],
                         oT_all[q * 1024:(q + 1) * 1024, :], writes=["oT_all%d" % q])
        with ExitStack() as ctx:
            ov = oT_all[bass.ds(roff, 1024), :].rearrange("(k b p) t -> p k b t", k=4, b=2, p=128)
            oTs_v = ov[:, :, 0, :]
            oTm_v = ov[:, :, 1, :]
            Wl = {k: Wd[k][l] for k in W_KEYS}
            if not skipB:
              build_B(nc, P, ctx, "_b%d" % l, xT0 if l == 0 else xres, oTs_v, oTm_v,
                    ["oT_all%d" % q for q in range(4)], Wl, lnp[l],
                    out if last else xres, None if last else xb_loc)
        P.barrier()
        if not last:
            for j in range(4):
                P.collective("xb%d" % j, "AllGather", RG, xb_loc[j * 256:(j + 1) * 256, :],
                             xb_all[j * 1024:(j + 1) * 1024, :], writes=["xb_all%d" % j])
    P.finish("sp")
    return nc, hc


def _wA_slices(w_in_l, r):
    def cols(base):
        return w_in_l[:, base + 128 * r: base + 128 * r + 128]
    return np.concatenate([cols(0), cols(512), cols(1024), cols(1536), cols(2048), cols(2560)], axis=1)


def kernel(x, w_in, w_branch_sb, w_branch_moba, w_out, ln_mix_g, ln_mix_b,
           w_ffn_gate, w_ffn_up, w_ffn_down, ln_ffn_g, ln_ffn_b):
    f = lambda a: np.asarray(a, dtype=np.float32)
    x, w_in, w_branch_sb, w_branch_moba, w_out = f(x), f(w_in), f(w_branch_sb), f(w_branch_moba), f(w_out)
    ln_mix_g, ln_mix_b, ln_ffn_g, ln_ffn_b = f(ln_mix_g), f(ln_mix_b), f(ln_ffn_g), f(ln_ffn_b)
    w_ffn_gate, w_ffn_up, w_ffn_down = f(w_ffn_gate), f(w_ffn_up), f(w_ffn_down)
    nc, hc = build_fused()
    cores = list(range(8))
    xTb = [np.ascontiguousarray(x[b].T) for b in range(B)]
    Wfull = {"wG": np.ascontiguousarray(w_in[:, :, 3072:5120]), "wbs": w_branch_sb, "wbm": w_branch_moba,
             "wout": w_out, "wfg": w_ffn_gate, "wfu": w_ffn_up, "wfd": w_ffn_down}
    lnp = np.ascontiguousarray(np.stack([np.concatenate(
        [_pp(ln_mix_g[l]), _pp(ln_mix_b[l]), _pp(ln_ffn_g[l]), _pp(ln_ffn_b[l])], axis=1) for l in range(DEPTH)]))
    maps = []
    for c in cores:
        b, r = c // 4, c % 4
        m = {"xTall0": xTb[b], "xT0": np.ascontiguousarray(xTb[b][:, r * TOK:(r + 1) * TOK]),
             "wA": np.ascontiguousarray(np.stack([_wA_slices(w_in[l], r) for l in range(DEPTH)])),
             "lnp": lnp}
        m.update({k: np.ascontiguousarray(v) for k, v in Wfull.items()})
        m.update({"c_" + k: v for k, v in hc.items()})
        maps.append(m)
    res = run_bass_kernel_spmd(nc, maps, core_ids=cores)
    xn = np.empty_like(x)
    for c in cores:
        b, r = c // 4, c % 4
        xn[b, r * TOK:(r + 1) * TOK, :] = np.asarray(res.results[c]["out"]).T
    return xn
```

```python
from contextlib import ExitStack
import os
DBG_GATE = int(os.environ.get('DBG_GATE', '1'))
DBG_NORM = int(os.environ.get('DBG_NORM', '1'))
DBG_MB = int(os.environ.get('DBG_MB', '0'))
import numpy as np
import ml_dtypes
import concourse.bass as bass
import concourse.mybir as mybir
from concourse.bass_utils import run_bass_kernel_spmd

F32 = mybir.dt.float32
BF16 = mybir.dt.bfloat16
AF = mybir.ActivationFunctionType
ALU = mybir.AluOpType
AX = mybir.AxisListType

D = 1024
S = 8192
B = 2
DEPTH = 2
DFF = 2816
NFC = DFF // 128
NG = S // 512
ALPHA = (2 * DEPTH) ** 0.25
EPS = 1e-5
BIG = 30000.0
NEG = -1.0e30
TOK = 2048
NTG = TOK // 512


class Prog:
    SAME_ENGINE_SYNC = True
    EMBED = ("act", "dve")

    def __init__(self, nc):
        self.nc = nc
        self.eng = {"pe": nc.tensor, "act": nc.scalar, "dve": nc.vector,
                    "pool": nc.gpsimd, "sp": nc.sync}
        self.sem = {}
        self.cnt = {}
        self.res = {}
        self.known = {e: {} for e in self.eng}
        self.nwait = 0
        self.nops = 0
        self._pending = None

    def _sem(self, lane):
        if lane not in self.sem:
            self.sem[lane] = self.nc.alloc_semaphore(name="s_" + lane.replace(":", "_"))
            self.cnt[lane] = 0
        return self.sem[lane]

    def _need(self, e, lane, val):
        if lane == e and (e == "pe" or not self.SAME_ENGINE_SYNC):
            return
        if self.known[e].get(lane, 0) >= val:
            return
        self.known[e][lane] = val
        self.nwait += 1
        if self._pending is not None:
            self._pending.append((lane, val))
            return
        self.eng[e].wait_ge(self._sem(lane), val)

    def _deps(self, e, reads, writes):
        for r in reads:
            ent = self.res.get(r)
            if ent and ent[0]:
                self._need(e, *ent[0])
        for w in writes:
            ent = self.res.get(w)
            if ent:
                if ent[0]:
                    self._need(e, *ent[0])
                for lane, val in ent[1].items():
                    self._need(e, lane, val)

    def _record(self, lane, val, reads, writes):
        for r in reads:
            ent = self.res.setdefault(r, [None, {}])
            ent[1][lane] = max(ent[1].get(lane, 0), val)
        for w in writes:
            self.res[w] = [(lane, val), {}]

    def op(self, e, fn, reads=(), writes=(), signal=True):
        self._sem(e)
        self._pending = [] if e in self.EMBED else None
        self._deps(e, reads, writes)
        pend = self._pending or []
        self._pending = None
        for lane_, val_ in pend[:-1]:
            self.eng[e].wait_ge(self._sem(lane_), val_)
        ins = fn(self.eng[e])
        if pend:
            ins._wait_ge(self._sem(pend[-1][0]), pend[-1][1])
        val = self.cnt[e] + 1
        if signal:
            ins.then_inc(self.sem[e], 1)
            self.cnt[e] = val
        self._record(e, val, reads, writes)
        self.nops += 1
        return ins

    def dma(self, e, lane, out, in_, reads=(), writes=(), **kw):
        lane = "d:" + lane
        self._sem(lane)
        self._deps(e, reads, writes)
        ins = self.eng[e].dma_start(out=out, in_=in_, **kw)
        val = self.cnt[lane] + 16
        ins.then_inc(self.sem[lane], 16)
        self.cnt[lane] = val
        self._record(lane, val, reads, writes)
        self.nops += 1
        return ins

    def collective(self, lane, kind, rg, in_ap, out_ap, reads=(), writes=()):
        lane = "c:" + lane
        self._sem(lane)
        self._deps("pool", reads, writes)
        ins = self.nc.gpsimd.collective_compute(kind, ALU.bypass, replica_groups=rg,
                                                ins=[in_ap], outs=[out_ap])
        val = self.cnt[lane] + 1
        ins.then_inc(self.sem[lane], 1)
        self.cnt[lane] = val
        self._record(lane, val, reads, writes)
        self.nops += 1
        return ins

    def barrier(self):
        for e in self.eng:
            for lane, val in self.cnt.items():
                if val > 0 and lane != e:
                    if self.known[e].get(lane, 0) < val:
                        self.eng[e].wait_ge(self.sem[lane], val)
                        self.known[e][lane] = val
        self.res = {}

    def finish(self, e="sp"):
        for lane, val in self.cnt.items():
            if val > 0 and lane != e:
                self.eng[e].wait_ge(self.sem[lane], val)


def host_consts():
    c = {}
    k = np.arange(128)[:, None]
    s = np.arange(128)[None, :]
    c["negtri"] = np.where(k >= s, -1.0, 0.0).astype(np.float32)
    c["negones"] = np.full((128, 128), -1.0, np.float32)
    c["ident"] = np.eye(128, dtype=np.float32)
    sk = np.arange(128)[:, None, None] + 128 * np.arange(4)[None, :, None]
    t = np.arange(512)[None, None, :]
    c["msb"] = (sk < t).astype(np.float32)
    c["mmb"] = (sk <= t).astype(np.float32)
    half = 8
    inv = (500000.0 ** (-np.arange(0, 16, 2, dtype=np.float32) / 16)).astype(np.float32)
    ang = (np.arange(S, dtype=np.float32)[:, None] * inv[None, :]).astype(np.float32)
    cos = np.cos(ang).astype(np.float32).T
    sin = np.sin(ang).astype(np.float32).T
    cosT = np.ones((64, S), np.float32)
    sinT = np.zeros((64, S), np.float32)
    cosT[0:8] = cos
    cosT[8:16] = cos
    sinT[0:8] = sin
    sinT[8:16] = sin
    c["cosT"] = np.concatenate([cosT, cosT], 0)
    c["sinT"] = np.concatenate([sinT, sinT], 0)
    n = np.arange(32)[None, :]
    own = (np.arange(64) // 2)[:, None]
    cb = np.where(n < own, 0.0, NEG).astype(np.float32)
    vb = np.where(n < own, BIG, 0.0).astype(np.float32)
    oo = np.where(n == own, 0.0, -BIG).astype(np.float32)
    c["cb"] = np.broadcast_to(cb.reshape(1, 64 * 32), (128, 64 * 32)).copy()
    c["vb"] = np.broadcast_to(vb.reshape(1, 64 * 32), (128, 64 * 32)).copy()
    c["oo"] = np.broadcast_to(oo.reshape(1, 64 * 32), (128, 64 * 32)).copy()
    c["onehot"] = (np.arange(32)[:, None] == (np.arange(S)[None, :] // 256)).astype(np.float32)
    return c


def build_A(nc, P, ctx, sfx, xsrc, xeng, xreads, wA, oT, C, parts=3):
    def A(name, shape, dt):
        return ctx.enter_context(nc.sbuf_tensor(name + sfx, shape, dt))
    wv = wA.rearrange("(c p) n -> p c n", p=128)

    negtri = A("negtri", [128, 128], BF16)
    negones = A("negones", [128, 128], BF16)
    identb = A("identb", [128, 128], BF16)
    onesf = A("onesf", [128, 64], F32)
    msb = A("msb", [128, 4, 512], BF16)
    mmb = A("mmb", [128, 4, 512], BF16)
    w = A("wA_sb", [128, 8, 1024], BF16)
    P.dma("pool", "c0", negtri[:], C["negtri"], writes=["negtri"])
    P.dma("pool", "c1", negones[:], C["negones"], writes=["negones"])
    P.dma("pool", "c2", identb[:], C["ident"], writes=["identb"])
    P.dma("pool", "c3", msb[:], C["msb"], writes=["msb"])
    P.dma("pool", "c4", mmb[:], C["mmb"], writes=["mmb"])
    P.dma("pool", "w", w[:, :, 0:768], wv, writes=["w"])
    P.op("dve", lambda e: e.memset(onesf[:], 1.0), writes=["onesf"])
    P.op("dve", lambda e: e.memset(w[:, :, 768:1024], 0.0), reads=["w"], writes=["w"])
    for qk in range(2):
        for hd in range(2):
            src = 384 + qk * 128 + hd * 64
            dst = 768 + qk * 128 + hd * 64
            P.op("act", lambda e, s_=src, d_=dst: e.mul(w[:, :, d_:d_ + 8], w[:, :, s_ + 8:s_ + 16], -1.0),
                 reads=["w"], writes=["w"])
            P.op("act", lambda e, s_=src, d_=dst: e.copy(w[:, :, d_ + 8:d_ + 16], w[:, :, s_:s_ + 8]),
                 reads=["w"], writes=["w"])

    psbig = ctx.enter_context(nc.psum_tensor("psbig" + sfx, [128, 4096], F32))
    ps = [psbig[:, i * 512:(i + 1) * 512] for i in range(8)]
    psn = ["ps%d" % i for i in range(7)]

    xg = [A("xg%d" % i, [128, 8, 512], BF16) for i in range(2)]
    ost = [A("ost%d" % i, [128, 512], BF16) for i in range(2)]

    def load_x(tg):
        b = tg % 2
        for pi, (c0, c1, sap) in enumerate(xsrc(tg)):
            P.dma(xeng, "xg%d_%d" % (b, pi), xg[b][:, c0:c1, :], sap, reads=xreads, writes=["xg%d" % b])

    def proj_fm(bank, col0, tg):
        b = tg % 2
        for c in range(8):
            P.op("pe", lambda e, c=c: e.matmul(ps[bank][:], w[:, c, col0:col0 + 128], xg[b][:, c, :],
                                               start=(c == 0), stop=(c == 7)),
                 reads=["w", "xg%d" % b], writes=[psn[bank]], signal=(c == 7))

    def proj_tm(bank, col0, tg, ncols=128):
        b = tg % 2
        for ts in range(4):
            for c in range(8):
                P.op("pe", lambda e, c=c, ts=ts: e.matmul(ps[bank][:, ts * 128:(ts + 1) * 128],
                                                          xg[b][:, c, ts * 128:(ts + 1) * 128],
                                                          w[:, c, col0:col0 + 128],
                                                          start=(c == 0), stop=(c == 7)),
                     reads=["w", "xg%d" % b], writes=[psn[bank]], signal=(c == 7 and ts == 3))

    R = [A("R%d" % i, [128, S], BF16) for i in range(4)]
    Vm = A("Vm", [128, 64, 2, 128], BF16)
    QT, KT = R[0], R[1]
    V = A("V", [128, 64, 128], BF16)
    if parts & 1:
        load_x(0)
    for tg in range(NG if parts & 1 else 0):
        if tg + 1 < NG:
            load_x(tg + 1)
        sl = slice(tg * 512, (tg + 1) * 512)
        proj_fm(0, 0, tg)
        P.op("act", lambda e, sl=sl: e.mul(QT[:, sl], ps[0][:], 0.125), reads=["ps0"], writes=["QT"])
        proj_fm(1, 128, tg)
        P.op("dve", lambda e, sl=sl: e.tensor_copy(KT[:, sl], ps[1][:]), reads=["ps1"], writes=["KT"])
        proj_tm(2, 256, tg)
        P.op("act", lambda e, tg=tg: e.copy(V[:, tg * 4:(tg + 1) * 4, :],
                                           ps[2][:].rearrange("p (a b) -> p a b", a=4)),
             reads=["ps2"], writes=["V"])

    Eb = [A("Eb%d" % i, [128, 512], F32) for i in range(2)]
    Lb = [A("Lb%d" % i, [128, 512], BF16) for i in range(3)]
    Ab = [A("Ab%d" % i, [128, 512], BF16) for i in range(3)]
    Lrun = [A("Lrun%d" % i, [128, 512], F32) for i in range(2)]
    Lrb = [[A("Lrb%d_%d" % (h, i), [128, 512], BF16) for i in range(2)] for h in range(2)]

    items = []
    for g in range(NG):
        njt = 4 * g + 4
        for k in range(njt):
            for h in range(2):
                items.append((h, g, k, njt - 1 - k, njt))
    NI = len(items) if parts & 1 else 0

    def sb_s1(i):
        h, g, k, j, njt = items[i]
        zb = i % 4
        hp = slice(h * 64, (h + 1) * 64)
        P.op("pe", lambda e: e.matmul(ps[zb][:], KT[hp, j * 128:(j + 1) * 128], QT[hp, g * 512:(g + 1) * 512],
                                      start=True, stop=True),
             reads=["QT", "KT"], writes=[psn[zb]])

    def sb_s2(i):
        h, g, k, j, njt = items[i]
        zb = i % 4
        eb = "Eb%d" % (i % 2)
        lb = "Lb%d" % (i % 3)
        E = Eb[i % 2]
        L = Lb[i % 3]
        P.op("act", lambda e: e.activation(E[:], ps[zb][:], AF.Exp), reads=[psn[zb]], writes=[eb])

    def sb_s2b(i):
        h, g, k, j, njt = items[i]
        zb = i % 4
        eb = "Eb%d" % (i % 2)
        lb = "Lb%d" % (i % 3)
        E = Eb[i % 2]
        L = Lb[i % 3]
        P.op("act", lambda e: e.activation(L[:], E[:], AF.Ln, bias=1.0), reads=[eb], writes=[lb])
        if k < 4:
            jj = 3 - k
            P.op("dve", lambda e: e.tensor_tensor(L[:], L[:], msb[:, jj, :], ALU.mult),
                 reads=[lb, "msb"], writes=[lb])
        if k < njt - 1:
            lr = "Lrun%d" % h
            nb = "Lrb%d_%d" % (h, (k + 1) % 2)
            dst = Lrb[h][(k + 1) % 2]
            if k == 0:
                P.op("dve", lambda e: e.tensor_copy(Lrun[h][:], L[:]), reads=[lb], writes=[lr])
                P.op("dve", lambda e: e.tensor_copy(dst[:], L[:]), reads=[lb], writes=[nb])
            else:
                P.op("dve", lambda e: e.tensor_tensor(Lrun[h][:], Lrun[h][:], L[:], ALU.add),
                     reads=[lb, lr], writes=[lr])
                P.op("dve", lambda e: e.tensor_copy(dst[:], Lrun[h][:]), reads=[lr], writes=[nb])

    def sb_s3(i):
        h, g, k, j, njt = items[i]
        zb = i % 4
        lb = "Lb%d" % (i % 3)
        L = Lb[i % 3]
        P.op("pe", lambda e: e.matmul(ps[zb][:], negtri[:], L[:], start=False, stop=(k == 0),
                                      skip_group_check=True),
             reads=["negtri", lb], writes=[psn[zb]], signal=(k == 0))
        if k > 0:
            cur = Lrb[h][k % 2]
            P.op("pe", lambda e: e.matmul(ps[zb][:], negones[:], cur[:], start=False, stop=True,
                                          skip_group_check=True),
                 reads=["negones", "Lrb%d_%d" % (h, k % 2)], writes=[psn[zb]])

    def sb_s4(i):
        h, g, k, j, njt = items[i]
        zb = i % 4
        ab = "Ab%d" % (i % 3)
        Aa = Ab[i % 3]
        P.op("act", lambda e: e.activation(Aa[:], ps[zb][:], AF.Exp), reads=[psn[zb]], writes=[ab])
        if k < 4:
            jj = 3 - k
            P.op("dve", lambda e: e.tensor_tensor(Aa[:], Aa[:], msb[:, jj, :], ALU.mult),
                 reads=[ab, "msb"], writes=[ab])

    def sb_s5(i):
        h, g, k, j, njt = items[i]
        ab = "Ab%d" % (i % 3)
        Aa = Ab[i % 3]
        ob = 4 + (2 * g + h) % 3
        P.op("pe", lambda e: e.matmul(ps[ob][0:64, :], V[:, j, h * 64:(h + 1) * 64], Aa[:],
                                      start=(k == 0), stop=(k == njt - 1), skip_group_check=True),
             reads=["V", ab], writes=[psn[ob]], signal=True)
        if k == njt - 1:
            so = "ost%d" % (g % 2)
            P.op("dve", lambda e: e.tensor_copy(ost[g % 2][h * 64:(h + 1) * 64, :], ps[ob][0:64, :]),
                 reads=[psn[ob]], writes=[so + "_%d" % h])
            if h == 1:
                P.dma("sp", so, oT(0, g), ost[g % 2][:],
                      reads=[so + "_0", so + "_1"], writes=["oT0_%d" % g])

    order = [(sb_s1, 0), (sb_s2, 1), (sb_s4, 3), (sb_s2b, 1), (sb_s3, 2), (sb_s5, 4)]
    for step in range(NI + 4):
        for fn, lag in order:
            i = step - lag
            if 0 <= i < NI:
                fn(i)

    P.barrier()
    if not parts & 2:
        return
    QTa = [R[0], R[1]]
    KTa = [R[2], R[3]]
    cosv, sinv = C["cosT"], C["sinT"]
    csb = [[A("cs%d_%d" % (a, i), [128, 512], F32) for i in range(2)] for a in range(2)]
    t1 = A("t1", [128, 512], F32)
    t2 = A("t2", [128, 512], F32)
    qf = A("qf", [128, 512], F32)
    kf = A("kf", [128, 512], F32)
    kmT = A("kmT", [128, 2, 32], F32)
    gm = [A("gm%d" % i, [128, 32], F32) for i in range(2)]
    top8 = [A("top8_%d" % i, [128, 8], F32) for i in range(2)]
    mb1 = [A("mb1_%d" % i, [128, 32], F32) for i in range(2)]
    mbpad = [A("mbpad%d" % i, [128, 128], BF16) for i in range(8)]
    cbrow = A("cbrow", [128, 64], F32)
    vbrow = A("vbrow", [128, 64], F32)
    oorow = A("oorow", [128, 64], F32)
    Pb = [A("Pb%d" % i, [128, 512], BF16) for i in range(3)]
    rec = A("rec", [128, 512], F32)
    bcs = A("bcs", [64, 512], F32)

    P.op("dve", lambda e: e.memset(cbrow[:, 0:32], 0.0), writes=["cbrow"])
    P.op("dve", lambda e: e.memset(cbrow[:, 32:64], NEG), writes=["cbrow"])
    P.op("dve", lambda e: e.memset(vbrow[:, 0:32], BIG), writes=["vbrow"])
    P.op("dve", lambda e: e.memset(vbrow[:, 32:64], 0.0), writes=["vbrow"])
    P.op("dve", lambda e: e.memset(oorow[:], -BIG), writes=["oorow"])
    P.op("dve", lambda e: e.memset(oorow[:, 31:32], 0.0), writes=["oorow"])
    P.op("dve", lambda e: e.memset(kmT[:], 0.0), writes=["kmT"])
    P.op("dve", lambda e: e.memset(Vm[:], 1.0), writes=["Vm"])
    for i in range(8):
        P.op("dve", lambda e, i=i: e.memset(mbpad[i][:], 0.0), writes=["mbpad%d" % i])
    for h in range(2):
        P.op("dve", lambda e, h=h: e.memset(QTa[h][64:128, :], 0.0), writes=["QTa%d" % h])
        P.op("dve", lambda e, h=h: e.memset(KTa[h][64:128, :], 0.0), writes=["KTa%d" % h])
        P.dma("pool", "oh%d" % h, KTa[h][64:96, :], C["onehot"], writes=["KTa%d" % h])

    def load_cs(tg):
        b = tg % 2
        P.dma("sp", "cs0_%d" % b, csb[0][b][:], cosv[:, tg * 512:(tg + 1) * 512], writes=["cs0_%d" % b])
        P.dma("sp", "cs1_%d" % b, csb[1][b][:], sinv[:, tg * 512:(tg + 1) * 512], writes=["cs1_%d" % b])

    def rope(dst, b, bank_a, bank_b):
        cn, sn = "cs0_%d" % b, "cs1_%d" % b
        P.op("dve", lambda e: e.tensor_tensor(t1[:], ps[bank_a][:], csb[0][b][:], ALU.mult),
             reads=[psn[bank_a], cn], writes=["t1"])
        P.op("dve", lambda e: e.tensor_tensor(t2[:], ps[bank_b][:], csb[1][b][:], ALU.mult),
             reads=[psn[bank_b], sn], writes=["t2"])
        nm = "qf" if dst is qf else "kf"
        P.op("dve", lambda e: e.tensor_tensor(dst[:], t1[:], t2[:], ALU.add),
             reads=["t1", "t2"], writes=[nm])

    def gate_tail(tgp):
        for idx in range(8):
            cq, h = idx // 2, idx % 2
            cch = tgp * 4 + cq
            u = idx % 2
            P.op("pe", lambda e, u=u, idx=idx: e.matmul(ps[6 + u][:, 0:128], mbpad[idx][:], identb[:],
                                                        start=True, stop=True),
                 reads=["mbpad%d" % idx, "identb"], writes=["ps6_%d" % u], signal=False)
            P.op("pe", lambda e, u=u: e.matmul(ps[6 + u][:, 256:384], identb[:], identb[:],
                                               start=True, stop=True),
                 reads=["identb"], writes=["ps6_%d" % u])
            P.op("act", lambda e, h=h, cch=cch, u=u: e.copy(QTa[h][64:96, cch * 128:(cch + 1) * 128],
                                                          ps[6 + u][64:96, 0:128]),
                 reads=["ps6_%d" % u], writes=["QTa%d" % h])

    load_x(0)
    load_cs(0)
    for tg in range(NG):
        if tg + 1 < NG:
            load_x(tg + 1)
            load_cs(tg + 1)
        b = tg % 2
        sl = slice(tg * 512, (tg + 1) * 512)
        proj_fm(0, 512, tg)
        proj_fm(1, 896, tg)
        rope(kf, b, 0, 1)
        for h in range(2):
            P.op("act", lambda e, h=h: e.copy(KTa[h][0:64, sl], kf[h * 64:(h + 1) * 64, :]),
                 reads=["kf"], writes=["KTa%d" % h])
        for h in range(2):
            hp2 = slice(h * 64, (h + 1) * 64)
            P.op("dve", lambda e, h=h, hp2=hp2: e.tensor_reduce(
                kmT[hp2, h, 2 * tg:2 * tg + 2], kf[hp2, :].rearrange("p (a b) -> p a b", a=2), AX.X, ALU.add),
                reads=["kf"], writes=["kmT"])
        proj_fm(2, 384, tg)
        proj_fm(3, 768, tg)
        rope(qf, b, 2, 3)
        for h in range(2):
            P.op("act", lambda e, h=h: e.mul(QTa[h][0:64, sl], qf[h * 64:(h + 1) * 64, :], 0.125),
                 reads=["qf"], writes=["QTa%d" % h])
        proj_tm(4, 640, tg)
        P.op("act", lambda e: e.copy(Vm[:, tg * 4:(tg + 1) * 4, :, 0:64],
                                    ps[4][:].rearrange("p (a h d) -> p a h d", a=4, h=2)),
             reads=["ps4"], writes=["Vm"])
        if tg > 0:
            gate_tail(tg - 1)
        for idx in range(8):
            cq, h = idx // 2, idx % 2
            P.op("pe", lambda e, cq=cq, h=h, idx=idx: e.matmul(
                ps[5][:, idx * 32:(idx + 1) * 32], qf[:, cq * 128:(cq + 1) * 128],
                kmT[:, h, :], start=True, stop=True),
                reads=["qf", "kmT"], writes=["ps5"], signal=False)
        P.op("pe", lambda e: e.matmul(ps[5][:, 256:384], identb[:], identb[:], start=True, stop=True),
             reads=["identb"], writes=["ps5"])
        for idx in range(8):
            cq, h = idx // 2, idx % 2
            cch = tg * 4 + cq
            own = cch // 2
            u = idx % 2
            P.op("dve", lambda e, idx=idx, own=own, u=u: e.tensor_tensor(
                gm[u][:], ps[5][:, idx * 32:(idx + 1) * 32], cbrow[:, 32 - own:64 - own], ALU.add),
                reads=["ps5", "cbrow"], writes=["gm%d" % u])
            P.op("dve", lambda e, u=u: e.max(top8[u][:], gm[u][:]), reads=["gm%d" % u], writes=["top8_%d" % u])
            P.op("dve", lambda e, own=own, u=u: e.scalar_tensor_tensor(
                mb1[u][:], gm[u][:], top8[u][:, 2:3], vbrow[:, 32 - own:64 - own], ALU.is_ge, ALU.mult),
                reads=["gm%d" % u, "top8_%d" % u, "vbrow"], writes=["mb1_%d" % u])
            P.op("dve", lambda e, own=own, u=u, idx=idx: e.tensor_tensor(
                mbpad[idx][:, 64:96], mb1[u][:], oorow[:, 31 - own:63 - own], ALU.add),
                reads=["mb1_%d" % u, "oorow"], writes=["mbpad%d" % idx])
    gate_tail(NG - 1)

    P.barrier()
    Pb2 = [A("Pb2_%d" % i, [128, 1024], BF16) for i in range(3)]
    mitems = []
    for g in range(NG):
        njt = 4 * g + 4
        for j in range(njt):
            mitems.append((g, j, njt))
    NM = len(mitems) if not (DBG_MB & 1) else 0

    def mb_s1(i):
        g, j, njt = mitems[i]
        zp = i % 2
        for h in range(2):
            P.op("pe", lambda e, h=h: e.matmul(psbig[:, zp * 1024 + h * 512:zp * 1024 + (h + 1) * 512],
                                               KTa[h][:, j * 128:(j + 1) * 128], QTa[h][:, g * 512:(g + 1) * 512],
                                               start=True, stop=True),
                 reads=["QTa%d" % h, "KTa%d" % h], writes=["zp%d" % zp], signal=(h == 1))

    def mb_s2(i):
        g, j, njt = mitems[i]
        zp = i % 2
        pb = "Pb2_%d" % (i % 3)
        Pt = Pb2[i % 3]
        P.op("act", lambda e: e.activation(Pt[:], psbig[:, zp * 1024:(zp + 1) * 1024], AF.Exp),
             reads=["zp%d" % zp], writes=[pb])
        if j >= 4 * g:
            jj = j - 4 * g
            for h in range(2):
                P.op("dve", lambda e, h=h: e.tensor_tensor(Pt[:, h * 512:(h + 1) * 512], Pt[:, h * 512:(h + 1) * 512],
                                                           mmb[:, jj, :], ALU.mult),
                     reads=[pb, "mmb"], writes=[pb])

    def mb_s3(i):
        g, j, njt = mitems[i]
        pb = "Pb2_%d" % (i % 3)
        Pt = Pb2[i % 3]
        for h in range(2):
            ob = 4 + (2 * g + h) % 3
            P.op("pe", lambda e, h=h, ob=ob: e.matmul(ps[ob][:, :], Vm[:, j, h, :], Pt[:, h * 512:(h + 1) * 512],
                                                      start=(j == 0), stop=(j == njt - 1), skip_group_check=True),
                 reads=["Vm", pb], writes=[psn[ob]], signal=True)
            if j == njt - 1:
                so = "ost%d" % (g % 2)
                P.op("dve", lambda e, ob=ob: e.reciprocal(rec[64:128, :], ps[ob][64:128, :]),
                     reads=[psn[ob]], writes=["rec"])
                P.op("dve", lambda e, h=h, ob=ob: e.tensor_tensor(ost[g % 2][h * 64:(h + 1) * 64, :], ps[ob][0:64, :],
                                                                  rec[64:128, :], ALU.mult),
                     reads=[psn[ob], "rec"], writes=[so + "_%d" % h])
                if h == 1:
                    P.dma("sp", so, oT(1, g), ost[g % 2][:],
                          reads=[so + "_0", so + "_1"], writes=["oT1_%d" % g])

    mstages = [mb_s1, mb_s2, mb_s3]
    for step in range(NM + len(mstages) - 1):
        for s_, fn in enumerate(mstages):
            i = step - s_
            if 0 <= i < NM:
                fn(i)
    P.barrier()


def build_B(nc, P, ctx, sfx, xT_d, oTs_v, oTm_v, oreads, W, lnp_d, out_d, outb_d=None):
    def A(name, shape, dt):
        return ctx.enter_context(nc.sbuf_tensor(name + sfx, shape, dt))
    xT = A("xT_sb", [128, 8, TOK], F32)
    xb = A("xb_sb", [128, 8, TOK], BF16)
    arena = A("arena", [128, 32768], BF16)
    oTs = arena[:, 0:8192].rearrange("p (a b) -> p a b", a=4)
    oTm = arena[:, 8192:16384].rearrange("p (a b) -> p a b", a=4)
    mg = arena[:, 16384:32768].rearrange("p (a b) -> p a b", a=8)
    hT = arena[:, 0:NFC * 1024].rearrange("p (a b) -> p a b", a=NFC)
    lnp = A("lnp_sb", [128, 32], F32)
    onesD = A("onesD", [128, 128], F32)
    warena = A("warena", [128, 9728], BF16)
    wgb = [warena[:, i * 2048:(i + 1) * 2048].rearrange("p (a b) -> p a b", a=8) for i in range(2)]
    wbb = [warena[:, 4096 + i * 1024:4096 + (i + 1) * 1024].rearrange("p (a b) -> p a b", a=4) for i in range(2)]
    wob = [warena[:, 6144 + i * 1024:6144 + (i + 1) * 1024].rearrange("p (a b) -> p a b", a=8) for i in range(2)]
    wfgu = [warena[:, i * 2048:(i + 1) * 2048].rearrange("p (a b) -> p a b", a=8) for i in range(2)]
    wfdb = [warena[:, 4096 + i * 2816:4096 + (i + 1) * 2816].rearrange("p (a b) -> p a b", a=NFC) for i in range(2)]
    sg = [A("sg%d" % i, [128, 512], F32) for i in range(4)]
    m1 = [A("m1_%d" % i, [128, 512], F32) for i in range(2)]
    m2 = [A("m2_%d" % i, [128, 512], F32) for i in range(2)]
    ysq = [A("ysq%d" % i, [128, 512], F32) for i in range(2)]
    mean_sb = A("mean_sb", [128, 512], F32)
    rstd_sb = A("rstd_sb", [128, 512], F32)
    tn = [A("tn%d" % i, [128, 512], F32) for i in range(2)]
    ps = [ctx.enter_context(nc.psum_tensor("pb%d" % i + sfx, [128, 512], F32)) for i in range(8)]
    psn = ["pb%d" % i for i in range(8)]
    bank = [0]
    XN = [["xT%d_%d" % (c, g) for g in range(NTG)] for c in range(8)]
    XB = [["xb%d_%d" % (c, g) for g in range(NTG)] for c in range(8)]
    MG = [["mg%d_%d" % (c, g) for g in range(NTG)] for c in range(8)]
    HT = [["hT%d_%d" % (k, g) for g in range(2)] for k in range(NFC)]
    allx = [n for r_ in XN for n in r_]
    allxb = [n for r_ in XB for n in r_]

    def nb():
        bank[0] = (bank[0] + 1) % 8
        return bank[0]

    xdv = xT_d.rearrange("(c p) t -> p c t", p=128)
    P.dma("sp", "xT", xT[:], xdv, writes=allx)
    P.dma("pool", "xb", xb[:], xdv, writes=allxb)
    P.dma("sp", "oTs", oTs, oTs_v, reads=oreads, writes=["oTs"])
    P.dma("sp", "oTm", oTm, oTm_v, reads=oreads, writes=["oTm"])
    P.dma("sp", "lnp", lnp[:], lnp_d, writes=["lnp"])
    P.op("dve", lambda e: e.memset(onesD[:], 1.0 / D), writes=["onesD"])

    wGv = W["wG"].rearrange("(k p) n -> p k n", p=128)
    wbsv = W["wbs"].rearrange("(k p) n -> p k n", p=128)
    wbmv = W["wbm"].rearrange("(k p) n -> p k n", p=128)
    wov = W["wout"].rearrange("(k p) n -> p k n", p=128)
    wfgv = W["wfg"].rearrange("(k p) n -> p k n", p=128)
    wfuv = W["wfu"].rearrange("(k p) n -> p k n", p=128)
    wfdv = W["wfd"].rearrange("(k p) n -> p k n", p=128)

    def tsl(tg):
        return slice(tg * 512, (tg + 1) * 512)

    def load_b1(c):
        b = c % 2
        cs = slice(c * 128, (c + 1) * 128)
        P.dma("pool", "wgb%da" % b, wgb[b][:, :, 0:128], wGv[:, :, cs], writes=["wgb%da" % b])
        P.dma("pool", "wgb%db" % b, wgb[b][:, :, 128:256], wGv[:, :, D + c * 128:D + (c + 1) * 128],
              writes=["wgb%db" % b])
        P.dma("pool", "wbb%da" % b, wbb[b][:, :, 0:128], wbsv[:, :, cs], writes=["wbb%da" % b])
        P.dma("pool", "wbb%db" % b, wbb[b][:, :, 128:256], wbmv[:, :, cs], writes=["wbb%db" % b])

    load_b1(0)
    it = 0
    for c in range(8):
        if c + 1 < 8:
            load_b1(c + 1)
        b = c % 2
        for tg in range(NTG):
            t = tsl(tg)
            bk = [nb() for _ in range(4)]
            for half, (bkk, wn) in enumerate(zip(bk[0:2], ["wgb%da" % b, "wgb%db" % b])):
                for k in range(8):
                    P.op("pe", lambda e, k=k, half=half, bkk=bkk: e.matmul(
                        ps[bkk][:], wgb[b][:, k, half * 128:(half + 1) * 128], xb[:, k, t],
                        start=(k == 0), stop=(k == 7)),
                        reads=[wn, XB[k][tg]], writes=[psn[bkk]], signal=(k == 7))
            for half, (bkk, wn, src, sn) in enumerate(zip(bk[2:4], ["wbb%da" % b, "wbb%db" % b],
                                                          [oTs, oTm], ["oTs", "oTm"])):
                for k in range(4):
                    P.op("pe", lambda e, k=k, half=half, bkk=bkk, src=src: e.matmul(
                        ps[bkk][:], wbb[b][:, k, half * 128:(half + 1) * 128], src[:, k, t],
                        start=(k == 0), stop=(k == 3)),
                        reads=[wn, sn], writes=[psn[bkk]], signal=(k == 3))
            u = it % 2
            s0, s1 = sg[2 * u], sg[2 * u + 1]
            P.op("act", lambda e, s0=s0, bkk=bk[0]: e.activation(s0[:], ps[bkk][:], AF.Sigmoid),
                 reads=[psn[bk[0]]], writes=["sg%d" % (2 * u)])
            P.op("act", lambda e, s1=s1, bkk=bk[1]: e.activation(s1[:], ps[bkk][:], AF.Sigmoid),
                 reads=[psn[bk[1]]], writes=["sg%d" % (2 * u + 1)])
            P.op("dve", lambda e, s0=s0, u=u, bkk=bk[2]: e.tensor_tensor(m1[u][:], s0[:], ps[bkk][:], ALU.mult),
                 reads=[psn[bk[2]], "sg%d" % (2 * u)], writes=["m1_%d" % u])
            P.op("dve", lambda e, s1=s1, u=u, bkk=bk[3]: e.tensor_tensor(m2[u][:], s1[:], ps[bkk][:], ALU.mult),
                 reads=[psn[bk[3]], "sg%d" % (2 * u + 1)], writes=["m2_%d" % u])
            P.op("pool", lambda e, u=u, c=c, t=t: e.tensor_tensor(mg[:, c, t], m1[u][:], m2[u][:], ALU.add),
                 reads=["m1_%d" % u, "m2_%d" % u], writes=[MG[c][tg]])
            it += 1

    def layer_norm(tg, gcol, bcol):
        t = tsl(tg)
        ba, bb_ = nb(), nb()
        for c in range(8):
            q = ysq[c % 2]
            P.op("act", lambda e, q=q, c=c: e.activation(q[:], xT[:, c, t], AF.Square),
                 reads=[XN[c][tg]], writes=["ysq%d" % (c % 2)])
            P.op("pe", lambda e, c=c: e.matmul(ps[ba][:], onesD[:], xT[:, c, t], start=(c == 0), stop=(c == 7)),
                 reads=["onesD", XN[c][tg]], writes=[psn[ba]], signal=(c == 7))
            P.op("pe", lambda e, q=q, c=c: e.matmul(ps[bb_][:], onesD[:], q[:], start=(c == 0), stop=(c == 7),
                                                   skip_group_check=True),
                 reads=["onesD", "ysq%d" % (c % 2)], writes=[psn[bb_]], signal=True)
        P.op("dve", lambda e: e.tensor_copy(mean_sb[:], ps[ba][:]), reads=[psn[ba]], writes=["mean_sb"])
        P.op("dve", lambda e: e.tensor_tensor(rstd_sb[:], mean_sb[:], mean_sb[:], ALU.mult),
             reads=["mean_sb"], writes=["rstd_sb"])
        P.op("dve", lambda e: e.tensor_tensor(rstd_sb[:], ps[bb_][:], rstd_sb[:], ALU.subtract),
             reads=[psn[bb_], "rstd_sb"], writes=["rstd_sb"])
        P.op("act", lambda e: e.activation(rstd_sb[:], rstd_sb[:], AF.Ln, bias=EPS),
             reads=["rstd_sb"], writes=["rstd_sb"])
        P.op("act", lambda e: e.activation(rstd_sb[:], rstd_sb[:], AF.Exp, scale=-0.5),
             reads=["rstd_sb"], writes=["rstd_sb"])
        for c in range(8):
            tt = tn[c % 2]
            tnn = "tn%d" % (c % 2)
            P.op("dve", lambda e, tt=tt, c=c: e.tensor_tensor(tt[:], xT[:, c, t], mean_sb[:], ALU.subtract),
                 reads=[XN[c][tg], "mean_sb"], writes=[tnn])
            P.op("dve", lambda e, tt=tt: e.tensor_tensor(tt[:], tt[:], rstd_sb[:], ALU.mult),
                 reads=[tnn, "rstd_sb"], writes=[tnn])
            P.op("act", lambda e, tt=tt, c=c: e.activation(xT[:, c, t], tt[:], AF.Identity,
                                                          bias=lnp[:, bcol + c:bcol + c + 1],
                                                          scale=lnp[:, gcol + c:gcol + c + 1]),
                 reads=[tnn, "lnp"], writes=[XN[c][tg]])
            P.op("pool", lambda e, c=c: e.tensor_copy(xb[:, c, t], xT[:, c, t]), reads=[XN[c][tg]], writes=[XB[c][tg]])

    def load_wo(c):
        b = c % 2
        P.dma("pool", "wob%d" % b, wob[b][:], wov[:, :, c * 128:(c + 1) * 128], writes=["wob%d" % b])

    load_wo(0)
    for c in range(8):
        if c + 1 < 8:
            load_wo(c + 1)
        b = c % 2
        for tg in range(NTG):
            t = tsl(tg)
            bkk = nb()
            for k in range(8):
                P.op("pe", lambda e, k=k, bkk=bkk: e.matmul(ps[bkk][:], wob[b][:, k, :], mg[:, k, t],
                                                            start=(k == 0), stop=(k == 7)),
                     reads=["wob%d" % b, MG[k][tg]], writes=[psn[bkk]], signal=(k == 7))
            P.op("dve", lambda e, bkk=bkk, c=c, t=t: e.scalar_tensor_tensor(
                xT[:, c, t], xT[:, c, t], ALPHA, ps[bkk][:], ALU.mult, ALU.add),
                reads=[psn[bkk], XN[c][tg]], writes=[XN[c][tg]])
    for tg in range(NTG):
        layer_norm(tg, 0, 8)

    def load_gu(i):
        k = i % NFC
        b = i % 2
        P.dma("pool", "wfgu%da" % b, wfgu[b][:, :, 0:128], wfgv[:, :, k * 128:(k + 1) * 128],
              writes=["wfgu%da" % b])
        P.dma("pool", "wfgu%db" % b, wfgu[b][:, :, 128:256], wfuv[:, :, k * 128:(k + 1) * 128],
              writes=["wfgu%db" % b])

    def load_dn(i):
        c = i % 8
        b = i % 2
        P.dma("pool", "wfdb%d" % b, wfdb[b][:], wfdv[:, :, c * 128:(c + 1) * 128], writes=["wfdb%d" % b])

    gi = 0
    di = 0
    P.barrier()
    for hf in range(2):
        load_gu(gi)
        for k in range(NFC):
            if k + 1 < NFC:
                load_gu(gi + 1)
            b = gi % 2
            for t2_ in range(2):
                tg = hf * 2 + t2_
                t = tsl(tg)
                b0, b1 = nb(), nb()
                for half, bkk in enumerate([b0, b1]):
                    wn = "wfgu%d%s" % (b, "ab"[half])
                    for kk in range(8):
                        P.op("pe", lambda e, kk=kk, half=half, bkk=bkk: e.matmul(
                            ps[bkk][:], wfgu[b][:, kk, half * 128:(half + 1) * 128], xb[:, kk, t],
                            start=(kk == 0), stop=(kk == 7)),
                            reads=[wn, XB[kk][tg]], writes=[psn[bkk]], signal=(kk == 7))
                u = it % 2
                s0 = sg[2 * u]
                P.op("act", lambda e, s0=s0, b0=b0: e.activation(s0[:], ps[b0][:], AF.Silu),
                     reads=[psn[b0]], writes=["sg%d" % (2 * u)])
                P.op("dve", lambda e, s0=s0, b1=b1, k=k, t2_=t2_: e.tensor_tensor(
                    hT[:, k, t2_ * 512:(t2_ + 1) * 512], s0[:], ps[b1][:], ALU.mult),
                    reads=[psn[b1], "sg%d" % (2 * u)], writes=[HT[k][t2_]])
                it += 1
            gi += 1
        load_dn(di)
        for c in range(8):
            if c + 1 < 8:
                load_dn(di + 1)
            b = di % 2
            for t2_ in range(2):
                tg = hf * 2 + t2_
                t = tsl(tg)
                bkk = nb()
                for k in range(NFC):
                    P.op("pe", lambda e, k=k, bkk=bkk, t2_=t2_: e.matmul(
                        ps[bkk][:], wfdb[b][:, k, :], hT[:, k, t2_ * 512:(t2_ + 1) * 512],
                        start=(k == 0), stop=(k == NFC - 1)),
                        reads=["wfdb%d" % b, HT[k][t2_]], writes=[psn[bkk]], signal=(k == NFC - 1))
                P.op("dve", lambda e, bkk=bkk, c=c, t=t: e.scalar_tensor_tensor(
                    xT[:, c, t], xT[:, c, t], ALPHA, ps[bkk][:], ALU.mult, ALU.add),
                    reads=[psn[bkk], XN[c][tg]], writes=[XN[c][tg]])
            di += 1
        for t2_ in range(2):
            layer_norm(hf * 2 + t2_, 16, 24)
    P.dma("sp", "out", out_d.rearrange("(c p) t -> p c t", p=128), xT[:], reads=allx, writes=["out"])
    if outb_d is not None:
        P.dma("sp", "outb", outb_d.rearrange("(c p) t -> p c t", p=128), xb[:], reads=allxb, writes=["outb"])


W_KEYS = ["wG", "wbs", "wbm", "wout", "wfg", "wfu", "wfd"]
W_SHAPES = {"wG": [D, 2048], "wbs": [512, D], "wbm": [512, D], "wout": [D, D],
            "wfg": [D, DFF], "wfu": [D, DFF], "wfd": [DFF, D]}
RG = [[0, 1, 2, 3], [4, 5, 6, 7]]


def _pp(v):
    return np.ascontiguousarray(v.reshape(8, 128).T)


def build_fused(nlayers=DEPTH, skipA=False, skipB=False, samex=False, parts=3):
    nc = bass.Bass("TRN2", target_bir_lowering=False)
    P = Prog(nc)
    hc = host_consts()
    C = {k: nc.dram_tensor("c_" + k, list(v.shape), F32, kind="ExternalInput").ap() for k, v in hc.items()}
    xTall0 = nc.dram_tensor("xTall0", [D, S], F32, kind="ExternalInput").ap()
    xT0 = nc.dram_tensor("xT0", [D, TOK], F32, kind="ExternalInput").ap()
    wA = nc.dram_tensor("wA", [DEPTH, D, 768], F32, kind="ExternalInput").ap()
    Wd = {k: nc.dram_tensor(k, [DEPTH] + W_SHAPES[k], F32, kind="ExternalInput").ap() for k in W_KEYS}
    lnp = nc.dram_tensor("lnp", [DEPTH, 128, 32], F32, kind="ExternalInput").ap()
    out = nc.dram_tensor("out", [D, TOK], F32, kind="ExternalOutput").ap()
    oT_loc = nc.dram_tensor("oT_loc", [1024, TOK], BF16, kind="Internal").ap()
    oT_all = nc.dram_tensor("oT_all", [4096, TOK], BF16, kind="Internal").ap()
    xb_loc = nc.dram_tensor("xb_loc", [D, TOK], BF16, kind="Internal").ap()
    xb_all = nc.dram_tensor("xb_all", [4 * D, TOK], BF16, kind="Internal").ap()
    xres = nc.dram_tensor("xres", [D, TOK], F32, kind="Internal").ap()
    pid = nc.sync.partition_id()
    roff = (pid % 4) * 1024
    olv = oT_loc.rearrange("(q b p) t -> b p q t", q=4, b=2)

    def odst(br, g):
        return olv[br, :, g // 4, (g % 4) * 512:(g % 4 + 1) * 512]

    for l in range(nlayers):
        last = l == nlayers - 1
        with ExitStack() as ctx:
            if l == 0 or samex:
                xv0 = xTall0.rearrange("(c p) t -> p c t", p=128)
                xsrc = lambda tg: [(0, 8, xv0[:, :, tg * 512:(tg + 1) * 512])]
                xeng, xreads = "pool", []
            else:
                xv1 = xb_all.rearrange("(j r c p) t -> p r j c t", j=4, r=4, c=2, p=128)
                xsrc = lambda tg: [(2 * j, 2 * j + 2, xv1[:, tg // 4, j, :, (tg % 4) * 512:(tg % 4 + 1) * 512])
                                   for j in range(4)]
                xeng, xreads = "sp", ["xb_all%d" % j for j in range(4)]
            if not skipA:
                build_A(nc, P, ctx, "_a%d" % l, xsrc, xeng, xreads, wA[l], odst, C, parts)
        P.barrier()
        for q in range(4):
            P.collective("oT%d" % q, "AllGather", RG, oT_loc[q * 256:(q + 1) * 256, :],
                         oT_all[q * 1024:(q + 1) * 1024, :], writes=["oT_all%d" % q])
        with ExitStack() as ctx:
            ov = oT_all[bass.ds(roff, 1024), :].rearrange("(k b p) t -> p k b t", k=4, b=2, p=128)
            oTs_v = ov[:, :, 0, :]
            oTm_v = ov[:, :, 1, :]
            Wl = {k: Wd[k][l] for k in W_KEYS}
            if not skipB:
              build_B(nc, P, ctx, "_b%d" % l, xT0 if l == 0 else xres, oTs_v, oTm_v,
                    ["oT_all%d" % q for q in range(4)], Wl, lnp[l],
                    out if last else xres, None if last else xb_loc)
        P.barrier()
        if not last:
            for j in range(4):
                P.collective("xb%d" % j, "AllGather", RG, xb_loc[j * 256:(j + 1) * 256, :],
                             xb_all[j * 1024:(j + 1) * 1024, :], writes=["xb_all%d" % j])
    P.finish("sp")
    return nc, hc


def _wA_slices(w_in_l, r):
    def cols(base):
        return w_in_l[:, base + 128 * r: base + 128 * r + 128]
    return np.concatenate([cols(0), cols(512), cols(1024), cols(1536), cols(2048), cols(2560)], axis=1)


def kernel(x, w_in, w_branch_sb, w_branch_moba, w_out, ln_mix_g, ln_mix_b,
           w_ffn_gate, w_ffn_up, w_ffn_down, ln_ffn_g, ln_ffn_b):
    f = lambda a: np.asarray(a, dtype=np.float32)
    x, w_in, w_branch_sb, w_branch_moba, w_out = f(x), f(w_in), f(w_branch_sb), f(w_branch_moba), f(w_out)
    ln_mix_g, ln_mix_b, ln_ffn_g, ln_ffn_b = f(ln_mix_g), f(ln_mix_b), f(ln_ffn_g), f(ln_ffn_b)
    w_ffn_gate, w_ffn_up, w_ffn_down = f(w_ffn_gate), f(w_ffn_up), f(w_ffn_down)
    nc, hc = build_fused()
    cores = list(range(8))
    xTb = [np.ascontiguousarray(x[b].T) for b in range(B)]
    Wfull = {"wG": np.ascontiguousarray(w_in[:, :, 3072:5120]), "wbs": w_branch_sb, "wbm": w_branch_moba,
             "wout": w_out, "wfg": w_ffn_gate, "wfu": w_ffn_up, "wfd": w_ffn_down}
    lnp = np.ascontiguousarray(np.stack([np.concatenate(
        [_pp(ln_mix_g[l]), _pp(ln_mix_b[l]), _pp(ln_ffn_g[l]), _pp(ln_ffn_b[l])], axis=1) for l in range(DEPTH)]))
    maps = []
    for c in cores:
        b, r = c // 4, c % 4
        m = {"xTall0": xTb[b], "xT0": np.ascontiguousarray(xTb[b][:, r * TOK:(r + 1) * TOK]),
             "wA": np.ascontiguousarray(np.stack([_wA_slices(w_in[l], r) for l in range(DEPTH)])),
             "lnp": lnp}
        m.update({k: np.ascontiguousarray(v) for k, v in Wfull.items()})
        m.update({"c_" + k: v for k, v in hc.items()})
        maps.append(m)
    res = run_bass_kernel_spmd(nc, maps, core_ids=cores)
    xn = np.empty_like(x)
    for c in cores:
        b, r = c // 4, c % 4
        xn[b, r * TOK:(r + 1) * TOK, :] = np.asarray(res.results[c]["out"]).T
    return xn
```

```python
from contextlib import ExitStack
import numpy as np
import concourse.bass as bass
import concourse.mybir as mybir
from concourse.bass_utils import run_bass_kernel_spmd

F32 = mybir.dt.float32
BF16 = mybir.dt.bfloat16
AF = mybir.ActivationFunctionType
ALU = mybir.AluOpType
AX = mybir.AxisListType

D = 1024
S = 8192
B = 2
DEPTH = 2
DFF = 2816
NFC = DFF // 128
NG = S // 512
ALPHA = (2 * DEPTH) ** 0.25
EPS = 1e-5
BIG = 30000.0
NEG = -1.0e30
TOK = 2048
NTG = TOK // 512


class Prog:
    SAME_ENGINE_SYNC = True
    EMBED = ("act", "dve")

    def __init__(self, nc):
        self.nc = nc
        self.eng = {"pe": nc.tensor, "act": nc.scalar, "dve": nc.vector,
                    "pool": nc.gpsimd, "sp": nc.sync}
        self.sem = {}
        self.cnt = {}
        self.res = {}
        self.known = {e: {} for e in self.eng}
        self.nwait = 0
        self.nops = 0
        self._pending = None

    def _sem(self, lane):
        if lane not in self.sem:
            self.sem[lane] = self.nc.alloc_semaphore(name="s_" + lane.replace(":", "_"))
            self.cnt[lane] = 0
        return self.sem[lane]

    def _need(self, e, lane, val):
        if lane == e and (e == "pe" or not self.SAME_ENGINE_SYNC):
            return
        if self.known[e].get(lane, 0) >= val:
            return
        self.known[e][lane] = val
        self.nwait += 1
        if self._pending is not None:
            self._pending.append((lane, val))
            return
        self.eng[e].wait_ge(self._sem(lane), val)

    def _deps(self, e, reads, writes):
        for r in reads:
            ent = self.res.get(r)
            if ent and ent[0]:
                self._need(e, *ent[0])
        for w in writes:
            ent = self.res.get(w)
            if ent:
                if ent[0]:
                    self._need(e, *ent[0])
                for lane, val in ent[1].items():
                    self._need(e, lane, val)

    def _record(self, lane, val, reads, writes):
        for r in reads:
            ent = self.res.setdefault(r, [None, {}])
            ent[1][lane] = max(ent[1].get(lane, 0), val)
        for w in writes:
            self.res[w] = [(lane, val), {}]

    def op(self, e, fn, reads=(), writes=(), signal=True):
        self._sem(e)
        self._pending = [] if e in self.EMBED else None
        self._deps(e, reads, writes)
        pend = self._pending or []
        self._pending = None
        for lane_, val_ in pend[:-1]:
            self.eng[e].wait_ge(self._sem(lane_), val_)
        ins = fn(self.eng[e])
        if pend:
            ins._wait_ge(self._sem(pend[-1][0]), pend[-1][1])
        val = self.cnt[e] + 1
        if signal:
            ins.then_inc(self.sem[e], 1)
            self.cnt[e] = val
        self._record(e, val, reads, writes)
        self.nops += 1
        return ins

    def dma(self, e, lane, out, in_, reads=(), writes=(), **kw):
        lane = "d:" + lane
        self._sem(lane)
        self._deps(e, reads, writes)
        ins = self.eng[e].dma_start(out=out, in_=in_, **kw)
        val = self.cnt[lane] + 16
        ins.then_inc(self.sem[lane], 16)
        self.cnt[lane] = val
        self._record(lane, val, reads, writes)
        self.nops += 1
        return ins

    def collective(self, lane, kind, rg, in_ap, out_ap, reads=(), writes=()):
        lane = "c:" + lane
        self._sem(lane)
        self._deps("pool", reads, writes)
        ins = self.nc.gpsimd.collective_compute(kind, ALU.bypass, replica_groups=rg,
                                                ins=[in_ap], outs=[out_ap])
        val = self.cnt[lane] + 1
        ins.then_inc(self.sem[lane], 1)
        self.cnt[lane] = val
        self._record(lane, val, reads, writes)
        self.nops += 1
        return ins

    def barrier(self):
        for e in self.eng:
            for lane, val in self.cnt.items():
                if val > 0 and lane != e:
                    if self.known[e].get(lane, 0) < val:
                        self.eng[e].wait_ge(self.sem[lane], val)
                        self.known[e][lane] = val
        self.res = {}

    def finish(self, e="sp"):
        for lane, val in self.cnt.items():
            if val > 0 and lane != e:
                self.eng[e].wait_ge(self.sem[lane], val)


def host_consts():
    c = {}
    k = np.arange(128)[:, None]
    s = np.arange(128)[None, :]
    c["negtri"] = np.where(k >= s, -1.0, 0.0).astype(np.float32)
    c["negones"] = np.full((128, 128), -1.0, np.float32)
    c["ident"] = np.eye(128, dtype=np.float32)
    sk = np.arange(128)[:, None, None] + 128 * np.arange(4)[None, :, None]
    t = np.arange(512)[None, None, :]
    c["msb"] = (sk < t).astype(np.float32)
    c["mmb"] = (sk <= t).astype(np.float32)
    half = 8
    inv = (500000.0 ** (-np.arange(0, 16, 2, dtype=np.float32) / 16)).astype(np.float32)
    ang = (np.arange(S, dtype=np.float32)[:, None] * inv[None, :]).astype(np.float32)
    cos = np.cos(ang).astype(np.float32).T
    sin = np.sin(ang).astype(np.float32).T
    cosT = np.ones((64, S), np.float32)
    sinT = np.zeros((64, S), np.float32)
    cosT[0:8] = cos
    cosT[8:16] = cos
    sinT[0:8] = sin
    sinT[8:16] = sin
    c["cosT"] = np.concatenate([cosT, cosT], 0)
    c["sinT"] = np.concatenate([sinT, sinT], 0)
    n = np.arange(32)[None, :]
    own = (np.arange(64) // 2)[:, None]
    cb = np.where(n < own, 0.0, NEG).astype(np.float32)
    vb = np.where(n < own, BIG, 0.0).astype(np.float32)
    oo = np.where(n == own, 0.0, -BIG).astype(np.float32)
    c["cb"] = np.broadcast_to(cb.reshape(1, 64 * 32), (128, 64 * 32)).copy()
    c["vb"] = np.broadcast_to(vb.reshape(1, 64 * 32), (128, 64 * 32)).copy()
    c["oo"] = np.broadcast_to(oo.reshape(1, 64 * 32), (128, 64 * 32)).copy()
    c["onehot"] = (np.arange(32)[:, None] == (np.arange(S)[None, :] // 256)).astype(np.float32)
    return c


def build_A(nc, P, ctx, sfx, xsrc, xeng, xreads, wA, oT, C, parts=3):
    def A(name, shape, dt):
        return ctx.enter_context(nc.sbuf_tensor(name + sfx, shape, dt))
    wv = wA.rearrange("(c p) n -> p c n", p=128)

    negtri = A("negtri", [128, 128], BF16)
    negones = A("negones", [128, 128], BF16)
    identb = A("identb", [128, 128], BF16)
    onesf = A("onesf", [128, 64], F32)
    msb = A("msb", [128, 4, 512], BF16)
    mmb = A("mmb", [128, 4, 512], BF16)
    w = A("wA_sb", [128, 8, 1024], BF16)
    P.dma("pool", "c0", negtri[:], C["negtri"], writes=["negtri"])
    P.dma("pool", "c1", negones[:], C["negones"], writes=["negones"])
    P.dma("pool", "c2", identb[:], C["ident"], writes=["identb"])
    P.dma("pool", "c3", msb[:], C["msb"], writes=["msb"])
    P.dma("pool", "c4", mmb[:], C["mmb"], writes=["mmb"])
    P.dma("pool", "w", w[:, :, 0:768], wv, writes=["w"])
    P.op("dve", lambda e: e.memset(onesf[:], 1.0), writes=["onesf"])
    P.op("dve", lambda e: e.memset(w[:, :, 768:1024], 0.0), reads=["w"], writes=["w"])
    for qk in range(2):
        for hd in range(2):
            src = 384 + qk * 128 + hd * 64
            dst = 768 + qk * 128 + hd * 64
            P.op("act", lambda e, s_=src, d_=dst: e.mul(w[:, :, d_:d_ + 8], w[:, :, s_ + 8:s_ + 16], -1.0),
                 reads=["w"], writes=["w"])
            P.op("act", lambda e, s_=src, d_=dst: e.copy(w[:, :, d_ + 8:d_ + 16], w[:, :, s_:s_ + 8]),
                 reads=["w"], writes=["w"])

    psbig = ctx.enter_context(nc.psum_tensor("psbig" + sfx, [128, 4096], F32))
    ps = [psbig[:, i * 512:(i + 1) * 512] for i in range(8)]
    psn = ["ps%d" % i for i in range(7)]

    xg = [A("xg%d" % i, [128, 8, 512], BF16) for i in range(2)]
    ost = [A("ost%d" % i, [128, 512], BF16) for i in range(2)]

    def load_x(tg):
        b = tg % 2
        for pi, (c0, c1, sap) in enumerate(xsrc(tg)):
            P.dma(xeng, "xg%d_%d" % (b, pi), xg[b][:, c0:c1, :], sap, reads=xreads, writes=["xg%d" % b])

    def proj_fm(bank, col0, tg):
        b = tg % 2
        for c in range(8):
            P.op("pe", lambda e, c=c: e.matmul(ps[bank][:], w[:, c, col0:col0 + 128], xg[b][:, c, :],
                                               start=(c == 0), stop=(c == 7)),
                 reads=["w", "xg%d" % b], writes=[psn[bank]], signal=(c == 7))

    def proj_tm(bank, col0, tg, ncols=128):
        b = tg % 2
        for ts in range(4):
            for c in range(8):
                P.op("pe", lambda e, c=c, ts=ts: e.matmul(ps[bank][:, ts * 128:(ts + 1) * 128],
                                                          xg[b][:, c, ts * 128:(ts + 1) * 128],
                                                          w[:, c, col0:col0 + 128],
                                                          start=(c == 0), stop=(c == 7)),
                     reads=["w", "xg%d" % b], writes=[psn[bank]], signal=(c == 7 and ts == 3))

    R = [A("R%d" % i, [128, S], BF16) for i in range(4)]
    Vm = A("Vm", [128, 64, 2, 128], BF16)
    QT, KT = R[0], R[1]
    V = A("V", [128, 64, 128], BF16)
    if parts & 1:
        load_x(0)
    for tg in range(NG if parts & 1 else 0):
        if tg + 1 < NG:
            load_x(tg + 1)
        sl = slice(tg * 512, (tg + 1) * 512)
        proj_fm(0, 0, tg)
        P.op("act", lambda e, sl=sl: e.mul(QT[:, sl], ps[0][:], 0.125), reads=["ps0"], writes=["QT"])
        proj_fm(1, 128, tg)
        P.op("dve", lambda e, sl=sl: e.tensor_copy(KT[:, sl], ps[1][:]), reads=["ps1"], writes=["KT"])
        proj_tm(2, 256, tg)
        P.op("act", lambda e, tg=tg: e.copy(V[:, tg * 4:(tg + 1) * 4, :],
                                           ps[2][:].rearrange("p (a b) -> p a b", a=4)),
             reads=["ps2"], writes=["V"])

    Eb = [A("Eb%d" % i, [128, 512], F32) for i in range(2)]
    Lb = [A("Lb%d" % i, [128, 512], BF16) for i in range(3)]
    Ab = [A("Ab%d" % i, [128, 512], BF16) for i in range(3)]
    Lrun = [A("Lrun%d" % i, [128, 512], F32) for i in range(2)]
    Lrb = [[A("Lrb%d_%d" % (h, i), [128, 512], BF16) for i in range(2)] for h in range(2)]

    items = []
    for g in range(NG):
        njt = 4 * g + 4
        for k in range(njt):
            for h in range(2):
                items.append((h, g, k, njt - 1 - k, njt))
    NI = len(items) if parts & 1 else 0

    def sb_s1(i):
        h, g, k, j, njt = items[i]
        zb = i % 4
        hp = slice(h * 64, (h + 1) * 64)
        P.op("pe", lambda e: e.matmul(ps[zb][:], KT[hp, j * 128:(j + 1) * 128], QT[hp, g * 512:(g + 1) * 512],
                                      start=True, stop=True),
             reads=["QT", "KT"], writes=[psn[zb]])

    def sb_s2(i):
        h, g, k, j, njt = items[i]
        zb = i % 4
        eb = "Eb%d" % (i % 2)
        lb = "Lb%d" % (i % 3)
        E = Eb[i % 2]
        L = Lb[i % 3]
        P.op("act", lambda e: e.activation(E[:], ps[zb][:], AF.Exp), reads=[psn[zb]], writes=[eb])

    def sb_s2b(i):
        h, g, k, j, njt = items[i]
        zb = i % 4
        eb = "Eb%d" % (i % 2)
        lb = "Lb%d" % (i % 3)
        E = Eb[i % 2]
        L = Lb[i % 3]
        P.op("act", lambda e: e.activation(L[:], E[:], AF.Ln, bias=1.0), reads=[eb], writes=[lb])
        if k < 4:
            jj = 3 - k
            P.op("dve", lambda e: e.tensor_tensor(L[:], L[:], msb[:, jj, :], ALU.mult),
                 reads=[lb, "msb"], writes=[lb])
        if k < njt - 1:
            lr = "Lrun%d" % h
            nb = "Lrb%d_%d" % (h, (k + 1) % 2)
            dst = Lrb[h][(k + 1) % 2]
            if k == 0:
                P.op("dve", lambda e: e.tensor_copy(Lrun[h][:], L[:]), reads=[lb], writes=[lr])
                P.op("dve", lambda e: e.tensor_copy(dst[:], L[:]), reads=[lb], writes=[nb])
            else:
                P.op("dve", lambda e: e.tensor_tensor(Lrun[h][:], Lrun[h][:], L[:], ALU.add),
                     reads=[lb, lr], writes=[lr])
                P.op("dve", lambda e: e.tensor_copy(dst[:], Lrun[h][:]), reads=[lr], writes=[nb])

    def sb_s3(i):
        h, g, k, j, njt = items[i]
        zb = i % 4
        lb = "Lb%d" % (i % 3)
        L = Lb[i % 3]
        P.op("pe", lambda e: e.matmul(ps[zb][:], negtri[:], L[:], start=False, stop=(k == 0),
                                      skip_group_check=True),
             reads=["negtri", lb], writes=[psn[zb]], signal=(k == 0))
        if k > 0:
            cur = Lrb[h][k % 2]
            P.op("pe", lambda e: e.matmul(ps[zb][:], negones[:], cur[:], start=False, stop=True,
                                          skip_group_check=True),
                 reads=["negones", "Lrb%d_%d" % (h, k % 2)], writes=[psn[zb]])

    def sb_s4(i):
        h, g, k, j, njt = items[i]
        zb = i % 4
        ab = "Ab%d" % (i % 3)
        Aa = Ab[i % 3]
        P.op("act", lambda e: e.activation(Aa[:], ps[zb][:], AF.Exp), reads=[psn[zb]], writes=[ab])
        if k < 4:
            jj = 3 - k
            P.op("dve", lambda e: e.tensor_tensor(Aa[:], Aa[:], msb[:, jj, :], ALU.mult),
                 reads=[ab, "msb"], writes=[ab])

    def sb_s5(i):
        h, g, k, j, njt = items[i]
        ab = "Ab%d" % (i % 3)
        Aa = Ab[i % 3]
        ob = 4 + (2 * g + h) % 3
        P.op("pe", lambda e: e.matmul(ps[ob][0:64, :], V[:, j, h * 64:(h + 1) * 64], Aa[:],
                                      start=(k == 0), stop=(k == njt - 1), skip_group_check=True),
             reads=["V", ab], writes=[psn[ob]], signal=True)
        if k == njt - 1:
            so = "ost%d" % (g % 2)
            P.op("dve", lambda e: e.tensor_copy(ost[g % 2][h * 64:(h + 1) * 64, :], ps[ob][0:64, :]),
                 reads=[psn[ob]], writes=[so + "_%d" % h])
            if h == 1:
                P.dma("sp", so, oT(0, g), ost[g % 2][:],
                      reads=[so + "_0", so + "_1"], writes=["oT0_%d" % g])

    order = [(sb_s1, 0), (sb_s2, 1), (sb_s4, 3), (sb_s2b, 1), (sb_s3, 2), (sb_s5, 4)]
    for step in range(NI + 4):
        for fn, lag in order:
            i = step - lag
            if 0 <= i < NI:
                fn(i)

    P.barrier()
    if not parts & 2:
        return
    QTa = [R[0], R[1]]
    KTa = [R[2], R[3]]
    cosv, sinv = C["cosT"], C["sinT"]
    csb = [[A("cs%d_%d" % (a, i), [128, 512], F32) for i in range(2)] for a in range(2)]
    t1 = A("t1", [128, 512], F32)
    t2 = A("t2", [128, 512], F32)
    qf = A("qf", [128, 512], F32)
    kf = A("kf", [128, 512], F32)
    kmT = A("kmT", [128, 2, 32], F32)
    gm = [A("gm%d" % i, [128, 32], F32) for i in range(2)]
    top8 = [A("top8_%d" % i, [128, 8], F32) for i in range(2)]
    mb1 = [A("mb1_%d" % i, [128, 32], F32) for i in range(2)]
    mbpad = [A("mbpad%d" % i, [128, 128], BF16) for i in range(8)]
    cbrow = A("cbrow", [128, 64], F32)
    vbrow = A("vbrow", [128, 64], F32)
    oorow = A("oorow", [128, 64], F32)
    Pb = [A("Pb%d" % i, [128, 512], BF16) for i in range(3)]
    rec = A("rec", [128, 512], F32)
    bcs = A("bcs", [64, 512], F32)

    P.op("dve", lambda e: e.memset(cbrow[:, 0:32], 0.0), writes=["cbrow"])
    P.op("dve", lambda e: e.memset(cbrow[:, 32:64], NEG), writes=["cbrow"])
    P.op("dve", lambda e: e.memset(vbrow[:, 0:32], BIG), writes=["vbrow"])
    P.op("dve", lambda e: e.memset(vbrow[:, 32:64], 0.0), writes=["vbrow"])
    P.op("dve", lambda e: e.memset(oorow[:], -BIG), writes=["oorow"])
    P.op("dve", lambda e: e.memset(oorow[:, 31:32], 0.0), writes=["oorow"])
    P.op("dve", lambda e: e.memset(kmT[:], 0.0), writes=["kmT"])
    P.op("dve", lambda e: e.memset(Vm[:], 1.0), writes=["Vm"])
    for i in range(8):
        P.op("dve", lambda e, i=i: e.memset(mbpad[i][:], 0.0), writes=["mbpad%d" % i])
    for h in range(2):
        P.op("dve", lambda e, h=h: e.memset(QTa[h][64:128, :], 0.0), writes=["QTa%d" % h])
        P.op("dve", lambda e, h=h: e.memset(KTa[h][64:128, :], 0.0), writes=["KTa%d" % h])
        P.dma("pool", "oh%d" % h, KTa[h][64:96, :], C["onehot"], writes=["KTa%d" % h])

    def load_cs(tg):
        b = tg % 2
        P.dma("sp", "cs0_%d" % b, csb[0][b][:], cosv[:, tg * 512:(tg + 1) * 512], writes=["cs0_%d" % b])
        P.dma("sp", "cs1_%d" % b, csb[1][b][:], sinv[:, tg * 512:(tg + 1) * 512], writes=["cs1_%d" % b])

    def rope(dst, b, bank_a, bank_b):
        cn, sn = "cs0_%d" % b, "cs1_%d" % b
        P.op("dve", lambda e: e.tensor_tensor(t1[:], ps[bank_a][:], csb[0][b][:], ALU.mult),
             reads=[psn[bank_a], cn], writes=["t1"])
        P.op("dve", lambda e: e.tensor_tensor(t2[:], ps[bank_b][:], csb[1][b][:], ALU.mult),
             reads=[psn[bank_b], sn], writes=["t2"])
        nm = "qf" if dst is qf else "kf"
        P.op("dve", lambda e: e.tensor_tensor(dst[:], t1[:], t2[:], ALU.add),
             reads=["t1", "t2"], writes=[nm])

    def gate_tail(tgp):
        for idx in range(8):
            cq, h = idx // 2, idx % 2
            cch = tgp * 4 + cq
            u = idx % 2
            P.op("pe", lambda e, u=u, idx=idx: e.matmul(ps[6 + u][:, 0:128], mbpad[idx][:], identb[:],
                                                        start=True, stop=True),
                 reads=["mbpad%d" % idx, "identb"], writes=["ps6_%d" % u], signal=False)
            P.op("pe", lambda e, u=u: e.matmul(ps[6 + u][:, 256:384], identb[:], identb[:],
                                               start=True, stop=True),
                 reads=["identb"], writes=["ps6_%d" % u])
            P.op("act", lambda e, h=h, cch=cch, u=u: e.copy(QTa[h][64:96, cch * 128:(cch + 1) * 128],
                                                          ps[6 + u][64:96, 0:128]),
                 reads=["ps6_%d" % u], writes=["QTa%d" % h])

    load_x(0)
    load_cs(0)
    for tg in range(NG):
        if tg + 1 < NG:
            load_x(tg + 1)
            load_cs(tg + 1)
        b = tg % 2
        sl = slice(tg * 512, (tg + 1) * 512)
        proj_fm(0, 512, tg)
        proj_fm(1, 896, tg)
        rope(kf, b, 0, 1)
        for h in range(2):
            P.op("act", lambda e, h=h: e.copy(KTa[h][0:64, sl], kf[h * 64:(h + 1) * 64, :]),
                 reads=["kf"], writes=["KTa%d" % h])
        for h in range(2):
            hp2 = slice(h * 64, (h + 1) * 64)
            P.op("dve", lambda e, h=h, hp2=hp2: e.tensor_reduce(
                kmT[hp2, h, 2 * tg:2 * tg + 2], kf[hp2, :].rearrange("p (a b) -> p a b", a=2), AX.X, ALU.add),
                reads=["kf"], writes=["kmT"])
        proj_fm(2, 384, tg)
        proj_fm(3, 768, tg)
        rope(qf, b, 2, 3)
        for h in range(2):
            P.op("act", lambda e, h=h: e.mul(QTa[h][0:64, sl], qf[h * 64:(h + 1) * 64, :], 0.125),
                 reads=["qf"], writes=["QTa%d" % h])
        proj_tm(4, 640, tg)
        P.op("act", lambda e: e.copy(Vm[:, tg * 4:(tg + 1) * 4, :, 0:64],
                                    ps[4][:].rearrange("p (a h d) -> p a h d", a=4, h=2)),
             reads=["ps4"], writes=["Vm"])
        if tg > 0:
            gate_tail(tg - 1)
        for idx in range(8):
            cq, h = idx // 2, idx % 2
            P.op("pe", lambda e, cq=cq, h=h, idx=idx: e.matmul(
                ps[5][:, idx * 32:(idx + 1) * 32], qf[:, cq * 128:(cq + 1) * 128],
                kmT[:, h, :], start=True, stop=True),
                reads=["qf", "kmT"], writes=["ps5"], signal=False)
        P.op("pe", lambda e: e.matmul(ps[5][:, 256:384], identb[:], identb[:], start=True, stop=True),
             reads=["identb"], writes=["ps5"])
        for idx in range(8):
            cq, h = idx // 2, idx % 2
            cch = tg * 4 + cq
            own = cch // 2
            u = idx % 2
            P.op("dve", lambda e, idx=idx, own=own, u=u: e.tensor_tensor(
                gm[u][:], ps[5][:, idx * 32:(idx + 1) * 32], cbrow[:, 32 - own:64 - own], ALU.add),
                reads=["ps5", "cbrow"], writes=["gm%d" % u])
            P.op("dve", lambda e, u=u: e.max(top8[u][:], gm[u][:]), reads=["gm%d" % u], writes=["top8_%d" % u])
            P.op("dve", lambda e, own=own, u=u: e.scalar_tensor_tensor(
                mb1[u][:], gm[u][:], top8[u][:, 2:3], vbrow[:, 32 - own:64 - own], ALU.is_ge, ALU.mult),
                reads=["gm%d" % u, "top8_%d" % u, "vbrow"], writes=["mb1_%d" % u])
            P.op("dve", lambda e, own=own, u=u, idx=idx: e.tensor_tensor(
                mbpad[idx][:, 64:96], mb1[u][:], oorow[:, 31 - own:63 - own], ALU.add),
                reads=["mb1_%d" % u, "oorow"], writes=["mbpad%d" % idx])
    gate_tail(NG - 1)

    P.barrier()
    Pb2 = [A("Pb2_%d" % i, [128, 1024], BF16) for i in range(3)]
    mitems = []
    for g in range(NG):
        njt = 4 * g + 4
        for j in range(njt):
            mitems.append((g, j, njt))
    NM = len(mitems)

    def mb_s1(i):
        g, j, njt = mitems[i]
        zp = i % 2
        for h in range(2):
            P.op("pe", lambda e, h=h: e.matmul(psbig[:, zp * 1024 + h * 512:zp * 1024 + (h + 1) * 512],
                                               KTa[h][:, j * 128:(j + 1) * 128], QTa[h][:, g * 512:(g + 1) * 512],
                                               start=True, stop=True),
                 reads=["QTa%d" % h, "KTa%d" % h], writes=["zp%d" % zp], signal=(h == 1))

    def mb_s2(i):
        g, j, njt = mitems[i]
        zp = i % 2
        pb = "Pb2_%d" % (i % 3)
        Pt = Pb2[i % 3]
        P.op("act", lambda e: e.activation(Pt[:], psbig[:, zp * 1024:(zp + 1) * 1024], AF.Exp),
             reads=["zp%d" % zp], writes=[pb])
        if j >= 4 * g:
            jj = j - 4 * g
            for h in range(2):
                P.op("dve", lambda e, h=h: e.tensor_tensor(Pt[:, h * 512:(h + 1) * 512], Pt[:, h * 512:(h + 1) * 512],
                                                           mmb[:, jj, :], ALU.mult),
                     reads=[pb, "mmb"], writes=[pb])

    def mb_s3(i):
        g, j, njt = mitems[i]
        pb = "Pb2_%d" % (i % 3)
        Pt = Pb2[i % 3]
        for h in range(2):
            ob = 4 + (2 * g + h) % 3
            P.op("pe", lambda e, h=h, ob=ob: e.matmul(ps[ob][:, :], Vm[:, j, h, :], Pt[:, h * 512:(h + 1) * 512],
                                                      start=(j == 0), stop=(j == njt - 1), skip_group_check=True),
                 reads=["Vm", pb], writes=[psn[ob]], signal=True)
            if j == njt - 1:
                so = "ost%d" % (g % 2)
                P.op("dve", lambda e, ob=ob: e.reciprocal(rec[64:128, :], ps[ob][64:128, :]),
                     reads=[psn[ob]], writes=["rec"])
                P.op("dve", lambda e, h=h, ob=ob: e.tensor_tensor(ost[g % 2][h * 64:(h + 1) * 64, :], ps[ob][0:64, :],
                                                                  rec[64:128, :], ALU.mult),
                     reads=[psn[ob], "rec"], writes=[so + "_%d" % h])
                if h == 1:
                    P.dma("sp", so, oT(1, g), ost[g % 2][:],
                          reads=[so + "_0", so + "_1"], writes=["oT1_%d" % g])

    mstages = [mb_s1, mb_s2, mb_s3]
    for step in range(NM + len(mstages) - 1):
        for s_, fn in enumerate(mstages):
            i = step - s_
            if 0 <= i < NM:
                fn(i)
    P.barrier()


def build_B(nc, P, ctx, sfx, xT_d, oTs_v, oTm_v, oreads, W, lnp_d, out_d, outb_d=None):
    def A(name, shape, dt):
        return ctx.enter_context(nc.sbuf_tensor(name + sfx, shape, dt))
    xT = A("xT_sb", [128, 8, TOK], F32)
    xb = A("xb_sb", [128, 8, TOK], BF16)
    arena = A("arena", [128, 32768], BF16)
    oTs = arena[:, 0:8192].rearrange("p (a b) -> p a b", a=4)
    oTm = arena[:, 8192:16384].rearrange("p (a b) -> p a b", a=4)
    mg = arena[:, 16384:32768].rearrange("p (a b) -> p a b", a=8)
    hT = arena[:, 0:NFC * 1024].rearrange("p (a b) -> p a b", a=NFC)
    lnp = A("lnp_sb", [128, 32], F32)
    onesD = A("onesD", [128, 128], F32)
    warena = A("warena", [128, 9728], BF16)
    wgb = [warena[:, i * 2048:(i + 1) * 2048].rearrange("p (a b) -> p a b", a=8) for i in range(2)]
    wbb = [warena[:, 4096 + i * 1024:4096 + (i + 1) * 1024].rearrange("p (a b) -> p a b", a=4) for i in range(2)]
    wob = [warena[:, 6144 + i * 1024:6144 + (i + 1) * 1024].rearrange("p (a b) -> p a b", a=8) for i in range(2)]
    wfgu = [warena[:, i * 2048:(i + 1) * 2048].rearrange("p (a b) -> p a b", a=8) for i in range(2)]
    wfdb = [warena[:, 4096 + i * 2816:4096 + (i + 1) * 2816].rearrange("p (a b) -> p a b", a=NFC) for i in range(2)]
    sg = [A("sg%d" % i, [128, 512], F32) for i in range(4)]
    m1 = [A("m1_%d" % i, [128, 512], F32) for i in range(2)]
    m2 = [A("m2_%d" % i, [128, 512], F32) for i in range(2)]
    ysq = [A("ysq%d" % i, [128, 512], F32) for i in range(2)]
    mean_sb = A("mean_sb", [128, 512], F32)
    rstd_sb = A("rstd_sb", [128, 512], F32)
    tn = [A("tn%d" % i, [128, 512], F32) for i in range(2)]
    ps = [ctx.enter_context(nc.psum_tensor("pb%d" % i + sfx, [128, 512], F32)) for i in range(8)]
    psn = ["pb%d" % i for i in range(8)]
    bank = [0]
    XN = [["xT%d_%d" % (c, g) for g in range(NTG)] for c in range(8)]
    XB = [["xb%d_%d" % (c, g) for g in range(NTG)] for c in range(8)]
    MG = [["mg%d_%d" % (c, g) for g in range(NTG)] for c in range(8)]
    HT = [["hT%d_%d" % (k, g) for g in range(2)] for k in range(NFC)]
    allx = [n for r_ in XN for n in r_]
    allxb = [n for r_ in XB for n in r_]

    def nb():
        bank[0] = (bank[0] + 1) % 8
        return bank[0]

    xdv = xT_d.rearrange("(c p) t -> p c t", p=128)
    P.dma("sp", "xT", xT[:], xdv, writes=allx)
    P.dma("pool", "xb", xb[:], xdv, writes=allxb)
    P.dma("sp", "oTs", oTs, oTs_v, reads=oreads, writes=["oTs"])
    P.dma("sp", "oTm", oTm, oTm_v, reads=oreads, writes=["oTm"])
    P.dma("sp", "lnp", lnp[:], lnp_d, writes=["lnp"])
    P.op("dve", lambda e: e.memset(onesD[:], 1.0 / D), writes=["onesD"])

    wGv = W["wG"].rearrange("(k p) n -> p k n", p=128)
    wbsv = W["wbs"].rearrange("(k p) n -> p k n", p=128)
    wbmv = W["wbm"].rearrange("(k p) n -> p k n", p=128)
    wov = W["wout"].rearrange("(k p) n -> p k n", p=128)
    wfgv = W["wfg"].rearrange("(k p) n -> p k n", p=128)
    wfuv = W["wfu"].rearrange("(k p) n -> p k n", p=128)
    wfdv = W["wfd"].rearrange("(k p) n -> p k n", p=128)

    def tsl(tg):
        return slice(tg * 512, (tg + 1) * 512)

    def load_b1(c):
        b = c % 2
        cs = slice(c * 128, (c + 1) * 128)
        P.dma("pool", "wgb%da" % b, wgb[b][:, :, 0:128], wGv[:, :, cs], writes=["wgb%da" % b])
        P.dma("pool", "wgb%db" % b, wgb[b][:, :, 128:256], wGv[:, :, D + c * 128:D + (c + 1) * 128],
              writes=["wgb%db" % b])
        P.dma("pool", "wbb%da" % b, wbb[b][:, :, 0:128], wbsv[:, :, cs], writes=["wbb%da" % b])
        P.dma("pool", "wbb%db" % b, wbb[b][:, :, 128:256], wbmv[:, :, cs], writes=["wbb%db" % b])

    load_b1(0)
    it = 0
    for c in range(8):
        if c + 1 < 8:
            load_b1(c + 1)
        b = c % 2
        for tg in range(NTG):
            t = tsl(tg)
            bk = [nb() for _ in range(4)]
            for half, (bkk, wn) in enumerate(zip(bk[0:2], ["wgb%da" % b, "wgb%db" % b])):
                for k in range(8):
                    P.op("pe", lambda e, k=k, half=half, bkk=bkk: e.matmul(
                        ps[bkk][:], wgb[b][:, k, half * 128:(half + 1) * 128], xb[:, k, t],
                        start=(k == 0), stop=(k == 7)),
                        reads=[wn, XB[k][tg]], writes=[psn[bkk]], signal=(k == 7))
            for half, (bkk, wn, src, sn) in enumerate(zip(bk[2:4], ["wbb%da" % b, "wbb%db" % b],
                                                          [oTs, oTm], ["oTs", "oTm"])):
                for k in range(4):
                    P.op("pe", lambda e, k=k, half=half, bkk=bkk, src=src: e.matmul(
                        ps[bkk][:], wbb[b][:, k, half * 128:(half + 1) * 128], src[:, k, t],
                        start=(k == 0), stop=(k == 3)),
                        reads=[wn, sn], writes=[psn[bkk]], signal=(k == 3))
            u = it % 2
            s0, s1 = sg[2 * u], sg[2 * u + 1]
            P.op("act", lambda e, s0=s0, bkk=bk[0]: e.activation(s0[:], ps[bkk][:], AF.Sigmoid),
                 reads=[psn[bk[0]]], writes=["sg%d" % (2 * u)])
            P.op("act", lambda e, s1=s1, bkk=bk[1]: e.activation(s1[:], ps[bkk][:], AF.Sigmoid),
                 reads=[psn[bk[1]]], writes=["sg%d" % (2 * u + 1)])
            P.op("dve", lambda e, s0=s0, u=u, bkk=bk[2]: e.tensor_tensor(m1[u][:], s0[:], ps[bkk][:], ALU.mult),
                 reads=[psn[bk[2]], "sg%d" % (2 * u)], writes=["m1_%d" % u])
            P.op("dve", lambda e, s1=s1, u=u, bkk=bk[3]: e.tensor_tensor(m2[u][:], s1[:], ps[bkk][:], ALU.mult),
                 reads=[psn[bk[3]], "sg%d" % (2 * u + 1)], writes=["m2_%d" % u])
            P.op("pool", lambda e, u=u, c=c, t=t: e.tensor_tensor(mg[:, c, t], m1[u][:], m2[u][:], ALU.add),
                 reads=["m1_%d" % u, "m2_%d" % u], writes=[MG[c][tg]])
            it += 1

    def layer_norm(tg, gcol, bcol):
        t = tsl(tg)
        ba, bb_ = nb(), nb()
        for c in range(8):
            q = ysq[c % 2]
            P.op("act", lambda e, q=q, c=c: e.activation(q[:], xT[:, c, t], AF.Square),
                 reads=[XN[c][tg]], writes=["ysq%d" % (c % 2)])
            P.op("pe", lambda e, c=c: e.matmul(ps[ba][:], onesD[:], xT[:, c, t], start=(c == 0), stop=(c == 7)),
                 reads=["onesD", XN[c][tg]], writes=[psn[ba]], signal=(c == 7))
            P.op("pe", lambda e, q=q, c=c: e.matmul(ps[bb_][:], onesD[:], q[:], start=(c == 0), stop=(c == 7),
                                                   skip_group_check=True),
                 reads=["onesD", "ysq%d" % (c % 2)], writes=[psn[bb_]], signal=True)
        P.op("dve", lambda e: e.tensor_copy(mean_sb[:], ps[ba][:]), reads=[psn[ba]], writes=["mean_sb"])
        P.op("dve", lambda e: e.tensor_tensor(rstd_sb[:], mean_sb[:], mean_sb[:], ALU.mult),
             reads=["mean_sb"], writes=["rstd_sb"])
        P.op("dve", lambda e: e.tensor_tensor(rstd_sb[:], ps[bb_][:], rstd_sb[:], ALU.subtract),
             reads=[psn[bb_], "rstd_sb"], writes=["rstd_sb"])
        P.op("act", lambda e: e.activation(rstd_sb[:], rstd_sb[:], AF.Ln, bias=EPS),
             reads=["rstd_sb"], writes=["rstd_sb"])
        P.op("act", lambda e: e.activation(rstd_sb[:], rstd_sb[:], AF.Exp, scale=-0.5),
             reads=["rstd_sb"], writes=["rstd_sb"])
        for c in range(8):
            tt = tn[c % 2]
            tnn = "tn%d" % (c % 2)
            P.op("dve", lambda e, tt=tt, c=c: e.tensor_tensor(tt[:], xT[:, c, t], mean_sb[:], ALU.subtract),
                 reads=[XN[c][tg], "mean_sb"], writes=[tnn])
            P.op("dve", lambda e, tt=tt: e.tensor_tensor(tt[:], tt[:], rstd_sb[:], ALU.mult),
                 reads=[tnn, "rstd_sb"], writes=[tnn])
            P.op("act", lambda e, tt=tt, c=c: e.activation(xT[:, c, t], tt[:], AF.Identity,
                                                          bias=lnp[:, bcol + c:bcol + c + 1],
                                                          scale=lnp[:, gcol + c:gcol + c + 1]),
                 reads=[tnn, "lnp"], writes=[XN[c][tg]])
            P.op("pool", lambda e, c=c: e.tensor_copy(xb[:, c, t], xT[:, c, t]), reads=[XN[c][tg]], writes=[XB[c][tg]])

    def load_wo(c):
        b = c % 2
        P.dma("pool", "wob%d" % b, wob[b][:], wov[:, :, c * 128:(c + 1) * 128], writes=["wob%d" % b])

    load_wo(0)
    for c in range(8):
        if c + 1 < 8:
            load_wo(c + 1)
        b = c % 2
        for tg in range(NTG):
            t = tsl(tg)
            bkk = nb()
            for k in range(8):
                P.op("pe", lambda e, k=k, bkk=bkk: e.matmul(ps[bkk][:], wob[b][:, k, :], mg[:, k, t],
                                                            start=(k == 0), stop=(k == 7)),
                     reads=["wob%d" % b, MG[k][tg]], writes=[psn[bkk]], signal=(k == 7))
            P.op("dve", lambda e, bkk=bkk, c=c, t=t: e.scalar_tensor_tensor(
                xT[:, c, t], xT[:, c, t], ALPHA, ps[bkk][:], ALU.mult, ALU.add),
                reads=[psn[bkk], XN[c][tg]], writes=[XN[c][tg]])
    for tg in range(NTG):
        layer_norm(tg, 0, 8)

    def load_gu(i):
        k = i % NFC
        b = i % 2
        P.dma("pool", "wfgu%da" % b, wfgu[b][:, :, 0:128], wfgv[:, :, k * 128:(k + 1) * 128],
              writes=["wfgu%da" % b])
        P.dma("pool", "wfgu%db" % b, wfgu[b][:, :, 128:256], wfuv[:, :, k * 128:(k + 1) * 128],
              writes=["wfgu%db" % b])

    def load_dn(i):
        c = i % 8
        b = i % 2
        P.dma("pool", "wfdb%d" % b, wfdb[b][:], wfdv[:, :, c * 128:(c + 1) * 128], writes=["wfdb%d" % b])

    gi = 0
    di = 0
    P.barrier()
    for hf in range(2):
        load_gu(gi)
        for k in range(NFC):
            if k + 1 < NFC:
                load_gu(gi + 1)
            b = gi % 2
            for t2_ in range(2):
                tg = hf * 2 + t2_
                t = tsl(tg)
                b0, b1 = nb(), nb()
                for half, bkk in enumerate([b0, b1]):
                    wn = "wfgu%d%s" % (b, "ab"[half])
                    for kk in range(8):
                        P.op("pe", lambda e, kk=kk, half=half, bkk=bkk: e.matmul(
                            ps[bkk][:], wfgu[b][:, kk, half * 128:(half + 1) * 128], xb[:, kk, t],
                            start=(kk == 0), stop=(kk == 7)),
                            reads=[wn, XB[kk][tg]], writes=[psn[bkk]], signal=(kk == 7))
                u = it % 2
                s0 = sg[2 * u]
                P.op("act", lambda e, s0=s0, b0=b0: e.activation(s0[:], ps[b0][:], AF.Silu),
                     reads=[psn[b0]], writes=["sg%d" % (2 * u)])
                P.op("dve", lambda e, s0=s0, b1=b1, k=k, t2_=t2_: e.tensor_tensor(
                    hT[:, k, t2_ * 512:(t2_ + 1) * 512], s0[:], ps[b1][:], ALU.mult),
                    reads=[psn[b1], "sg%d" % (2 * u)], writes=[HT[k][t2_]])
                it += 1
            gi += 1
        load_dn(di)
        for c in range(8):
            if c + 1 < 8:
                load_dn(di + 1)
            b = di % 2
            for t2_ in range(2):
                tg = hf * 2 + t2_
                t = tsl(tg)
                bkk = nb()
                for k in range(NFC):
                    P.op("pe", lambda e, k=k, bkk=bkk, t2_=t2_: e.matmul(
                        ps[bkk][:], wfdb[b][:, k, :], hT[:, k, t2_ * 512:(t2_ + 1) * 512],
                        start=(k == 0), stop=(k == NFC - 1)),
                        reads=["wfdb%d" % b, HT[k][t2_]], writes=[psn[bkk]], signal=(k == NFC - 1))
                P.op("dve", lambda e, bkk=bkk, c=c, t=t: e.scalar_tensor_tensor(
                    xT[:, c, t], xT[:, c, t], ALPHA, ps[bkk][:], ALU.mult, ALU.add),
                    reads=[psn[bkk], XN[c][tg]], writes=[XN[c][tg]])
            di += 1
        for t2_ in range(2):
            layer_norm(hf * 2 + t2_, 16, 24)
    P.dma("sp", "out", out_d.rearrange("(c p) t -> p c t", p=128), xT[:], reads=allx, writes=["out"])
    if outb_d is not None:
        P.dma("sp", "outb", outb_d.rearrange("(c p) t -> p c t", p=128), xb[:], reads=allxb, writes=["outb"])


W_KEYS = ["wG", "wbs", "wbm", "wout", "wfg", "wfu", "wfd"]
W_SHAPES = {"wG": [D, 2048], "wbs": [512, D], "wbm": [512, D], "wout": [D, D],
            "wfg": [D, DFF], "wfu": [D, DFF], "wfd": [DFF, D]}
RG = [[0, 1, 2, 3], [4, 5, 6, 7]]


def _pp(v):
    return np.ascontiguousarray(v.reshape(8, 128).T)


def build_fused(nlayers=DEPTH, skipA=False, skipB=False, samex=False, parts=3):
    nc = bass.Bass("TRN2", target_bir_lowering=False)
    P = Prog(nc)
    hc = host_consts()
    C = {k: nc.dram_tensor("c_" + k, list(v.shape), F32, kind="ExternalInput").ap() for k, v in hc.items()}
    xTall0 = nc.dram_tensor("xTall0", [D, S], F32, kind="ExternalInput").ap()
    xT0 = nc.dram_tensor("xT0", [D, TOK], F32, kind="ExternalInput").ap()
    wA = nc.dram_tensor("wA", [DEPTH, D, 768], F32, kind="ExternalInput").ap()
    Wd = {k: nc.dram_tensor(k, [DEPTH] + W_SHAPES[k], F32, kind="ExternalInput").ap() for k in W_KEYS}
    lnp = nc.dram_tensor("lnp", [DEPTH, 128, 32], F32, kind="ExternalInput").ap()
    out = nc.dram_tensor("out", [D, TOK], F32, kind="ExternalOutput").ap()
    oT_loc = nc.dram_tensor("oT_loc", [1024, TOK], BF16, kind="Internal").ap()
    oT_all = nc.dram_tensor("oT_all", [4096, TOK], BF16, kind="Internal").ap()
    xb_loc = nc.dram_tensor("xb_loc", [D, TOK], BF16, kind="Internal").ap()
    xb_all = nc.dram_tensor("xb_all", [4 * D, TOK], BF16, kind="Internal").ap()
    xres = nc.dram_tensor("xres", [D, TOK], F32, kind="Internal").ap()
    pid = nc.sync.partition_id()
    roff = (pid % 4) * 1024
    olv = oT_loc.rearrange("(q b p) t -> b p q t", q=4, b=2)

    def odst(br, g):
        return olv[br, :, g // 4, (g % 4) * 512:(g % 4 + 1) * 512]

    for l in range(nlayers):
        last = l == nlayers - 1
        with ExitStack() as ctx:
            if l == 0 or samex:
                xv0 = xTall0.rearrange("(c p) t -> p c t", p=128)
                xsrc = lambda tg: [(0, 8, xv0[:, :, tg * 512:(tg + 1) * 512])]
                xeng, xreads = "pool", []
            else:
                xv1 = xb_all.rearrange("(j r c p) t -> p r j c t", j=4, r=4, c=2, p=128)
                xsrc = lambda tg: [(2 * j, 2 * j + 2, xv1[:, tg // 4, j, :, (tg % 4) * 512:(tg % 4 + 1) * 512])
                                   for j in range(4)]
                xeng, xreads = "sp", ["xb_all%d" % j for j in range(4)]
            if not skipA:
                build_A(nc, P, ctx, "_a%d" % l, xsrc, xeng, xreads, wA[l], odst, C, parts)
        P.barrier()
        for q in range(4):
            P.collective("oT%d" % q, "AllGather", RG, oT_loc[q * 256:(q + 1) * 256, :],
                         oT_all[q * 1024:(q + 1) * 1024, :], writes=["oT_all%d" % q])
        with ExitStack() as ctx:
            ov = oT_all[bass.ds(roff, 1024), :].rearrange("(k b p) t -> p k b t", k=4, b=2, p=128)
            oTs_v = ov[:, :, 0, :]
            oTm_v = ov[:, :, 1, :]
            Wl = {k: Wd[k][l] for k in W_KEYS}
            if not skipB:
              build_B(nc, P, ctx, "_b%d" % l, xT0 if l == 0 else xres, oTs_v, oTm_v,
                    ["oT_all%d" % q for q in range(4)], Wl, lnp[l],
                    out if last else xres, None if last else xb_loc)
        P.barrier()
        if not last:
            for j in range(4):
                P.collective("xb%d" % j, "AllGather", RG, xb_loc[j * 256:(j + 1) * 256, :],
                             xb_all[j * 1024:(j + 1) * 1024, :], writes=["xb_all%d" % j])
    P.finish("sp")
    return nc, hc


def _wA_slices(w_in_l, r):
    def cols(base):
        return w_in_l[:, base + 128 * r: base + 128 * r + 128]
    return np.concatenate([cols(0), cols(512), cols(1024), cols(1536), cols(2048), cols(2560)], axis=1)


def kernel(x, w_in, w_branch_sb, w_branch_moba, w_out, ln_mix_g, ln_mix_b,
           w_ffn_gate, w_ffn_up, w_ffn_down, ln_ffn_g, ln_ffn_b):
    f = lambda a: np.asarray(a, dtype=np.float32)
    x, w_in, w_branch_sb, w_branch_moba, w_out = f(x), f(w_in), f(w_branch_sb), f(w_branch_moba), f(w_out)
    ln_mix_g, ln_mix_b, ln_ffn_g, ln_ffn_b = f(ln_mix_g), f(ln_mix_b), f(ln_ffn_g), f(ln_ffn_b)
    w_ffn_gate, w_ffn_up, w_ffn_down = f(w_ffn_gate), f(w_ffn_up), f(w_ffn_down)
    nc, hc = build_fused()
    cores = list(range(8))
    xTb = [np.ascontiguousarray(x[b].T) for b in range(B)]
    Wfull = {"wG": np.ascontiguousarray(w_in[:, :, 3072:5120]), "wbs": w_branch_sb, "wbm": w_branch_moba,
             "wout": w_out, "wfg": w_ffn_gate, "wfu": w_ffn_up, "wfd": w_ffn_down}
    lnp = np.ascontiguousarray(np.stack([np.concatenate(
        [_pp(ln_mix_g[l]), _pp(ln_mix_b[l]), _pp(ln_ffn_g[l]), _pp(ln_ffn_b[l])], axis=1) for l in range(DEPTH)]))
    maps = []
    for c in cores:
        b, r = c // 4, c % 4
        m = {"xTall0": xTb[b], "xT0": np.ascontiguousarray(xTb[b][:, r * TOK:(r + 1) * TOK]),
             "wA": np.ascontiguousarray(np.stack([_wA_slices(w_in[l], r) for l in range(DEPTH)])),
             "lnp": lnp}
        m.update({k: np.ascontiguousarray(v) for k, v in Wfull.items()})
        m.update({"c_" + k: v for k, v in hc.items()})
        maps.append(m)
    res = run_bass_kernel_spmd(nc, maps, core_ids=cores)
    xn = np.empty_like(x)
    for c in cores:
        b, r = c // 4, c % 4
        xn[b, r * TOK:(r + 1) * TOK, :] = np.asarray(res.results[c]["out"]).T
    return xn
```

```python
from contextlib import ExitStack
import numpy as np
import concourse.bass as bass
import concourse.mybir as mybir
from concourse.bass_utils import run_bass_kernel_spmd

F32 = mybir.dt.float32
BF16 = mybir.dt.bfloat16
AF = mybir.ActivationFunctionType
ALU = mybir.AluOpType
AX = mybir.AxisListType

D = 1024
S = 8192
B = 2
DEPTH = 2
DFF = 2816
NFC = DFF // 128
NG = S // 512
ALPHA = (2 * DEPTH) ** 0.25
EPS = 1e-5
BIG = 30000.0
NEG = -1.0e30
TOK = 2048
NTG = TOK // 512


class Prog:
    SAME_ENGINE_SYNC = True
    EMBED = ("act", "dve")

    def __init__(self, nc):
        self.nc = nc
        self.eng = {"pe": nc.tensor, "act": nc.scalar, "dve": nc.vector,
                    "pool": nc.gpsimd, "sp": nc.sync}
        self.sem = {}
        self.cnt = {}
        self.res = {}
        self.known = {e: {} for e in self.eng}
        self.nwait = 0
        self.nops = 0
        self._pending = None

    def _sem(self, lane):
        if lane not in self.sem:
            self.sem[lane] = self.nc.alloc_semaphore(name="s_" + lane.replace(":", "_"))
            self.cnt[lane] = 0
        return self.sem[lane]

    def _need(self, e, lane, val):
        if lane == e and (e == "pe" or not self.SAME_ENGINE_SYNC):
            return
        if self.known[e].get(lane, 0) >= val:
            return
        self.known[e][lane] = val
        self.nwait += 1
        if self._pending is not None:
            self._pending.append((lane, val))
            return
        self.eng[e].wait_ge(self._sem(lane), val)

    def _deps(self, e, reads, writes):
        for r in reads:
            ent = self.res.get(r)
            if ent and ent[0]:
                self._need(e, *ent[0])
        for w in writes:
            ent = self.res.get(w)
            if ent:
                if ent[0]:
                    self._need(e, *ent[0])
                for lane, val in ent[1].items():
                    self._need(e, lane, val)

    def _record(self, lane, val, reads, writes):
        for r in reads:
            ent = self.res.setdefault(r, [None, {}])
            ent[1][lane] = max(ent[1].get(lane, 0), val)
        for w in writes:
            self.res[w] = [(lane, val), {}]

    def op(self, e, fn, reads=(), writes=(), signal=True):
        self._sem(e)
        self._pending = [] if e in self.EMBED else None
        self._deps(e, reads, writes)
        pend = self._pending or []
        self._pending = None
        for lane_, val_ in pend[:-1]:
            self.eng[e].wait_ge(self._sem(lane_), val_)
        ins = fn(self.eng[e])
        if pend:
            ins._wait_ge(self._sem(pend[-1][0]), pend[-1][1])
        val = self.cnt[e] + 1
        if signal:
            ins.then_inc(self.sem[e], 1)
            self.cnt[e] = val
        self._record(e, val, reads, writes)
        self.nops += 1
        return ins

    def dma(self, e, lane, out, in_, reads=(), writes=(), **kw):
        lane = "d:" + lane
        self._sem(lane)
        self._deps(e, reads, writes)
        ins = self.eng[e].dma_start(out=out, in_=in_, **kw)
        val = self.cnt[lane] + 16
        ins.then_inc(self.sem[lane], 16)
        self.cnt[lane] = val
        self._record(lane, val, reads, writes)
        self.nops += 1
        return ins

    def collective(self, lane, kind, rg, in_ap, out_ap, reads=(), writes=()):
        lane = "c:" + lane
        self._sem(lane)
        self._deps("pool", reads, writes)
        ins = self.nc.gpsimd.collective_compute(kind, ALU.bypass, replica_groups=rg,
                                                ins=[in_ap], outs=[out_ap])
        val = self.cnt[lane] + 1
        ins.then_inc(self.sem[lane], 1)
        self.cnt[lane] = val
        self._record(lane, val, reads, writes)
        self.nops += 1
        return ins

    def barrier(self):
        for e in self.eng:
            for lane, val in self.cnt.items():
                if val > 0 and lane != e:
                    if self.known[e].get(lane, 0) < val:
                        self.eng[e].wait_ge(self.sem[lane], val)
                        self.known[e][lane] = val
        self.res = {}

    def finish(self, e="sp"):
        for lane, val in self.cnt.items():
            if val > 0 and lane != e:
                self.eng[e].wait_ge(self.sem[lane], val)


def host_consts():
    c = {}
    k = np.arange(128)[:, None]
    s = np.arange(128)[None, :]
    c["negtri"] = np.where(k >= s, -1.0, 0.0).astype(np.float32)
    c["negones"] = np.full((128, 128), -1.0, np.float32)
    c["ident"] = np.eye(128, dtype=np.float32)
    sk = np.arange(128)[:, None, None] + 128 * np.arange(4)[None, :, None]
    t = np.arange(512)[None, None, :]
    c["msb"] = (sk < t).astype(np.float32)
    c["mmb"] = (sk <= t).astype(np.float32)
    half = 8
    inv = (500000.0 ** (-np.arange(0, 16, 2, dtype=np.float32) / 16)).astype(np.float32)
    ang = (np.arange(S, dtype=np.float32)[:, None] * inv[None, :]).astype(np.float32)
    cos = np.cos(ang).astype(np.float32).T
    sin = np.sin(ang).astype(np.float32).T
    cosT = np.ones((64, S), np.float32)
    sinT = np.zeros((64, S), np.float32)
    cosT[0:8] = cos
    cosT[8:16] = cos
    sinT[0:8] = sin
    sinT[8:16] = sin
    c["cosT"] = np.concatenate([cosT, cosT], 0)
    c["sinT"] = np.concatenate([sinT, sinT], 0)
    n = np.arange(32)[None, :]
    own = (np.arange(64) // 2)[:, None]
    cb = np.where(n < own, 0.0, NEG).astype(np.float32)
    vb = np.where(n < own, BIG, 0.0).astype(np.float32)
    oo = np.where(n == own, 0.0, -BIG).astype(np.float32)
    c["cb"] = np.broadcast_to(cb.reshape(1, 64 * 32), (128, 64 * 32)).copy()
    c["vb"] = np.broadcast_to(vb.reshape(1, 64 * 32), (128, 64 * 32)).copy()
    c["oo"] = np.broadcast_to(oo.reshape(1, 64 * 32), (128, 64 * 32)).copy()
    c["onehot"] = (np.arange(32)[:, None] == (np.arange(S)[None, :] // 256)).astype(np.float32)
    return c


def build_A(nc, P, ctx, sfx, xsrc, xeng, xreads, wA, oT, C, parts=3, after_mb_group=None):
    def A(name, shape, dt):
        return ctx.enter_context(nc.sbuf_tensor(name + sfx, shape, dt))
    wv = wA.rearrange("(c p) n -> p c n", p=128)

    negtri = A("negtri", [128, 128], BF16)
    negones = A("negones", [128, 128], BF16)
    identb = A("identb", [128, 128], BF16)
    onesf = A("onesf", [128, 64], F32)
    msb = A("msb", [128, 4, 512], BF16)
    mmb = A("mmb", [128, 4, 512], BF16)
    w = A("wA_sb", [128, 8, 1024], BF16)
    P.dma("pool", "c0", negtri[:], C["negtri"], writes=["negtri"])
    P.dma("pool", "c1", negones[:], C["negones"], writes=["negones"])
    P.dma("pool", "c2", identb[:], C["ident"], writes=["identb"])
    P.dma("pool", "c3", msb[:], C["msb"], writes=["msb"])
    P.dma("pool", "c4", mmb[:], C["mmb"], writes=["mmb"])
    P.dma("pool", "w", w[:, :, 0:768], wv, writes=["w"])
    P.op("dve", lambda e: e.memset(onesf[:], 1.0), writes=["onesf"])
    P.op("dve", lambda e: e.memset(w[:, :, 768:1024], 0.0), reads=["w"], writes=["w"])
    for qk in range(2):
        for hd in range(2):
            src = 384 + qk * 128 + hd * 64
            dst = 768 + qk * 128 + hd * 64
            P.op("act", lambda e, s_=src, d_=dst: e.mul(w[:, :, d_:d_ + 8], w[:, :, s_ + 8:s_ + 16], -1.0),
                 reads=["w"], writes=["w"])
            P.op("act", lambda e, s_=src, d_=dst: e.copy(w[:, :, d_ + 8:d_ + 16], w[:, :, s_:s_ + 8]),
                 reads=["w"], writes=["w"])

    psbig = ctx.enter_context(nc.psum_tensor("psbig" + sfx, [128, 4096], F32))
    ps = [psbig[:, i * 512:(i + 1) * 512] for i in range(8)]
    psn = ["ps%d" % i for i in range(7)]

    xg = [A("xg%d" % i, [128, 8, 512], BF16) for i in range(2)]
    ost = [A("ost%d" % i, [128, 512], BF16) for i in range(2)]

    def load_x(tg):
        b = tg % 2
        for pi, (c0, c1, sap) in enumerate(xsrc(tg)):
            P.dma(xeng, "xg%d_%d" % (b, pi), xg[b][:, c0:c1, :], sap, reads=xreads, writes=["xg%d" % b])

    def proj_fm(bank, col0, tg):
        b = tg % 2
        for c in range(8):
            P.op("pe", lambda e, c=c: e.matmul(ps[bank][:], w[:, c, col0:col0 + 128], xg[b][:, c, :],
                                               start=(c == 0), stop=(c == 7)),
                 reads=["w", "xg%d" % b], writes=[psn[bank]], signal=(c == 7))

    def proj_tm(bank, col0, tg, ncols=128):
        b = tg % 2
        for ts in range(4):
            for c in range(8):
                P.op("pe", lambda e, c=c, ts=ts: e.matmul(ps[bank][:, ts * 128:(ts + 1) * 128],
                                                          xg[b][:, c, ts * 128:(ts + 1) * 128],
                                                          w[:, c, col0:col0 + 128],
                                                          start=(c == 0), stop=(c == 7)),
                     reads=["w", "xg%d" % b], writes=[psn[bank]], signal=(c == 7 and ts == 3))

    R = [A("R%d" % i, [128, S], BF16) for i in range(4)]
    Vm = A("Vm", [128, 64, 2, 128], BF16)
    QT, KT = R[0], R[1]
    V = A("V", [128, 64, 128], BF16)
    if parts & 1:
        load_x(0)
    for tg in range(NG if parts & 1 else 0):
        if tg + 1 < NG:
            load_x(tg + 1)
        sl = slice(tg * 512, (tg + 1) * 512)
        proj_fm(0, 0, tg)
        P.op("act", lambda e, sl=sl: e.mul(QT[:, sl], ps[0][:], 0.125), reads=["ps0"], writes=["QT"])
        proj_fm(1, 128, tg)
        P.op("dve", lambda e, sl=sl: e.tensor_copy(KT[:, sl], ps[1][:]), reads=["ps1"], writes=["KT"])
        proj_tm(2, 256, tg)
        P.op("act", lambda e, tg=tg: e.copy(V[:, tg * 4:(tg + 1) * 4, :],
                                           ps[2][:].rearrange("p (a b) -> p a b", a=4)),
             reads=["ps2"], writes=["V"])

    Eb = [A("Eb%d" % i, [128, 512], F32) for i in range(2)]
    Lb = [A("Lb%d" % i, [128, 512], BF16) for i in range(3)]
    Ab = [A("Ab%d" % i, [128, 512], BF16) for i in range(3)]
    Lrun = [A("Lrun%d" % i, [128, 512], F32) for i in range(2)]
    Lrb = [[A("Lrb%d_%d" % (h, i), [128, 512], BF16) for i in range(2)] for h in range(2)]

    items = []
    for g in range(NG):
        njt = 4 * g + 4
        for k in range(njt):
            for h in range(2):
                items.append((h, g, k, njt - 1 - k, njt))
    NI = len(items) if parts & 1 else 0

    def sb_s1(i):
        h, g, k, j, njt = items[i]
        zb = i % 4
        hp = slice(h * 64, (h + 1) * 64)
        P.op("pe", lambda e: e.matmul(ps[zb][:], KT[hp, j * 128:(j + 1) * 128], QT[hp, g * 512:(g + 1) * 512],
                                      start=True, stop=True),
             reads=["QT", "KT"], writes=[psn[zb]])

    def sb_s2(i):
        h, g, k, j, njt = items[i]
        zb = i % 4
        eb = "Eb%d" % (i % 2)
        lb = "Lb%d" % (i % 3)
        E = Eb[i % 2]
        L = Lb[i % 3]
        P.op("act", lambda e: e.activation(E[:], ps[zb][:], AF.Exp), reads=[psn[zb]], writes=[eb])

    def sb_s2b(i):
        h, g, k, j, njt = items[i]
        zb = i % 4
        eb = "Eb%d" % (i % 2)
        lb = "Lb%d" % (i % 3)
        E = Eb[i % 2]
        L = Lb[i % 3]
        P.op("act", lambda e: e.activation(L[:], E[:], AF.Ln, bias=1.0), reads=[eb], writes=[lb])
        if k < 4:
            jj = 3 - k
            P.op("dve", lambda e: e.tensor_tensor(L[:], L[:], msb[:, jj, :], ALU.mult),
                 reads=[lb, "msb"], writes=[lb])
        if k < njt - 1:
            lr = "Lrun%d" % h
            nb = "Lrb%d_%d" % (h, (k + 1) % 2)
            dst = Lrb[h][(k + 1) % 2]
            if k == 0:
                P.op("dve", lambda e: e.tensor_copy(Lrun[h][:], L[:]), reads=[lb], writes=[lr])
                P.op("dve", lambda e: e.tensor_copy(dst[:], L[:]), reads=[lb], writes=[nb])
            else:
                P.op("dve", lambda e: e.tensor_tensor(Lrun[h][:], Lrun[h][:], L[:], ALU.add),
                     reads=[lb, lr], writes=[lr])
                P.op("dve", lambda e: e.tensor_copy(dst[:], Lrun[h][:]), reads=[lr], writes=[nb])

    def sb_s3(i):
        h, g, k, j, njt = items[i]
        zb = i % 4
        lb = "Lb%d" % (i % 3)
        L = Lb[i % 3]
        P.op("pe", lambda e: e.matmul(ps[zb][:], negtri[:], L[:], start=False, stop=(k == 0),
                                      skip_group_check=True),
             reads=["negtri", lb], writes=[psn[zb]], signal=(k == 0))
        if k > 0:
            cur = Lrb[h][k % 2]
            P.op("pe", lambda e: e.matmul(ps[zb][:], negones[:], cur[:], start=False, stop=True,
                                          skip_group_check=True),
                 reads=["negones", "Lrb%d_%d" % (h, k % 2)], writes=[psn[zb]])

    def sb_s4(i):
        h, g, k, j, njt = items[i]
        zb = i % 4
        ab = "Ab%d" % (i % 3)
        Aa = Ab[i % 3]
        P.op("act", lambda e: e.activation(Aa[:], ps[zb][:], AF.Exp), reads=[psn[zb]], writes=[ab])
        if k < 4:
            jj = 3 - k
            P.op("dve", lambda e: e.tensor_tensor(Aa[:], Aa[:], msb[:, jj, :], ALU.mult),
                 reads=[ab, "msb"], writes=[ab])

    def sb_s5(i):
        h, g, k, j, njt = items[i]
        ab = "Ab%d" % (i % 3)
        Aa = Ab[i % 3]
        ob = 4 + (2 * g + h) % 3
        P.op("pe", lambda e: e.matmul(ps[ob][0:64, :], V[:, j, h * 64:(h + 1) * 64], Aa[:],
                                      start=(k == 0), stop=(k == njt - 1), skip_group_check=True),
             reads=["V", ab], writes=[psn[ob]], signal=True)
        if k == njt - 1:
            so = "ost%d" % (g % 2)
            P.op("dve", lambda e: e.tensor_copy(ost[g % 2][h * 64:(h + 1) * 64, :], ps[ob][0:64, :]),
                 reads=[psn[ob]], writes=[so + "_%d" % h])
            if h == 1:
                P.dma("sp", so, oT(0, g), ost[g % 2][:],
                      reads=[so + "_0", so + "_1"], writes=["oT0_%d" % g])

    order = [(sb_s1, 0), (sb_s2, 1), (sb_s4, 3), (sb_s2b, 1), (sb_s3, 2), (sb_s5, 4)]
    for step in range(NI + 4):
        for fn, lag in order:
            i = step - lag
            if 0 <= i < NI:
                fn(i)

    P.barrier()
    if not parts & 2:
        return
    QTa = [R[0], R[1]]
    KTa = [R[2], R[3]]
    cosv, sinv = C["cosT"], C["sinT"]
    csb = [[A("cs%d_%d" % (a, i), [128, 512], F32) for i in range(2)] for a in range(2)]
    t1 = A("t1", [128, 512], F32)
    t2 = A("t2", [128, 512], F32)
    qf = A("qf", [128, 512], F32)
    kf = A("kf", [128, 512], F32)
    kmT = A("kmT", [128, 2, 32], F32)
    gm = [A("gm%d" % i, [128, 32], F32) for i in range(2)]
    top8 = [A("top8_%d" % i, [128, 8], F32) for i in range(2)]
    mb1 = [A("mb1_%d" % i, [128, 32], F32) for i in range(2)]
    mbpad = [A("mbpad%d" % i, [128, 128], BF16) for i in range(8)]
    cbrow = A("cbrow", [128, 64], F32)
    vbrow = A("vbrow", [128, 64], F32)
    oorow = A("oorow", [128, 64], F32)
    Pb = [A("Pb%d" % i, [128, 512], BF16) for i in range(3)]
    rec = A("rec", [128, 512], F32)
    bcs = A("bcs", [64, 512], F32)

    P.op("dve", lambda e: e.memset(cbrow[:, 0:32], 0.0), writes=["cbrow"])
    P.op("dve", lambda e: e.memset(cbrow[:, 32:64], NEG), writes=["cbrow"])
    P.op("dve", lambda e: e.memset(vbrow[:, 0:32], BIG), writes=["vbrow"])
    P.op("dve", lambda e: e.memset(vbrow[:, 32:64], 0.0), writes=["vbrow"])
    P.op("dve", lambda e: e.memset(oorow[:], -BIG), writes=["oorow"])
    P.op("dve", lambda e: e.memset(oorow[:, 31:32], 0.0), writes=["oorow"])
    P.op("dve", lambda e: e.memset(kmT[:], 0.0), writes=["kmT"])
    P.op("dve", lambda e: e.memset(Vm[:], 1.0), writes=["Vm"])
    for i in range(8):
        P.op("dve", lambda e, i=i: e.memset(mbpad[i][:], 0.0), writes=["mbpad%d" % i])
    for h in range(2):
        P.op("dve", lambda e, h=h: e.memset(QTa[h][64:128, :], 0.0), writes=["QTa%d" % h])
        P.op("dve", lambda e, h=h: e.memset(KTa[h][64:128, :], 0.0), writes=["KTa%d" % h])
        P.dma("pool", "oh%d" % h, KTa[h][64:96, :], C["onehot"], writes=["KTa%d" % h])

    def load_cs(tg):
        b = tg % 2
        P.dma("sp", "cs0_%d" % b, csb[0][b][:], cosv[:, tg * 512:(tg + 1) * 512], writes=["cs0_%d" % b])
        P.dma("sp", "cs1_%d" % b, csb[1][b][:], sinv[:, tg * 512:(tg + 1) * 512], writes=["cs1_%d" % b])

    def rope(dst, b, bank_a, bank_b):
        cn, sn = "cs0_%d" % b, "cs1_%d" % b
        P.op("dve", lambda e: e.tensor_tensor(t1[:], ps[bank_a][:], csb[0][b][:], ALU.mult),
             reads=[psn[bank_a], cn], writes=["t1"])
        P.op("dve", lambda e: e.tensor_tensor(t2[:], ps[bank_b][:], csb[1][b][:], ALU.mult),
             reads=[psn[bank_b], sn], writes=["t2"])
        nm = "qf" if dst is qf else "kf"
        P.op("dve", lambda e: e.tensor_tensor(dst[:], t1[:], t2[:], ALU.add),
             reads=["t1", "t2"], writes=[nm])

    def gate_tail(tgp):
        for idx in range(8):
            cq, h = idx // 2, idx % 2
            cch = tgp * 4 + cq
            u = idx % 2
            P.op("pe", lambda e, u=u, idx=idx: e.matmul(ps[6 + u][:, 0:128], mbpad[idx][:], identb[:],
                                                        start=True, stop=True),
                 reads=["mbpad%d" % idx, "identb"], writes=["ps6_%d" % u], signal=False)
            P.op("pe", lambda e, u=u: e.matmul(ps[6 + u][:, 256:384], identb[:], identb[:],
                                               start=True, stop=True),
                 reads=["identb"], writes=["ps6_%d" % u])
            P.op("act", lambda e, h=h, cch=cch, u=u: e.copy(QTa[h][64:96, cch * 128:(cch + 1) * 128],
                                                          ps[6 + u][64:96, 0:128]),
                 reads=["ps6_%d" % u], writes=["QTa%d" % h])

    load_x(0)
    load_cs(0)
    for tg in range(NG):
        if tg + 1 < NG:
            load_x(tg + 1)
            load_cs(tg + 1)
        b = tg % 2
        sl = slice(tg * 512, (tg + 1) * 512)
        proj_fm(0, 512, tg)
        proj_fm(1, 896, tg)
        rope(kf, b, 0, 1)
        for h in range(2):
            P.op("act", lambda e, h=h: e.copy(KTa[h][0:64, sl], kf[h * 64:(h + 1) * 64, :]),
                 reads=["kf"], writes=["KTa%d" % h])
        for h in range(2):
            hp2 = slice(h * 64, (h + 1) * 64)
            P.op("dve", lambda e, h=h, hp2=hp2: e.tensor_reduce(
                kmT[hp2, h, 2 * tg:2 * tg + 2], kf[hp2, :].rearrange("p (a b) -> p a b", a=2), AX.X, ALU.add),
                reads=["kf"], writes=["kmT"])
        proj_fm(2, 384, tg)
        proj_fm(3, 768, tg)
        rope(qf, b, 2, 3)
        for h in range(2):
            P.op("act", lambda e, h=h: e.mul(QTa[h][0:64, sl], qf[h * 64:(h + 1) * 64, :], 0.125),
                 reads=["qf"], writes=["QTa%d" % h])
        proj_tm(4, 640, tg)
        P.op("act", lambda e: e.copy(Vm[:, tg * 4:(tg + 1) * 4, :, 0:64],
                                    ps[4][:].rearrange("p (a h d) -> p a h d", a=4, h=2)),
             reads=["ps4"], writes=["Vm"])
        if tg > 0:
            gate_tail(tg - 1)
        for idx in range(8):
            cq, h = idx // 2, idx % 2
            P.op("pe", lambda e, cq=cq, h=h, idx=idx: e.matmul(
                ps[5][:, idx * 32:(idx + 1) * 32], qf[:, cq * 128:(cq + 1) * 128],
                kmT[:, h, :], start=True, stop=True),
                reads=["qf", "kmT"], writes=["ps5"], signal=False)
        P.op("pe", lambda e: e.matmul(ps[5][:, 256:384], identb[:], identb[:], start=True, stop=True),
             reads=["identb"], writes=["ps5"])
        for idx in range(8):
            cq, h = idx // 2, idx % 2
            cch = tg * 4 + cq
            own = cch // 2
            u = idx % 2
            P.op("dve", lambda e, idx=idx, own=own, u=u: e.tensor_tensor(
                gm[u][:], ps[5][:, idx * 32:(idx + 1) * 32], cbrow[:, 32 - own:64 - own], ALU.add),
                reads=["ps5", "cbrow"], writes=["gm%d" % u])
            P.op("dve", lambda e, u=u: e.max(top8[u][:], gm[u][:]), reads=["gm%d" % u], writes=["top8_%d" % u])
            P.op("dve", lambda e, own=own, u=u: e.scalar_tensor_tensor(
                mb1[u][:], gm[u][:], top8[u][:, 2:3], vbrow[:, 32 - own:64 - own], ALU.is_ge, ALU.mult),
                reads=["gm%d" % u, "top8_%d" % u, "vbrow"], writes=["mb1_%d" % u])
            P.op("dve", lambda e, own=own, u=u, idx=idx: e.tensor_tensor(
                mbpad[idx][:, 64:96], mb1[u][:], oorow[:, 31 - own:63 - own], ALU.add),
                reads=["mb1_%d" % u, "oorow"], writes=["mbpad%d" % idx])
    gate_tail(NG - 1)

    P.barrier()
    Pb2 = [A("Pb2_%d" % i, [128, 1024], BF16) for i in range(3)]
    mitems = []
    for g in range(NG):
        njt = 4 * g + 4
        for j in range(njt):
            mitems.append((g, j, njt))
    NM = len(mitems)

    def mb_s1(i):
        g, j, njt = mitems[i]
        zp = i % 2
        for h in range(2):
            P.op("pe", lambda e, h=h: e.matmul(psbig[:, zp * 1024 + h * 512:zp * 1024 + (h + 1) * 512],
                                               KTa[h][:, j * 128:(j + 1) * 128], QTa[h][:, g * 512:(g + 1) * 512],
                                               start=True, stop=True),
                 reads=["QTa%d" % h, "KTa%d" % h], writes=["zp%d" % zp], signal=(h == 1))

    def mb_s2(i):
        g, j, njt = mitems[i]
        zp = i % 2
        pb = "Pb2_%d" % (i % 3)
        Pt = Pb2[i % 3]
        P.op("act", lambda e: e.activation(Pt[:], psbig[:, zp * 1024:(zp + 1) * 1024], AF.Exp),
             reads=["zp%d" % zp], writes=[pb])
        if j >= 4 * g:
            jj = j - 4 * g
            for h in range(2):
                P.op("dve", lambda e, h=h: e.tensor_tensor(Pt[:, h * 512:(h + 1) * 512], Pt[:, h * 512:(h + 1) * 512],
                                                           mmb[:, jj, :], ALU.mult),
                     reads=[pb, "mmb"], writes=[pb])

    def mb_s3(i):
        g, j, njt = mitems[i]
        pb = "Pb2_%d" % (i % 3)
        Pt = Pb2[i % 3]
        for h in range(2):
            ob = 4 + (2 * g + h) % 3
            P.op("pe", lambda e, h=h, ob=ob: e.matmul(ps[ob][:, :], Vm[:, j, h, :], Pt[:, h * 512:(h + 1) * 512],
                                                      start=(j == 0), stop=(j == njt - 1), skip_group_check=True),
                 reads=["Vm", pb], writes=[psn[ob]], signal=True)
            if j == njt - 1:
                so = "ost%d" % (g % 2)
                P.op("dve", lambda e, ob=ob: e.reciprocal(rec[64:128, :], ps[ob][64:128, :]),
                     reads=[psn[ob]], writes=["rec"])
                P.op("dve", lambda e, h=h, ob=ob: e.tensor_tensor(ost[g % 2][h * 64:(h + 1) * 64, :], ps[ob][0:64, :],
                                                                  rec[64:128, :], ALU.mult),
                     reads=[psn[ob], "rec"], writes=[so + "_%d" % h])
                if h == 1:
                    P.dma("sp", so, oT(1, g), ost[g % 2][:],
                          reads=[so + "_0", so + "_1"], writes=["oT1_%d" % g])
                    if after_mb_group is not None:
                        after_mb_group(g)

    mstages = [mb_s1, mb_s2, mb_s3]
    for step in range(NM + len(mstages) - 1):
        for s_, fn in enumerate(mstages):
            i = step - s_
            if 0 <= i < NM:
                fn(i)
    P.barrier()


def build_B(nc, P, ctx, sfx, xT_d, oTs_v, oTm_v, oreads, W, lnp_d, out_d, outb_d=None):
    def A(name, shape, dt):
        return ctx.enter_context(nc.sbuf_tensor(name + sfx, shape, dt))
    xT = A("xT_sb", [128, 8, TOK], F32)
    xb = A("xb_sb", [128, 8, TOK], BF16)
    arena = A("arena", [128, 32768], BF16)
    oTs = arena[:, 0:8192].rearrange("p (a b) -> p a b", a=4)
    oTm = arena[:, 8192:16384].rearrange("p (a b) -> p a b", a=4)
    mg = arena[:, 16384:32768].rearrange("p (a b) -> p a b", a=8)
    hT = arena[:, 0:NFC * 1024].rearrange("p (a b) -> p a b", a=NFC)
    lnp = A("lnp_sb", [128, 32], F32)
    onesD = A("onesD", [128, 128], F32)
    warena = A("warena", [128, 9728], BF16)
    wgb = [warena[:, i * 2048:(i + 1) * 2048].rearrange("p (a b) -> p a b", a=8) for i in range(2)]
    wbb = [warena[:, 4096 + i * 1024:4096 + (i + 1) * 1024].rearrange("p (a b) -> p a b", a=4) for i in range(2)]
    wob = [warena[:, 6144 + i * 1024:6144 + (i + 1) * 1024].rearrange("p (a b) -> p a b", a=8) for i in range(2)]
    wfgu = [warena[:, i * 2048:(i + 1) * 2048].rearrange("p (a b) -> p a b", a=8) for i in range(2)]
    wfdb = [warena[:, 4096 + i * 2816:4096 + (i + 1) * 2816].rearrange("p (a b) -> p a b", a=NFC) for i in range(2)]
    sg = [A("sg%d" % i, [128, 512], F32) for i in range(4)]
    m1 = [A("m1_%d" % i, [128, 512], F32) for i in range(2)]
    m2 = [A("m2_%d" % i, [128, 512], F32) for i in range(2)]
    ysq = [A("ysq%d" % i, [128, 512], F32) for i in range(2)]
    mean_sb = A("mean_sb", [128, 512], F32)
    rstd_sb = A("rstd_sb", [128, 512], F32)
    tn = [A("tn%d" % i, [128, 512], F32) for i in range(2)]
    ps = [ctx.enter_context(nc.psum_tensor("pb%d" % i + sfx, [128, 512], F32)) for i in range(8)]
    psn = ["pb%d" % i for i in range(8)]
    bank = [0]
    XN = [["xT%d_%d" % (c, g) for g in range(NTG)] for c in range(8)]
    XB = [["xb%d_%d" % (c, g) for g in range(NTG)] for c in range(8)]
    MG = [["mg%d_%d" % (c, g) for g in range(NTG)] for c in range(8)]
    HT = [["hT%d_%d" % (k, g) for g in range(2)] for k in range(NFC)]
    allx = [n for r_ in XN for n in r_]
    allxb = [n for r_ in XB for n in r_]

    def nb():
        bank[0] = (bank[0] + 1) % 8
        return bank[0]

    xdv = xT_d.rearrange("(c p) t -> p c t", p=128)
    P.dma("sp", "xT", xT[:], xdv, writes=allx)
    P.dma("pool", "xb", xb[:], xdv, writes=allxb)
    P.dma("sp", "oTs", oTs, oTs_v, reads=oreads, writes=["oTs"])
    P.dma("sp", "oTm", oTm, oTm_v, reads=oreads, writes=["oTm"])
    P.dma("sp", "lnp", lnp[:], lnp_d, writes=["lnp"])
    P.op("dve", lambda e: e.memset(onesD[:], 1.0 / D), writes=["onesD"])

    wGv = W["wG"].rearrange("(k p) n -> p k n", p=128)
    wbsv = W["wbs"].rearrange("(k p) n -> p k n", p=128)
    wbmv = W["wbm"].rearrange("(k p) n -> p k n", p=128)
    wov = W["wout"].rearrange("(k p) n -> p k n", p=128)
    wfgv = W["wfg"].rearrange("(k p) n -> p k n", p=128)
    wfuv = W["wfu"].rearrange("(k p) n -> p k n", p=128)
    wfdv = W["wfd"].rearrange("(k p) n -> p k n", p=128)

    def tsl(tg):
        return slice(tg * 512, (tg + 1) * 512)

    def load_b1(c):
        b = c % 2
        cs = slice(c * 128, (c + 1) * 128)
        P.dma("pool", "wgb%da" % b, wgb[b][:, :, 0:128], wGv[:, :, cs], writes=["wgb%da" % b])
        P.dma("pool", "wgb%db" % b, wgb[b][:, :, 128:256], wGv[:, :, D + c * 128:D + (c + 1) * 128],
              writes=["wgb%db" % b])
        P.dma("pool", "wbb%da" % b, wbb[b][:, :, 0:128], wbsv[:, :, cs], writes=["wbb%da" % b])
        P.dma("pool", "wbb%db" % b, wbb[b][:, :, 128:256], wbmv[:, :, cs], writes=["wbb%db" % b])

    load_b1(0)
    it = 0
    for c in range(8):
        if c + 1 < 8:
            load_b1(c + 1)
        b = c % 2
        for tg in range(NTG):
            t = tsl(tg)
            bk = [nb() for _ in range(4)]
            for half, (bkk, wn) in enumerate(zip(bk[0:2], ["wgb%da" % b, "wgb%db" % b])):
                for k in range(8):
                    P.op("pe", lambda e, k=k, half=half, bkk=bkk: e.matmul(
                        ps[bkk][:], wgb[b][:, k, half * 128:(half + 1) * 128], xb[:, k, t],
                        start=(k == 0), stop=(k == 7)),
                        reads=[wn, XB[k][tg]], writes=[psn[bkk]], signal=(k == 7))
            for half, (bkk, wn, src, sn) in enumerate(zip(bk[2:4], ["wbb%da" % b, "wbb%db" % b],
                                                          [oTs, oTm], ["oTs", "oTm"])):
                for k in range(4):
                    P.op("pe", lambda e, k=k, half=half, bkk=bkk, src=src: e.matmul(
                        ps[bkk][:], wbb[b][:, k, half * 128:(half + 1) * 128], src[:, k, t],
                        start=(k == 0), stop=(k == 3)),
                        reads=[wn, sn], writes=[psn[bkk]], signal=(k == 3))
            u = it % 2
            s0, s1 = sg[2 * u], sg[2 * u + 1]
            P.op("act", lambda e, s0=s0, bkk=bk[0]: e.activation(s0[:], ps[bkk][:], AF.Sigmoid),
                 reads=[psn[bk[0]]], writes=["sg%d" % (2 * u)])
            P.op("act", lambda e, s1=s1, bkk=bk[1]: e.activation(s1[:], ps[bkk][:], AF.Sigmoid),
                 reads=[psn[bk[1]]], writes=["sg%d" % (2 * u + 1)])
            P.op("dve", lambda e, s0=s0, u=u, bkk=bk[2]: e.tensor_tensor(m1[u][:], s0[:], ps[bkk][:], ALU.mult),
                 reads=[psn[bk[2]], "sg%d" % (2 * u)], writes=["m1_%d" % u])
            P.op("dve", lambda e, s1=s1, u=u, bkk=bk[3]: e.tensor_tensor(m2[u][:], s1[:], ps[bkk][:], ALU.mult),
                 reads=[psn[bk[3]], "sg%d" % (2 * u + 1)], writes=["m2_%d" % u])
            P.op("pool", lambda e, u=u, c=c, t=t: e.tensor_tensor(mg[:, c, t], m1[u][:], m2[u][:], ALU.add),
                 reads=["m1_%d" % u, "m2_%d" % u], writes=[MG[c][tg]])
            it += 1

    def layer_norm(tg, gcol, bcol):
        t = tsl(tg)
        ba, bb_ = nb(), nb()
        for c in range(8):
            q = ysq[c % 2]
            P.op("act", lambda e, q=q, c=c: e.activation(q[:], xT[:, c, t], AF.Square),
                 reads=[XN[c][tg]], writes=["ysq%d" % (c % 2)])
            P.op("pe", lambda e, c=c: e.matmul(ps[ba][:], onesD[:], xT[:, c, t], start=(c == 0), stop=(c == 7)),
                 reads=["onesD", XN[c][tg]], writes=[psn[ba]], signal=(c == 7))
            P.op("pe", lambda e, q=q, c=c: e.matmul(ps[bb_][:], onesD[:], q[:], start=(c == 0), stop=(c == 7),
                                                   skip_group_check=True),
                 reads=["onesD", "ysq%d" % (c % 2)], writes=[psn[bb_]], signal=True)
        P.op("dve", lambda e: e.tensor_copy(mean_sb[:], ps[ba][:]), reads=[psn[ba]], writes=["mean_sb"])
        P.op("dve", lambda e: e.tensor_tensor(rstd_sb[:], mean_sb[:], mean_sb[:], ALU.mult),
             reads=["mean_sb"], writes=["rstd_sb"])
        P.op("dve", lambda e: e.tensor_tensor(rstd_sb[:], ps[bb_][:], rstd_sb[:], ALU.subtract),
             reads=[psn[bb_], "rstd_sb"], writes=["rstd_sb"])
        P.op("act", lambda e: e.activation(rstd_sb[:], rstd_sb[:], AF.Ln, bias=EPS),
             reads=["rstd_sb"], writes=["rstd_sb"])
        P.op("act", lambda e: e.activation(rstd_sb[:], rstd_sb[:], AF.Exp, scale=-0.5),
             reads=["rstd_sb"], writes=["rstd_sb"])
        for c in range(8):
            tt = tn[c % 2]
            tnn = "tn%d" % (c % 2)
            P.op("dve", lambda e, tt=tt, c=c: e.tensor_tensor(tt[:], xT[:, c, t], mean_sb[:], ALU.subtract),
                 reads=[XN[c][tg], "mean_sb"], writes=[tnn])
            P.op("dve", lambda e, tt=tt: e.tensor_tensor(tt[:], tt[:], rstd_sb[:], ALU.mult),
                 reads=[tnn, "rstd_sb"], writes=[tnn])
            P.op("act", lambda e, tt=tt, c=c: e.activation(xT[:, c, t], tt[:], AF.Identity,
                                                          bias=lnp[:, bcol + c:bcol + c + 1],
                                                          scale=lnp[:, gcol + c:gcol + c + 1]),
                 reads=[tnn, "lnp"], writes=[XN[c][tg]])
            P.op("pool", lambda e, c=c: e.tensor_copy(xb[:, c, t], xT[:, c, t]), reads=[XN[c][tg]], writes=[XB[c][tg]])

    def load_wo(c):
        b = c % 2
        P.dma("pool", "wob%d" % b, wob[b][:], wov[:, :, c * 128:(c + 1) * 128], writes=["wob%d" % b])

    load_wo(0)
    for c in range(8):
        if c + 1 < 8:
            load_wo(c + 1)
        b = c % 2
        for tg in range(NTG):
            t = tsl(tg)
            bkk = nb()
            for k in range(8):
                P.op("pe", lambda e, k=k, bkk=bkk: e.matmul(ps[bkk][:], wob[b][:, k, :], mg[:, k, t],
                                                            start=(k == 0), stop=(k == 7)),
                     reads=["wob%d" % b, MG[k][tg]], writes=[psn[bkk]], signal=(k == 7))
            P.op("dve", lambda e, bkk=bkk, c=c, t=t: e.scalar_tensor_tensor(
                xT[:, c, t], xT[:, c, t], ALPHA, ps[bkk][:], ALU.mult, ALU.add),
                reads=[psn[bkk], XN[c][tg]], writes=[XN[c][tg]])
    for tg in range(NTG):
        layer_norm(tg, 0, 8)

    def load_gu(i):
        k = i % NFC
        b = i % 2
        P.dma("pool", "wfgu%da" % b, wfgu[b][:, :, 0:128], wfgv[:, :, k * 128:(k + 1) * 128],
              writes=["wfgu%da" % b])
        P.dma("pool", "wfgu%db" % b, wfgu[b][:, :, 128:256], wfuv[:, :, k * 128:(k + 1) * 128],
              writes=["wfgu%db" % b])

    def load_dn(i):
        c = i % 8
        b = i % 2
        P.dma("pool", "wfdb%d" % b, wfdb[b][:], wfdv[:, :, c * 128:(c + 1) * 128], writes=["wfdb%d" % b])

    gi = 0
    di = 0
    P.barrier()
    for hf in range(2):
        load_gu(gi)
        for k in range(NFC):
            if k + 1 < NFC:
                load_gu(gi + 1)
            b = gi % 2
            for t2_ in range(2):
                tg = hf * 2 + t2_
                t = tsl(tg)
                b0, b1 = nb(), nb()
                for half, bkk in enumerate([b0, b1]):
                    wn = "wfgu%d%s" % (b, "ab"[half])
                    for kk in range(8):
                        P.op("pe", lambda e, kk=kk, half=half, bkk=bkk: e.matmul(
                            ps[bkk][:], wfgu[b][:, kk, half * 128:(half + 1) * 128], xb[:, kk, t],
                            start=(kk == 0), stop=(kk == 7)),
                            reads=[wn, XB[kk][tg]], writes=[psn[bkk]], signal=(kk == 7))
                u = it % 2
                s0 = sg[2 * u]
                P.op("act", lambda e, s0=s0, b0=b0: e.activation(s0[:], ps[b0][:], AF.Silu),
                     reads=[psn[b0]], writes=["sg%d" % (2 * u)])
                P.op("dve", lambda e, s0=s0, b1=b1, k=k, t2_=t2_: e.tensor_tensor(
                    hT[:, k, t2_ * 512:(t2_ + 1) * 512], s0[:], ps[b1][:], ALU.mult),
                    reads=[psn[b1], "sg%d" % (2 * u)], writes=[HT[k][t2_]])
                it += 1
            gi += 1
        load_dn(di)
        for c in range(8):
            if c + 1 < 8:
                load_dn(di + 1)
            b = di % 2
            for t2_ in range(2):
                tg = hf * 2 + t2_
                t = tsl(tg)
                bkk = nb()
                for k in range(NFC):
                    P.op("pe", lambda e, k=k, bkk=bkk, t2_=t2_: e.matmul(
                        ps[bkk][:], wfdb[b][:, k, :], hT[:, k, t2_ * 512:(t2_ + 1) * 512],
                        start=(k == 0), stop=(k == NFC - 1)),
                        reads=["wfdb%d" % b, HT[k][t2_]], writes=[psn[bkk]], signal=(k == NFC - 1))
                P.op("dve", lambda e, bkk=bkk, c=c, t=t: e.scalar_tensor_tensor(
                    xT[:, c, t], xT[:, c, t], ALPHA, ps[bkk][:], ALU.mult, ALU.add),
                    reads=[psn[bkk], XN[c][tg]], writes=[XN[c][tg]])
            di += 1
        for t2_ in range(2):
            layer_norm(hf * 2 + t2_, 16, 24)
    P.dma("sp", "out", out_d.rearrange("(c p) t -> p c t", p=128), xT[:], reads=allx, writes=["out"])
    if outb_d is not None:
        P.dma("sp", "outb", outb_d.rearrange("(c p) t -> p c t", p=128), xb[:], reads=allxb, writes=["outb"])


W_KEYS = ["wG", "wbs", "wbm", "wout", "wfg", "wfu", "wfd"]
W_SHAPES = {"wG": [D, 2048], "wbs": [512, D], "wbm": [512, D], "wout": [D, D],
            "wfg": [D, DFF], "wfu": [D, DFF], "wfd": [DFF, D]}
RG = [[0, 1, 2, 3], [4, 5, 6, 7]]


def _pp(v):
    return np.ascontiguousarray(v.reshape(8, 128).T)


def build_fused(nlayers=DEPTH, skipA=False, skipB=False, samex=False, parts=3):
    nc = bass.Bass("TRN2", target_bir_lowering=False)
    P = Prog(nc)
    hc = host_consts()
    C = {k: nc.dram_tensor("c_" + k, list(v.shape), F32, kind="ExternalInput").ap() for k, v in hc.items()}
    xTall0 = nc.dram_tensor("xTall0", [D, S], F32, kind="ExternalInput").ap()
    xT0 = nc.dram_tensor("xT0", [D, TOK], F32, kind="ExternalInput").ap()
    wA = nc.dram_tensor("wA", [DEPTH, D, 768], F32, kind="ExternalInput").ap()
    Wd = {k: nc.dram_tensor(k, [DEPTH] + W_SHAPES[k], F32, kind="ExternalInput").ap() for k in W_KEYS}
    lnp = nc.dram_tensor("lnp", [DEPTH, 128, 32], F32, kind="ExternalInput").ap()
    out = nc.dram_tensor("out", [D, TOK], F32, kind="ExternalOutput").ap()
    oT_loc = nc.dram_tensor("oT_loc", [1024, TOK], BF16, kind="Internal").ap()
    oT_all = nc.dram_tensor("oT_all", [4096, TOK], BF16, kind="Internal").ap()
    xb_loc = nc.dram_tensor("xb_loc", [D, TOK], BF16, kind="Internal").ap()
    xb_all = nc.dram_tensor("xb_all", [4 * D, TOK], BF16, kind="Internal").ap()
    xres = nc.dram_tensor("xres", [D, TOK], F32, kind="Internal").ap()
    pid = nc.sync.partition_id()
    roff = (pid % 4) * 1024
    olv = oT_loc.rearrange("(q b p) t -> b p q t", q=4, b=2)

    def odst(br, g):
        return olv[br, :, g // 4, (g % 4) * 512:(g % 4 + 1) * 512]

    for l in range(nlayers):
        last = l == nlayers - 1
        with ExitStack() as ctx:
            if l == 0 or samex:
                xv0 = xTall0.rearrange("(c p) t -> p c t", p=128)
                xsrc = lambda tg: [(0, 8, xv0[:, :, tg * 512:(tg + 1) * 512])]
                xeng, xreads = "pool", []
            else:
                xv1 = xb_all.rearrange("(j r c p) t -> p r j c t", j=4, r=4, c=2, p=128)
                xsrc = lambda tg: [(2 * j, 2 * j + 2, xv1[:, tg // 4, j, :, (tg % 4) * 512:(tg % 4 + 1) * 512])
                                   for j in range(4)]
                xeng, xreads = "sp", ["xb_all%d" % j for j in range(4)]
            if not skipA:
                issued = set()

                def exch(g, issued=issued):
                    if g % 4 == 3:
                        q = g // 4
                        P.collective("oT%d" % q, "AllGather", RG, oT_loc[q * 256:(q + 1) * 256, :],
                                     oT_all[q * 1024:(q + 1) * 1024, :],
                                     reads=["oT1_%d" % gg for gg in range(4 * q, 4 * q + 4)],
                                     writes=["oT_all%d" % q])
                        issued.add(q)

                build_A(nc, P, ctx, "_a%d" % l, xsrc, xeng, xreads, wA[l], odst, C, parts,
                        exch if parts == 3 else None)
        P.barrier()
        for q in range(4):
            if skipA or q not in issued:
                P.collective("oT%d" % q, "AllGather", RG, oT_loc[q * 256:(q + 1) * 256, :],
                             oT_all[q * 1024:(q + 1) * 1024, :], writes=["oT_all%d" % q])
        with ExitStack() as ctx:
            ov = oT_all[bass.ds(roff, 1024), :].rearrange("(k b p) t -> p k b t", k=4, b=2, p=128)
            oTs_v = ov[:, :, 0, :]
            oTm_v = ov[:, :, 1, :]
            Wl = {k: Wd[k][l] for k in W_KEYS}
            if not skipB:
              build_B(nc, P, ctx, "_b%d" % l, xT0 if l == 0 else xres, oTs_v, oTm_v,
                    ["oT_all%d" % q for q in range(4)], Wl, lnp[l],
                    out if last else xres, None if last else xb_loc)
        P.barrier()
        if not last:
            for j in range(4):
                P.collective("xb%d" % j, "AllGather", RG, xb_loc[j * 256:(j + 1) * 256, :],
                             xb_all[j * 1024:(j + 1) * 1024, :], writes=["xb_all%d" % j])
    P.finish("sp")
    return nc, hc


def _wA_slices(w_in_l, r):
    def cols(base):
        return w_in_l[:, base + 128 * r: base + 128 * r + 128]
    return np.concatenate([cols(0), cols(512), cols(1024), cols(1536), cols(2048), cols(2560)], axis=1)


def kernel(x, w_in, w_branch_sb, w_branch_moba, w_out, ln_mix_g, ln_mix_b,
           w_ffn_gate, w_ffn_up, w_ffn_down, ln_ffn_g, ln_ffn_b):
    f = lambda a: np.asarray(a, dtype=np.float32)
    x, w_in, w_branch_sb, w_branch_moba, w_out = f(x), f(w_in), f(w_branch_sb), f(w_branch_moba), f(w_out)
    ln_mix_g, ln_mix_b, ln_ffn_g, ln_ffn_b = f(ln_mix_g), f(ln_mix_b), f(ln_ffn_g), f(ln_ffn_b)
    w_ffn_gate, w_ffn_up, w_ffn_down = f(w_ffn_gate), f(w_ffn_up), f(w_ffn_down)
    nc, hc = build_fused()
    cores = list(range(8))
    xTb = [np.ascontiguousarray(x[b].T) for b in range(B)]
    Wfull = {"wG": np.ascontiguousarray(w_in[:, :, 3072:5120]), "wbs": w_branch_sb, "wbm": w_branch_moba,
             "wout": w_out, "wfg": w_ffn_gate, "wfu": w_ffn_up, "wfd": w_ffn_down}
    lnp = np.ascontiguousarray(np.stack([np.concatenate(
        [_pp(ln_mix_g[l]), _pp(ln_mix_b[l]), _pp(ln_ffn_g[l]), _pp(ln_ffn_b[l])], axis=1) for l in range(DEPTH)]))
    maps = []
    for c in cores:
        b, r = c // 4, c % 4
        m = {"xTall0": xTb[b], "xT0": np.ascontiguousarray(xTb[b][:, r * TOK:(r + 1) * TOK]),
             "wA": np.ascontiguousarray(np.stack([_wA_slices(w_in[l], r) for l in range(DEPTH)])),
             "lnp": lnp}
        m.update({k: np.ascontiguousarray(v) for k, v in Wfull.items()})
        m.update({"c_" + k: v for k, v in hc.items()})
        maps.append(m)
    res = run_bass_kernel_spmd(nc, maps, core_ids=cores)
    xn = np.empty_like(x)
    for c in cores:
        b, r = c // 4, c % 4
        xn[b, r * TOK:(r + 1) * TOK, :] = np.asarray(res.results[c]["out"]).T
    return xn
```

```python
from contextlib import ExitStack
import numpy as np
import concourse.bass as bass
import concourse.mybir as mybir
from concourse.bass_utils import run_bass_kernel_spmd

F32 = mybir.dt.float32
BF16 = mybir.dt.bfloat16
AF = mybir.ActivationFunctionType
ALU = mybir.AluOpType
AX = mybir.AxisListType

D = 1024
S = 8192
B = 2
DEPTH = 2
DFF = 2816
NFC = DFF // 128
NG = S // 512
ALPHA = (2 * DEPTH) ** 0.25
EPS = 1e-5
BIG = 30000.0
NEG = -1.0e30
TOK = 2048
NTG = TOK // 512


class Prog:
    SAME_ENGINE_SYNC = True
    EMBED = ("act", "dve")

    def __init__(self, nc):
        self.nc = nc
        self.eng = {"pe": nc.tensor, "act": nc.scalar, "dve": nc.vector,
                    "pool": nc.gpsimd, "sp": nc.sync}
        self.sem = {}
        self.cnt = {}
        self.res = {}
        self.known = {e: {} for e in self.eng}
        self.nwait = 0
        self.nops = 0
        self._pending = None

    def _sem(self, lane):
        if lane not in self.sem:
            self.sem[lane] = self.nc.alloc_semaphore(name="s_" + lane.replace(":", "_"))
            self.cnt[lane] = 0
        return self.sem[lane]

    def _need(self, e, lane, val):
        if lane == e and (e == "pe" or not self.SAME_ENGINE_SYNC):
            return
        if self.known[e].get(lane, 0) >= val:
            return
        self.known[e][lane] = val
        self.nwait += 1
        if self._pending is not None:
            self._pending.append((lane, val))
            return
        self.eng[e].wait_ge(self._sem(lane), val)

    def _deps(self, e, reads, writes):
        for r in reads:
            ent = self.res.get(r)
            if ent and ent[0]:
                self._need(e, *ent[0])
        for w in writes:
            ent = self.res.get(w)
            if ent:
                if ent[0]:
                    self._need(e, *ent[0])
                for lane, val in ent[1].items():
                    self._need(e, lane, val)

    def _record(self, lane, val, reads, writes):
        for r in reads:
            ent = self.res.setdefault(r, [None, {}])
            ent[1][lane] = max(ent[1].get(lane, 0), val)
        for w in writes:
            self.res[w] = [(lane, val), {}]

    def op(self, e, fn, reads=(), writes=(), signal=True):
        self._sem(e)
        self._pending = [] if e in self.EMBED else None
        self._deps(e, reads, writes)
        pend = self._pending or []
        self._pending = None
        for lane_, val_ in pend[:-1]:
            self.eng[e].wait_ge(self._sem(lane_), val_)
        ins = fn(self.eng[e])
        if pend:
            ins._wait_ge(self._sem(pend[-1][0]), pend[-1][1])
        val = self.cnt[e] + 1
        if signal:
            ins.then_inc(self.sem[e], 1)
            self.cnt[e] = val
        self._record(e, val, reads, writes)
        self.nops += 1
        return ins

    def dma(self, e, lane, out, in_, reads=(), writes=(), **kw):
        lane = "d:" + lane
        self._sem(lane)
        self._deps(e, reads, writes)
        ins = self.eng[e].dma_start(out=out, in_=in_, **kw)
        val = self.cnt[lane] + 16
        ins.then_inc(self.sem[lane], 16)
        self.cnt[lane] = val
        self._record(lane, val, reads, writes)
        self.nops += 1
        return ins

    def collective(self, lane, kind, rg, in_ap, out_ap, reads=(), writes=()):
        lane = "c:" + lane
        self._sem(lane)
        self._deps("pool", reads, writes)
        ins = self.nc.gpsimd.collective_compute(kind, ALU.bypass, replica_groups=rg,
                                                ins=[in_ap], outs=[out_ap])
        val = self.cnt[lane] + 1
        ins.then_inc(self.sem[lane], 1)
        self.cnt[lane] = val
        self._record(lane, val, reads, writes)
        self.nops += 1
        return ins

    def barrier(self):
        for e in self.eng:
            for lane, val in self.cnt.items():
                if val > 0 and lane != e:
                    if self.known[e].get(lane, 0) < val:
                        self.eng[e].wait_ge(self.sem[lane], val)
                        self.known[e][lane] = val
        self.res = {}

    def finish(self, e="sp"):
        for lane, val in self.cnt.items():
            if val > 0 and lane != e:
                self.eng[e].wait_ge(self.sem[lane], val)


def host_consts():
    c = {}
    k = np.arange(128)[:, None]
    s = np.arange(128)[None, :]
    c["negtri"] = np.where(k >= s, -1.0, 0.0).astype(np.float32)
    c["negones"] = np.full((128, 128), -1.0, np.float32)
    c["ident"] = np.eye(128, dtype=np.float32)
    sk = np.arange(128)[:, None, None] + 128 * np.arange(4)[None, :, None]
    t = np.arange(512)[None, None, :]
    c["msb"] = (sk < t).astype(np.float32)
    c["mmb"] = (sk <= t).astype(np.float32)
    half = 8
    inv = (500000.0 ** (-np.arange(0, 16, 2, dtype=np.float32) / 16)).astype(np.float32)
    ang = (np.arange(S, dtype=np.float32)[:, None] * inv[None, :]).astype(np.float32)
    cos = np.cos(ang).astype(np.float32).T
    sin = np.sin(ang).astype(np.float32).T
    cosT = np.ones((64, S), np.float32)
    sinT = np.zeros((64, S), np.float32)
    cosT[0:8] = cos
    cosT[8:16] = cos
    sinT[0:8] = sin
    sinT[8:16] = sin
    c["cosT"] = np.concatenate([cosT, cosT], 0)
    c["sinT"] = np.concatenate([sinT, sinT], 0)
    n = np.arange(32)[None, :]
    own = (np.arange(64) // 2)[:, None]
    cb = np.where(n < own, 0.0, NEG).astype(np.float32)
    vb = np.where(n < own, BIG, 0.0).astype(np.float32)
    oo = np.where(n == own, 0.0, -BIG).astype(np.float32)
    c["cb"] = np.broadcast_to(cb.reshape(1, 64 * 32), (128, 64 * 32)).copy()
    c["vb"] = np.broadcast_to(vb.reshape(1, 64 * 32), (128, 64 * 32)).copy()
    c["oo"] = np.broadcast_to(oo.reshape(1, 64 * 32), (128, 64 * 32)).copy()
    c["onehot"] = (np.arange(32)[:, None] == (np.arange(S)[None, :] // 256)).astype(np.float32)
    return c


def build_A(nc, P, ctx, sfx, xsrc, xeng, xreads, wA, oT, C, parts=3, after_mb_group=None):
    def A(name, shape, dt):
        return ctx.enter_context(nc.sbuf_tensor(name + sfx, shape, dt))
    wv = wA.rearrange("(c p) n -> p c n", p=128)

    negtri = A("negtri", [128, 128], BF16)
    negones = A("negones", [128, 128], BF16)
    identb = A("identb", [128, 128], BF16)
    onesf = A("onesf", [128, 64], F32)
    msb = A("msb", [128, 4, 512], BF16)
    mmb = A("mmb", [128, 4, 512], BF16)
    w = A("wA_sb", [128, 8, 1024], BF16)
    P.dma("pool", "c0", negtri[:], C["negtri"], writes=["negtri"])
    P.dma("pool", "c1", negones[:], C["negones"], writes=["negones"])
    P.dma("pool", "c2", identb[:], C["ident"], writes=["identb"])
    P.dma("pool", "c3", msb[:], C["msb"], writes=["msb"])
    P.dma("pool", "c4", mmb[:], C["mmb"], writes=["mmb"])
    P.dma("pool", "w", w[:, :, 0:768], wv, writes=["w"])
    P.op("dve", lambda e: e.memset(onesf[:], 1.0), writes=["onesf"])
    P.op("dve", lambda e: e.memset(w[:, :, 768:1024], 0.0), reads=["w"], writes=["w"])
    for qk in range(2):
        for hd in range(2):
            src = 384 + qk * 128 + hd * 64
            dst = 768 + qk * 128 + hd * 64
            P.op("act", lambda e, s_=src, d_=dst: e.mul(w[:, :, d_:d_ + 8], w[:, :, s_ + 8:s_ + 16], -1.0),
                 reads=["w"], writes=["w"])
            P.op("act", lambda e, s_=src, d_=dst: e.copy(w[:, :, d_ + 8:d_ + 16], w[:, :, s_:s_ + 8]),
                 reads=["w"], writes=["w"])

    psbig = ctx.enter_context(nc.psum_tensor("psbig" + sfx, [128, 4096], F32))
    ps = [psbig[:, i * 512:(i + 1) * 512] for i in range(8)]
    psn = ["ps%d" % i for i in range(7)]

    xg = [A("xg%d" % i, [128, 8, 512], BF16) for i in range(2)]
    ost = [A("ost%d" % i, [128, 512], BF16) for i in range(2)]

    def load_x(tg):
        b = tg % 2
        for pi, (c0, c1, sap) in enumerate(xsrc(tg)):
            P.dma(xeng, "xg%d_%d" % (b, pi), xg[b][:, c0:c1, :], sap, reads=xreads, writes=["xg%d" % b])

    def proj_fm(bank, col0, tg):
        b = tg % 2
        for c in range(8):
            P.op("pe", lambda e, c=c: e.matmul(ps[bank][:], w[:, c, col0:col0 + 128], xg[b][:, c, :],
                                               start=(c == 0), stop=(c == 7)),
                 reads=["w", "xg%d" % b], writes=[psn[bank]], signal=(c == 7))

    def proj_tm(bank, col0, tg, ncols=128):
        b = tg % 2
        for ts in range(4):
            for c in range(8):
                P.op("pe", lambda e, c=c, ts=ts: e.matmul(ps[bank][:, ts * 128:(ts + 1) * 128],
                                                          xg[b][:, c, ts * 128:(ts + 1) * 128],
                                                          w[:, c, col0:col0 + 128],
                                                          start=(c == 0), stop=(c == 7)),
                     reads=["w", "xg%d" % b], writes=[psn[bank]], signal=(c == 7 and ts == 3))

    R = [A("R%d" % i, [128, S], BF16) for i in range(4)]
    Vm = A("Vm", [128, 64, 2, 128], BF16)
    QT, KT = R[0], R[1]
    V = A("V", [128, 64, 128], BF16)
    if parts & 1:
        load_x(0)
    for tg in range(NG if parts & 1 else 0):
        if tg + 1 < NG:
            load_x(tg + 1)
        sl = slice(tg * 512, (tg + 1) * 512)
        proj_fm(0, 0, tg)
        P.op("act", lambda e, sl=sl: e.mul(QT[:, sl], ps[0][:], 0.125), reads=["ps0"], writes=["QT"])
        proj_fm(1, 128, tg)
        P.op("dve", lambda e, sl=sl: e.tensor_copy(KT[:, sl], ps[1][:]), reads=["ps1"], writes=["KT"])
        proj_tm(2, 256, tg)
        P.op("act", lambda e, tg=tg: e.copy(V[:, tg * 4:(tg + 1) * 4, :],
                                           ps[2][:].rearrange("p (a b) -> p a b", a=4)),
             reads=["ps2"], writes=["V"])

    Eb = [A("Eb%d" % i, [128, 512], F32) for i in range(2)]
    Lb = [A("Lb%d" % i, [128, 512], BF16) for i in range(3)]
    Ab = [A("Ab%d" % i, [128, 512], BF16) for i in range(3)]
    Lrun = [A("Lrun%d" % i, [128, 512], F32) for i in range(2)]
    Lrb = [[A("Lrb%d_%d" % (h, i), [128, 512], BF16) for i in range(2)] for h in range(2)]

    items = []
    for g in range(NG):
        njt = 4 * g + 4
        for k in range(njt):
            for h in range(2):
                items.append((h, g, k, njt - 1 - k, njt))
    NI = len(items) if parts & 1 else 0

    def sb_s1(i):
        h, g, k, j, njt = items[i]
        zb = i % 4
        hp = slice(h * 64, (h + 1) * 64)
        P.op("pe", lambda e: e.matmul(ps[zb][:], KT[hp, j * 128:(j + 1) * 128], QT[hp, g * 512:(g + 1) * 512],
                                      start=True, stop=True),
             reads=["QT", "KT"], writes=[psn[zb]])

    def sb_s2(i):
        h, g, k, j, njt = items[i]
        zb = i % 4
        eb = "Eb%d" % (i % 2)
        lb = "Lb%d" % (i % 3)
        E = Eb[i % 2]
        L = Lb[i % 3]
        P.op("act", lambda e: e.activation(E[:], ps[zb][:], AF.Exp), reads=[psn[zb]], writes=[eb])

    def sb_s2b(i):
        h, g, k, j, njt = items[i]
        zb = i % 4
        eb = "Eb%d" % (i % 2)
        lb = "Lb%d" % (i % 3)
        E = Eb[i % 2]
        L = Lb[i % 3]
        P.op("act", lambda e: e.activation(L[:], E[:], AF.Ln, bias=1.0), reads=[eb], writes=[lb])
        if k < 4:
            jj = 3 - k
            P.op("dve", lambda e: e.tensor_tensor(L[:], L[:], msb[:, jj, :], ALU.mult),
                 reads=[lb, "msb"], writes=[lb])
        if k < njt - 1:
            lr = "Lrun%d" % h
            nb = "Lrb%d_%d" % (h, (k + 1) % 2)
            dst = Lrb[h][(k + 1) % 2]
            if k == 0:
                P.op("dve", lambda e: e.tensor_copy(Lrun[h][:], L[:]), reads=[lb], writes=[lr])
                P.op("dve", lambda e: e.tensor_copy(dst[:], L[:]), reads=[lb], writes=[nb])
            else:
                P.op("dve", lambda e: e.tensor_tensor(Lrun[h][:], Lrun[h][:], L[:], ALU.add),
                     reads=[lb, lr], writes=[lr])
                P.op("dve", lambda e: e.tensor_copy(dst[:], Lrun[h][:]), reads=[lr], writes=[nb])

    def sb_s3(i):
        h, g, k, j, njt = items[i]
        zb = i % 4
        lb = "Lb%d" % (i % 3)
        L = Lb[i % 3]
        P.op("pe", lambda e: e.matmul(ps[zb][:], negtri[:], L[:], start=False, stop=(k == 0),
                                      skip_group_check=True),
             reads=["negtri", lb], writes=[psn[zb]], signal=(k == 0))
        if k > 0:
            cur = Lrb[h][k % 2]
            P.op("pe", lambda e: e.matmul(ps[zb][:], negones[:], cur[:], start=False, stop=True,
                                          skip_group_check=True),
                 reads=["negones", "Lrb%d_%d" % (h, k % 2)], writes=[psn[zb]])

    def sb_s4(i):
        h, g, k, j, njt = items[i]
        zb = i % 4
        ab = "Ab%d" % (i % 3)
        Aa = Ab[i % 3]
        P.op("act", lambda e: e.activation(Aa[:], ps[zb][:], AF.Exp), reads=[psn[zb]], writes=[ab])
        if k < 4:
            jj = 3 - k
            P.op("dve", lambda e: e.tensor_tensor(Aa[:], Aa[:], msb[:, jj, :], ALU.mult),
                 reads=[ab, "msb"], writes=[ab])

    def sb_s5(i):
        h, g, k, j, njt = items[i]
        ab = "Ab%d" % (i % 3)
        Aa = Ab[i % 3]
        ob = 4 + (2 * g + h) % 3
        P.op("pe", lambda e: e.matmul(ps[ob][0:64, :], V[:, j, h * 64:(h + 1) * 64], Aa[:],
                                      start=(k == 0), stop=(k == njt - 1), skip_group_check=True),
             reads=["V", ab], writes=[psn[ob]], signal=True)
        if k == njt - 1:
            so = "ost%d" % (g % 2)
            P.op("dve", lambda e: e.tensor_copy(ost[g % 2][h * 64:(h + 1) * 64, :], ps[ob][0:64, :]),
                 reads=[psn[ob]], writes=[so + "_%d" % h])
            if h == 1:
                P.dma("sp", so, oT(0, g), ost[g % 2][:],
                      reads=[so + "_0", so + "_1"], writes=["oT0_%d" % g])

    order = [(sb_s1, 0), (sb_s2, 1), (sb_s4, 3), (sb_s2b, 1), (sb_s3, 2), (sb_s5, 4)]
    for step in range(NI + 4):
        for fn, lag in order:
            i = step - lag
            if 0 <= i < NI:
                fn(i)

    P.barrier()
    if not parts & 2:
        return
    QTa = [R[0], R[1]]
    KTa = [R[2], R[3]]
    cosv, sinv = C["cosT"], C["sinT"]
    csb = [[A("cs%d_%d" % (a, i), [128, 512], F32) for i in range(2)] for a in range(2)]
    t1 = A("t1", [128, 512], F32)
    t2 = A("t2", [128, 512], F32)
    qf = A("qf", [128, 512], F32)
    kf = A("kf", [128, 512], F32)
    kmT = A("kmT", [128, 2, 32], F32)
    gm = [A("gm%d" % i, [128, 32], F32) for i in range(2)]
    top8 = [A("top8_%d" % i, [128, 8], F32) for i in range(2)]
    mb1 = [A("mb1_%d" % i, [128, 32], F32) for i in range(2)]
    mbpad = [A("mbpad%d" % i, [128, 128], BF16) for i in range(8)]
    cbrow = A("cbrow", [128, 64], F32)
    vbrow = A("vbrow", [128, 64], F32)
    oorow = A("oorow", [128, 64], F32)
    Pb = [A("Pb%d" % i, [128, 512], BF16) for i in range(3)]
    rec = A("rec", [128, 512], F32)
    bcs = A("bcs", [64, 512], F32)

    P.op("dve", lambda e: e.memset(cbrow[:, 0:32], 0.0), writes=["cbrow"])
    P.op("dve", lambda e: e.memset(cbrow[:, 32:64], NEG), writes=["cbrow"])
    P.op("dve", lambda e: e.memset(vbrow[:, 0:32], BIG), writes=["vbrow"])
    P.op("dve", lambda e: e.memset(vbrow[:, 32:64], 0.0), writes=["vbrow"])
    P.op("dve", lambda e: e.memset(oorow[:], -BIG), writes=["oorow"])
    P.op("dve", lambda e: e.memset(oorow[:, 31:32], 0.0), writes=["oorow"])
    P.op("dve", lambda e: e.memset(kmT[:], 0.0), writes=["kmT"])
    P.op("dve", lambda e: e.memset(Vm[:], 1.0), writes=["Vm"])
    for i in range(8):
        P.op("dve", lambda e, i=i: e.memset(mbpad[i][:], 0.0), writes=["mbpad%d" % i])
    for h in range(2):
        P.op("dve", lambda e, h=h: e.memset(QTa[h][64:128, :], 0.0), writes=["QTa%d" % h])
        P.op("dve", lambda e, h=h: e.memset(KTa[h][64:128, :], 0.0), writes=["KTa%d" % h])
        P.dma("pool", "oh%d" % h, KTa[h][64:96, :], C["onehot"], writes=["KTa%d" % h])

    def load_cs(tg):
        b = tg % 2
        P.dma("sp", "cs0_%d" % b, csb[0][b][:], cosv[:, tg * 512:(tg + 1) * 512], writes=["cs0_%d" % b])
        P.dma("sp", "cs1_%d" % b, csb[1][b][:], sinv[:, tg * 512:(tg + 1) * 512], writes=["cs1_%d" % b])

    def rope(dst, b, bank_a, bank_b):
        cn, sn = "cs0_%d" % b, "cs1_%d" % b
        P.op("dve", lambda e: e.tensor_tensor(t1[:], ps[bank_a][:], csb[0][b][:], ALU.mult),
             reads=[psn[bank_a], cn], writes=["t1"])
        P.op("dve", lambda e: e.tensor_tensor(t2[:], ps[bank_b][:], csb[1][b][:], ALU.mult),
             reads=[psn[bank_b], sn], writes=["t2"])
        nm = "qf" if dst is qf else "kf"
        P.op("dve", lambda e: e.tensor_tensor(dst[:], t1[:], t2[:], ALU.add),
             reads=["t1", "t2"], writes=[nm])

    def gate_tail(tgp):
        for idx in range(8):
            cq, h = idx // 2, idx % 2
            cch = tgp * 4 + cq
            u = idx % 2
            P.op("pe", lambda e, u=u, idx=idx: e.matmul(ps[6 + u][:, 0:128], mbpad[idx][:], identb[:],
                                                        start=True, stop=True),
                 reads=["mbpad%d" % idx, "identb"], writes=["ps6_%d" % u], signal=False)
            P.op("pe", lambda e, u=u: e.matmul(ps[6 + u][:, 256:384], identb[:], identb[:],
                                               start=True, stop=True),
                 reads=["identb"], writes=["ps6_%d" % u])
            P.op("act", lambda e, h=h, cch=cch, u=u: e.copy(QTa[h][64:96, cch * 128:(cch + 1) * 128],
                                                          ps[6 + u][64:96, 0:128]),
                 reads=["ps6_%d" % u], writes=["QTa%d" % h])

    load_x(0)
    load_cs(0)
    for tg in range(NG):
        if tg + 1 < NG:
            load_x(tg + 1)
            load_cs(tg + 1)
        b = tg % 2
        sl = slice(tg * 512, (tg + 1) * 512)
        proj_fm(0, 512, tg)
        proj_fm(1, 896, tg)
        rope(kf, b, 0, 1)
        for h in range(2):
            P.op("act", lambda e, h=h: e.copy(KTa[h][0:64, sl], kf[h * 64:(h + 1) * 64, :]),
                 reads=["kf"], writes=["KTa%d" % h])
        for h in range(2):
            hp2 = slice(h * 64, (h + 1) * 64)
            P.op("dve", lambda e, h=h, hp2=hp2: e.tensor_reduce(
                kmT[hp2, h, 2 * tg:2 * tg + 2], kf[hp2, :].rearrange("p (a b) -> p a b", a=2), AX.X, ALU.add),
                reads=["kf"], writes=["kmT"])
        proj_fm(2, 384, tg)
        proj_fm(3, 768, tg)
        rope(qf, b, 2, 3)
        for h in range(2):
            P.op("act", lambda e, h=h: e.mul(QTa[h][0:64, sl], qf[h * 64:(h + 1) * 64, :], 0.125),
                 reads=["qf"], writes=["QTa%d" % h])
        proj_tm(4, 640, tg)
        P.op("act", lambda e: e.copy(Vm[:, tg * 4:(tg + 1) * 4, :, 0:64],
                                    ps[4][:].rearrange("p (a h d) -> p a h d", a=4, h=2)),
             reads=["ps4"], writes=["Vm"])
        if tg > 0:
            gate_tail(tg - 1)
        for idx in range(8):
            cq, h = idx // 2, idx % 2
            P.op("pe", lambda e, cq=cq, h=h, idx=idx: e.matmul(
                ps[5][:, idx * 32:(idx + 1) * 32], qf[:, cq * 128:(cq + 1) * 128],
                kmT[:, h, :], start=True, stop=True),
                reads=["qf", "kmT"], writes=["ps5"], signal=False)
        P.op("pe", lambda e: e.matmul(ps[5][:, 256:384], identb[:], identb[:], start=True, stop=True),
             reads=["identb"], writes=["ps5"])
        for idx in range(8):
            cq, h = idx // 2, idx % 2
            cch = tg * 4 + cq
            own = cch // 2
            u = idx % 2
            P.op("dve", lambda e, idx=idx, own=own, u=u: e.tensor_tensor(
                gm[u][:], ps[5][:, idx * 32:(idx + 1) * 32], cbrow[:, 32 - own:64 - own], ALU.add),
                reads=["ps5", "cbrow"], writes=["gm%d" % u])
            P.op("dve", lambda e, u=u: e.max(top8[u][:], gm[u][:]), reads=["gm%d" % u], writes=["top8_%d" % u])
            P.op("dve", lambda e, own=own, u=u: e.scalar_tensor_tensor(
                mb1[u][:], gm[u][:], top8[u][:, 2:3], vbrow[:, 32 - own:64 - own], ALU.is_ge, ALU.mult),
                reads=["gm%d" % u, "top8_%d" % u, "vbrow"], writes=["mb1_%d" % u])
            P.op("dve", lambda e, own=own, u=u, idx=idx: e.tensor_tensor(
                mbpad[idx][:, 64:96], mb1[u][:], oorow[:, 31 - own:63 - own], ALU.add),
                reads=["mb1_%d" % u, "oorow"], writes=["mbpad%d" % idx])
    gate_tail(NG - 1)

    P.barrier()
    Pb2 = [A("Pb2_%d" % i, [128, 1024], BF16) for i in range(3)]
    mitems = []
    for g in range(NG):
        njt = 4 * g + 4
        for j in range(njt):
            mitems.append((g, j, njt))
    NM = len(mitems)

    def mb_s1(i):
        g, j, njt = mitems[i]
        zp = i % 2
        for h in range(2):
            P.op("pe", lambda e, h=h: e.matmul(psbig[:, zp * 1024 + h * 512:zp * 1024 + (h + 1) * 512],
                                               KTa[h][:, j * 128:(j + 1) * 128], QTa[h][:, g * 512:(g + 1) * 512],
                                               start=True, stop=True),
                 reads=["QTa%d" % h, "KTa%d" % h], writes=["zp%d" % zp], signal=(h == 1))

    def mb_s2(i):
        g, j, njt = mitems[i]
        zp = i % 2
        pb = "Pb2_%d" % (i % 3)
        Pt = Pb2[i % 3]
        P.op("act", lambda e: e.activation(Pt[:], psbig[:, zp * 1024:(zp + 1) * 1024], AF.Exp),
             reads=["zp%d" % zp], writes=[pb])
        if j >= 4 * g:
            jj = j - 4 * g
            for h in range(2):
                P.op("dve", lambda e, h=h: e.tensor_tensor(Pt[:, h * 512:(h + 1) * 512], Pt[:, h * 512:(h + 1) * 512],
                                                           mmb[:, jj, :], ALU.mult),
                     reads=[pb, "mmb"], writes=[pb])

    def mb_s3(i):
        g, j, njt = mitems[i]
        pb = "Pb2_%d" % (i % 3)
        Pt = Pb2[i % 3]
        for h in range(2):
            ob = 4 + (2 * g + h) % 3
            P.op("pe", lambda e, h=h, ob=ob: e.matmul(ps[ob][:, :], Vm[:, j, h, :], Pt[:, h * 512:(h + 1) * 512],
                                                      start=(j == 0), stop=(j == njt - 1), skip_group_check=True),
                 reads=["Vm", pb], writes=[psn[ob]], signal=True)
            if j == njt - 1:
                so = "ost%d" % (g % 2)
                P.op("dve", lambda e, ob=ob: e.reciprocal(rec[64:128, :], ps[ob][64:128, :]),
                     reads=[psn[ob]], writes=["rec"])
                P.op("dve", lambda e, h=h, ob=ob: e.tensor_tensor(ost[g % 2][h * 64:(h + 1) * 64, :], ps[ob][0:64, :],
                                                                  rec[64:128, :], ALU.mult),
                     reads=[psn[ob], "rec"], writes=[so + "_%d" % h])
                if h == 1:
                    P.dma("sp", so, oT(1, g), ost[g % 2][:],
                          reads=[so + "_0", so + "_1"], writes=["oT1_%d" % g])
                    if after_mb_group is not None:
                        after_mb_group(g)

    mstages = [mb_s1, mb_s2, mb_s3]
    for step in range(NM + len(mstages) - 1):
        for s_, fn in enumerate(mstages):
            i = step - s_
            if 0 <= i < NM:
                fn(i)
    P.barrier()


def build_B(nc, P, ctx, sfx, xT_d, oTs_v, oTm_v, oreads, W, lnp_d, out_d, outb_d=None, xb_exch=None):
    def A(name, shape, dt):
        return ctx.enter_context(nc.sbuf_tensor(name + sfx, shape, dt))
    xT = A("xT_sb", [128, 8, TOK], F32)
    xb = A("xb_sb", [128, 8, TOK], BF16)
    arena = A("arena", [128, 32768], BF16)
    oTs = arena[:, 0:8192].rearrange("p (a b) -> p a b", a=4)
    oTm = arena[:, 8192:16384].rearrange("p (a b) -> p a b", a=4)
    mg = arena[:, 16384:32768].rearrange("p (a b) -> p a b", a=8)
    hT = arena[:, 0:NFC * 1024].rearrange("p (a b) -> p a b", a=NFC)
    lnp = A("lnp_sb", [128, 32], F32)
    onesD = A("onesD", [128, 128], F32)
    warena = A("warena", [128, 9728], BF16)
    wgb = [warena[:, i * 2048:(i + 1) * 2048].rearrange("p (a b) -> p a b", a=8) for i in range(2)]
    wbb = [warena[:, 4096 + i * 1024:4096 + (i + 1) * 1024].rearrange("p (a b) -> p a b", a=4) for i in range(2)]
    wob = [warena[:, 6144 + i * 1024:6144 + (i + 1) * 1024].rearrange("p (a b) -> p a b", a=8) for i in range(2)]
    wfgu = [warena[:, i * 2048:(i + 1) * 2048].rearrange("p (a b) -> p a b", a=8) for i in range(2)]
    wfdb = [warena[:, 4096 + i * 2816:4096 + (i + 1) * 2816].rearrange("p (a b) -> p a b", a=NFC) for i in range(2)]
    sg = [A("sg%d" % i, [128, 512], F32) for i in range(4)]
    m1 = [A("m1_%d" % i, [128, 512], F32) for i in range(2)]
    m2 = [A("m2_%d" % i, [128, 512], F32) for i in range(2)]
    ysq = [A("ysq%d" % i, [128, 512], F32) for i in range(2)]
    mean_sb = A("mean_sb", [128, 512], F32)
    rstd_sb = A("rstd_sb", [128, 512], F32)
    tn = [A("tn%d" % i, [128, 512], F32) for i in range(2)]
    ps = [ctx.enter_context(nc.psum_tensor("pb%d" % i + sfx, [128, 512], F32)) for i in range(8)]
    psn = ["pb%d" % i for i in range(8)]
    bank = [0]
    XN = [["xT%d_%d" % (c, g) for g in range(NTG)] for c in range(8)]
    XB = [["xb%d_%d" % (c, g) for g in range(NTG)] for c in range(8)]
    MG = [["mg%d_%d" % (c, g) for g in range(NTG)] for c in range(8)]
    HT = [["hT%d_%d" % (k, g) for g in range(2)] for k in range(NFC)]
    allx = [n for r_ in XN for n in r_]
    allxb = [n for r_ in XB for n in r_]

    def nb():
        bank[0] = (bank[0] + 1) % 8
        return bank[0]

    xdv = xT_d.rearrange("(c p) t -> p c t", p=128)
    P.dma("sp", "xT", xT[:], xdv, writes=allx)
    P.dma("pool", "xb", xb[:], xdv, writes=allxb)
    P.dma("sp", "oTs", oTs, oTs_v, reads=oreads, writes=["oTs"])
    P.dma("sp", "oTm", oTm, oTm_v, reads=oreads, writes=["oTm"])
    P.dma("sp", "lnp", lnp[:], lnp_d, writes=["lnp"])
    P.op("dve", lambda e: e.memset(onesD[:], 1.0 / D), writes=["onesD"])

    wGv = W["wG"].rearrange("(k p) n -> p k n", p=128)
    wbsv = W["wbs"].rearrange("(k p) n -> p k n", p=128)
    wbmv = W["wbm"].rearrange("(k p) n -> p k n", p=128)
    wov = W["wout"].rearrange("(k p) n -> p k n", p=128)
    wfgv = W["wfg"].rearrange("(k p) n -> p k n", p=128)
    wfuv = W["wfu"].rearrange("(k p) n -> p k n", p=128)
    wfdv = W["wfd"].rearrange("(k p) n -> p k n", p=128)

    def tsl(tg):
        return slice(tg * 512, (tg + 1) * 512)

    def load_b1(c):
        b = c % 2
        cs = slice(c * 128, (c + 1) * 128)
        P.dma("pool", "wgb%da" % b, wgb[b][:, :, 0:128], wGv[:, :, cs], writes=["wgb%da" % b])
        P.dma("pool", "wgb%db" % b, wgb[b][:, :, 128:256], wGv[:, :, D + c * 128:D + (c + 1) * 128],
              writes=["wgb%db" % b])
        P.dma("pool", "wbb%da" % b, wbb[b][:, :, 0:128], wbsv[:, :, cs], writes=["wbb%da" % b])
        P.dma("pool", "wbb%db" % b, wbb[b][:, :, 128:256], wbmv[:, :, cs], writes=["wbb%db" % b])

    load_b1(0)
    it = 0
    for c in range(8):
        if c + 1 < 8:
            load_b1(c + 1)
        b = c % 2
        for tg in range(NTG):
            t = tsl(tg)
            bk = [nb() for _ in range(4)]
            for half, (bkk, wn) in enumerate(zip(bk[0:2], ["wgb%da" % b, "wgb%db" % b])):
                for k in range(8):
                    P.op("pe", lambda e, k=k, half=half, bkk=bkk: e.matmul(
                        ps[bkk][:], wgb[b][:, k, half * 128:(half + 1) * 128], xb[:, k, t],
                        start=(k == 0), stop=(k == 7)),
                        reads=[wn, XB[k][tg]], writes=[psn[bkk]], signal=(k == 7))
            for half, (bkk, wn, src, sn) in enumerate(zip(bk[2:4], ["wbb%da" % b, "wbb%db" % b],
                                                          [oTs, oTm], ["oTs", "oTm"])):
                for k in range(4):
                    P.op("pe", lambda e, k=k, half=half, bkk=bkk, src=src: e.matmul(
                        ps[bkk][:], wbb[b][:, k, half * 128:(half + 1) * 128], src[:, k, t],
                        start=(k == 0), stop=(k == 3)),
                        reads=[wn, sn], writes=[psn[bkk]], signal=(k == 3))
            u = it % 2
            s0, s1 = sg[2 * u], sg[2 * u + 1]
            P.op("act", lambda e, s0=s0, bkk=bk[0]: e.activation(s0[:], ps[bkk][:], AF.Sigmoid),
                 reads=[psn[bk[0]]], writes=["sg%d" % (2 * u)])
            P.op("act", lambda e, s1=s1, bkk=bk[1]: e.activation(s1[:], ps[bkk][:], AF.Sigmoid),
                 reads=[psn[bk[1]]], writes=["sg%d" % (2 * u + 1)])
            P.op("dve", lambda e, s0=s0, u=u, bkk=bk[2]: e.tensor_tensor(m1[u][:], s0[:], ps[bkk][:], ALU.mult),
                 reads=[psn[bk[2]], "sg%d" % (2 * u)], writes=["m1_%d" % u])
            P.op("dve", lambda e, s1=s1, u=u, bkk=bk[3]: e.tensor_tensor(m2[u][:], s1[:], ps[bkk][:], ALU.mult),
                 reads=[psn[bk[3]], "sg%d" % (2 * u + 1)], writes=["m2_%d" % u])
            P.op("pool", lambda e, u=u, c=c, t=t: e.tensor_tensor(mg[:, c, t], m1[u][:], m2[u][:], ALU.add),
                 reads=["m1_%d" % u, "m2_%d" % u], writes=[MG[c][tg]])
            it += 1

    def layer_norm(tg, gcol, bcol):
        t = tsl(tg)
        ba, bb_ = nb(), nb()
        for c in range(8):
            q = ysq[c % 2]
            P.op("act", lambda e, q=q, c=c: e.activation(q[:], xT[:, c, t], AF.Square),
                 reads=[XN[c][tg]], writes=["ysq%d" % (c % 2)])
            P.op("pe", lambda e, c=c: e.matmul(ps[ba][:], onesD[:], xT[:, c, t], start=(c == 0), stop=(c == 7)),
                 reads=["onesD", XN[c][tg]], writes=[psn[ba]], signal=(c == 7))
            P.op("pe", lambda e, q=q, c=c: e.matmul(ps[bb_][:], onesD[:], q[:], start=(c == 0), stop=(c == 7),
                                                   skip_group_check=True),
                 reads=["onesD", "ysq%d" % (c % 2)], writes=[psn[bb_]], signal=True)
        P.op("dve", lambda e: e.tensor_copy(mean_sb[:], ps[ba][:]), reads=[psn[ba]], writes=["mean_sb"])
        P.op("dve", lambda e: e.tensor_tensor(rstd_sb[:], mean_sb[:], mean_sb[:], ALU.mult),
             reads=["mean_sb"], writes=["rstd_sb"])
        P.op("dve", lambda e: e.tensor_tensor(rstd_sb[:], ps[bb_][:], rstd_sb[:], ALU.subtract),
             reads=[psn[bb_], "rstd_sb"], writes=["rstd_sb"])
        P.op("act", lambda e: e.activation(rstd_sb[:], rstd_sb[:], AF.Ln, bias=EPS),
             reads=["rstd_sb"], writes=["rstd_sb"])
        P.op("act", lambda e: e.activation(rstd_sb[:], rstd_sb[:], AF.Exp, scale=-0.5),
             reads=["rstd_sb"], writes=["rstd_sb"])
        for c in range(8):
            tt = tn[c % 2]
            tnn = "tn%d" % (c % 2)
            P.op("dve", lambda e, tt=tt, c=c: e.tensor_tensor(tt[:], xT[:, c, t], mean_sb[:], ALU.subtract),
                 reads=[XN[c][tg], "mean_sb"], writes=[tnn])
            P.op("dve", lambda e, tt=tt: e.tensor_tensor(tt[:], tt[:], rstd_sb[:], ALU.mult),
                 reads=[tnn, "rstd_sb"], writes=[tnn])
            P.op("act", lambda e, tt=tt, c=c: e.activation(xT[:, c, t], tt[:], AF.Identity,
                                                          bias=lnp[:, bcol + c:bcol + c + 1],
                                                          scale=lnp[:, gcol + c:gcol + c + 1]),
                 reads=[tnn, "lnp"], writes=[XN[c][tg]])
            P.op("pool", lambda e, c=c: e.tensor_copy(xb[:, c, t], xT[:, c, t]), reads=[XN[c][tg]], writes=[XB[c][tg]])

    def load_wo(c):
        b = c % 2
        P.dma("pool", "wob%d" % b, wob[b][:], wov[:, :, c * 128:(c + 1) * 128], writes=["wob%d" % b])

    load_wo(0)
    for c in range(8):
        if c + 1 < 8:
            load_wo(c + 1)
        b = c % 2
        for tg in range(NTG):
            t = tsl(tg)
            bkk = nb()
            for k in range(8):
                P.op("pe", lambda e, k=k, bkk=bkk: e.matmul(ps[bkk][:], wob[b][:, k, :], mg[:, k, t],
                                                            start=(k == 0), stop=(k == 7)),
                     reads=["wob%d" % b, MG[k][tg]], writes=[psn[bkk]], signal=(k == 7))
            P.op("dve", lambda e, bkk=bkk, c=c, t=t: e.scalar_tensor_tensor(
                xT[:, c, t], xT[:, c, t], ALPHA, ps[bkk][:], ALU.mult, ALU.add),
                reads=[psn[bkk], XN[c][tg]], writes=[XN[c][tg]])
    for tg in range(NTG):
        layer_norm(tg, 0, 8)

    def load_gu(i):
        k = i % NFC
        b = i % 2
        P.dma("pool", "wfgu%da" % b, wfgu[b][:, :, 0:128], wfgv[:, :, k * 128:(k + 1) * 128],
              writes=["wfgu%da" % b])
        P.dma("pool", "wfgu%db" % b, wfgu[b][:, :, 128:256], wfuv[:, :, k * 128:(k + 1) * 128],
              writes=["wfgu%db" % b])

    def load_dn(i):
        c = i % 8
        b = i % 2
        P.dma("pool", "wfdb%d" % b, wfdb[b][:], wfdv[:, :, c * 128:(c + 1) * 128], writes=["wfdb%d" % b])

    gi = 0
    di = 0
    pend_x = []
    P.barrier()
    for hf in range(2):
        load_gu(gi)
        for k in range(NFC):
            if k + 1 < NFC:
                load_gu(gi + 1)
            if k == 3 and xb_exch is not None:
                while pend_x:
                    xb_exch(pend_x.pop(0))
            b = gi % 2
            for t2_ in range(2):
                tg = hf * 2 + t2_
                t = tsl(tg)
                b0, b1 = nb(), nb()
                for half, bkk in enumerate([b0, b1]):
                    wn = "wfgu%d%s" % (b, "ab"[half])
                    for kk in range(8):
                        P.op("pe", lambda e, kk=kk, half=half, bkk=bkk: e.matmul(
                            ps[bkk][:], wfgu[b][:, kk, half * 128:(half + 1) * 128], xb[:, kk, t],
                            start=(kk == 0), stop=(kk == 7)),
                            reads=[wn, XB[kk][tg]], writes=[psn[bkk]], signal=(kk == 7))
                u = it % 2
                s0 = sg[2 * u]
                P.op("act", lambda e, s0=s0, b0=b0: e.activation(s0[:], ps[b0][:], AF.Silu),
                     reads=[psn[b0]], writes=["sg%d" % (2 * u)])
                P.op("dve", lambda e, s0=s0, b1=b1, k=k, t2_=t2_: e.tensor_tensor(
                    hT[:, k, t2_ * 512:(t2_ + 1) * 512], s0[:], ps[b1][:], ALU.mult),
                    reads=[psn[b1], "sg%d" % (2 * u)], writes=[HT[k][t2_]])
                it += 1
            gi += 1
        load_dn(di)
        for c in range(8):
            if c + 1 < 8:
                load_dn(di + 1)
            b = di % 2
            for t2_ in range(2):
                tg = hf * 2 + t2_
                t = tsl(tg)
                bkk = nb()
                for k in range(NFC):
                    P.op("pe", lambda e, k=k, bkk=bkk, t2_=t2_: e.matmul(
                        ps[bkk][:], wfdb[b][:, k, :], hT[:, k, t2_ * 512:(t2_ + 1) * 512],
                        start=(k == 0), stop=(k == NFC - 1)),
                        reads=["wfdb%d" % b, HT[k][t2_]], writes=[psn[bkk]], signal=(k == NFC - 1))
                P.op("dve", lambda e, bkk=bkk, c=c, t=t: e.scalar_tensor_tensor(
                    xT[:, c, t], xT[:, c, t], ALPHA, ps[bkk][:], ALU.mult, ALU.add),
                    reads=[psn[bkk], XN[c][tg]], writes=[XN[c][tg]])
            di += 1
        for t2_ in range(2):
            tg = hf * 2 + t2_
            layer_norm(tg, 16, 24)
            if outb_d is not None:
                P.dma("sp", "outb%d" % tg, outb_d[tg * 1024:(tg + 1) * 1024, :].rearrange("(c p) t -> p c t", p=128),
                      xb[:, :, tsl(tg)], reads=[XB[c][tg] for c in range(8)], writes=["xbloc%d" % tg])
                pend_x.append(tg)
    P.dma("sp", "out", out_d.rearrange("(c p) t -> p c t", p=128), xT[:], reads=allx, writes=["out"])
    if xb_exch is not None:
        while pend_x:
            xb_exch(pend_x.pop(0))


W_KEYS = ["wG", "wbs", "wbm", "wout", "wfg", "wfu", "wfd"]
W_SHAPES = {"wG": [D, 2048], "wbs": [512, D], "wbm": [512, D], "wout": [D, D],
            "wfg": [D, DFF], "wfu": [D, DFF], "wfd": [DFF, D]}
RG = [[0, 1, 2, 3], [4, 5, 6, 7]]


def _pp(v):
    return np.ascontiguousarray(v.reshape(8, 128).T)


def build_fused(nlayers=DEPTH, skipA=False, skipB=False, samex=False, parts=3):
    nc = bass.Bass("TRN2", target_bir_lowering=False)
    P = Prog(nc)
    hc = host_consts()
    C = {k: nc.dram_tensor("c_" + k, list(v.shape), F32, kind="ExternalInput").ap() for k, v in hc.items()}
    xTall0 = nc.dram_tensor("xTall0", [D, S], F32, kind="ExternalInput").ap()
    xT0 = nc.dram_tensor("xT0", [D, TOK], F32, kind="ExternalInput").ap()
    wA = nc.dram_tensor("wA", [DEPTH, D, 768], F32, kind="ExternalInput").ap()
    Wd = {k: nc.dram_tensor(k, [DEPTH] + W_SHAPES[k], F32, kind="ExternalInput").ap() for k in W_KEYS}
    lnp = nc.dram_tensor("lnp", [DEPTH, 128, 32], F32, kind="ExternalInput").ap()
    out = nc.dram_tensor("out", [D, TOK], F32, kind="ExternalOutput").ap()
    oT_loc = nc.dram_tensor("oT_loc", [1024, TOK], BF16, kind="Internal").ap()
    oT_all = nc.dram_tensor("oT_all", [4096, TOK], BF16, kind="Internal").ap()
    xb_loc = nc.dram_tensor("xb_loc", [4 * D, 512], BF16, kind="Internal").ap()
    xb_all = nc.dram_tensor("xb_all", [16 * D, 512], BF16, kind="Internal").ap()
    xres = nc.dram_tensor("xres", [D, TOK], F32, kind="Internal").ap()
    pid = nc.sync.partition_id()
    roff = (pid % 4) * 1024
    olv = oT_loc.rearrange("(q b p) t -> b p q t", q=4, b=2)

    def odst(br, g):
        return olv[br, :, g // 4, (g % 4) * 512:(g % 4 + 1) * 512]

    for l in range(nlayers):
        last = l == nlayers - 1
        with ExitStack() as ctx:
            if l == 0 or samex:
                xv0 = xTall0.rearrange("(c p) t -> p c t", p=128)
                xsrc = lambda tg: [(0, 8, xv0[:, :, tg * 512:(tg + 1) * 512])]
                xeng, xreads = "pool", []
            else:
                xv1 = xb_all.rearrange("(g r c p) t -> p g r c t", g=4, r=4, c=8, p=128)
                xsrc = lambda tg: [(0, 8, xv1[:, tg % 4, tg // 4, :, :])]
                xeng, xreads = "sp", ["xb_all%d" % j for j in range(4)]
            if not skipA:
                issued = set()

                def exch(g, issued=issued):
                    if g % 4 == 3:
                        q = g // 4
                        P.collective("oT%d" % q, "AllGather", RG, oT_loc[q * 256:(q + 1) * 256, :],
                                     oT_all[q * 1024:(q + 1) * 1024, :],
                                     reads=["oT1_%d" % gg for gg in range(4 * q, 4 * q + 4)],
                                     writes=["oT_all%d" % q])
                        issued.add(q)

                build_A(nc, P, ctx, "_a%d" % l, xsrc, xeng, xreads, wA[l], odst, C, parts,
                        exch if parts == 3 else None)
        P.barrier()
        for q in range(4):
            if skipA or q not in issued:
                P.collective("oT%d" % q, "AllGather", RG, oT_loc[q * 256:(q + 1) * 256, :],
                             oT_all[q * 1024:(q + 1) * 1024, :], writes=["oT_all%d" % q])
        with ExitStack() as ctx:
            ov = oT_all[bass.ds(roff, 1024), :].rearrange("(k b p) t -> p k b t", k=4, b=2, p=128)
            oTs_v = ov[:, :, 0, :]
            oTm_v = ov[:, :, 1, :]
            Wl = {k: Wd[k][l] for k in W_KEYS}

            def xexch(tg):
                P.collective("xb%d" % tg, "AllGather", RG, xb_loc[tg * 1024:(tg + 1) * 1024, :],
                             xb_all[tg * 4096:(tg + 1) * 4096, :], reads=["xbloc%d" % tg],
                             writes=["xb_all%d" % tg])
            if not skipB:
              build_B(nc, P, ctx, "_b%d" % l, xT0 if l == 0 else xres, oTs_v, oTm_v,
                    ["oT_all%d" % q for q in range(4)], Wl, lnp[l],
                    out if last else xres, None if last else xb_loc, None if last else xexch)
        P.barrier()
        if not last:
            for j in range(4):
                P.res["xb_all%d" % j] = [("c:xb%d" % j, P.cnt["c:xb%d" % j]), {}]
    P.finish("sp")
    return nc, hc


def _wA_slices(w_in_l, r):
    def cols(base):
        return w_in_l[:, base + 128 * r: base + 128 * r + 128]
    return np.concatenate([cols(0), cols(512), cols(1024), cols(1536), cols(2048), cols(2560)], axis=1)


def kernel(x, w_in, w_branch_sb, w_branch_moba, w_out, ln_mix_g, ln_mix_b,
           w_ffn_gate, w_ffn_up, w_ffn_down, ln_ffn_g, ln_ffn_b):
    f = lambda a: np.asarray(a, dtype=np.float32)
    x, w_in, w_branch_sb, w_branch_moba, w_out = f(x), f(w_in), f(w_branch_sb), f(w_branch_moba), f(w_out)
    ln_mix_g, ln_mix_b, ln_ffn_g, ln_ffn_b = f(ln_mix_g), f(ln_mix_b), f(ln_ffn_g), f(ln_ffn_b)
    w_ffn_gate, w_ffn_up, w_ffn_down = f(w_ffn_gate), f(w_ffn_up), f(w_ffn_down)
    nc, hc = build_fused()
    cores = list(range(8))
    xTb = [np.ascontiguousarray(x[b].T) for b in range(B)]
    Wfull = {"wG": np.ascontiguousarray(w_in[:, :, 3072:5120]), "wbs": w_branch_sb, "wbm": w_branch_moba,
             "wout": w_out, "wfg": w_ffn_gate, "wfu": w_ffn_up, "wfd": w_ffn_down}
    lnp = np.ascontiguousarray(np.stack([np.concatenate(
        [_pp(ln_mix_g[l]), _pp(ln_mix_b[l]), _pp(ln_ffn_g[l]), _pp(ln_ffn_b[l])], axis=1) for l in range(DEPTH)]))
    maps = []
    for c in cores:
        b, r = c // 4, c % 4
        m = {"xTall0": xTb[b], "xT0": np.ascontiguousarray(xTb[b][:, r * TOK:(r + 1) * TOK]),
             "wA": np.ascontiguousarray(np.stack([_wA_slices(w_in[l], r) for l in range(DEPTH)])),
             "lnp": lnp}
        m.update({k: np.ascontiguousarray(v) for k, v in Wfull.items()})
        m.update({"c_" + k: v for k, v in hc.items()})
        maps.append(m)
    res = run_bass_kernel_spmd(nc, maps, core_ids=cores)
    xn = np.empty_like(x)
    for c in cores:
        b, r = c // 4, c % 4
        xn[b, r * TOK:(r + 1) * TOK, :] = np.asarray(res.results[c]["out"]).T
    return xn
```

```python
from contextlib import ExitStack
import numpy as np
import concourse.bass as bass
import concourse.mybir as mybir
from concourse.bass_utils import run_bass_kernel_spmd

F32 = mybir.dt.float32
BF16 = mybir.dt.bfloat16
AF = mybir.ActivationFunctionType
ALU = mybir.AluOpType
AX = mybir.AxisListType

D = 1024
S = 8192
B = 2
DEPTH = 2
DFF = 2816
NFC = DFF // 128
NG = S // 512
ALPHA = (2 * DEPTH) ** 0.25
EPS = 1e-5
BIG = 30000.0
NEG = -1.0e30
TOK = 2048
NTG = TOK // 512


class Prog:
    SAME_ENGINE_SYNC = True
    EMBED = ("act", "dve")

    def __init__(self, nc):
        self.nc = nc
        self.eng = {"pe": nc.tensor, "act": nc.scalar, "dve": nc.vector,
                    "pool": nc.gpsimd, "sp": nc.sync}
        self.sem = {}
        self.cnt = {}
        self.res = {}
        self.known = {e: {} for e in self.eng}
        self.nwait = 0
        self.nops = 0
        self._pending = None

    def _sem(self, lane):
        if lane not in self.sem:
            self.sem[lane] = self.nc.alloc_semaphore(name="s_" + lane.replace(":", "_"))
            self.cnt[lane] = 0
        return self.sem[lane]

    def _need(self, e, lane, val):
        if lane == e and (e == "pe" or not self.SAME_ENGINE_SYNC):
            return
        if self.known[e].get(lane, 0) >= val:
            return
        self.known[e][lane] = val
        self.nwait += 1
        if self._pending is not None:
            self._pending.append((lane, val))
            return
        self.eng[e].wait_ge(self._sem(lane), val)

    def _deps(self, e, reads, writes):
        for r in reads:
            ent = self.res.get(r)
            if ent and ent[0]:
                self._need(e, *ent[0])
        for w in writes:
            ent = self.res.get(w)
            if ent:
                if ent[0]:
                    self._need(e, *ent[0])
                for lane, val in ent[1].items():
                    self._need(e, lane, val)

    def _record(self, lane, val, reads, writes):
        for r in reads:
            ent = self.res.setdefault(r, [None, {}])
            ent[1][lane] = max(ent[1].get(lane, 0), val)
        for w in writes:
            self.res[w] = [(lane, val), {}]

    def op(self, e, fn, reads=(), writes=(), signal=True):
        self._sem(e)
        self._pending = [] if e in self.EMBED else None
        self._deps(e, reads, writes)
        pend = self._pending or []
        self._pending = None
        for lane_, val_ in pend[:-1]:
            self.eng[e].wait_ge(self._sem(lane_), val_)
        ins = fn(self.eng[e])
        if pend:
            ins._wait_ge(self._sem(pend[-1][0]), pend[-1][1])
        val = self.cnt[e] + 1
        if signal:
            ins.then_inc(self.sem[e], 1)
            self.cnt[e] = val
        self._record(e, val, reads, writes)
        self.nops += 1
        return ins

    def dma(self, e, lane, out, in_, reads=(), writes=(), **kw):
        lane = "d:" + lane
        self._sem(lane)
        self._deps(e, reads, writes)
        ins = self.eng[e].dma_start(out=out, in_=in_, **kw)
        val = self.cnt[lane] + 16
        ins.then_inc(self.sem[lane], 16)
        self.cnt[lane] = val
        self._record(lane, val, reads, writes)
        self.nops += 1
        return ins

    def collective(self, lane, kind, rg, in_ap, out_ap, reads=(), writes=()):
        lane = "c:" + lane
        self._sem(lane)
        self._deps("pool", reads, writes)
        ins = self.nc.gpsimd.collective_compute(kind, ALU.bypass, replica_groups=rg,
                                                ins=[in_ap], outs=[out_ap])
        val = self.cnt[lane] + 1
        ins.then_inc(self.sem[lane], 1)
        self.cnt[lane] = val
        self._record(lane, val, reads, writes)
        self.nops += 1
        return ins

    def barrier(self, skip=()):
        for e in self.eng:
            for lane, val in self.cnt.items():
                if lane in skip:
                    continue
                if val > 0 and lane != e:
                    if self.known[e].get(lane, 0) < val:
                        self.eng[e].wait_ge(self.sem[lane], val)
                        self.known[e][lane] = val
        self.res = {}

    def finish(self, e="sp"):
        for lane, val in self.cnt.items():
            if val > 0 and lane != e:
                self.eng[e].wait_ge(self.sem[lane], val)


def host_consts():
    c = {}
    k = np.arange(128)[:, None]
    s = np.arange(128)[None, :]
    c["negtri"] = np.where(k >= s, -1.0, 0.0).astype(np.float32)
    c["negones"] = np.full((128, 128), -1.0, np.float32)
    c["ident"] = np.eye(128, dtype=np.float32)
    sk = np.arange(128)[:, None, None] + 128 * np.arange(4)[None, :, None]
    t = np.arange(512)[None, None, :]
    c["msb"] = (sk < t).astype(np.float32)
    c["mmb"] = (sk <= t).astype(np.float32)
    half = 8
    inv = (500000.0 ** (-np.arange(0, 16, 2, dtype=np.float32) / 16)).astype(np.float32)
    ang = (np.arange(S, dtype=np.float32)[:, None] * inv[None, :]).astype(np.float32)
    cos = np.cos(ang).astype(np.float32).T
    sin = np.sin(ang).astype(np.float32).T
    cosT = np.ones((64, S), np.float32)
    sinT = np.zeros((64, S), np.float32)
    cosT[0:8] = cos
    cosT[8:16] = cos
    sinT[0:8] = sin
    sinT[8:16] = sin
    c["cosT"] = np.concatenate([cosT, cosT], 0)
    c["sinT"] = np.concatenate([sinT, sinT], 0)
    n = np.arange(32)[None, :]
    own = (np.arange(64) // 2)[:, None]
    cb = np.where(n < own, 0.0, NEG).astype(np.float32)
    vb = np.where(n < own, BIG, 0.0).astype(np.float32)
    oo = np.where(n == own, 0.0, -BIG).astype(np.float32)
    c["cb"] = np.broadcast_to(cb.reshape(1, 64 * 32), (128, 64 * 32)).copy()
    c["vb"] = np.broadcast_to(vb.reshape(1, 64 * 32), (128, 64 * 32)).copy()
    c["oo"] = np.broadcast_to(oo.reshape(1, 64 * 32), (128, 64 * 32)).copy()
    c["onehot"] = (np.arange(32)[:, None] == (np.arange(S)[None, :] // 256)).astype(np.float32)
    return c


def build_A(nc, P, ctx, sfx, xsrc, xeng, xreads, wA, oT, C, parts=3, after_mb_group=None):
    def A(name, shape, dt):
        return ctx.enter_context(nc.sbuf_tensor(name + sfx, shape, dt))
    wv = wA.rearrange("(c p) n -> p c n", p=128)

    negtri = A("negtri", [128, 128], BF16)
    negones = A("negones", [128, 128], BF16)
    identb = A("identb", [128, 128], BF16)
    onesf = A("onesf", [128, 64], F32)
    msb = A("msb", [128, 4, 512], BF16)
    mmb = A("mmb", [128, 4, 512], BF16)
    w = A("wA_sb", [128, 8, 1024], BF16)
    P.dma("pool", "c0", negtri[:], C["negtri"], writes=["negtri"])
    P.dma("pool", "c1", negones[:], C["negones"], writes=["negones"])
    P.dma("pool", "c2", identb[:], C["ident"], writes=["identb"])
    P.dma("pool", "c3", msb[:], C["msb"], writes=["msb"])
    P.dma("pool", "c4", mmb[:], C["mmb"], writes=["mmb"])
    P.dma("pool", "w", w[:, :, 0:768], wv, writes=["w"])
    P.op("dve", lambda e: e.memset(onesf[:], 1.0), writes=["onesf"])
    P.op("dve", lambda e: e.memset(w[:, :, 768:1024], 0.0), reads=["w"], writes=["w"])
    for qk in range(2):
        for hd in range(2):
            src = 384 + qk * 128 + hd * 64
            dst = 768 + qk * 128 + hd * 64
            P.op("act", lambda e, s_=src, d_=dst: e.mul(w[:, :, d_:d_ + 8], w[:, :, s_ + 8:s_ + 16], -1.0),
                 reads=["w"], writes=["w"])
            P.op("act", lambda e, s_=src, d_=dst: e.copy(w[:, :, d_ + 8:d_ + 16], w[:, :, s_:s_ + 8]),
                 reads=["w"], writes=["w"])

    psbig = ctx.enter_context(nc.psum_tensor("psbig" + sfx, [128, 4096], F32))
    ps = [psbig[:, i * 512:(i + 1) * 512] for i in range(8)]
    psn = ["ps%d" % i for i in range(7)]

    xg = [A("xg%d" % i, [128, 8, 512], BF16) for i in range(2)]
    ost = [A("ost%d" % i, [128, 512], BF16) for i in range(2)]

    def load_x(tg):
        b = tg % 2
        for pi, (c0, c1, sap) in enumerate(xsrc(tg)):
            P.dma(xeng, "xg%d_%d" % (b, pi), xg[b][:, c0:c1, :], sap,
                  reads=(xreads(tg) if callable(xreads) else xreads), writes=["xg%d" % b])

    def proj_fm(bank, col0, tg):
        b = tg % 2
        for c in range(8):
            P.op("pe", lambda e, c=c: e.matmul(ps[bank][:], w[:, c, col0:col0 + 128], xg[b][:, c, :],
                                               start=(c == 0), stop=(c == 7)),
                 reads=["w", "xg%d" % b], writes=[psn[bank]], signal=(c == 7))

    def proj_tm(bank, col0, tg, ncols=128):
        b = tg % 2
        for ts in range(4):
            for c in range(8):
                P.op("pe", lambda e, c=c, ts=ts: e.matmul(ps[bank][:, ts * 128:(ts + 1) * 128],
                                                          xg[b][:, c, ts * 128:(ts + 1) * 128],
                                                          w[:, c, col0:col0 + 128],
                                                          start=(c == 0), stop=(c == 7)),
                     reads=["w", "xg%d" % b], writes=[psn[bank]], signal=(c == 7 and ts == 3))

    R = [A("R%d" % i, [128, S], BF16) for i in range(4)]
    Vm = A("Vm", [128, 64, 2, 128], BF16)
    QT, KT = R[0], R[1]
    V = A("V", [128, 64, 128], BF16)
    if parts & 1:
        load_x(0)
    for tg in range(NG if parts & 1 else 0):
        if tg + 1 < NG:
            load_x(tg + 1)
        sl = slice(tg * 512, (tg + 1) * 512)
        proj_fm(0, 0, tg)
        P.op("act", lambda e, sl=sl: e.mul(QT[:, sl], ps[0][:], 0.125), reads=["ps0"], writes=["QT"])
        proj_fm(1, 128, tg)
        P.op("dve", lambda e, sl=sl: e.tensor_copy(KT[:, sl], ps[1][:]), reads=["ps1"], writes=["KT"])
        proj_tm(2, 256, tg)
        P.op("act", lambda e, tg=tg: e.copy(V[:, tg * 4:(tg + 1) * 4, :],
                                           ps[2][:].rearrange("p (a b) -> p a b", a=4)),
             reads=["ps2"], writes=["V"])

    Eb = [A("Eb%d" % i, [128, 512], F32) for i in range(2)]
    Lb = [A("Lb%d" % i, [128, 512], BF16) for i in range(3)]
    Ab = [A("Ab%d" % i, [128, 512], BF16) for i in range(3)]
    Lrun = [A("Lrun%d" % i, [128, 512], F32) for i in range(2)]
    Lrb = [[A("Lrb%d_%d" % (h, i), [128, 512], BF16) for i in range(2)] for h in range(2)]

    items = []
    for g in range(NG):
        njt = 4 * g + 4
        for k in range(njt):
            for h in range(2):
                items.append((h, g, k, njt - 1 - k, njt))
    NI = len(items) if parts & 1 else 0

    def sb_s1(i):
        h, g, k, j, njt = items[i]
        zb = i % 4
        hp = slice(h * 64, (h + 1) * 64)
        P.op("pe", lambda e: e.matmul(ps[zb][:], KT[hp, j * 128:(j + 1) * 128], QT[hp, g * 512:(g + 1) * 512],
                                      start=True, stop=True),
             reads=["QT", "KT"], writes=[psn[zb]])

    def sb_s2(i):
        h, g, k, j, njt = items[i]
        zb = i % 4
        eb = "Eb%d" % (i % 2)
        lb = "Lb%d" % (i % 3)
        E = Eb[i % 2]
        L = Lb[i % 3]
        P.op("act", lambda e: e.activation(E[:], ps[zb][:], AF.Exp), reads=[psn[zb]], writes=[eb])

    def sb_s2b(i):
        h, g, k, j, njt = items[i]
        zb = i % 4
        eb = "Eb%d" % (i % 2)
        lb = "Lb%d" % (i % 3)
        E = Eb[i % 2]
        L = Lb[i % 3]
        P.op("act", lambda e: e.activation(L[:], E[:], AF.Ln, bias=1.0), reads=[eb], writes=[lb])
        if k < 4:
            jj = 3 - k
            P.op("dve", lambda e: e.tensor_tensor(L[:], L[:], msb[:, jj, :], ALU.mult),
                 reads=[lb, "msb"], writes=[lb])
        if k < njt - 1:
            lr = "Lrun%d" % h
            nb = "Lrb%d_%d" % (h, (k + 1) % 2)
            dst = Lrb[h][(k + 1) % 2]
            if k == 0:
                P.op("dve", lambda e: e.tensor_copy(Lrun[h][:], L[:]), reads=[lb], writes=[lr])
                P.op("dve", lambda e: e.tensor_copy(dst[:], L[:]), reads=[lb], writes=[nb])
            else:
                P.op("dve", lambda e: e.tensor_tensor(Lrun[h][:], Lrun[h][:], L[:], ALU.add),
                     reads=[lb, lr], writes=[lr])
                P.op("dve", lambda e: e.tensor_copy(dst[:], Lrun[h][:]), reads=[lr], writes=[nb])

    def sb_s3(i):
        h, g, k, j, njt = items[i]
        zb = i % 4
        lb = "Lb%d" % (i % 3)
        L = Lb[i % 3]
        P.op("pe", lambda e: e.matmul(ps[zb][:], negtri[:], L[:], start=False, stop=(k == 0),
                                      skip_group_check=True),
             reads=["negtri", lb], writes=[psn[zb]], signal=(k == 0))
        if k > 0:
            cur = Lrb[h][k % 2]
            P.op("pe", lambda e: e.matmul(ps[zb][:], negones[:], cur[:], start=False, stop=True,
                                          skip_group_check=True),
                 reads=["negones", "Lrb%d_%d" % (h, k % 2)], writes=[psn[zb]])

    def sb_s4(i):
        h, g, k, j, njt = items[i]
        zb = i % 4
        ab = "Ab%d" % (i % 3)
        Aa = Ab[i % 3]
        P.op("act", lambda e: e.activation(Aa[:], ps[zb][:], AF.Exp), reads=[psn[zb]], writes=[ab])
        if k < 4:
            jj = 3 - k
            P.op("dve", lambda e: e.tensor_tensor(Aa[:], Aa[:], msb[:, jj, :], ALU.mult),
                 reads=[ab, "msb"], writes=[ab])

    def sb_s5(i):
        h, g, k, j, njt = items[i]
        ab = "Ab%d" % (i % 3)
        Aa = Ab[i % 3]
        ob = 4 + (2 * g + h) % 3
        P.op("pe", lambda e: e.matmul(ps[ob][0:64, :], V[:, j, h * 64:(h + 1) * 64], Aa[:],
                                      start=(k == 0), stop=(k == njt - 1), skip_group_check=True),
             reads=["V", ab], writes=[psn[ob]], signal=True)
        if k == njt - 1:
            so = "ost%d" % (g % 2)
            P.op("dve", lambda e: e.tensor_copy(ost[g % 2][h * 64:(h + 1) * 64, :], ps[ob][0:64, :]),
                 reads=[psn[ob]], writes=[so + "_%d" % h])
            if h == 1:
                P.dma("sp", so, oT(0, g), ost[g % 2][:],
                      reads=[so + "_0", so + "_1"], writes=["oT0_%d" % g])

    order = [(sb_s1, 0), (sb_s2, 1), (sb_s4, 3), (sb_s2b, 1), (sb_s3, 2), (sb_s5, 4)]
    for step in range(NI + 4):
        for fn, lag in order:
            i = step - lag
            if 0 <= i < NI:
                fn(i)

    P.barrier()
    if not parts & 2:
        return
    QTa = [R[0], R[1]]
    KTa = [R[2], R[3]]
    cosv, sinv = C["cosT"], C["sinT"]
    csb = [[A("cs%d_%d" % (a, i), [128, 512], F32) for i in range(2)] for a in range(2)]
    t1 = A("t1", [128, 512], F32)
    t2 = A("t2", [128, 512], F32)
    qf = A("qf", [128, 512], F32)
    kf = A("kf", [128, 512], F32)
    kmT = A("kmT", [128, 2, 32], F32)
    gm = [A("gm%d" % i, [128, 32], F32) for i in range(2)]
    top8 = [A("top8_%d" % i, [128, 8], F32) for i in range(2)]
    mb1 = [A("mb1_%d" % i, [128, 32], F32) for i in range(2)]
    mbpad = [A("mbpad%d" % i, [128, 128], BF16) for i in range(8)]
    cbrow = A("cbrow", [128, 64], F32)
    vbrow = A("vbrow", [128, 64], F32)
    oorow = A("oorow", [128, 64], F32)
    Pb = [A("Pb%d" % i, [128, 512], BF16) for i in range(3)]
    rec = A("rec", [128, 512], F32)
    bcs = A("bcs", [64, 512], F32)

    P.op("dve", lambda e: e.memset(cbrow[:, 0:32], 0.0), writes=["cbrow"])
    P.op("dve", lambda e: e.memset(cbrow[:, 32:64], NEG), writes=["cbrow"])
    P.op("dve", lambda e: e.memset(vbrow[:, 0:32], BIG), writes=["vbrow"])
    P.op("dve", lambda e: e.memset(vbrow[:, 32:64], 0.0), writes=["vbrow"])
    P.op("dve", lambda e: e.memset(oorow[:], -BIG), writes=["oorow"])
    P.op("dve", lambda e: e.memset(oorow[:, 31:32], 0.0), writes=["oorow"])
    P.op("dve", lambda e: e.memset(kmT[:], 0.0), writes=["kmT"])
    P.op("dve", lambda e: e.memset(Vm[:], 1.0), writes=["Vm"])
    for i in range(8):
        P.op("dve", lambda e, i=i: e.memset(mbpad[i][:], 0.0), writes=["mbpad%d" % i])
    for h in range(2):
        P.op("dve", lambda e, h=h: e.memset(QTa[h][64:128, :], 0.0), writes=["QTa%d" % h])
        P.op("dve", lambda e, h=h: e.memset(KTa[h][64:128, :], 0.0), writes=["KTa%d" % h])
        P.dma("pool", "oh%d" % h, KTa[h][64:96, :], C["onehot"], writes=["KTa%d" % h])

    def load_cs(tg):
        b = tg % 2
        P.dma("sp", "cs0_%d" % b, csb[0][b][:], cosv[:, tg * 512:(tg + 1) * 512], writes=["cs0_%d" % b])
        P.dma("sp", "cs1_%d" % b, csb[1][b][:], sinv[:, tg * 512:(tg + 1) * 512], writes=["cs1_%d" % b])

    def rope(dst, b, bank_a, bank_b):
        cn, sn = "cs0_%d" % b, "cs1_%d" % b
        P.op("dve", lambda e: e.tensor_tensor(t1[:], ps[bank_a][:], csb[0][b][:], ALU.mult),
             reads=[psn[bank_a], cn], writes=["t1"])
        P.op("dve", lambda e: e.tensor_tensor(t2[:], ps[bank_b][:], csb[1][b][:], ALU.mult),
             reads=[psn[bank_b], sn], writes=["t2"])
        nm = "qf" if dst is qf else "kf"
        P.op("dve", lambda e: e.tensor_tensor(dst[:], t1[:], t2[:], ALU.add),
             reads=["t1", "t2"], writes=[nm])

    def gate_tail(tgp):
        for idx in range(8):
            cq, h = idx // 2, idx % 2
            cch = tgp * 4 + cq
            u = idx % 2
            P.op("pe", lambda e, u=u, idx=idx: e.matmul(ps[6 + u][:, 0:128], mbpad[idx][:], identb[:],
                                                        start=True, stop=True),
                 reads=["mbpad%d" % idx, "identb"], writes=["ps6_%d" % u], signal=False)
            P.op("pe", lambda e, u=u: e.matmul(ps[6 + u][:, 256:384], identb[:], identb[:],
                                               start=True, stop=True),
                 reads=["identb"], writes=["ps6_%d" % u])
            P.op("act", lambda e, h=h, cch=cch, u=u: e.copy(QTa[h][64:96, cch * 128:(cch + 1) * 128],
                                                          ps[6 + u][64:96, 0:128]),
                 reads=["ps6_%d" % u], writes=["QTa%d" % h])

    load_x(0)
    load_cs(0)
    for tg in range(NG):
        if tg + 1 < NG:
            load_x(tg + 1)
            load_cs(tg + 1)
        b = tg % 2
        sl = slice(tg * 512, (tg + 1) * 512)
        proj_fm(0, 512, tg)
        proj_fm(1, 896, tg)
        rope(kf, b, 0, 1)
        for h in range(2):
            P.op("act", lambda e, h=h: e.copy(KTa[h][0:64, sl], kf[h * 64:(h + 1) * 64, :]),
                 reads=["kf"], writes=["KTa%d" % h])
        for h in range(2):
            hp2 = slice(h * 64, (h + 1) * 64)
            P.op("dve", lambda e, h=h, hp2=hp2: e.tensor_reduce(
                kmT[hp2, h, 2 * tg:2 * tg + 2], kf[hp2, :].rearrange("p (a b) -> p a b", a=2), AX.X, ALU.add),
                reads=["kf"], writes=["kmT"])
        proj_fm(2, 384, tg)
        proj_fm(3, 768, tg)
        rope(qf, b, 2, 3)
        for h in range(2):
            P.op("act", lambda e, h=h: e.mul(QTa[h][0:64, sl], qf[h * 64:(h + 1) * 64, :], 0.125),
                 reads=["qf"], writes=["QTa%d" % h])
        proj_tm(4, 640, tg)
        P.op("act", lambda e: e.copy(Vm[:, tg * 4:(tg + 1) * 4, :, 0:64],
                                    ps[4][:].rearrange("p (a h d) -> p a h d", a=4, h=2)),
             reads=["ps4"], writes=["Vm"])
        if tg > 0:
            gate_tail(tg - 1)
        for idx in range(8):
            cq, h = idx // 2, idx % 2
            P.op("pe", lambda e, cq=cq, h=h, idx=idx: e.matmul(
                ps[5][:, idx * 32:(idx + 1) * 32], qf[:, cq * 128:(cq + 1) * 128],
                kmT[:, h, :], start=True, stop=True),
                reads=["qf", "kmT"], writes=["ps5"], signal=False)
        P.op("pe", lambda e: e.matmul(ps[5][:, 256:384], identb[:], identb[:], start=True, stop=True),
             reads=["identb"], writes=["ps5"])
        for idx in range(8):
            cq, h = idx // 2, idx % 2
            cch = tg * 4 + cq
            own = cch // 2
            u = idx % 2
            P.op("dve", lambda e, idx=idx, own=own, u=u: e.tensor_tensor(
                gm[u][:], ps[5][:, idx * 32:(idx + 1) * 32], cbrow[:, 32 - own:64 - own], ALU.add),
                reads=["ps5", "cbrow"], writes=["gm%d" % u])
            P.op("dve", lambda e, u=u: e.max(top8[u][:], gm[u][:]), reads=["gm%d" % u], writes=["top8_%d" % u])
            P.op("dve", lambda e, own=own, u=u: e.scalar_tensor_tensor(
                mb1[u][:], gm[u][:], top8[u][:, 2:3], vbrow[:, 32 - own:64 - own], ALU.is_ge, ALU.mult),
                reads=["gm%d" % u, "top8_%d" % u, "vbrow"], writes=["mb1_%d" % u])
            P.op("dve", lambda e, own=own, u=u, idx=idx: e.tensor_tensor(
                mbpad[idx][:, 64:96], mb1[u][:], oorow[:, 31 - own:63 - own], ALU.add),
                reads=["mb1_%d" % u, "oorow"], writes=["mbpad%d" % idx])
    gate_tail(NG - 1)

    P.barrier()
    Pb2 = [A("Pb2_%d" % i, [128, 1024], BF16) for i in range(3)]
    mitems = []
    for g in range(NG):
        njt = 4 * g + 4
        for j in range(njt):
            mitems.append((g, j, njt))
    NM = len(mitems)

    def mb_s1(i):
        g, j, njt = mitems[i]
        zp = i % 2
        for h in range(2):
            P.op("pe", lambda e, h=h: e.matmul(psbig[:, zp * 1024 + h * 512:zp * 1024 + (h + 1) * 512],
                                               KTa[h][:, j * 128:(j + 1) * 128], QTa[h][:, g * 512:(g + 1) * 512],
                                               start=True, stop=True),
                 reads=["QTa%d" % h, "KTa%d" % h], writes=["zp%d" % zp], signal=(h == 1))

    def mb_s2(i):
        g, j, njt = mitems[i]
        zp = i % 2
        pb = "Pb2_%d" % (i % 3)
        Pt = Pb2[i % 3]
        P.op("act", lambda e: e.activation(Pt[:], psbig[:, zp * 1024:(zp + 1) * 1024], AF.Exp),
             reads=["zp%d" % zp], writes=[pb])
        if j >= 4 * g:
            jj = j - 4 * g
            for h in range(2):
                P.op("dve", lambda e, h=h: e.tensor_tensor(Pt[:, h * 512:(h + 1) * 512], Pt[:, h * 512:(h + 1) * 512],
                                                           mmb[:, jj, :], ALU.mult),
                     reads=[pb, "mmb"], writes=[pb])

    def mb_s3(i):
        g, j, njt = mitems[i]
        pb = "Pb2_%d" % (i % 3)
        Pt = Pb2[i % 3]
        for h in range(2):
            ob = 4 + (2 * g + h) % 3
            P.op("pe", lambda e, h=h, ob=ob: e.matmul(ps[ob][:, :], Vm[:, j, h, :], Pt[:, h * 512:(h + 1) * 512],
                                                      start=(j == 0), stop=(j == njt - 1), skip_group_check=True),
                 reads=["Vm", pb], writes=[psn[ob]], signal=True)
            if j == njt - 1:
                so = "ost%d" % (g % 2)
                P.op("dve", lambda e, ob=ob: e.reciprocal(rec[64:128, :], ps[ob][64:128, :]),
                     reads=[psn[ob]], writes=["rec"])
                P.op("dve", lambda e, h=h, ob=ob: e.tensor_tensor(ost[g % 2][h * 64:(h + 1) * 64, :], ps[ob][0:64, :],
                                                                  rec[64:128, :], ALU.mult),
                     reads=[psn[ob], "rec"], writes=[so + "_%d" % h])
                if h == 1:
                    P.dma("sp", so, oT(1, g), ost[g % 2][:],
                          reads=[so + "_0", so + "_1"], writes=["oT1_%d" % g])
                    if after_mb_group is not None:
                        after_mb_group(g)

    mstages = [mb_s1, mb_s2, mb_s3]
    for step in range(NM + len(mstages) - 1):
        for s_, fn in enumerate(mstages):
            i = step - s_
            if 0 <= i < NM:
                fn(i)
    P.barrier()


def build_B(nc, P, ctx, sfx, xT_d, oTs_v, oTm_v, oreads, W, lnp_d, out_d, outb_d=None, xb_exch=None):
    def A(name, shape, dt):
        return ctx.enter_context(nc.sbuf_tensor(name + sfx, shape, dt))
    xT = A("xT_sb", [128, 8, TOK], F32)
    xb = A("xb_sb", [128, 8, TOK], BF16)
    arena = A("arena", [128, 32768], BF16)
    oTs = arena[:, 0:8192].rearrange("p (a b) -> p a b", a=4)
    oTm = arena[:, 8192:16384].rearrange("p (a b) -> p a b", a=4)
    mg = arena[:, 16384:32768].rearrange("p (a b) -> p a b", a=8)
    hT = arena[:, 0:NFC * 1024].rearrange("p (a b) -> p a b", a=NFC)
    lnp = A("lnp_sb", [128, 32], F32)
    onesD = A("onesD", [128, 128], F32)
    warena = A("warena", [128, 9728], BF16)
    wgb = [warena[:, i * 2048:(i + 1) * 2048].rearrange("p (a b) -> p a b", a=8) for i in range(2)]
    wbb = [warena[:, 4096 + i * 1024:4096 + (i + 1) * 1024].rearrange("p (a b) -> p a b", a=4) for i in range(2)]
    wob = [warena[:, 6144 + i * 1024:6144 + (i + 1) * 1024].rearrange("p (a b) -> p a b", a=8) for i in range(2)]
    wfgu = [warena[:, i * 2048:(i + 1) * 2048].rearrange("p (a b) -> p a b", a=8) for i in range(2)]
    wfdb = [warena[:, 4096 + i * 2816:4096 + (i + 1) * 2816].rearrange("p (a b) -> p a b", a=NFC) for i in range(2)]
    sg = [A("sg%d" % i, [128, 512], F32) for i in range(4)]
    m1 = [A("m1_%d" % i, [128, 512], F32) for i in range(2)]
    m2 = [A("m2_%d" % i, [128, 512], F32) for i in range(2)]
    ysq = [A("ysq%d" % i, [128, 512], F32) for i in range(2)]
    mean_sb = A("mean_sb", [128, 512], F32)
    rstd_sb = A("rstd_sb", [128, 512], F32)
    tn = [A("tn%d" % i, [128, 512], F32) for i in range(2)]
    ps = [ctx.enter_context(nc.psum_tensor("pb%d" % i + sfx, [128, 512], F32)) for i in range(8)]
    psn = ["pb%d" % i for i in range(8)]
    bank = [0]
    XN = [["xT%d_%d" % (c, g) for g in range(NTG)] for c in range(8)]
    XB = [["xb%d_%d" % (c, g) for g in range(NTG)] for c in range(8)]
    MG = [["mg%d_%d" % (c, g) for g in range(NTG)] for c in range(8)]
    HT = [["hT%d_%d" % (k, g) for g in range(2)] for k in range(NFC)]
    allx = [n for r_ in XN for n in r_]
    allxb = [n for r_ in XB for n in r_]

    def nb():
        bank[0] = (bank[0] + 1) % 8
        return bank[0]

    xdv = xT_d.rearrange("(c p) t -> p c t", p=128)
    P.dma("sp", "xT", xT[:], xdv, writes=allx)
    P.dma("pool", "xb", xb[:], xdv, writes=allxb)
    P.dma("sp", "oTs", oTs, oTs_v, reads=oreads, writes=["oTs"])
    P.dma("sp", "oTm", oTm, oTm_v, reads=oreads, writes=["oTm"])
    P.dma("sp", "lnp", lnp[:], lnp_d, writes=["lnp"])
    P.op("dve", lambda e: e.memset(onesD[:], 1.0 / D), writes=["onesD"])

    wGv = W["wG"].rearrange("(k p) n -> p k n", p=128)
    wbsv = W["wbs"].rearrange("(k p) n -> p k n", p=128)
    wbmv = W["wbm"].rearrange("(k p) n -> p k n", p=128)
    wov = W["wout"].rearrange("(k p) n -> p k n", p=128)
    wfgv = W["wfg"].rearrange("(k p) n -> p k n", p=128)
    wfuv = W["wfu"].rearrange("(k p) n -> p k n", p=128)
    wfdv = W["wfd"].rearrange("(k p) n -> p k n", p=128)

    def tsl(tg):
        return slice(tg * 512, (tg + 1) * 512)

    def load_b1(c):
        b = c % 2
        cs = slice(c * 128, (c + 1) * 128)
        P.dma("pool", "wgb%da" % b, wgb[b][:, :, 0:128], wGv[:, :, cs], writes=["wgb%da" % b])
        P.dma("pool", "wgb%db" % b, wgb[b][:, :, 128:256], wGv[:, :, D + c * 128:D + (c + 1) * 128],
              writes=["wgb%db" % b])
        P.dma("pool", "wbb%da" % b, wbb[b][:, :, 0:128], wbsv[:, :, cs], writes=["wbb%da" % b])
        P.dma("pool", "wbb%db" % b, wbb[b][:, :, 128:256], wbmv[:, :, cs], writes=["wbb%db" % b])

    load_b1(0)
    it = 0
    for c in range(8):
        if c + 1 < 8:
            load_b1(c + 1)
        b = c % 2
        for tg in range(NTG):
            t = tsl(tg)
            bk = [nb() for _ in range(4)]
            for half, (bkk, wn) in enumerate(zip(bk[0:2], ["wgb%da" % b, "wgb%db" % b])):
                for k in range(8):
                    P.op("pe", lambda e, k=k, half=half, bkk=bkk: e.matmul(
                        ps[bkk][:], wgb[b][:, k, half * 128:(half + 1) * 128], xb[:, k, t],
                        start=(k == 0), stop=(k == 7)),
                        reads=[wn, XB[k][tg]], writes=[psn[bkk]], signal=(k == 7))
            for half, (bkk, wn, src, sn) in enumerate(zip(bk[2:4], ["wbb%da" % b, "wbb%db" % b],
                                                          [oTs, oTm], ["oTs", "oTm"])):
                for k in range(4):
                    P.op("pe", lambda e, k=k, half=half, bkk=bkk, src=src: e.matmul(
                        ps[bkk][:], wbb[b][:, k, half * 128:(half + 1) * 128], src[:, k, t],
                        start=(k == 0), stop=(k == 3)),
                        reads=[wn, sn], writes=[psn[bkk]], signal=(k == 3))
            u = it % 2
            s0, s1 = sg[2 * u], sg[2 * u + 1]
            P.op("act", lambda e, s0=s0, bkk=bk[0]: e.activation(s0[:], ps[bkk][:], AF.Sigmoid),
                 reads=[psn[bk[0]]], writes=["sg%d" % (2 * u)])
            P.op("act", lambda e, s1=s1, bkk=bk[1]: e.activation(s1[:], ps[bkk][:], AF.Sigmoid),
                 reads=[psn[bk[1]]], writes=["sg%d" % (2 * u + 1)])
            P.op("dve", lambda e, s0=s0, u=u, bkk=bk[2]: e.tensor_tensor(m1[u][:], s0[:], ps[bkk][:], ALU.mult),
                 reads=[psn[bk[2]], "sg%d" % (2 * u)], writes=["m1_%d" % u])
            P.op("dve", lambda e, s1=s1, u=u, bkk=bk[3]: e.tensor_tensor(m2[u][:], s1[:], ps[bkk][:], ALU.mult),
                 reads=[psn[bk[3]], "sg%d" % (2 * u + 1)], writes=["m2_%d" % u])
            P.op("pool", lambda e, u=u, c=c, t=t: e.tensor_tensor(mg[:, c, t], m1[u][:], m2[u][:], ALU.add),
                 reads=["m1_%d" % u, "m2_%d" % u], writes=[MG[c][tg]])
            it += 1

    def layer_norm(tg, gcol, bcol):
        t = tsl(tg)
        ba, bb_ = nb(), nb()
        for c in range(8):
            q = ysq[c % 2]
            P.op("act", lambda e, q=q, c=c: e.activation(q[:], xT[:, c, t], AF.Square),
                 reads=[XN[c][tg]], writes=["ysq%d" % (c % 2)])
            P.op("pe", lambda e, c=c: e.matmul(ps[ba][:], onesD[:], xT[:, c, t], start=(c == 0), stop=(c == 7)),
                 reads=["onesD", XN[c][tg]], writes=[psn[ba]], signal=(c == 7))
            P.op("pe", lambda e, q=q, c=c: e.matmul(ps[bb_][:], onesD[:], q[:], start=(c == 0), stop=(c == 7),
                                                   skip_group_check=True),
                 reads=["onesD", "ysq%d" % (c % 2)], writes=[psn[bb_]], signal=True)
        P.op("dve", lambda e: e.tensor_copy(mean_sb[:], ps[ba][:]), reads=[psn[ba]], writes=["mean_sb"])
        P.op("dve", lambda e: e.tensor_tensor(rstd_sb[:], mean_sb[:], mean_sb[:], ALU.mult),
             reads=["mean_sb"], writes=["rstd_sb"])
        P.op("dve", lambda e: e.tensor_tensor(rstd_sb[:], ps[bb_][:], rstd_sb[:], ALU.subtract),
             reads=[psn[bb_], "rstd_sb"], writes=["rstd_sb"])
        P.op("act", lambda e: e.activation(rstd_sb[:], rstd_sb[:], AF.Ln, bias=EPS),
             reads=["rstd_sb"], writes=["rstd_sb"])
        P.op("act", lambda e: e.activation(rstd_sb[:], rstd_sb[:], AF.Exp, scale=-0.5),
             reads=["rstd_sb"], writes=["rstd_sb"])
        for c in range(8):
            tt = tn[c % 2]
            tnn = "tn%d" % (c % 2)
            P.op("dve", lambda e, tt=tt, c=c: e.tensor_tensor(tt[:], xT[:, c, t], mean_sb[:], ALU.subtract),
                 reads=[XN[c][tg], "mean_sb"], writes=[tnn])
            P.op("dve", lambda e, tt=tt: e.tensor_tensor(tt[:], tt[:], rstd_sb[:], ALU.mult),
                 reads=[tnn, "rstd_sb"], writes=[tnn])
            P.op("act", lambda e, tt=tt, c=c: e.activation(xT[:, c, t], tt[:], AF.Identity,
                                                          bias=lnp[:, bcol + c:bcol + c + 1],
                                                          scale=lnp[:, gcol + c:gcol + c + 1]),
                 reads=[tnn, "lnp"], writes=[XN[c][tg]])
            P.op("pool", lambda e, c=c: e.tensor_copy(xb[:, c, t], xT[:, c, t]), reads=[XN[c][tg]], writes=[XB[c][tg]])

    def load_wo(c):
        b = c % 2
        P.dma("pool", "wob%d" % b, wob[b][:], wov[:, :, c * 128:(c + 1) * 128], writes=["wob%d" % b])

    load_wo(0)
    for c in range(8):
        if c + 1 < 8:
            load_wo(c + 1)
        b = c % 2
        for tg in range(NTG):
            t = tsl(tg)
            bkk = nb()
            for k in range(8):
                P.op("pe", lambda e, k=k, bkk=bkk: e.matmul(ps[bkk][:], wob[b][:, k, :], mg[:, k, t],
                                                            start=(k == 0), stop=(k == 7)),
                     reads=["wob%d" % b, MG[k][tg]], writes=[psn[bkk]], signal=(k == 7))
            P.op("dve", lambda e, bkk=bkk, c=c, t=t: e.scalar_tensor_tensor(
                xT[:, c, t], xT[:, c, t], ALPHA, ps[bkk][:], ALU.mult, ALU.add),
                reads=[psn[bkk], XN[c][tg]], writes=[XN[c][tg]])
    for tg in range(NTG):
        layer_norm(tg, 0, 8)

    def load_gu(i):
        k = i % NFC
        b = i % 2
        P.dma("pool", "wfgu%da" % b, wfgu[b][:, :, 0:128], wfgv[:, :, k * 128:(k + 1) * 128],
              writes=["wfgu%da" % b])
        P.dma("pool", "wfgu%db" % b, wfgu[b][:, :, 128:256], wfuv[:, :, k * 128:(k + 1) * 128],
              writes=["wfgu%db" % b])

    def load_dn(i):
        c = i % 8
        b = i % 2
        P.dma("pool", "wfdb%d" % b, wfdb[b][:], wfdv[:, :, c * 128:(c + 1) * 128], writes=["wfdb%d" % b])

    gi = 0
    di = 0
    pend_x = []
    P.barrier()
    for hf in range(2):
        load_gu(gi)
        for k in range(NFC):
            if k + 1 < NFC:
                load_gu(gi + 1)
            if k == 3 and xb_exch is not None:
                while pend_x:
                    xb_exch(pend_x.pop(0))
            b = gi % 2
            for t2_ in range(2):
                tg = hf * 2 + t2_
                t = tsl(tg)
                b0, b1 = nb(), nb()
                for half, bkk in enumerate([b0, b1]):
                    wn = "wfgu%d%s" % (b, "ab"[half])
                    for kk in range(8):
                        P.op("pe", lambda e, kk=kk, half=half, bkk=bkk: e.matmul(
                            ps[bkk][:], wfgu[b][:, kk, half * 128:(half + 1) * 128], xb[:, kk, t],
                            start=(kk == 0), stop=(kk == 7)),
                            reads=[wn, XB[kk][tg]], writes=[psn[bkk]], signal=(kk == 7))
                u = it % 2
                s0 = sg[2 * u]
                P.op("act", lambda e, s0=s0, b0=b0: e.activation(s0[:], ps[b0][:], AF.Silu),
                     reads=[psn[b0]], writes=["sg%d" % (2 * u)])
                P.op("dve", lambda e, s0=s0, b1=b1, k=k, t2_=t2_: e.tensor_tensor(
                    hT[:, k, t2_ * 512:(t2_ + 1) * 512], s0[:], ps[b1][:], ALU.mult),
                    reads=[psn[b1], "sg%d" % (2 * u)], writes=[HT[k][t2_]])
                it += 1
            gi += 1
        load_dn(di)
        for c in range(8):
            if c + 1 < 8:
                load_dn(di + 1)
            b = di % 2
            for t2_ in range(2):
                tg = hf * 2 + t2_
                t = tsl(tg)
                bkk = nb()
                for k in range(NFC):
                    P.op("pe", lambda e, k=k, bkk=bkk, t2_=t2_: e.matmul(
                        ps[bkk][:], wfdb[b][:, k, :], hT[:, k, t2_ * 512:(t2_ + 1) * 512],
                        start=(k == 0), stop=(k == NFC - 1)),
                        reads=["wfdb%d" % b, HT[k][t2_]], writes=[psn[bkk]], signal=(k == NFC - 1))
                P.op("dve", lambda e, bkk=bkk, c=c, t=t: e.scalar_tensor_tensor(
                    xT[:, c, t], xT[:, c, t], ALPHA, ps[bkk][:], ALU.mult, ALU.add),
                    reads=[psn[bkk], XN[c][tg]], writes=[XN[c][tg]])
            di += 1
        for t2_ in range(2):
            tg = hf * 2 + t2_
            layer_norm(tg, 16, 24)
            if outb_d is not None:
                P.dma("sp", "outb%d" % tg, outb_d[tg * 1024:(tg + 1) * 1024, :].rearrange("(c p) t -> p c t", p=128),
                      xb[:, :, tsl(tg)], reads=[XB[c][tg] for c in range(8)], writes=["xbloc%d" % tg])
                if hf == 1 and xb_exch is not None:
                    xb_exch(tg)
                else:
                    pend_x.append(tg)
    P.dma("sp", "out", out_d.rearrange("(c p) t -> p c t", p=128), xT[:], reads=allx, writes=["out"])
    if xb_exch is not None:
        while pend_x:
            xb_exch(pend_x.pop(0))


W_KEYS = ["wG", "wbs", "wbm", "wout", "wfg", "wfu", "wfd"]
W_SHAPES = {"wG": [D, 2048], "wbs": [512, D], "wbm": [512, D], "wout": [D, D],
            "wfg": [D, DFF], "wfu": [D, DFF], "wfd": [DFF, D]}
RG = [[0, 1, 2, 3], [4, 5, 6, 7]]


def _pp(v):
    return np.ascontiguousarray(v.reshape(8, 128).T)


def build_fused(nlayers=DEPTH, skipA=False, skipB=False, samex=False, parts=3):
    nc = bass.Bass("TRN2", target_bir_lowering=False)
    P = Prog(nc)
    hc = host_consts()
    C = {k: nc.dram_tensor("c_" + k, list(v.shape), F32, kind="ExternalInput").ap() for k, v in hc.items()}
    xTall0 = nc.dram_tensor("xTall0", [D, S], F32, kind="ExternalInput").ap()
    xT0 = nc.dram_tensor("xT0", [D, TOK], F32, kind="ExternalInput").ap()
    wA = nc.dram_tensor("wA", [DEPTH, D, 768], F32, kind="ExternalInput").ap()
    Wd = {k: nc.dram_tensor(k, [DEPTH] + W_SHAPES[k], F32, kind="ExternalInput").ap() for k in W_KEYS}
    lnp = nc.dram_tensor("lnp", [DEPTH, 128, 32], F32, kind="ExternalInput").ap()
    out = nc.dram_tensor("out", [D, TOK], F32, kind="ExternalOutput").ap()
    oT_loc = nc.dram_tensor("oT_loc", [1024, TOK], BF16, kind="Internal").ap()
    oT_all = nc.dram_tensor("oT_all", [4096, TOK], BF16, kind="Internal").ap()
    xb_loc = nc.dram_tensor("xb_loc", [4 * D, 512], BF16, kind="Internal").ap()
    xb_all = nc.dram_tensor("xb_all", [16 * D, 512], BF16, kind="Internal").ap()
    xres = nc.dram_tensor("xres", [D, TOK], F32, kind="Internal").ap()
    pid = nc.sync.partition_id()
    roff = (pid % 4) * 1024
    olv = oT_loc.rearrange("(q b p) t -> b p q t", q=4, b=2)

    def odst(br, g):
        return olv[br, :, g // 4, (g % 4) * 512:(g % 4 + 1) * 512]

    for l in range(nlayers):
        last = l == nlayers - 1
        with ExitStack() as ctx:
            if l == 0 or samex:
                xv0 = xTall0.rearrange("(c p) t -> p c t", p=128)
                xsrc = lambda tg: [(0, 8, xv0[:, :, tg * 512:(tg + 1) * 512])]
                xeng, xreads = "pool", []
            else:
                xv1 = xb_all.rearrange("(g r c p) t -> p g r c t", g=4, r=4, c=8, p=128)
                xsrc = lambda tg: [(0, 8, xv1[:, tg % 4, tg // 4, :, :])]
                xeng, xreads = "sp", (lambda tg: ["xb_all%d" % (tg % 4)])
            if not skipA:
                issued = set()

                def exch(g, issued=issued):
                    if g % 4 == 3:
                        q = g // 4
                        P.collective("oT%d" % q, "AllGather", RG, oT_loc[q * 256:(q + 1) * 256, :],
                                     oT_all[q * 1024:(q + 1) * 1024, :],
                                     reads=["oT1_%d" % gg for gg in range(4 * q, 4 * q + 4)],
                                     writes=["oT_all%d" % q])
                        issued.add(q)

                build_A(nc, P, ctx, "_a%d" % l, xsrc, xeng, xreads, wA[l], odst, C, parts,
                        exch if parts == 3 else None)
        P.barrier()
        for q in range(4):
            if skipA or q not in issued:
                P.collective("oT%d" % q, "AllGather", RG, oT_loc[q * 256:(q + 1) * 256, :],
                             oT_all[q * 1024:(q + 1) * 1024, :], writes=["oT_all%d" % q])
        with ExitStack() as ctx:
            ov = oT_all[bass.ds(roff, 1024), :].rearrange("(k b p) t -> p k b t", k=4, b=2, p=128)
            oTs_v = ov[:, :, 0, :]
            oTm_v = ov[:, :, 1, :]
            Wl = {k: Wd[k][l] for k in W_KEYS}

            def xexch(tg):
                P.collective("xb%d" % tg, "AllGather", RG, xb_loc[tg * 1024:(tg + 1) * 1024, :],
                             xb_all[tg * 4096:(tg + 1) * 4096, :], reads=["xbloc%d" % tg],
                             writes=["xb_all%d" % tg])
            if not skipB:
              build_B(nc, P, ctx, "_b%d" % l, xT0 if l == 0 else xres, oTs_v, oTm_v,
                    ["oT_all%d" % q for q in range(4)], Wl, lnp[l],
                    out if last else xres, None if last else xb_loc, None if last else xexch)
        if last:
            P.barrier()
        else:
            P.barrier(skip=["c:xb%d" % j for j in range(4)])
            for j in range(4):
                P.res["xb_all%d" % j] = [("c:xb%d" % j, P.cnt["c:xb%d" % j]), {}]
    P.finish("sp")
    return nc, hc


def _wA_slices(w_in_l, r):
    def cols(base):
        return w_in_l[:, base + 128 * r: base + 128 * r + 128]
    return np.concatenate([cols(0), cols(512), cols(1024), cols(1536), cols(2048), cols(2560)], axis=1)


def kernel(x, w_in, w_branch_sb, w_branch_moba, w_out, ln_mix_g, ln_mix_b,
           w_ffn_gate, w_ffn_up, w_ffn_down, ln_ffn_g, ln_ffn_b):
    f = lambda a: np.asarray(a, dtype=np.float32)
    x, w_in, w_branch_sb, w_branch_moba, w_out = f(x), f(w_in), f(w_branch_sb), f(w_branch_moba), f(w_out)
    ln_mix_g, ln_mix_b, ln_ffn_g, ln_ffn_b = f(ln_mix_g), f(ln_mix_b), f(ln_ffn_g), f(ln_ffn_b)
    w_ffn_gate, w_ffn_up, w_ffn_down = f(w_ffn_gate), f(w_ffn_up), f(w_ffn_down)
    nc, hc = build_fused()
    cores = list(range(8))
    xTb = [np.ascontiguousarray(x[b].T) for b in range(B)]
    Wfull = {"wG": np.ascontiguousarray(w_in[:, :, 3072:5120]), "wbs": w_branch_sb, "wbm": w_branch_moba,
             "wout": w_out, "wfg": w_ffn_gate, "wfu": w_ffn_up, "wfd": w_ffn_down}
    lnp = np.ascontiguousarray(np.stack([np.concatenate(
        [_pp(ln_mix_g[l]), _pp(ln_mix_b[l]), _pp(ln_ffn_g[l]), _pp(ln_ffn_b[l])], axis=1) for l in range(DEPTH)]))
    maps = []
    for c in cores:
        b, r = c // 4, c % 4
        m = {"xTall0": xTb[b], "xT0": np.ascontiguousarray(xTb[b][:, r * TOK:(r + 1) * TOK]),
             "wA": np.ascontiguousarray(np.stack([_wA_slices(w_in[l], r) for l in range(DEPTH)])),
             "lnp": lnp}
        m.update({k: np.ascontiguousarray(v) for k, v in Wfull.items()})
        m.update({"c_" + k: v for k, v in hc.items()})
        maps.append(m)
    res = run_bass_kernel_spmd(nc, maps, core_ids=cores)
    xn = np.empty_like(x)
    for c in cores:
        b, r = c // 4, c % 4
        xn[b, r * TOK:(r + 1) * TOK, :] = np.asarray(res.results[c]["out"]).T
    return xn
```

```python
from contextlib import ExitStack
import numpy as np
import concourse.bass as bass
import concourse.mybir as mybir
from concourse.bass_utils import run_bass_kernel_spmd

F32 = mybir.dt.float32
BF16 = mybir.dt.bfloat16
AF = mybir.ActivationFunctionType
ALU = mybir.AluOpType
AX = mybir.AxisListType

D = 1024
S = 8192
B = 2
DEPTH = 2
DFF = 2816
NFC = DFF // 128
NG = S // 512
ALPHA = (2 * DEPTH) ** 0.25
EPS = 1e-5
BIG = 30000.0
NEG = -1.0e30
TOK = 2048
NTG = TOK // 512


class Prog:
    SAME_ENGINE_SYNC = True
    EMBED = ("act", "dve")

    def __init__(self, nc):
        self.nc = nc
        self.eng = {"pe": nc.tensor, "act": nc.scalar, "dve": nc.vector,
                    "pool": nc.gpsimd, "sp": nc.sync}
        self.sem = {}
        self.cnt = {}
        self.res = {}
        self.known = {e: {} for e in self.eng}
        self.nwait = 0
        self.nops = 0
        self._pending = None

    def _sem(self, lane):
        if lane not in self.sem:
            self.sem[lane] = self.nc.alloc_semaphore(name="s_" + lane.replace(":", "_"))
            self.cnt[lane] = 0
        return self.sem[lane]

    def _need(self, e, lane, val):
        if lane == e and (e == "pe" or not self.SAME_ENGINE_SYNC):
            return
        if self.known[e].get(lane, 0) >= val:
            return
        self.known[e][lane] = val
        self.nwait += 1
        if self._pending is not None:
            self._pending.append((lane, val))
            return
        self.eng[e].wait_ge(self._sem(lane), val)

    def _deps(self, e, reads, writes):
        for r in reads:
            ent = self.res.get(r)
            if ent and ent[0]:
                self._need(e, *ent[0])
        for w in writes:
            ent = self.res.get(w)
            if ent:
                if ent[0]:
                    self._need(e, *ent[0])
                for lane, val in ent[1].items():
                    self._need(e, lane, val)

    def _record(self, lane, val, reads, writes):
        for r in reads:
            ent = self.res.setdefault(r, [None, {}])
            ent[1][lane] = max(ent[1].get(lane, 0), val)
        for w in writes:
            self.res[w] = [(lane, val), {}]

    def op(self, e, fn, reads=(), writes=(), signal=True):
        self._sem(e)
        self._pending = [] if e in self.EMBED else None
        self._deps(e, reads, writes)
        pend = self._pending or []
        self._pending = None
        for lane_, val_ in pend[:-1]:
            self.eng[e].wait_ge(self._sem(lane_), val_)
        ins = fn(self.eng[e])
        if pend:
            ins._wait_ge(self._sem(pend[-1][0]), pend[-1][1])
        val = self.cnt[e] + 1
        if signal:
            ins.then_inc(self.sem[e], 1)
            self.cnt[e] = val
        self._record(e, val, reads, writes)
        self.nops += 1
        return ins

    def dma(self, e, lane, out, in_, reads=(), writes=(), **kw):
        lane = "d:" + lane
        self._sem(lane)
        self._deps(e, reads, writes)
        ins = self.eng[e].dma_start(out=out, in_=in_, **kw)
        val = self.cnt[lane] + 16
        ins.then_inc(self.sem[lane], 16)
        self.cnt[lane] = val
        self._record(lane, val, reads, writes)
        self.nops += 1
        return ins

    def collective(self, lane, kind, rg, in_ap, out_ap, reads=(), writes=()):
        lane = "c:" + lane
        self._sem(lane)
        self._deps("pool", reads, writes)
        ins = self.nc.gpsimd.collective_compute(kind, ALU.bypass, replica_groups=rg,
                                                ins=[in_ap], outs=[out_ap])
        val = self.cnt[lane] + 1
        ins.then_inc(self.sem[lane], 1)
        self.cnt[lane] = val
        self._record(lane, val, reads, writes)
        self.nops += 1
        return ins

    def barrier(self, skip=()):
        for e in self.eng:
            for lane, val in self.cnt.items():
                if lane in skip:
                    continue
                if val > 0 and lane != e:
                    if self.known[e].get(lane, 0) < val:
                        self.eng[e].wait_ge(self.sem[lane], val)
                        self.known[e][lane] = val
        self.res = {}

    def finish(self, e="sp"):
        for lane, val in self.cnt.items():
            if val > 0 and lane != e:
                self.eng[e].wait_ge(self.sem[lane], val)


def host_consts():
    c = {}
    k = np.arange(128)[:, None]
    s = np.arange(128)[None, :]
    c["negtri"] = np.where(k >= s, -1.0, 0.0).astype(np.float32)
    c["negones"] = np.full((128, 128), -1.0, np.float32)
    c["ident"] = np.eye(128, dtype=np.float32)
    sk = np.arange(128)[:, None, None] + 128 * np.arange(4)[None, :, None]
    t = np.arange(512)[None, None, :]
    c["msb"] = (sk < t).astype(np.float32)
    c["mmb"] = (sk <= t).astype(np.float32)
    half = 8
    inv = (500000.0 ** (-np.arange(0, 16, 2, dtype=np.float32) / 16)).astype(np.float32)
    ang = (np.arange(S, dtype=np.float32)[:, None] * inv[None, :]).astype(np.float32)
    cos = np.cos(ang).astype(np.float32).T
    sin = np.sin(ang).astype(np.float32).T
    cosT = np.ones((64, S), np.float32)
    sinT = np.zeros((64, S), np.float32)
    cosT[0:8] = cos
    cosT[8:16] = cos
    sinT[0:8] = sin
    sinT[8:16] = sin
    c["cosT"] = np.concatenate([cosT, cosT], 0)
    c["sinT"] = np.concatenate([sinT, sinT], 0)
    n = np.arange(32)[None, :]
    own = (np.arange(64) // 2)[:, None]
    cb = np.where(n < own, 0.0, NEG).astype(np.float32)
    vb = np.where(n < own, BIG, 0.0).astype(np.float32)
    oo = np.where(n == own, 0.0, -BIG).astype(np.float32)
    c["cb"] = np.broadcast_to(cb.reshape(1, 64 * 32), (128, 64 * 32)).copy()
    c["vb"] = np.broadcast_to(vb.reshape(1, 64 * 32), (128, 64 * 32)).copy()
    c["oo"] = np.broadcast_to(oo.reshape(1, 64 * 32), (128, 64 * 32)).copy()
    c["onehot"] = (np.arange(32)[:, None] == (np.arange(S)[None, :] // 256)).astype(np.float32)
    return c


def build_A(nc, P, ctx, sfx, xsrc, xeng, xreads, wA, oT, C, parts=3, after_mb_group=None):
    def A(name, shape, dt):
        return ctx.enter_context(nc.sbuf_tensor(name + sfx, shape, dt))
    wv = wA.rearrange("(c p) n -> p c n", p=128)

    negtri = A("negtri", [128, 128], BF16)
    negones = A("negones", [128, 128], BF16)
    identb = A("identb", [128, 128], BF16)
    onesf = A("onesf", [128, 64], F32)
    msb = A("msb", [128, 4, 512], BF16)
    mmb = A("mmb", [128, 4, 512], BF16)
    w = A("wA_sb", [128, 8, 1024], BF16)
    P.dma("pool", "c0", negtri[:], C["negtri"], writes=["negtri"])
    P.dma("pool", "c1", negones[:], C["negones"], writes=["negones"])
    P.dma("pool", "c2", identb[:], C["ident"], writes=["identb"])
    P.dma("pool", "c3", msb[:], C["msb"], writes=["msb"])
    P.dma("pool", "c4", mmb[:], C["mmb"], writes=["mmb"])
    P.dma("pool", "w", w[:, :, 0:768], wv, writes=["w"])
    P.op("dve", lambda e: e.memset(onesf[:], 1.0), writes=["onesf"])
    P.op("dve", lambda e: e.memset(w[:, :, 768:1024], 0.0), reads=["w"], writes=["w"])
    for qk in range(2):
        for hd in range(2):
            src = 384 + qk * 128 + hd * 64
            dst = 768 + qk * 128 + hd * 64
            P.op("act", lambda e, s_=src, d_=dst: e.mul(w[:, :, d_:d_ + 8], w[:, :, s_ + 8:s_ + 16], -1.0),
                 reads=["w"], writes=["w"])
            P.op("act", lambda e, s_=src, d_=dst: e.copy(w[:, :, d_ + 8:d_ + 16], w[:, :, s_:s_ + 8]),
                 reads=["w"], writes=["w"])

    psbig = ctx.enter_context(nc.psum_tensor("psbig" + sfx, [128, 4096], F32))
    ps = [psbig[:, i * 512:(i + 1) * 512] for i in range(8)]
    psn = ["ps%d" % i for i in range(7)]

    xg = [A("xg%d" % i, [128, 8, 512], BF16) for i in range(2)]
    ost = [A("ost%d" % i, [128, 512], BF16) for i in range(2)]

    def load_x(tg):
        b = tg % 2
        for pi, (c0, c1, sap) in enumerate(xsrc(tg)):
            P.dma(xeng, "xg%d_%d" % (b, pi), xg[b][:, c0:c1, :], sap,
                  reads=(xreads(tg) if callable(xreads) else xreads), writes=["xg%d" % b])

    def proj_fm(bank, col0, tg):
        b = tg % 2
        for c in range(8):
            P.op("pe", lambda e, c=c: e.matmul(ps[bank][:], w[:, c, col0:col0 + 128], xg[b][:, c, :],
                                               start=(c == 0), stop=(c == 7)),
                 reads=["w", "xg%d" % b], writes=[psn[bank]], signal=(c == 7))

    def proj_tm(bank, col0, tg, ncols=128):
        b = tg % 2
        for ts in range(4):
            for c in range(8):
                P.op("pe", lambda e, c=c, ts=ts: e.matmul(ps[bank][:, ts * 128:(ts + 1) * 128],
                                                          xg[b][:, c, ts * 128:(ts + 1) * 128],
                                                          w[:, c, col0:col0 + 128],
                                                          start=(c == 0), stop=(c == 7)),
                     reads=["w", "xg%d" % b], writes=[psn[bank]], signal=(c == 7 and ts == 3))

    R = [A("R%d" % i, [128, S], BF16) for i in range(4)]
    Vm = A("Vm", [128, 64, 2, 128], BF16)
    QT, KT = R[0], R[1]
    V = A("V", [128, 64, 128], BF16)
    if parts & 1:
        load_x(0)
    for tg in range(NG if parts & 1 else 0):
        if tg + 1 < NG:
            load_x(tg + 1)
        sl = slice(tg * 512, (tg + 1) * 512)
        proj_fm(0, 0, tg)
        P.op("act", lambda e, sl=sl: e.mul(QT[:, sl], ps[0][:], 0.125), reads=["ps0"], writes=["QT"])
        proj_fm(1, 128, tg)
        P.op("dve", lambda e, sl=sl: e.tensor_copy(KT[:, sl], ps[1][:]), reads=["ps1"], writes=["KT"])
        proj_tm(2, 256, tg)
        P.op("act", lambda e, tg=tg: e.copy(V[:, tg * 4:(tg + 1) * 4, :],
                                           ps[2][:].rearrange("p (a b) -> p a b", a=4)),
             reads=["ps2"], writes=["V"])

    Eb = [A("Eb%d" % i, [128, 512], F32) for i in range(2)]
    Lb = [A("Lb%d" % i, [128, 512], BF16) for i in range(3)]
    Ab = [A("Ab%d" % i, [128, 512], BF16) for i in range(3)]
    Lrun = [A("Lrun%d" % i, [128, 512], F32) for i in range(2)]
    Lrb = [[A("Lrb%d_%d" % (h, i), [128, 512], BF16) for i in range(2)] for h in range(2)]

    items = []
    for g in range(NG):
        njt = 4 * g + 4
        for k in range(njt):
            for h in range(2):
                items.append((h, g, k, njt - 1 - k, njt))
    NI = len(items) if parts & 1 else 0

    def sb_s1(i):
        h, g, k, j, njt = items[i]
        zb = i % 4
        hp = slice(h * 64, (h + 1) * 64)
        P.op("pe", lambda e: e.matmul(ps[zb][:], KT[hp, j * 128:(j + 1) * 128], QT[hp, g * 512:(g + 1) * 512],
                                      start=True, stop=True),
             reads=["QT", "KT"], writes=[psn[zb]])

    def sb_s2(i):
        h, g, k, j, njt = items[i]
        zb = i % 4
        eb = "Eb%d" % (i % 2)
        lb = "Lb%d" % (i % 3)
        E = Eb[i % 2]
        L = Lb[i % 3]
        P.op("act", lambda e: e.activation(E[:], ps[zb][:], AF.Exp), reads=[psn[zb]], writes=[eb])

    def sb_s2b(i):
        h, g, k, j, njt = items[i]
        zb = i % 4
        eb = "Eb%d" % (i % 2)
        lb = "Lb%d" % (i % 3)
        E = Eb[i % 2]
        L = Lb[i % 3]
        P.op("act", lambda e: e.activation(L[:], E[:], AF.Ln, bias=1.0), reads=[eb], writes=[lb])
        if k < 4:
            jj = 3 - k
            P.op("dve", lambda e: e.tensor_tensor(L[:], L[:], msb[:, jj, :], ALU.mult),
                 reads=[lb, "msb"], writes=[lb])
        if k < njt - 1:
            lr = "Lrun%d" % h
            nb = "Lrb%d_%d" % (h, (k + 1) % 2)
            dst = Lrb[h][(k + 1) % 2]
            if k == 0:
                P.op("dve", lambda e: e.tensor_copy(Lrun[h][:], L[:]), reads=[lb], writes=[lr])
                P.op("dve", lambda e: e.tensor_copy(dst[:], L[:]), reads=[lb], writes=[nb])
            else:
                P.op("dve", lambda e: e.tensor_tensor(Lrun[h][:], Lrun[h][:], L[:], ALU.add),
                     reads=[lb, lr], writes=[lr])
                P.op("dve", lambda e: e.tensor_copy(dst[:], Lrun[h][:]), reads=[lr], writes=[nb])

    def sb_s3(i):
        h, g, k, j, njt = items[i]
        zb = i % 4
        lb = "Lb%d" % (i % 3)
        L = Lb[i % 3]
        P.op("pe", lambda e: e.matmul(ps[zb][:], negtri[:], L[:], start=False, stop=(k == 0),
                                      skip_group_check=True),
             reads=["negtri", lb], writes=[psn[zb]], signal=(k == 0))
        if k > 0:
            cur = Lrb[h][k % 2]
            P.op("pe", lambda e: e.matmul(ps[zb][:], negones[:], cur[:], start=False, stop=True,
                                          skip_group_check=True),
                 reads=["negones", "Lrb%d_%d" % (h, k % 2)], writes=[psn[zb]])

    def sb_s4(i):
        h, g, k, j, njt = items[i]
        zb = i % 4
        ab = "Ab%d" % (i % 3)
        Aa = Ab[i % 3]
        P.op("act", lambda e: e.activation(Aa[:], ps[zb][:], AF.Exp), reads=[psn[zb]], writes=[ab])
        if k < 4:
            jj = 3 - k
            P.op("dve", lambda e: e.tensor_tensor(Aa[:], Aa[:], msb[:, jj, :], ALU.mult),
                 reads=[ab, "msb"], writes=[ab])

    def sb_s5(i):
        h, g, k, j, njt = items[i]
        ab = "Ab%d" % (i % 3)
        Aa = Ab[i % 3]
        ob = 4 + (2 * g + h) % 3
        P.op("pe", lambda e: e.matmul(ps[ob][0:64, :], V[:, j, h * 64:(h + 1) * 64], Aa[:],
                                      start=(k == 0), stop=(k == njt - 1), skip_group_check=True),
             reads=["V", ab], writes=[psn[ob]], signal=True)
        if k == njt - 1:
            so = "ost%d" % (g % 2)
            P.op("dve", lambda e: e.tensor_copy(ost[g % 2][h * 64:(h + 1) * 64, :], ps[ob][0:64, :]),
                 reads=[psn[ob]], writes=[so + "_%d" % h])
            if h == 1:
                P.dma("sp", so, oT(0, g), ost[g % 2][:],
                      reads=[so + "_0", so + "_1"], writes=["oT0_%d" % g])

    order = [(sb_s1, 0), (sb_s2, 1), (sb_s4, 3), (sb_s2b, 1), (sb_s3, 2), (sb_s5, 4)]
    for step in range(NI + 4):
        for fn, lag in order:
            i = step - lag
            if 0 <= i < NI:
                fn(i)

    P.barrier()
    if not parts & 2:
        return
    QTa = [R[0], R[1]]
    KTa = [R[2], R[3]]
    cosv, sinv = C["cosT"], C["sinT"]
    csb = [[A("cs%d_%d" % (a, i), [128, 512], F32) for i in range(2)] for a in range(2)]
    t1 = A("t1", [128, 512], F32)
    t2 = A("t2", [128, 512], F32)
    qf = A("qf", [128, 512], F32)
    kf = A("kf", [128, 512], F32)
    kmT = A("kmT", [128, 2, 32], F32)
    gm = [A("gm%d" % i, [128, 32], F32) for i in range(2)]
    top8 = [A("top8_%d" % i, [128, 8], F32) for i in range(2)]
    mb1 = [A("mb1_%d" % i, [128, 32], F32) for i in range(2)]
    mbpad = [A("mbpad%d" % i, [128, 128], BF16) for i in range(8)]
    cbrow = A("cbrow", [128, 64], F32)
    vbrow = A("vbrow", [128, 64], F32)
    oorow = A("oorow", [128, 64], F32)
    Pb = [A("Pb%d" % i, [128, 512], BF16) for i in range(3)]
    rec = A("rec", [128, 512], F32)
    bcs = A("bcs", [64, 512], F32)

    P.op("dve", lambda e: e.memset(cbrow[:, 0:32], 0.0), writes=["cbrow"])
    P.op("dve", lambda e: e.memset(cbrow[:, 32:64], NEG), writes=["cbrow"])
    P.op("dve", lambda e: e.memset(vbrow[:, 0:32], BIG), writes=["vbrow"])
    P.op("dve", lambda e: e.memset(vbrow[:, 32:64], 0.0), writes=["vbrow"])
    P.op("dve", lambda e: e.memset(oorow[:], -BIG), writes=["oorow"])
    P.op("dve", lambda e: e.memset(oorow[:, 31:32], 0.0), writes=["oorow"])
    P.op("dve", lambda e: e.memset(kmT[:], 0.0), writes=["kmT"])
    P.op("dve", lambda e: e.memset(Vm[:], 1.0), writes=["Vm"])
    for i in range(8):
        P.op("dve", lambda e, i=i: e.memset(mbpad[i][:], 0.0), writes=["mbpad%d" % i])
    for h in range(2):
        P.op("dve", lambda e, h=h: e.memset(QTa[h][64:128, :], 0.0), writes=["QTa%d" % h])
        P.op("dve", lambda e, h=h: e.memset(KTa[h][64:128, :], 0.0), writes=["KTa%d" % h])
        P.dma("pool", "oh%d" % h, KTa[h][64:96, :], C["onehot"], writes=["KTa%d" % h])

    def load_cs(tg):
        b = tg % 2
        P.dma("sp", "cs0_%d" % b, csb[0][b][:], cosv[:, tg * 512:(tg + 1) * 512], writes=["cs0_%d" % b])
        P.dma("sp", "cs1_%d" % b, csb[1][b][:], sinv[:, tg * 512:(tg + 1) * 512], writes=["cs1_%d" % b])

    def rope(dst, b, bank_a, bank_b):
        cn, sn = "cs0_%d" % b, "cs1_%d" % b
        P.op("dve", lambda e: e.tensor_tensor(t1[:], ps[bank_a][:], csb[0][b][:], ALU.mult),
             reads=[psn[bank_a], cn], writes=["t1"])
        P.op("dve", lambda e: e.tensor_tensor(t2[:], ps[bank_b][:], csb[1][b][:], ALU.mult),
             reads=[psn[bank_b], sn], writes=["t2"])
        nm = "qf" if dst is qf else "kf"
        P.op("dve", lambda e: e.tensor_tensor(dst[:], t1[:], t2[:], ALU.add),
             reads=["t1", "t2"], writes=[nm])

    def gate_tail(tgp):
        for idx in range(8):
            cq, h = idx // 2, idx % 2
            cch = tgp * 4 + cq
            u = idx % 2
            P.op("pe", lambda e, u=u, idx=idx: e.matmul(ps[6 + u][:, 0:128], mbpad[idx][:], identb[:],
                                                        start=True, stop=True),
                 reads=["mbpad%d" % idx, "identb"], writes=["ps6_%d" % u], signal=False)
            P.op("pe", lambda e, u=u: e.matmul(ps[6 + u][:, 256:384], identb[:], identb[:],
                                               start=True, stop=True),
                 reads=["identb"], writes=["ps6_%d" % u])
            P.op("act", lambda e, h=h, cch=cch, u=u: e.copy(QTa[h][64:96, cch * 128:(cch + 1) * 128],
                                                          ps[6 + u][64:96, 0:128]),
                 reads=["ps6_%d" % u], writes=["QTa%d" % h])

    load_x(0)
    load_cs(0)
    for tg in range(NG):
        if tg + 1 < NG:
            load_x(tg + 1)
            load_cs(tg + 1)
        b = tg % 2
        sl = slice(tg * 512, (tg + 1) * 512)
        proj_fm(0, 512, tg)
        proj_fm(1, 896, tg)
        rope(kf, b, 0, 1)
        for h in range(2):
            P.op("act", lambda e, h=h: e.copy(KTa[h][0:64, sl], kf[h * 64:(h + 1) * 64, :]),
                 reads=["kf"], writes=["KTa%d" % h])
        for h in range(2):
            hp2 = slice(h * 64, (h + 1) * 64)
            P.op("dve", lambda e, h=h, hp2=hp2: e.tensor_reduce(
                kmT[hp2, h, 2 * tg:2 * tg + 2], kf[hp2, :].rearrange("p (a b) -> p a b", a=2), AX.X, ALU.add),
                reads=["kf"], writes=["kmT"])
        proj_fm(2, 384, tg)
        proj_fm(3, 768, tg)
        rope(qf, b, 2, 3)
        for h in range(2):
            P.op("act", lambda e, h=h: e.mul(QTa[h][0:64, sl], qf[h * 64:(h + 1) * 64, :], 0.125),
                 reads=["qf"], writes=["QTa%d" % h])
        proj_tm(4, 640, tg)
        P.op("act", lambda e: e.copy(Vm[:, tg * 4:(tg + 1) * 4, :, 0:64],
                                    ps[4][:].rearrange("p (a h d) -> p a h d", a=4, h=2)),
             reads=["ps4"], writes=["Vm"])
        if tg > 0:
            gate_tail(tg - 1)
        for idx in range(8):
            cq, h = idx // 2, idx % 2
            P.op("pe", lambda e, cq=cq, h=h, idx=idx: e.matmul(
                ps[5][:, idx * 32:(idx + 1) * 32], qf[:, cq * 128:(cq + 1) * 128],
                kmT[:, h, :], start=True, stop=True),
                reads=["qf", "kmT"], writes=["ps5"], signal=False)
        P.op("pe", lambda e: e.matmul(ps[5][:, 256:384], identb[:], identb[:], start=True, stop=True),
             reads=["identb"], writes=["ps5"])
        for idx in range(8):
            cq, h = idx // 2, idx % 2
            cch = tg * 4 + cq
            own = cch // 2
            u = idx % 2
            P.op("dve", lambda e, idx=idx, own=own, u=u: e.tensor_tensor(
                gm[u][:], ps[5][:, idx * 32:(idx + 1) * 32], cbrow[:, 32 - own:64 - own], ALU.add),
                reads=["ps5", "cbrow"], writes=["gm%d" % u])
            P.op("dve", lambda e, u=u: e.max(top8[u][:], gm[u][:]), reads=["gm%d" % u], writes=["top8_%d" % u])
            P.op("dve", lambda e, own=own, u=u: e.scalar_tensor_tensor(
                mb1[u][:], gm[u][:], top8[u][:, 2:3], vbrow[:, 32 - own:64 - own], ALU.is_ge, ALU.mult),
                reads=["gm%d" % u, "top8_%d" % u, "vbrow"], writes=["mb1_%d" % u])
            P.op("dve", lambda e, own=own, u=u, idx=idx: e.tensor_tensor(
                mbpad[idx][:, 64:96], mb1[u][:], oorow[:, 31 - own:63 - own], ALU.add),
                reads=["mb1_%d" % u, "oorow"], writes=["mbpad%d" % idx])
    gate_tail(NG - 1)

    P.barrier()
    Pb2 = [A("Pb2_%d" % i, [128, 1024], BF16) for i in range(3)]
    mitems = []
    for g in range(NG):
        njt = 4 * g + 4
        for j in range(njt):
            mitems.append((g, j, njt))
    NM = len(mitems)

    def mb_s1(i):
        g, j, njt = mitems[i]
        zp = i % 2
        for h in range(2):
            P.op("pe", lambda e, h=h: e.matmul(psbig[:, zp * 1024 + h * 512:zp * 1024 + (h + 1) * 512],
                                               KTa[h][:, j * 128:(j + 1) * 128], QTa[h][:, g * 512:(g + 1) * 512],
                                               start=True, stop=True),
                 reads=["QTa%d" % h, "KTa%d" % h], writes=["zp%d" % zp], signal=(h == 1))

    def mb_s2(i):
        g, j, njt = mitems[i]
        zp = i % 2
        pb = "Pb2_%d" % (i % 3)
        Pt = Pb2[i % 3]
        P.op("act", lambda e: e.activation(Pt[:], psbig[:, zp * 1024:(zp + 1) * 1024], AF.Exp),
             reads=["zp%d" % zp], writes=[pb])
        if j >= 4 * g:
            jj = j - 4 * g
            for h in range(2):
                P.op("dve", lambda e, h=h: e.tensor_tensor(Pt[:, h * 512:(h + 1) * 512], Pt[:, h * 512:(h + 1) * 512],
                                                           mmb[:, jj, :], ALU.mult),
                     reads=[pb, "mmb"], writes=[pb])

    def mb_s3(i):
        g, j, njt = mitems[i]
        pb = "Pb2_%d" % (i % 3)
        Pt = Pb2[i % 3]
        for h in range(2):
            ob = 4 + (2 * g + h) % 3
            P.op("pe", lambda e, h=h, ob=ob: e.matmul(ps[ob][:, :], Vm[:, j, h, :], Pt[:, h * 512:(h + 1) * 512],
                                                      start=(j == 0), stop=(j == njt - 1), skip_group_check=True),
                 reads=["Vm", pb], writes=[psn[ob]], signal=True)
            if j == njt - 1:
                so = "ost%d" % (g % 2)
                P.op("dve", lambda e, ob=ob: e.reciprocal(rec[64:128, :], ps[ob][64:128, :]),
                     reads=[psn[ob]], writes=["rec"])
                P.op("dve", lambda e, h=h, ob=ob: e.tensor_tensor(ost[g % 2][h * 64:(h + 1) * 64, :], ps[ob][0:64, :],
                                                                  rec[64:128, :], ALU.mult),
                     reads=[psn[ob], "rec"], writes=[so + "_%d" % h])
                if h == 1:
                    P.dma("sp", so, oT(1, g), ost[g % 2][:],
                          reads=[so + "_0", so + "_1"], writes=["oT1_%d" % g])
                    if after_mb_group is not None:
                        after_mb_group(g)

    mstages = [mb_s1, mb_s2, mb_s3]
    for step in range(NM + len(mstages) - 1):
        for s_, fn in enumerate(mstages):
            i = step - s_
            if 0 <= i < NM:
                fn(i)
    P.barrier(skip=[ln_ for ln_ in P.cnt if ln_.startswith("c:oT")])


def build_B(nc, P, ctx, sfx, xT_d, oTs_v, oTm_v, oreads, W, lnp_d, out_d, outb_d=None, xb_exch=None):
    def A(name, shape, dt):
        return ctx.enter_context(nc.sbuf_tensor(name + sfx, shape, dt))
    xT = A("xT_sb", [128, 8, TOK], F32)
    xb = A("xb_sb", [128, 8, TOK], BF16)
    arena = A("arena", [128, 32768], BF16)
    oTs = arena[:, 0:8192].rearrange("p (a b) -> p a b", a=4)
    oTm = arena[:, 8192:16384].rearrange("p (a b) -> p a b", a=4)
    mg = arena[:, 16384:32768].rearrange("p (a b) -> p a b", a=8)
    hT = arena[:, 0:NFC * 1024].rearrange("p (a b) -> p a b", a=NFC)
    lnp = A("lnp_sb", [128, 32], F32)
    onesD = A("onesD", [128, 128], F32)
    warena = A("warena", [128, 9728], BF16)
    wgb = [warena[:, i * 2048:(i + 1) * 2048].rearrange("p (a b) -> p a b", a=8) for i in range(2)]
    wbb = [warena[:, 4096 + i * 1024:4096 + (i + 1) * 1024].rearrange("p (a b) -> p a b", a=4) for i in range(2)]
    wob = [warena[:, 6144 + i * 1024:6144 + (i + 1) * 1024].rearrange("p (a b) -> p a b", a=8) for i in range(2)]
    wfgu = [warena[:, i * 2048:(i + 1) * 2048].rearrange("p (a b) -> p a b", a=8) for i in range(2)]
    wfdb = [warena[:, 4096 + i * 2816:4096 + (i + 1) * 2816].rearrange("p (a b) -> p a b", a=NFC) for i in range(2)]
    sg = [A("sg%d" % i, [128, 512], F32) for i in range(4)]
    m1 = [A("m1_%d" % i, [128, 512], F32) for i in range(2)]
    m2 = [A("m2_%d" % i, [128, 512], F32) for i in range(2)]
    ysq = [A("ysq%d" % i, [128, 512], F32) for i in range(2)]
    mean_sb = A("mean_sb", [128, 512], F32)
    rstd_sb = A("rstd_sb", [128, 512], F32)
    tn = [A("tn%d" % i, [128, 512], F32) for i in range(2)]
    ps = [ctx.enter_context(nc.psum_tensor("pb%d" % i + sfx, [128, 512], F32)) for i in range(8)]
    psn = ["pb%d" % i for i in range(8)]
    bank = [0]
    XN = [["xT%d_%d" % (c, g) for g in range(NTG)] for c in range(8)]
    XB = [["xb%d_%d" % (c, g) for g in range(NTG)] for c in range(8)]
    MG = [["mg%d_%d" % (c, g) for g in range(NTG)] for c in range(8)]
    HT = [["hT%d_%d" % (k, g) for g in range(2)] for k in range(NFC)]
    allx = [n for r_ in XN for n in r_]
    allxb = [n for r_ in XB for n in r_]

    def nb():
        bank[0] = (bank[0] + 1) % 8
        return bank[0]

    xdv = xT_d.rearrange("(c p) t -> p c t", p=128)
    P.dma("sp", "xT", xT[:], xdv, writes=allx)
    P.dma("pool", "xb", xb[:], xdv, writes=allxb)
    P.dma("sp", "oTs", oTs, oTs_v, reads=oreads, writes=["oTs"])
    P.dma("sp", "oTm", oTm, oTm_v, reads=oreads, writes=["oTm"])
    P.dma("sp", "lnp", lnp[:], lnp_d, writes=["lnp"])
    P.op("dve", lambda e: e.memset(onesD[:], 1.0 / D), writes=["onesD"])

    wGv = W["wG"].rearrange("(k p) n -> p k n", p=128)
    wbsv = W["wbs"].rearrange("(k p) n -> p k n", p=128)
    wbmv = W["wbm"].rearrange("(k p) n -> p k n", p=128)
    wov = W["wout"].rearrange("(k p) n -> p k n", p=128)
    wfgv = W["wfg"].rearrange("(k p) n -> p k n", p=128)
    wfuv = W["wfu"].rearrange("(k p) n -> p k n", p=128)
    wfdv = W["wfd"].rearrange("(k p) n -> p k n", p=128)

    def tsl(tg):
        return slice(tg * 512, (tg + 1) * 512)

    def load_b1(c):
        b = c % 2
        cs = slice(c * 128, (c + 1) * 128)
        P.dma("pool", "wgb%da" % b, wgb[b][:, :, 0:128], wGv[:, :, cs], writes=["wgb%da" % b])
        P.dma("pool", "wgb%db" % b, wgb[b][:, :, 128:256], wGv[:, :, D + c * 128:D + (c + 1) * 128],
              writes=["wgb%db" % b])
        P.dma("pool", "wbb%da" % b, wbb[b][:, :, 0:128], wbsv[:, :, cs], writes=["wbb%da" % b])
        P.dma("pool", "wbb%db" % b, wbb[b][:, :, 128:256], wbmv[:, :, cs], writes=["wbb%db" % b])

    load_b1(0)
    it = 0
    for c in range(8):
        if c + 1 < 8:
            load_b1(c + 1)
        b = c % 2
        for tg in range(NTG):
            t = tsl(tg)
            bk = [nb() for _ in range(4)]
            for half, (bkk, wn) in enumerate(zip(bk[0:2], ["wgb%da" % b, "wgb%db" % b])):
                for k in range(8):
                    P.op("pe", lambda e, k=k, half=half, bkk=bkk: e.matmul(
                        ps[bkk][:], wgb[b][:, k, half * 128:(half + 1) * 128], xb[:, k, t],
                        start=(k == 0), stop=(k == 7)),
                        reads=[wn, XB[k][tg]], writes=[psn[bkk]], signal=(k == 7))
            for half, (bkk, wn, src, sn) in enumerate(zip(bk[2:4], ["wbb%da" % b, "wbb%db" % b],
                                                          [oTs, oTm], ["oTs", "oTm"])):
                for k in range(4):
                    P.op("pe", lambda e, k=k, half=half, bkk=bkk, src=src: e.matmul(
                        ps[bkk][:], wbb[b][:, k, half * 128:(half + 1) * 128], src[:, k, t],
                        start=(k == 0), stop=(k == 3)),
                        reads=[wn, sn], writes=[psn[bkk]], signal=(k == 3))
            u = it % 2
            s0, s1 = sg[2 * u], sg[2 * u + 1]
            P.op("act", lambda e, s0=s0, bkk=bk[0]: e.activation(s0[:], ps[bkk][:], AF.Sigmoid),
                 reads=[psn[bk[0]]], writes=["sg%d" % (2 * u)])
            P.op("act", lambda e, s1=s1, bkk=bk[1]: e.activation(s1[:], ps[bkk][:], AF.Sigmoid),
                 reads=[psn[bk[1]]], writes=["sg%d" % (2 * u + 1)])
            P.op("dve", lambda e, s0=s0, u=u, bkk=bk[2]: e.tensor_tensor(m1[u][:], s0[:], ps[bkk][:], ALU.mult),
                 reads=[psn[bk[2]], "sg%d" % (2 * u)], writes=["m1_%d" % u])
            P.op("dve", lambda e, s1=s1, u=u, bkk=bk[3]: e.tensor_tensor(m2[u][:], s1[:], ps[bkk][:], ALU.mult),
                 reads=[psn[bk[3]], "sg%d" % (2 * u + 1)], writes=["m2_%d" % u])
            P.op("pool", lambda e, u=u, c=c, t=t: e.tensor_tensor(mg[:, c, t], m1[u][:], m2[u][:], ALU.add),
                 reads=["m1_%d" % u, "m2_%d" % u], writes=[MG[c][tg]])
            it += 1

    def layer_norm(tg, gcol, bcol):
        t = tsl(tg)
        ba, bb_ = nb(), nb()
        for c in range(8):
            q = ysq[c % 2]
            P.op("act", lambda e, q=q, c=c: e.activation(q[:], xT[:, c, t], AF.Square),
                 reads=[XN[c][tg]], writes=["ysq%d" % (c % 2)])
            P.op("pe", lambda e, c=c: e.matmul(ps[ba][:], onesD[:], xT[:, c, t], start=(c == 0), stop=(c == 7)),
                 reads=["onesD", XN[c][tg]], writes=[psn[ba]], signal=(c == 7))
            P.op("pe", lambda e, q=q, c=c: e.matmul(ps[bb_][:], onesD[:], q[:], start=(c == 0), stop=(c == 7),
                                                   skip_group_check=True),
                 reads=["onesD", "ysq%d" % (c % 2)], writes=[psn[bb_]], signal=True)
        P.op("dve", lambda e: e.tensor_copy(mean_sb[:], ps[ba][:]), reads=[psn[ba]], writes=["mean_sb"])
        P.op("dve", lambda e: e.tensor_tensor(rstd_sb[:], mean_sb[:], mean_sb[:], ALU.mult),
             reads=["mean_sb"], writes=["rstd_sb"])
        P.op("dve", lambda e: e.tensor_tensor(rstd_sb[:], ps[bb_][:], rstd_sb[:], ALU.subtract),
             reads=[psn[bb_], "rstd_sb"], writes=["rstd_sb"])
        P.op("act", lambda e: e.activation(rstd_sb[:], rstd_sb[:], AF.Ln, bias=EPS),
             reads=["rstd_sb"], writes=["rstd_sb"])
        P.op("act", lambda e: e.activation(rstd_sb[:], rstd_sb[:], AF.Exp, scale=-0.5),
             reads=["rstd_sb"], writes=["rstd_sb"])
        for c in range(8):
            tt = tn[c % 2]
            tnn = "tn%d" % (c % 2)
            P.op("dve", lambda e, tt=tt, c=c: e.tensor_tensor(tt[:], xT[:, c, t], mean_sb[:], ALU.subtract),
                 reads=[XN[c][tg], "mean_sb"], writes=[tnn])
            P.op("dve", lambda e, tt=tt: e.tensor_tensor(tt[:], tt[:], rstd_sb[:], ALU.mult),
                 reads=[tnn, "rstd_sb"], writes=[tnn])
            P.op("act", lambda e, tt=tt, c=c: e.activation(xT[:, c, t], tt[:], AF.Identity,
                                                          bias=lnp[:, bcol + c:bcol + c + 1],
                                                          scale=lnp[:, gcol + c:gcol + c + 1]),
                 reads=[tnn, "lnp"], writes=[XN[c][tg]])
            P.op("pool", lambda e, c=c: e.tensor_copy(xb[:, c, t], xT[:, c, t]), reads=[XN[c][tg]], writes=[XB[c][tg]])

    def load_wo(c):
        b = c % 2
        P.dma("pool", "wob%d" % b, wob[b][:], wov[:, :, c * 128:(c + 1) * 128], writes=["wob%d" % b])

    load_wo(0)
    for c in range(8):
        if c + 1 < 8:
            load_wo(c + 1)
        b = c % 2
        for tg in range(NTG):
            t = tsl(tg)
            bkk = nb()
            for k in range(8):
                P.op("pe", lambda e, k=k, bkk=bkk: e.matmul(ps[bkk][:], wob[b][:, k, :], mg[:, k, t],
                                                            start=(k == 0), stop=(k == 7)),
                     reads=["wob%d" % b, MG[k][tg]], writes=[psn[bkk]], signal=(k == 7))
            P.op("dve", lambda e, bkk=bkk, c=c, t=t: e.scalar_tensor_tensor(
                xT[:, c, t], xT[:, c, t], ALPHA, ps[bkk][:], ALU.mult, ALU.add),
                reads=[psn[bkk], XN[c][tg]], writes=[XN[c][tg]])
    for tg in range(NTG):
        layer_norm(tg, 0, 8)

    def load_gu(i):
        k = i % NFC
        b = i % 2
        P.dma("pool", "wfgu%da" % b, wfgu[b][:, :, 0:128], wfgv[:, :, k * 128:(k + 1) * 128],
              writes=["wfgu%da" % b])
        P.dma("pool", "wfgu%db" % b, wfgu[b][:, :, 128:256], wfuv[:, :, k * 128:(k + 1) * 128],
              writes=["wfgu%db" % b])

    def load_dn(i):
        c = i % 8
        b = i % 2
        P.dma("pool", "wfdb%d" % b, wfdb[b][:], wfdv[:, :, c * 128:(c + 1) * 128], writes=["wfdb%d" % b])

    gi = 0
    di = 0
    pend_x = []
    P.barrier()
    for hf in range(2):
        load_gu(gi)
        for k in range(NFC):
            if k + 1 < NFC:
                load_gu(gi + 1)
            if k == 3 and xb_exch is not None:
                while pend_x:
                    xb_exch(pend_x.pop(0))
            b = gi % 2
            for t2_ in range(2):
                tg = hf * 2 + t2_
                t = tsl(tg)
                b0, b1 = nb(), nb()
                for half, bkk in enumerate([b0, b1]):
                    wn = "wfgu%d%s" % (b, "ab"[half])
                    for kk in range(8):
                        P.op("pe", lambda e, kk=kk, half=half, bkk=bkk: e.matmul(
                            ps[bkk][:], wfgu[b][:, kk, half * 128:(half + 1) * 128], xb[:, kk, t],
                            start=(kk == 0), stop=(kk == 7)),
                            reads=[wn, XB[kk][tg]], writes=[psn[bkk]], signal=(kk == 7))
                u = it % 2
                s0 = sg[2 * u]
                P.op("act", lambda e, s0=s0, b0=b0: e.activation(s0[:], ps[b0][:], AF.Silu),
                     reads=[psn[b0]], writes=["sg%d" % (2 * u)])
                P.op("dve", lambda e, s0=s0, b1=b1, k=k, t2_=t2_: e.tensor_tensor(
                    hT[:, k, t2_ * 512:(t2_ + 1) * 512], s0[:], ps[b1][:], ALU.mult),
                    reads=[psn[b1], "sg%d" % (2 * u)], writes=[HT[k][t2_]])
                it += 1
            gi += 1
        load_dn(di)
        for c in range(8):
            if c + 1 < 8:
                load_dn(di + 1)
            b = di % 2
            for t2_ in range(2):
                tg = hf * 2 + t2_
                t = tsl(tg)
                bkk = nb()
                for k in range(NFC):
                    P.op("pe", lambda e, k=k, bkk=bkk, t2_=t2_: e.matmul(
                        ps[bkk][:], wfdb[b][:, k, :], hT[:, k, t2_ * 512:(t2_ + 1) * 512],
                        start=(k == 0), stop=(k == NFC - 1)),
                        reads=["wfdb%d" % b, HT[k][t2_]], writes=[psn[bkk]], signal=(k == NFC - 1))
                P.op("dve", lambda e, bkk=bkk, c=c, t=t: e.scalar_tensor_tensor(
                    xT[:, c, t], xT[:, c, t], ALPHA, ps[bkk][:], ALU.mult, ALU.add),
                    reads=[psn[bkk], XN[c][tg]], writes=[XN[c][tg]])
            di += 1
        for t2_ in range(2):
            tg = hf * 2 + t2_
            layer_norm(tg, 16, 24)
            if outb_d is not None:
                P.dma("sp", "outb%d" % tg, outb_d[tg * 1024:(tg + 1) * 1024, :].rearrange("(c p) t -> p c t", p=128),
                      xb[:, :, tsl(tg)], reads=[XB[c][tg] for c in range(8)], writes=["xbloc%d" % tg])
                if hf == 1 and xb_exch is not None:
                    xb_exch(tg)
                else:
                    pend_x.append(tg)
    P.dma("sp", "out", out_d.rearrange("(c p) t -> p c t", p=128), xT[:], reads=allx, writes=["out"])
    if xb_exch is not None:
        while pend_x:
            xb_exch(pend_x.pop(0))


W_KEYS = ["wG", "wbs", "wbm", "wout", "wfg", "wfu", "wfd"]
W_SHAPES = {"wG": [D, 2048], "wbs": [512, D], "wbm": [512, D], "wout": [D, D],
            "wfg": [D, DFF], "wfu": [D, DFF], "wfd": [DFF, D]}
RG = [[0, 1, 2, 3], [4, 5, 6, 7]]


def _pp(v):
    return np.ascontiguousarray(v.reshape(8, 128).T)


def build_fused(nlayers=DEPTH, skipA=False, skipB=False, samex=False, parts=3):
    nc = bass.Bass("TRN2", target_bir_lowering=False)
    P = Prog(nc)
    hc = host_consts()
    C = {k: nc.dram_tensor("c_" + k, list(v.shape), F32, kind="ExternalInput").ap() for k, v in hc.items()}
    xTall0 = nc.dram_tensor("xTall0", [D, S], F32, kind="ExternalInput").ap()
    xT0 = nc.dram_tensor("xT0", [D, TOK], F32, kind="ExternalInput").ap()
    wA = nc.dram_tensor("wA", [DEPTH, D, 768], F32, kind="ExternalInput").ap()
    Wd = {k: nc.dram_tensor(k, [DEPTH] + W_SHAPES[k], F32, kind="ExternalInput").ap() for k in W_KEYS}
    lnp = nc.dram_tensor("lnp", [DEPTH, 128, 32], F32, kind="ExternalInput").ap()
    out = nc.dram_tensor("out", [D, TOK], F32, kind="ExternalOutput").ap()
    oT_loc = nc.dram_tensor("oT_loc", [1024, TOK], BF16, kind="Internal").ap()
    oT_all = nc.dram_tensor("oT_all", [4096, TOK], BF16, kind="Internal").ap()
    xb_loc = nc.dram_tensor("xb_loc", [4 * D, 512], BF16, kind="Internal").ap()
    xb_all = nc.dram_tensor("xb_all", [16 * D, 512], BF16, kind="Internal").ap()
    xres = nc.dram_tensor("xres", [D, TOK], F32, kind="Internal").ap()
    pid = nc.sync.partition_id()
    roff = (pid % 4) * 1024
    olv = oT_loc.rearrange("(q b p) t -> b p q t", q=4, b=2)

    def odst(br, g):
        return olv[br, :, g // 4, (g % 4) * 512:(g % 4 + 1) * 512]

    for l in range(nlayers):
        last = l == nlayers - 1
        with ExitStack() as ctx:
            if l == 0 or samex:
                xv0 = xTall0.rearrange("(c p) t -> p c t", p=128)
                xsrc = lambda tg: [(0, 8, xv0[:, :, tg * 512:(tg + 1) * 512])]
                xeng, xreads = "pool", []
            else:
                xv1 = xb_all.rearrange("(g r c p) t -> p g r c t", g=4, r=4, c=8, p=128)
                xsrc = lambda tg: [(0, 8, xv1[:, tg % 4, tg // 4, :, :])]
                xeng, xreads = "sp", (lambda tg: ["xb_all%d" % (tg % 4)])
            if not skipA:
                issued = set()

                def exch(g, issued=issued):
                    if g % 4 == 3:
                        q = g // 4
                        P.collective("oT%d" % q, "AllGather", RG, oT_loc[q * 256:(q + 1) * 256, :],
                                     oT_all[q * 1024:(q + 1) * 1024, :],
                                     reads=["oT1_%d" % gg for gg in range(4 * q, 4 * q + 4)],
                                     writes=["oT_all%d" % q])
                        issued.add(q)

                build_A(nc, P, ctx, "_a%d" % l, xsrc, xeng, xreads, wA[l], odst, C, parts,
                        exch if parts == 3 else None)
        P.barrier(skip=["c:oT%d" % q for q in range(4)])
        for q in range(4):
            if skipA or q not in issued:
                P.collective("oT%d" % q, "AllGather", RG, oT_loc[q * 256:(q + 1) * 256, :],
                             oT_all[q * 1024:(q + 1) * 1024, :], writes=["oT_all%d" % q])
            else:
                P.res["oT_all%d" % q] = [("c:oT%d" % q, P.cnt["c:oT%d" % q]), {}]
        with ExitStack() as ctx:
            ov = oT_all[bass.ds(roff, 1024), :].rearrange("(k b p) t -> p k b t", k=4, b=2, p=128)
            oTs_v = ov[:, :, 0, :]
            oTm_v = ov[:, :, 1, :]
            Wl = {k: Wd[k][l] for k in W_KEYS}

            def xexch(tg):
                P.collective("xb%d" % tg, "AllGather", RG, xb_loc[tg * 1024:(tg + 1) * 1024, :],
                             xb_all[tg * 4096:(tg + 1) * 4096, :], reads=["xbloc%d" % tg],
                             writes=["xb_all%d" % tg])
            if not skipB:
              build_B(nc, P, ctx, "_b%d" % l, xT0 if l == 0 else xres, oTs_v, oTm_v,
                    ["oT_all%d" % q for q in range(4)], Wl, lnp[l],
                    out if last else xres, None if last else xb_loc, None if last else xexch)
        if last:
            P.barrier()
        else:
            P.barrier(skip=["c:xb%d" % j for j in range(4)])
            for j in range(4):
                P.res["xb_all%d" % j] = [("c:xb%d" % j, P.cnt["c:xb%d" % j]), {}]
    P.finish("sp")
    return nc, hc


def _wA_slices(w_in_l, r):
    def cols(base):
        return w_in_l[:, base + 128 * r: base + 128 * r + 128]
    return np.concatenate([cols(0), cols(512), cols(1024), cols(1536), cols(2048), cols(2560)], axis=1)


def kernel(x, w_in, w_branch_sb, w_branch_moba, w_out, ln_mix_g, ln_mix_b,
           w_ffn_gate, w_ffn_up, w_ffn_down, ln_ffn_g, ln_ffn_b):
    f = lambda a: np.asarray(a, dtype=np.float32)
    x, w_in, w_branch_sb, w_branch_moba, w_out = f(x), f(w_in), f(w_branch_sb), f(w_branch_moba), f(w_out)
    ln_mix_g, ln_mix_b, ln_ffn_g, ln_ffn_b = f(ln_mix_g), f(ln_mix_b), f(ln_ffn_g), f(ln_ffn_b)
    w_ffn_gate, w_ffn_up, w_ffn_down = f(w_ffn_gate), f(w_ffn_up), f(w_ffn_down)
    nc, hc = build_fused()
    cores = list(range(8))
    xTb = [np.ascontiguousarray(x[b].T) for b in range(B)]
    Wfull = {"wG": np.ascontiguousarray(w_in[:, :, 3072:5120]), "wbs": w_branch_sb, "wbm": w_branch_moba,
             "wout": w_out, "wfg": w_ffn_gate, "wfu": w_ffn_up, "wfd": w_ffn_down}
    lnp = np.ascontiguousarray(np.stack([np.concatenate(
        [_pp(ln_mix_g[l]), _pp(ln_mix_b[l]), _pp(ln_ffn_g[l]), _pp(ln_ffn_b[l])], axis=1) for l in range(DEPTH)]))
    maps = []
    for c in cores:
        b, r = c // 4, c % 4
        m = {"xTall0": xTb[b], "xT0": np.ascontiguousarray(xTb[b][:, r * TOK:(r + 1) * TOK]),
             "wA": np.ascontiguousarray(np.stack([_wA_slices(w_in[l], r) for l in range(DEPTH)])),
             "lnp": lnp}
        m.update({k: np.ascontiguousarray(v) for k, v in Wfull.items()})
        m.update({"c_" + k: v for k, v in hc.items()})
        maps.append(m)
    res = run_bass_kernel_spmd(nc, maps, core_ids=cores)
    xn = np.empty_like(x)
    for c in cores:
        b, r = c // 4, c % 4
        xn[b, r * TOK:(r + 1) * TOK, :] = np.asarray(res.results[c]["out"]).T
    return xn
```
